# Optimizing a Trainium2 kernel written in Bass

```python
import math
import jax
import jax.numpy as jnp
from jax import lax
import numpy as np

D_MODEL = 2048
BATCH = 2
SEQ = 4096
DEPTH = 4

GRID_W = 64
CTX_LEN = 256
N_MIXERS = 3
N_LAYERS_A = (DEPTH + 2) // N_MIXERS
N_LAYERS_B = (DEPTH + 1) // N_MIXERS
N_LAYERS_C = DEPTH // N_MIXERS
N_MOD = 9
D_FF = 5632
RMS_EPS = 1e-6
NEG_INF = -1e30
CHUNK = 128
A_WIDTH = D_MODEL
A_GROUPS = 16
A_GROUP_DIM = A_WIDTH // A_GROUPS
N_HEADS = 16
HEAD_DIM = D_MODEL // N_HEADS
WIN_H = 8
WIN_W = 16
C_WIDTH = D_MODEL
C_GROUP = 16
C_GROUPS = C_WIDTH // C_GROUP
C_STATE = 64
DT_MIN = 1e-3
DT_MAX = 1e-1

kernel_name = "hybrid_dit_gmlp_nat_s5"


def _rms_norm(x, gain):
    xf = x.astype(jnp.float32)
    xf = xf * lax.rsqrt(jnp.mean(xf * xf, axis=-1, keepdims=True) + RMS_EPS)
    return (xf * gain.astype(jnp.float32)).astype(x.dtype)


def _adaln(cond, w, b):
    m = jax.nn.silu(cond) @ w + b
    return m.reshape(cond.shape[0], N_MOD, D_MODEL)


def _sublayer_in(h, gain, mod, s):
    shift = mod[:, 3 * s, None, :]
    scale = mod[:, 3 * s + 1, None, :]
    return _rms_norm(h, gain) * (1.0 + scale) + shift


def _sublayer_gate(mod, s):
    return mod[:, 3 * s + 2, None, :]


def _swiglu(x, w_gu, w_down):
    g, u = jnp.split(x @ w_gu, 2, axis=-1)
    return (jax.nn.silu(g) * u) @ w_down


def _chunk_gmlp(x, w_in, v_gain, w_s, b_s, w_out):
    bsz, length, _ = x.shape
    u, v = jnp.split(jax.nn.gelu(x @ w_in), 2, axis=-1)
    v = _rms_norm(v, v_gain).reshape(bsz, length // CHUNK, CHUNK, A_GROUPS, A_GROUP_DIM)
    s = jnp.einsum("gpq,bnqgc->bnpgc", w_s, v) + b_s.T[None, None, :, :, None]
    return (u * s.reshape(bsz, length, A_WIDTH)) @ w_out


def _nat_project(h, w_qkv, q_gain, k_gain):
    bsz, length, _ = h.shape
    qkv = (h @ w_qkv).reshape(bsz, length, 3, N_HEADS, HEAD_DIM)
    return _rms_norm(qkv[:, :, 0], q_gain), _rms_norm(qkv[:, :, 1], k_gain), qkv[:, :, 2]


def _neighbourhood_attention(h, hc, w_qkv, q_gain, k_gain, rpb, w_out, ctx_out):
    bsz, length, _ = h.shape
    rows = length // GRID_W
    kh = min(WIN_H, rows)
    scale = HEAD_DIM ** -0.5
    q, k, v = _nat_project(h, w_qkv, q_gain, k_gain)
    qc, kc, vc = _nat_project(hc, w_qkv, q_gain, k_gain)
    grid = (bsz, rows, GRID_W, N_HEADS, HEAD_DIM)
    q, k, v = q.reshape(grid), k.reshape(grid), v.reshape(grid)
    qcol = jnp.arange(GRID_W)[:, None]
    kcol = jnp.arange(GRID_W)[None, :]
    cstart = jnp.clip(qcol - WIN_W // 2, 0, GRID_W - WIN_W)
    col_valid = (kcol >= cstart) & (kcol < cstart + WIN_W)
    dc_idx = jnp.clip(kcol - qcol, 1 - WIN_W, WIN_W - 1) + (WIN_W - 1)
    n_win = kh * GRID_W

    def row_block(r):
        rstart = jnp.clip(r - kh // 2, 0, rows - kh)
        q_r = lax.dynamic_index_in_dim(q, r, axis=1, keepdims=False)
        k_r = lax.dynamic_slice_in_dim(k, rstart, kh, axis=1)
        v_r = lax.dynamic_slice_in_dim(v, rstart, kh, axis=1)
        dr_idx = rstart + jnp.arange(kh) - r + (WIN_H - 1)
        bias = rpb[:, dr_idx[None, :, None], dc_idx[:, None, :]]
        s_win = jnp.einsum("bqhd,bjkhd->bhqjk", q_r, k_r, preferred_element_type=jnp.float32) * scale
        s_win = jnp.where(col_valid[:, None, :], s_win + bias, NEG_INF)
        s_ctx = jnp.einsum("bqhd,bchd->bhqc", q_r, kc, preferred_element_type=jnp.float32) * scale
        p = jax.nn.softmax(jnp.concatenate([s_win.reshape(bsz, N_HEADS, GRID_W, n_win), s_ctx], axis=-1), axis=-1)
        p_win = p[..., :n_win].reshape(bsz, N_HEADS, GRID_W, kh, GRID_W).astype(v.dtype)
        p_ctx = p[..., n_win:].astype(vc.dtype)
        return jnp.einsum("bhqjk,bjkhd->bqhd", p_win, v_r) + jnp.einsum("bhqc,bchd->bqhd", p_ctx, vc)

    o = lax.map(row_block, jnp.arange(rows))
    y = jnp.moveaxis(o, 0, 1).reshape(bsz, length, D_MODEL) @ w_out
    y_ctx = None
    if ctx_out:
        s = jnp.einsum("bqhd,bkhd->bhqk", qc, kc, preferred_element_type=jnp.float32) * scale
        p = jax.nn.softmax(s, axis=-1).astype(vc.dtype)
        y_ctx = jnp.einsum("bhqk,bkhd->bqhd", p, vc).reshape(bsz, hc.shape[1], D_MODEL) @ w_out
    return y, y_ctx


def _ssm_combine(e1, e2):
    a1, b1 = e1
    a2, b2 = e2
    return a1 * a2, a2 * b1 + b2


def _s5_direction(u, uc, a_re, a_im, log_dt, b_re, b_im, c_re, c_im, reverse, ctx_out):
    f32 = jnp.float32
    lam = lax.complex(a_re.astype(f32), a_im.astype(f32))
    dt = jnp.exp(log_dt.astype(f32))[:, None]
    a_bar = jnp.exp(lam * dt)
    b_bar = ((a_bar - 1.0) / lam)[..., None] * lax.complex(b_re.astype(f32), b_im.astype(f32))
    b_bar_re, b_bar_im = jnp.real(b_bar), jnp.imag(b_bar)
    c_re32, c_im32 = c_re.astype(f32), c_im.astype(f32)

    def drive(w):
        wg = w.astype(f32).reshape(w.shape[0], w.shape[1], C_GROUPS, C_GROUP)
        return lax.complex(jnp.einsum("gpc,blgc->blgp", b_bar_re, wg),
                           jnp.einsum("gpc,blgc->blgp", b_bar_im, wg))

    def scan(bu):
        a = jnp.broadcast_to(a_bar, (1,) + bu.shape[1:])
        return lax.associative_scan(_ssm_combine, (a, bu), reverse=reverse, axis=1)[1]

    def readout(hs):
        y = (jnp.einsum("gcp,blgp->blgc", c_re32, jnp.real(hs))
             - jnp.einsum("gcp,blgp->blgc", c_im32, jnp.imag(hs)))
        return y.reshape(hs.shape[0], hs.shape[1], C_WIDTH)

    hs_ctx = scan(drive(uc))
    h0 = hs_ctx[:, 0] if reverse else hs_ctx[:, -1]
    start = -1 if reverse else 0
    bu = drive(u).at[:, start].add(a_bar * h0)
    y = readout(scan(bu))
    y_ctx = readout(hs_ctx) if ctx_out else None
    return y, y_ctx


def _glu(y, w_glu):
    a, g = jnp.split(jax.nn.gelu(y) @ w_glu, 2, axis=-1)
    return a * jax.nn.sigmoid(g)


def _s5_mixer(h, hc, w_in, a_re, a_im, log_dt, b_re, b_im, c_re, c_im, d_skip, w_glu, ctx_out):
    u = h @ w_in
    uc = hc @ w_in
    y = d_skip * u.astype(jnp.float32)
    y_ctx = d_skip * uc.astype(jnp.float32) if ctx_out else None
    for di, rev in enumerate((False, True)):
        yd, yd_ctx = _s5_direction(u, uc, a_re[di], a_im[di], log_dt[di], b_re[di], b_im[di],
                                   c_re[di], c_im[di], rev, ctx_out)
        y = y + yd
        if ctx_out:
            y_ctx = y_ctx + yd_ctx
    out_ctx = _glu(y_ctx, w_glu) if ctx_out else None
    return _glu(y, w_glu), out_ctx


def setup_inputs(seed: int = 0) -> dict:
    key = jax.random.key(seed)
    ks = iter(jax.random.split(key, 40))
    f32 = jnp.float32

    def nrm(shape, s):
        return jax.random.normal(next(ks), shape, f32) * s

    D = D_MODEL
    n_idx = jnp.arange(C_STATE, dtype=f32)
    return {
        "x": nrm((BATCH, SEQ, D), 1.0),
        "c": nrm((BATCH, D), 1.0),
        "ctx": nrm((BATCH, CTX_LEN, D), 1.0),
        "c_ctx": nrm((D,), 1.0),
        "w_ada": nrm((DEPTH, D, N_MOD * D), 0.5 * D ** -0.5),
        "b_ada": nrm((DEPTH, N_MOD * D), 0.02),
        "norm_g": 1.0 + nrm((DEPTH, 3, D), 0.02),
        "ffn_w_gu": nrm((DEPTH, 2, D, 2 * D_FF), D ** -0.5),
        "ffn_w_down": nrm((DEPTH, 2, D_FF, D), D_FF ** -0.5),
        "a_w_in": nrm((N_LAYERS_A, D, 2 * A_WIDTH), D ** -0.5),
        "a_v_gain": 1.0 + nrm((N_LAYERS_A, A_WIDTH), 0.02),
        "a_w_s": nrm((N_LAYERS_A, A_GROUPS, CHUNK, CHUNK), CHUNK ** -0.5),
        "a_b_s": nrm((N_LAYERS_A, A_GROUPS, CHUNK), 0.02),
        "a_w_out": nrm((N_LAYERS_A, A_WIDTH, D), A_WIDTH ** -0.5),
        "b_w_qkv": nrm((N_LAYERS_B, D, 3 * D), D ** -0.5),
        "b_q_gain": 1.0 + nrm((N_LAYERS_B, HEAD_DIM), 0.02),
        "b_k_gain": 1.0 + nrm((N_LAYERS_B, HEAD_DIM), 0.02),
        "b_rpb": nrm((N_LAYERS_B, N_HEADS, 2 * WIN_H - 1, 2 * WIN_W - 1), 0.02),
        "b_w_out": nrm((N_LAYERS_B, D, D), D ** -0.5),
        "c_w_in": nrm((N_LAYERS_C, D, C_WIDTH), D ** -0.5),
        "c_a_re": -0.5 + nrm((N_LAYERS_C, 2, C_GROUPS, C_STATE), 0.01),
        "c_a_im": math.pi * n_idx + nrm((N_LAYERS_C, 2, C_GROUPS, C_STATE), 0.01),
        "c_log_dt": jax.random.uniform(next(ks), (N_LAYERS_C, 2, C_GROUPS), f32,
                                       minval=math.log(DT_MIN), maxval=math.log(DT_MAX)),
        "c_b_re": nrm((N_LAYERS_C, 2, C_GROUPS, C_STATE, C_GROUP), (2 * C_GROUP) ** -0.5),
        "c_b_im": nrm((N_LAYERS_C, 2, C_GROUPS, C_STATE, C_GROUP), (2 * C_GROUP) ** -0.5),
        "c_c_re": nrm((N_LAYERS_C, 2, C_GROUPS, C_GROUP, C_STATE), (2 * C_STATE) ** -0.5),
        "c_c_im": nrm((N_LAYERS_C, 2, C_GROUPS, C_GROUP, C_STATE), (2 * C_STATE) ** -0.5),
        "c_d": nrm((N_LAYERS_C, C_WIDTH), 1.0),
        "c_w_glu": nrm((N_LAYERS_C, C_WIDTH, 2 * D), C_WIDTH ** -0.5),
    }


def reference(x, c, ctx, c_ctx, w_ada, b_ada, norm_g, ffn_w_gu, ffn_w_down,
              a_w_in, a_v_gain, a_w_s, a_b_s, a_w_out,
              b_w_qkv, b_q_gain, b_k_gain, b_rpb, b_w_out,
              c_w_in, c_a_re, c_a_im, c_log_dt, c_b_re, c_b_im, c_c_re, c_c_im, c_d, c_w_glu):
    h, hc = x, ctx
    for i in range(DEPTH):
        kind = i % N_MIXERS
        j = i // N_MIXERS
        last = i == DEPTH - 1
        ctx_in = (not last) or kind != 0
        ctx_out = not last
        mod = _adaln(c, w_ada[i], b_ada[i])
        mod_c = _adaln(c_ctx[None], w_ada[i], b_ada[i])

        h = h + 0.5 * _sublayer_gate(mod, 0) * _swiglu(_sublayer_in(h, norm_g[i, 0], mod, 0),
                                                       ffn_w_gu[i, 0], ffn_w_down[i, 0])
        if ctx_in:
            hc = hc + 0.5 * _sublayer_gate(mod_c, 0) * _swiglu(_sublayer_in(hc, norm_g[i, 0], mod_c, 0),
                                                               ffn_w_gu[i, 0], ffn_w_down[i, 0])
        xin = _sublayer_in(h, norm_g[i, 1], mod, 1)
        if kind == 0:
            y = _chunk_gmlp(xin, a_w_in[j], a_v_gain[j], a_w_s[j], a_b_s[j], a_w_out[j])
            y_c = None
            if ctx_out:
                xin_c = _sublayer_in(hc, norm_g[i, 1], mod_c, 1)
                y_c = _chunk_gmlp(xin_c, a_w_in[j], a_v_gain[j], a_w_s[j], a_b_s[j], a_w_out[j])
        elif kind == 1:
            xin_c = _sublayer_in(hc, norm_g[i, 1], mod_c, 1)
            y, y_c = _neighbourhood_attention(xin, xin_c, b_w_qkv[j], b_q_gain[j], b_k_gain[j],
                                              b_rpb[j], b_w_out[j], ctx_out)
        else:
            xin_c = _sublayer_in(hc, norm_g[i, 1], mod_c, 1)
            y, y_c = _s5_mixer(xin, xin_c, c_w_in[j], c_a_re[j], c_a_im[j], c_log_dt[j],
                               c_b_re[j], c_b_im[j], c_c_re[j], c_c_im[j], c_d[j], c_w_glu[j], ctx_out)
        h = h + _sublayer_gate(mod, 1) * y
        h = h + 0.5 * _sublayer_gate(mod, 2) * _swiglu(_sublayer_in(h, norm_g[i, 2], mod, 2),
                                                       ffn_w_gu[i, 1], ffn_w_down[i, 1])
        if ctx_out:
            hc = hc + _sublayer_gate(mod_c, 1) * y_c
            hc = hc + 0.5 * _sublayer_gate(mod_c, 2) * _swiglu(_sublayer_in(hc, norm_g[i, 2], mod_c, 2),
                                                               ffn_w_gu[i, 1], ffn_w_down[i, 1])
    return h
```

```python
import numpy as np
import concourse.bass as bass
import concourse.mybir as mybir
from concourse.bass_utils import run_bass_kernel_spmd
from concourse.alu_op_type import AluOpType as ALU

F32 = mybir.dt.float32
BF16 = mybir.dt.bfloat16
AF = mybir.ActivationFunctionType

D = 2048
KC = 16
DFF = 5632
FC = 44
NCORES = 8
TL = 1024
TCX = 128
T = TL + TCX
EPS = 1e-6


class Buf:
    __slots__ = ("lw", "rd", "name")

    def __init__(self, name=""):
        self.lw = None
        self.rd = {}
        self.name = name


class Prog:
    CENG = ("pe", "act", "dve", "pool")

    def __init__(self, nc, stack, n_dsem=20):
        self.nc = nc
        self.eng = {"pe": nc.tensor, "act": nc.scalar, "dve": nc.vector,
                    "pool": nc.gpsimd, "sp": nc.sync}
        self.q = {e: [] for e in self.eng}
        self.cnt = {e: 0 for e in self.CENG}
        self.seen = {e: {} for e in self.eng}
        self.csem = {e: stack.enter_context(nc.semaphore("c_" + e)) for e in self.CENG}
        self.dsem = {}
        self.dcum = {}
        self.drr = {}
        for qn in ("sp", "pool", "act"):
            n = n_dsem if qn != "act" else 6
            self.dsem[qn] = [stack.enter_context(nc.semaphore("d_%s%d" % (qn, i))) for i in range(n)]
            self.dcum[qn] = [0] * n
            self.drr[qn] = 0

    def _need(self, eng, tok, waits):
        if tok is None:
            return
        key, val = tok
        if key == ("c", "pe") and eng == "pe":
            return
        if self.seen[eng].get(key, 0) >= val:
            return
        if waits.get(key, 0) < val:
            waits[key] = val

    def _deps(self, eng, reads, writes):
        waits = {}
        for b in reads:
            self._need(eng, b.lw, waits)
        for b in writes:
            self._need(eng, b.lw, waits)
            for k, v in b.rd.items():
                self._need(eng, (k, v), waits)
        for k, v in waits.items():
            self.seen[eng][k] = v
        return waits

    def _commit(self, tok, reads, writes):
        key, val = tok
        for b in reads:
            if b.rd.get(key, 0) < val:
                b.rd[key] = val
        for b in writes:
            b.lw = tok
            b.rd = {}

    def op(self, eng, fn, reads=(), writes=()):
        waits = self._deps(eng, reads, writes)
        self.cnt[eng] += 1
        tok = (("c", eng), self.cnt[eng])
        self.q[eng].append((list(waits.items()), fn, ("c", eng, 1)))
        self._commit(tok, reads, writes)
        return tok

    def dma(self, qn, fns, reads=(), writes=()):
        if not isinstance(fns, (list, tuple)):
            fns = [fns]
        waits = self._deps(qn, reads, writes)
        i = self.drr[qn]
        self.drr[qn] = (i + 1) % len(self.dsem[qn])
        key = ("d", qn, i)
        prev = self.dcum[qn][i]
        if prev > 0 and self.seen[qn].get(key, 0) < prev:
            waits[key] = prev
            self.seen[qn][key] = prev
        for j, fn in enumerate(fns):
            self.dcum[qn][i] += 16
            self.q[qn].append((list(waits.items()) if j == 0 else [], fn, ("d", qn, i)))
        tok = (key, self.dcum[qn][i])
        self._commit(tok, reads, writes)
        return tok

    def wait_tokens(self, eng, toks):
        waits = {}
        for t in toks:
            self._need(eng, t, waits)
        for k, v in waits.items():
            self.seen[eng][k] = v
        self.q[eng].append((list(waits.items()), None, None))

    def _sem(self, key):
        if key[0] == "c":
            return self.csem[key[1]]
        return self.dsem[key[1]][key[2]]

    def emit(self, block):
        decos = {"pe": block.tensor, "act": block.scalar, "dve": block.vector,
                 "pool": block.gpsimd, "sp": block.sync}
        for e in self.eng:
            items = self.q[e]
            if not items:
                continue

            def body(engine, items=items):
                for waits, fn, inc in items:
                    for key, val in waits:
                        engine.wait_ge(self._sem(key), val)
                    if fn is None:
                        continue
                    ins = fn(engine)
                    if inc[0] == "c":
                        ins.then_inc(self.csem[inc[1]], 1)
                    else:
                        ins.then_inc(self.dsem[inc[1]][inc[2]], 16)

            decos[e](body)


class Ctx:
    def __init__(self, nc, stack):
        self.nc = nc
        self.stack = stack
        self.p = Prog(nc, stack)
        self._n = 0

    def sb(self, shape, dt, name=None):
        self._n += 1
        return self.stack.enter_context(self.nc.sbuf_tensor(name or ("sb%d" % self._n), list(shape), dt))

    def ps(self, shape, dt=F32, name=None):
        self._n += 1
        return self.stack.enter_context(self.nc.psum_tensor(name or ("ps%d" % self._n), list(shape), dt))

    def din(self, name, shape, dt=F32):
        return self.nc.dram_tensor(name, list(shape), dt, kind="ExternalInput").ap()

    def dout(self, name, shape, dt=F32):
        return self.nc.dram_tensor(name, list(shape), dt, kind="ExternalOutput").ap()


class Ring:
    def __init__(self, cx, n, shape, dt, name, psum=False):
        self.t = [(cx.ps(shape, dt, "%s%d" % (name, i)) if psum else cx.sb(shape, dt, "%s%d" % (name, i)))
                  for i in range(n)]
        self.b = [Buf("%s%d" % (name, i)) for i in range(n)]
        self.i = 0

    def next(self):
        i = self.i
        self.i = (i + 1) % len(self.t)
        return self.t[i], self.b[i]


def make_consts(cx):
    p = cx.p
    ones = cx.sb([128, 128], F32, "ones")
    epsc = cx.sb([128, 1], F32, "epsc")
    b = Buf("consts")
    p.op("pool", lambda e: e.memset(ones[:], 1.0), writes=[b])
    p.op("pool", lambda e: e.memset(epsc[:], EPS), writes=[b])
    return ones, epsc, b


def load_mod(cx, modv, gv):
    p = cx.p
    m = cx.sb([128, KC, 3, 2], F32)
    g = cx.sb([128, KC], F32)
    A = cx.sb([128, KC, 2], F32)
    bm, bg, bA = Buf("m"), Buf("g"), Buf("A")
    p.dma("sp", lambda e: e.dma_start(out=m[:], in_=modv), writes=[bm])
    p.dma("sp", lambda e: e.dma_start(out=g[:], in_=gv), writes=[bg])
    for s in range(2):
        p.op("dve", lambda e, s=s: e.scalar_tensor_tensor(out=A[:, :, s], in0=m[:, :, 1, s], scalar=1.0,
                                                           in1=g[:, :], op0=ALU.add, op1=ALU.mult),
             reads=[bm, bg], writes=[bA])
    return m, A, bm, bA


def rms_adaln(cx, src, bsrc, dst, bdst, blocks, m, A, bm, bA, consts, rings):
    p = cx.p
    ones, epsc, bc = consts
    sq_r, ps_r, rs_r, tmp_r = rings
    for (t0, n, seg) in blocks:
        _rms_block(p, src, bsrc, dst, bdst, t0, n, seg, m, A, bm, bA, ones, epsc, bc, sq_r, ps_r, rs_r, tmp_r)


def _rms_block(p, src, bsrc, dst, bdst, t0, n, seg, m, A, bm, bA, ones, epsc, bc, sq_r, ps_r, rs_r, tmp_r, s0=None):
    s0 = t0 if s0 is None else s0
    ss, bss = ps_r.next()
    for kc in range(KC):
        sq, bsq = sq_r.next()
        p.op("act", lambda e, sq=sq, kc=kc: e.activation(out=sq[:, :n], in_=src[:, kc, s0:s0 + n], func=AF.Square),
             reads=[bsrc], writes=[bsq])
        p.op("pe", lambda e, sq=sq, kc=kc: e.matmul(ss[:, :n], lhsT=ones[:], rhs=sq[:, :n],
                                                    start=(kc == 0), stop=(kc == KC - 1)),
             reads=[bsq, bc], writes=[bss])
    rs, brs = rs_r.next()
    p.op("act", lambda e: e.activation(out=rs[:, :n], in_=ss[:, :n], func=AF.Sqrt, bias=epsc[:], scale=1.0 / D),
         reads=[bss, bc], writes=[brs])
    p.op("dve", lambda e: e.reciprocal(out=rs[:, :n], in_=rs[:, :n]), reads=[brs], writes=[brs])
    for kc in range(KC):
        tmp, btmp = tmp_r.next()
        p.op("dve", lambda e, tmp=tmp, kc=kc: e.scalar_tensor_tensor(
            out=tmp[:, :n], in0=src[:, kc, s0:s0 + n], scalar=A[:, kc, seg:seg + 1], in1=rs[:, :n],
            op0=ALU.mult, op1=ALU.mult), reads=[bsrc, brs, bA], writes=[btmp])
        p.op("act", lambda e, tmp=tmp, kc=kc: e.activation(out=dst[:, kc, t0:t0 + n], in_=tmp[:, :n],
                                                           func=AF.Identity, bias=m[:, kc, 0, seg:seg + 1], scale=1.0),
             reads=[btmp, bm], writes=[bdst])


BLOCKS = [(0, 512, 0), (512, 512, 0), (1024, 128, 1)]


def wview(w):
    return w.rearrange("(kc p) f -> p kc f", p=128)


def build_ffn(dbg=False):
    from contextlib import ExitStack
    nc = bass.Bass("TRN2", target_bir_lowering=False)
    with ExitStack() as stack:
        cx = Ctx(nc, stack)
        p = cx.p
        hT = cx.din("hT", [D, T])
        modv = cx.din("modv", [128, KC, 3, 2])
        gv = cx.din("gv", [128, KC])
        wgu = cx.din("wgu", [D, 2 * DFF])
        wdn = cx.din("wdn", [DFF, D])
        oT = cx.dout("oT", [D, T])

        consts = make_consts(cx)
        h = cx.sb([128, KC, T], F32, "h")
        bh = Buf("h")
        hv = hT.rearrange("(kc p) t -> p kc t", p=128)
        for q in range(4):
            p.dma("sp", lambda e, q=q: e.dma_start(out=h[:, 4 * q:4 * q + 4, :], in_=hv[:, 4 * q:4 * q + 4, :]),
                  writes=[bh])
        m, A, bm, bA = load_mod(cx, modv, gv)
        G = cx.sb([128, KC, 2], F32, "G")
        bG = Buf("G")
        p.op("dve", lambda e: e.tensor_scalar(out=G[:], in0=m[:, :, 2, :], scalar1=0.5, scalar2=None, op0=ALU.mult),
             reads=[bm], writes=[bG])

        xn = cx.sb([128, KC, T], BF16, "xn")
        bxn = Buf("xn")
        rings = (Ring(cx, 2, [128, 512], F32, "sq"), Ring(cx, 1, [128, 512], F32, "ssps", psum=True),
                 Ring(cx, 2, [128, 512], F32, "rs"), Ring(cx, 2, [128, 512], F32, "tmp"))
        rms_adaln(cx, h, bh, xn, bxn, BLOCKS, m, A, bm, bA, consts, rings)

        if dbg:
            xo = cx.dout("xo", [128, KC, T], BF16)
            p.wait_tokens("sp", [p.dma("sp", lambda e: e.dma_start(out=xo, in_=xn[:]), reads=[bxn])])
        ffn_core(cx, xn, bxn, h, bh, G, bG, wgu, wdn)

        ov = oT.rearrange("(kc p) t -> p kc t", p=128)
        toks = []
        for q in range(4):
            toks.append(p.dma("sp", lambda e, q=q: e.dma_start(out=ov[:, 4 * q:4 * q + 4, :], in_=h[:, 4 * q:4 * q + 4, :]),
                              reads=[bh]))
        p.wait_tokens("sp", toks)
        with nc.Block() as block:
            p.emit(block)
    return nc


def ffn_core(cx, xn, bxn, h, bh, G, bG, wgu, wdn, GRP=4):
    p = cx.p
    wguv = wview(wgu)
    wdnv = wdn.rearrange("(fc p) d -> p fc d", p=128)
    wg_r = Ring(cx, 2, [128, KC, 256], BF16, "wg")
    wu_r = Ring(cx, 2, [128, KC, 256], BF16, "wu")
    wd_r = Ring(cx, 2, [128, GRP, D], BF16, "wd")
    act_r = Ring(cx, 2, [128, GRP, T], BF16, "act")
    gps_r = Ring(cx, 2, [128, 512], F32, "gps", psum=True)
    ups_r = Ring(cx, 2, [128, 512], F32, "ups", psum=True)
    yps_r = Ring(cx, 2, [128, 512], F32, "yps", psum=True)
    sg_r = Ring(cx, 2, [128, 512], F32, "sg")
    wg = wu = None
    for grp in range(FC // GRP):
        act, bact = act_r.next()
        wd, bwd = wd_r.next()
        for fl in range(GRP):
            p.dma("pool", lambda e, wd=wd, fl=fl, grp=grp: e.dma_start(out=wd[:, fl, :], in_=wdnv[:, grp * GRP + fl, :]),
                  writes=[bwd])
        for fl in range(GRP):
            fc = grp * GRP + fl
            if fc % 2 == 0:
                wg, bwg = wg_r.next()
                wu, bwu = wu_r.next()
                p.dma("pool", lambda e, wg=wg, fc=fc: e.dma_start(out=wg[:], in_=wguv[:, :, fc * 128:fc * 128 + 256]),
                      writes=[bwg])
                p.dma("pool", lambda e, wu=wu, fc=fc: e.dma_start(out=wu[:], in_=wguv[:, :, DFF + fc * 128:DFF + fc * 128 + 256]),
                      writes=[bwu])
            c0 = (fc % 2) * 128
            for (t0, n, seg) in BLOCKS:
                gps, bgps = gps_r.next()
                ups, bups = ups_r.next()
                for kc in range(KC):
                    p.op("pe", lambda e, gps=gps, wg=wg, kc=kc, c0=c0, t0=t0, n=n: e.matmul(
                        gps[:, :n], lhsT=wg[:, kc, c0:c0 + 128], rhs=xn[:, kc, t0:t0 + n],
                        start=(kc == 0), stop=(kc == KC - 1)), reads=[bwg, bxn], writes=[bgps])
                for kc in range(KC):
                    p.op("pe", lambda e, ups=ups, wu=wu, kc=kc, c0=c0, t0=t0, n=n: e.matmul(
                        ups[:, :n], lhsT=wu[:, kc, c0:c0 + 128], rhs=xn[:, kc, t0:t0 + n],
                        start=(kc == 0), stop=(kc == KC - 1)), reads=[bwu, bxn], writes=[bups])
                sg, bsg = sg_r.next()
                p.op("act", lambda e, sg=sg, gps=gps, n=n: e.activation(out=sg[:, :n], in_=gps[:, :n], func=AF.Silu),
                     reads=[bgps], writes=[bsg])
                p.op("dve", lambda e, sg=sg, ups=ups, act=act, fl=fl, t0=t0, n=n: e.tensor_tensor(
                    out=act[:, fl, t0:t0 + n], in0=ups[:, :n], in1=sg[:, :n], op=ALU.mult),
                    reads=[bups, bsg], writes=[bact])
        for dc in range(KC):
            for (t0, n, seg) in BLOCKS:
                yps, byps = yps_r.next()
                for fl in range(GRP):
                    p.op("pe", lambda e, yps=yps, wd=wd, fl=fl, dc=dc, act=act, t0=t0, n=n: e.matmul(
                        yps[:, :n], lhsT=wd[:, fl, dc * 128:(dc + 1) * 128], rhs=act[:, fl, t0:t0 + n],
                        start=(fl == 0), stop=(fl == GRP - 1)), reads=[bwd, bact], writes=[byps])
                p.op("dve", lambda e, yps=yps, dc=dc, t0=t0, n=n, seg=seg: e.scalar_tensor_tensor(
                    out=h[:, dc, t0:t0 + n], in0=yps[:, :n], scalar=G[:, dc, seg:seg + 1], in1=h[:, dc, t0:t0 + n],
                    op0=ALU.mult, op1=ALU.add), reads=[byps, bG, bh], writes=[bh])


NF_ADA = 4 * 9 * KC // NCORES


def build_ada():
    from contextlib import ExitStack
    nc = bass.Bass("TRN2", target_bir_lowering=False)
    with ExitStack() as stack:
        cx = Ctx(nc, stack)
        p = cx.p
        condT = cx.din("condT", [128, KC, 4])
        wa = cx.din("wa", [D, NF_ADA * 128])
        ba = cx.din("ba", [128, NF_ADA])
        mo = cx.dout("mo", [128, NF_ADA, 4])
        ct = cx.sb([128, KC, 4], F32)
        cs = cx.sb([128, KC, 4], BF16)
        bt = cx.sb([128, NF_ADA], F32)
        res = cx.sb([128, NF_ADA, 4], F32)
        bct, bcs, bbt, bres = Buf(), Buf(), Buf(), Buf()
        p.dma("sp", lambda e: e.dma_start(out=ct[:], in_=condT), writes=[bct])
        p.dma("sp", lambda e: e.dma_start(out=bt[:], in_=ba), writes=[bbt])
        p.op("act", lambda e: e.activation(out=cs[:], in_=ct[:], func=AF.Silu), reads=[bct], writes=[bcs])
        w_r = Ring(cx, 3, [128, KC, 512], BF16, "w")
        ps_r = Ring(cx, 2, [128, 4, 4], F32, "ps", psum=True)
        wav = wview(wa)
        for g4 in range(NF_ADA // 4):
            w, bw = w_r.next()
            p.dma("pool", lambda e, w=w, g4=g4: e.dma_start(out=w[:], in_=wav[:, :, g4 * 512:(g4 + 1) * 512]), writes=[bw])
            ps, bps = ps_r.next()
            for j in range(4):
                for kc in range(KC):
                    p.op("pe", lambda e, ps=ps, w=w, j=j, kc=kc: e.matmul(
                        ps[:, j, :], lhsT=w[:, kc, j * 128:(j + 1) * 128], rhs=cs[:, kc, :],
                        start=(kc == 0), stop=(kc == KC - 1)), reads=[bw, bcs], writes=[bps])
            for j in range(4):
                f = g4 * 4 + j
                p.op("dve", lambda e, ps=ps, j=j, f=f: e.tensor_scalar(out=res[:, f, :], in0=ps[:, j, :], scalar1=bt[:, f:f + 1],
                                                                      scalar2=None, op0=ALU.add),
                     reads=[bps, bbt], writes=[bres])
        tok = p.dma("sp", lambda e: e.dma_start(out=mo, in_=res[:]), reads=[bres])
        p.wait_tokens("sp", [tok])
        with nc.Block() as block:
            p.emit(block)
    return nc


def mm(p, out, lhsT, rhs, start, stop, reads, writes):
    return p.op("pe", lambda e: e.matmul(out, lhsT=lhsT, rhs=rhs, start=start, stop=stop), reads=reads, writes=writes)


def actf(p, out, in_, func, reads, writes, bias=None, scale=None, accum_out=None):
    kw = {}
    if bias is not None:
        kw["bias"] = bias
    if scale is not None:
        kw["scale"] = scale
    if accum_out is not None:
        kw["accum_out"] = accum_out
    return p.op("act", lambda e: e.activation(out=out, in_=in_, func=func, **kw), reads=reads, writes=writes)


def tt(p, eng, out, in0, in1, op, reads, writes):
    return p.op(eng, lambda e: e.tensor_tensor(out=out, in0=in0, in1=in1, op=op), reads=reads, writes=writes)


def ts(p, eng, out, in0, s1, s2, op0, op1, reads, writes):
    if op1 is None:
        return p.op(eng, lambda e: e.tensor_scalar(out=out, in0=in0, scalar1=s1, scalar2=None, op0=op0), reads=reads, writes=writes)
    return p.op(eng, lambda e: e.tensor_scalar(out=out, in0=in0, scalar1=s1, scalar2=s2, op0=op0, op1=op1), reads=reads, writes=writes)


def stt(p, out, in0, scalar, in1, op0, op1, reads, writes):
    return p.op("dve", lambda e: e.scalar_tensor_tensor(out=out, in0=in0, scalar=scalar, in1=in1, op0=op0, op1=op1),
                reads=reads, writes=writes)


def dmaq(p, q, out, in_, reads=(), writes=()):
    return p.dma(q, lambda e: e.dma_start(out=out, in_=in_), reads=reads, writes=writes)


class Gelu:
    def __init__(self, cx, width=512):
        self.a = Ring(cx, 2, [128, width], F32, "gl_a")
        self.b = Ring(cx, 2, [128, width], F32, "gl_b")

    def __call__(self, p, out, ps, n, rd, wr, accum_sq=None):
        a, ba = self.a.next()
        b, bb = self.b.next()
        actf(p, a[:, :n], ps, AF.Square, rd, [ba])
        ts(p, "dve", a[:, :n], a[:, :n], 0.044715, 1.0, ALU.mult, ALU.add, [ba], [ba])
        tt(p, "dve", a[:, :n], a[:, :n], ps, ALU.mult, [ba] + rd, [ba])
        actf(p, b[:, :n], a[:, :n], AF.Sigmoid, [ba], [bb], scale=1.5957691216057308)
        tt(p, "dve", out, b[:, :n], ps, ALU.mult, [bb] + rd, wr)


def load_h(cx, hT, name="h"):
    p = cx.p
    h = cx.sb([128, KC, T], F32, name)
    bh = Buf(name)
    hv = hT.rearrange("(kc p) t -> p kc t", p=128)
    for q in range(4):
        dmaq(p, "sp", h[:, 4 * q:4 * q + 4, :], hv[:, 4 * q:4 * q + 4, :], writes=[bh])
    return h, bh


def store_h(cx, oT, h, bh):
    p = cx.p
    ov = oT.rearrange("(kc p) t -> p kc t", p=128)
    toks = [dmaq(p, "sp", ov[:, 4 * q:4 * q + 4, :], h[:, 4 * q:4 * q + 4, :], reads=[bh]) for q in range(4)]
    p.wait_tokens("sp", toks)


def rms_adaln_stream(cx, hT, dst, bdst, m, A, bm, bA, consts, rings, blocks=BLOCKS):
    p = cx.p
    ones, epsc, bc = consts
    hb = cx.sb([128, KC, 512], F32, "hblk")
    bhb = Buf("hblk")
    hv = hT.rearrange("(kc p) t -> p kc t", p=128)
    for (t0, n, seg) in blocks:
        for q in range(2):
            dmaq(p, "sp", hb[:, 8 * q:8 * q + 8, :n], hv[:, 8 * q:8 * q + 8, t0:t0 + n], writes=[bhb])
        _rms_block(p, hb, bhb, dst, bdst, t0, n, seg, m, A, bm, bA, ones, epsc, bc, *rings, s0=0)


def make_residual_evac(cx, hT, oT, m, bm, ring, toks, gate_idx=2):
    p = cx.p
    hv = hT.rearrange("(kc p) t -> p kc t", p=128)
    ov = oT.rearrange("(kc p) t -> p kc t", p=128)

    def evac(j, t0, n, seg, ps, bps):
        hb, bhb = ring.next()
        dmaq(p, "sp", hb[:, :n], hv[:, j, t0:t0 + n], writes=[bhb])
        stt(p, hb[:, :n], ps[:, :n], m[:, j, gate_idx, seg:seg + 1], hb[:, :n], ALU.mult, ALU.add, [bps, bm, bhb], [bhb])
        toks.append(dmaq(p, "sp", ov[:, j, t0:t0 + n], hb[:, :n], reads=[bhb]))

    return evac


def norm_rings(cx):
    return (Ring(cx, 2, [128, 512], F32, "sq"), Ring(cx, 1, [128, 512], F32, "ssps", psum=True),
            Ring(cx, 2, [128, 512], F32, "rs"), Ring(cx, 2, [128, 512], F32, "tmp"))


def linear_fm(cx, wv, col0, nchunks, x, bx, evac, w_r, ps_r, nk=KC, blocks=BLOCKS):
    p = cx.p
    w = bw = None
    for j in range(nchunks):
        if j % 2 == 0:
            w, bw = w_r.next()
            c = col0 + j * 128
            wd = 256 if j + 1 < nchunks else 128
            dmaq(p, "pool", w[:, :, :wd], wv[:, :, c:c + wd], writes=[bw])
        c0 = (j % 2) * 128
        for (t0, n, seg) in blocks:
            ps, bps = ps_r.next()
            for kc in range(nk):
                mm(p, ps[:, :n], w[:, kc, c0:c0 + 128], x[:, kc, t0:t0 + n], kc == 0, kc == nk - 1, [bw, bx], [bps])
            evac(j, t0, n, seg, ps, bps)


NT = T // 128


def build_mixa():
    from contextlib import ExitStack
    nc = bass.Bass("TRN2", target_bir_lowering=False)
    with ExitStack() as stack:
        cx = Ctx(nc, stack)
        p = cx.p
        hT = cx.din("hT", [D, T])
        modv = cx.din("modv", [128, KC, 3, 2])
        gv = cx.din("gv", [128, KC])
        w_in = cx.din("w_in", [D, 2 * D])
        vg = cx.din("vg", [128, KC])
        wsT = cx.din("wsT", [128, 16, 128])
        bsb = cx.din("bsb", [128, 16, 128])
        w_out = cx.din("w_out", [D, D])
        oT = cx.dout("oT", [D, T])

        consts = make_consts(cx)
        m, A, bm, bA = load_mod(cx, modv, gv)
        xn = cx.sb([128, KC, T], BF16, "xn")
        bxn = Buf("xn")
        nr = norm_rings(cx)
        rms_adaln_stream(cx, hT, xn, bxn, m, A, bm, bA, consts, nr)

        vgt = cx.sb([128, KC], F32, "vgt")
        wst = cx.sb([128, 16, 128], F32, "wst")
        bst = cx.sb([128, 16, 128], F32, "bst")
        bvg, bws, bbs = Buf(), Buf(), Buf()
        dmaq(p, "sp", vgt[:], vg, writes=[bvg])
        dmaq(p, "sp", wst[:], wsT, writes=[bws])
        dmaq(p, "sp", bst[:], bsb, writes=[bbs])

        gelu = Gelu(cx)
        w_r = Ring(cx, 2, [128, KC, 256], BF16, "w")
        ps_r = Ring(cx, 2, [128, 512], F32, "ps", psum=True)
        winv = wview(w_in)

        uT = cx.sb([128, KC, T], BF16, "uT")
        buT = Buf("uT")

        def evac_u(j, t0, n, seg, ps, bps):
            gelu(p, uT[:, j, t0:t0 + n], ps[:, :n], n, [bps], [buT])

        linear_fm(cx, winv, 0, KC, xn, bxn, evac_u, w_r, ps_r)

        v = cx.sb([128, NT, D], BF16, "v")
        bv = Buf("v")
        NCB = 8
        ssq = cx.sb([128, NT, NCB], F32, "ssq")
        bssq = Buf("ssq")
        for cb in range(NCB):
            wv_, bwv = w_r.next()
            dmaq(p, "pool", wv_[:, :, :], winv[:, :, D + cb * 256:D + (cb + 1) * 256], writes=[bwv])
            for ti in range(NT):
                ps, bps = ps_r.next()
                for kc in range(KC):
                    mm(p, ps[:, :256], xn[:, kc, ti * 128:(ti + 1) * 128], wv_[:, kc, :], kc == 0, kc == KC - 1, [bxn, bwv], [bps])
                vt, bvt = nr[3].next()
                gelu(p, vt[:, :256], ps[:, :256], 256, [bps], [bvt])
                junk, bjunk = nr[0].next()
                actf(p, junk[:, :256], vt[:, :256], AF.Square, [bvt], [bjunk, bssq], accum_out=ssq[:, ti, cb:cb + 1])
                p.op("pool", lambda e, vt=vt, ti=ti, cb=cb: e.tensor_copy(out=v[:, ti, cb * 256:(cb + 1) * 256], in_=vt[:, :256]),
                     reads=[bvt], writes=[bv])
        rv = cx.sb([128, NT], F32, "rv")
        brv = Buf("rv")
        p.op("dve", lambda e: e.tensor_reduce(out=rv[:, :], in_=ssq[:, :, :], axis=mybir.AxisListType.X, op=ALU.add),
             reads=[bssq], writes=[brv])
        actf(p, rv[:, :], rv[:, :], AF.Sqrt, [brv, consts[2]], [brv], bias=consts[1][:], scale=1.0 / D)
        p.op("dve", lambda e: e.reciprocal(out=rv[:, :], in_=rv[:, :]), reads=[brv], writes=[brv])

        z, bz = xn, bxn
        wp_r = Ring(cx, 3, [128, 128], BF16, "wp")
        st_r = Ring(cx, 2, [128, 128], F32, "st")
        sps_r = Ring(cx, 2, [128, 128], F32, "sps", psum=True)
        for ti in range(NT):
            for g in range(16):
                wp, bwp = wp_r.next()
                ts(p, "pool", wp[:, :], wst[:, g, :], rv[:, ti:ti + 1], None, ALU.mult, None, [bws, brv], [bwp])
                sp_, bsp = sps_r.next()
                mm(p, sp_[:, :], v[:, ti, g * 128:(g + 1) * 128], wp[:, :], True, True, [bv, bwp], [bsp])
                st, bst_ = st_r.next()
                stt(p, st[:, :], sp_[:, :], vgt[:, g:g + 1], bst[:, g, :], ALU.mult, ALU.add, [bsp, bvg, bbs], [bst_])
                tt(p, "dve", z[:, g, ti * 128:(ti + 1) * 128], st[:, :], uT[:, g, ti * 128:(ti + 1) * 128], ALU.mult,
                   [bst_, buT], [bz])

        woutv = wview(w_out)

        toks = []
        evac_o = make_residual_evac(cx, hT, oT, m, bm, nr[0], toks)
        linear_fm(cx, woutv, 0, KC, z, bz, evac_o, w_r, ps_r)
        p.wait_tokens("sp", toks)
        with nc.Block() as block:
            p.emit(block)
    return nc


HD = 128
NH = 16
ATT_SCALE = HD ** -0.5


def build_nat1():
    from contextlib import ExitStack
    nc = bass.Bass("TRN2", target_bir_lowering=False)
    with ExitStack() as stack:
        cx = Ctx(nc, stack)
        p = cx.p
        hT = cx.din("hT", [D, T])
        modv = cx.din("modv", [128, KC, 3, 2])
        gv = cx.din("gv", [128, KC])
        wqkv = cx.din("wqkv", [D, 3 * D])
        qkg = cx.din("qkg", [128, 2])
        qo = cx.dout("qo", [NH, 128, T], BF16)
        ko = cx.dout("ko", [NH, 128, T], BF16)
        vo = cx.dout("vo", [T, D], BF16)

        consts = make_consts(cx)
        ones, epsc, bc = consts
        m, A, bm, bA = load_mod(cx, modv, gv)
        xn = cx.sb([128, KC, T], BF16, "xn")
        bxn = Buf("xn")
        nr = norm_rings(cx)
        rms_adaln_stream(cx, hT, xn, bxn, m, A, bm, bA, consts, nr)
        gq = cx.sb([128, 2], F32, "gq")
        bgq = Buf()
        dmaq(p, "sp", gq[:], qkg, writes=[bgq])
        ts(p, "dve", gq[:, 0:1], gq[:, 0:1], ATT_SCALE, None, ALU.mult, None, [bgq], [bgq])

        w_r = Ring(cx, 2, [128, KC, 256], BF16, "w")
        ps_r = Ring(cx, 2, [128, 512], F32, "ps", psum=True)
        wv = wview(wqkv)
        qf_r = Ring(cx, 2, [128, 512], F32, "qf")
        st_r = Ring(cx, 3, [128, 512], BF16, "stg")
        toks = []
        for which, dst in ((0, qo), (1, ko)):
            def evac(j, t0, n, seg, ps, bps, which=which, dst=dst):
                qf, bqf = qf_r.next()
                actf(p, qf[:, :n], ps[:, :n], AF.Identity, [bps], [bqf])
                sq, bsq = nr[0].next()
                actf(p, sq[:, :n], qf[:, :n], AF.Square, [bqf], [bsq])
                ss, bss = nr[1].next()
                mm(p, ss[:, :n], ones[:], sq[:, :n], True, True, [bsq, bc], [bss])
                rs, brs = nr[2].next()
                actf(p, rs[:, :n], ss[:, :n], AF.Sqrt, [bss, bc], [brs], bias=epsc[:], scale=1.0 / HD)
                p.op("dve", lambda e: e.reciprocal(out=rs[:, :n], in_=rs[:, :n]), reads=[brs], writes=[brs])
                sg, bsg = st_r.next()
                stt(p, sg[:, :n], qf[:, :n], gq[:, which:which + 1], rs[:, :n], ALU.mult, ALU.mult, [bqf, bgq, brs], [bsg])
                toks.append(dmaq(p, "sp", dst[j, :, t0:t0 + n], sg[:, :n], reads=[bsg]))

            linear_fm(cx, wv, which * D, KC, xn, bxn, evac, w_r, ps_r)
        vs = cx.sb([128, NT, D], BF16, "vs")
        bvs = Buf("vs")
        for cb in range(8):
            w, bw = w_r.next()
            dmaq(p, "pool", w[:, :, :], wv[:, :, 2 * D + cb * 256:2 * D + (cb + 1) * 256], writes=[bw])
            for ti in range(NT):
                ps, bps = ps_r.next()
                for kc in range(KC):
                    mm(p, ps[:, :256], xn[:, kc, ti * 128:(ti + 1) * 128], w[:, kc, :], kc == 0, kc == KC - 1, [bxn, bw], [bps])
                actf(p, vs[:, ti, cb * 256:(cb + 1) * 256], ps[:, :256], AF.Identity, [bps], [bvs])
        vov = vo.rearrange("(ti p) f -> p ti f", p=128)
        toks.append(dmaq(p, "sp", vov, vs[:], reads=[bvs]))
        p.wait_tokens("sp", toks)
        with nc.Block() as block:
            p.emit(block)
    return nc


NSLAB = 14
NKT = NSLAB + 2
NOFF = 7


def build_nat2():
    from contextlib import ExitStack
    nc = bass.Bass("TRN2", target_bir_lowering=False)
    with ExitStack() as stack:
        cx = Ctx(nc, stack)
        p = cx.p
        hT = cx.din("hT", [D, T])
        modv = cx.din("modv", [128, KC, 3, 2])
        qT = cx.din("qT", [NH, 128, T], BF16)
        kx = cx.din("kx", [NH, 128, NKT * 128], BF16)
        vx = cx.din("vx", [NKT * 128, D], BF16)
        bias = cx.din("bias", [NH, 128, 8 * NOFF, 128])
        w_out = cx.din("w_out", [D, D])
        oT = cx.dout("oT", [D, T])

        m = cx.sb([128, KC, 3, 2], F32, "m")
        bm = Buf("m")
        dmaq(p, "sp", m[:], modv, writes=[bm])
        onesb = cx.sb([128, 128], BF16, "onesb")
        bob = Buf("onesb")
        p.op("pool", lambda e: e.memset(onesb[:], 1.0), writes=[bob])

        q = cx.sb([128, NH, T], BF16, "q")
        bq = Buf("q")
        for hh in range(4):
            dmaq(p, "sp", q[:, 4 * hh:4 * hh + 4, :], qT.rearrange("h d t -> d h t")[:, 4 * hh:4 * hh + 4, :], writes=[bq])
        o = cx.sb([128, NH, T], BF16, "o")
        bo = Buf("o")

        kh_r = Ring(cx, 2, [128, NKT * 128], BF16, "kh")
        vh_r = Ring(cx, 2, [128, NKT, 128], BF16, "vh")
        bi_r = Ring(cx, 2, [128, 8 * NOFF, 128], F32, "bi")
        s_r = Ring(cx, 2, [128, 128], F32, "sps", psum=True)
        o_r = Ring(cx, 2, [128, 128], F32, "ops", psum=True)
        d_r = Ring(cx, 2, [128, 128], F32, "dps", psum=True)
        t_r = Ring(cx, 3, [128, 128], F32, "tmpa")
        p_r = Ring(cx, 3, [128, 128], BF16, "pa")
        r_r = Ring(cx, 2, [128, 128], F32, "rden")
        vxv = vx.rearrange("(kt p) f -> p kt f", p=128)
        for h in range(NH):
            kh, bkh = kh_r.next()
            vh, bvh = vh_r.next()
            bi, bbi = bi_r.next()
            dmaq(p, "sp", kh[:], kx[h], writes=[bkh])
            dmaq(p, "sp", vh[:], vxv[:, :, h * 128:(h + 1) * 128], writes=[bvh])
            for hf in range(2):
                dmaq(p, "sp", bi[:, hf * 28:(hf + 1) * 28, :], bias[h, :, hf * 28:(hf + 1) * 28, :], writes=[bbi])
            for i in range(NT):
                qs = q[:, h, i * 128:(i + 1) * 128]
                tiles = [(i + oo, oo) for oo in range(NOFF)] if i < 8 else []
                tiles += [(NSLAB, None), (NSLAB + 1, None)]
                ops, bops = o_r.next()
                dps, bdps = d_r.next()
                for n_, (j, oo) in enumerate(tiles):
                    sps, bsps = s_r.next()
                    mm(p, sps[:, :], kh[:, j * 128:(j + 1) * 128], qs, True, True, [bkh, bq], [bsps])
                    pa, bpa = p_r.next()
                    if oo is not None:
                        tm, btm = t_r.next()
                        tt(p, "dve", tm[:, :], sps[:, :], bi[:, i * NOFF + oo, :], ALU.add, [bsps, bbi], [btm])
                        actf(p, pa[:, :], tm[:, :], AF.Exp, [btm], [bpa])
                    else:
                        actf(p, pa[:, :], sps[:, :], AF.Exp, [bsps], [bpa])
                    first, last = n_ == 0, n_ == len(tiles) - 1
                    mm(p, ops[:, :], vh[:, j, :], pa[:, :], first, last, [bvh, bpa], [bops])
                    mm(p, dps[:, :], onesb[:], pa[:, :], first, last, [bob, bpa], [bdps])
                rd, brd = r_r.next()
                p.op("dve", lambda e, rd=rd, dps=dps: e.reciprocal(out=rd[:, :], in_=dps[:, :]), reads=[bdps], writes=[brd])
                tt(p, "dve", o[:, h, i * 128:(i + 1) * 128], ops[:, :], rd[:, :], ALU.mult, [bops, brd], [bo])

        w_r = Ring(cx, 2, [128, KC, 256], BF16, "w")
        ps_r = Ring(cx, 2, [128, 512], F32, "ps", psum=True)
        hb_r = Ring(cx, 2, [128, 512], F32, "hb")
        toks = []
        evac_o = make_residual_evac(cx, hT, oT, m, bm, hb_r, toks)
        linear_fm(cx, wview(w_out), 0, KC, o, bo, evac_o, w_r, ps_r)
        p.wait_tokens("sp", toks)
        with nc.Block() as block:
            p.emit(block)
    return nc


import ml_dtypes
NPBF = ml_dtypes.bfloat16
GRID_W = 64
ROWS = 64
WIN_H, WIN_W = 8, 16
NEG = -1e30
_cache = {}


def _prog(name, builder):
    if name not in _cache:
        _cache[name] = builder()
    return _cache[name]


def _run(name, builder, in_maps):
    nc = _prog(name, builder)
    res = run_bass_kernel_spmd(nc, in_maps, core_ids=list(range(NCORES)))
    return res.results


def fmaj(v):
    return np.ascontiguousarray(np.asarray(v).reshape(-1, 128).T)


def to_cores(x, ctx):
    outs = []
    for c in range(NCORES):
        b, k = divmod(c, 4)
        a = np.zeros((T, D), np.float32)
        a[:TL] = x[b, k * TL:(k + 1) * TL]
        if k < 2:
            a[TL:] = ctx[b, k * TCX:(k + 1) * TCX]
        outs.append(np.ascontiguousarray(a.T))
    return outs


def from_cores(hs):
    out = np.empty((2, 4096, D), np.float32)
    for c in range(NCORES):
        b, k = divmod(c, 4)
        out[b, k * TL:(k + 1) * TL] = hs[c][:, :TL].T
    return out


def modv_for(modT_layer, s, c):
    b = c // 4
    mv = np.empty((128, KC, 3, 2), np.float32)
    for j in range(3):
        blk = modT_layer[:, (3 * s + j) * KC:(3 * s + j + 1) * KC, :]
        mv[:, :, j, 0] = blk[:, :, b]
        mv[:, :, j, 1] = blk[:, :, 2]
    return mv


def run_ada(c, c_ctx, w_ada, b_ada):
    cond = np.zeros((4, D), np.float32)
    cond[0:2] = c
    cond[2] = c_ctx
    condT = np.ascontiguousarray(cond.T.reshape(KC, 128, 4).transpose(1, 0, 2))
    in_maps = []
    for core in range(NCORES):
        layer, half = divmod(core, 2)
        cols = slice(half * 9216, (half + 1) * 9216)
        in_maps.append({"condT": condT, "wa": np.ascontiguousarray(w_ada[layer][:, cols]),
                        "ba": np.ascontiguousarray(b_ada[layer][cols].reshape(72, 128).T)})
    res = _run("ada", build_ada, in_maps)
    modT = []
    for layer in range(4):
        modT.append(np.concatenate([res[2 * layer]["mo"], res[2 * layer + 1]["mo"]], axis=1))
    return modT


def run_ffn(hs, modT_l, s, g, wgu, wdn):
    gv = fmaj(g)
    in_maps = [{"hT": hs[c], "modv": modv_for(modT_l, s, c), "gv": gv, "wgu": wgu, "wdn": wdn} for c in range(NCORES)]
    res = _run("ffn", build_ffn, in_maps)
    return [r["oT"] for r in res]


def run_mixa(hs, modT_l, g, w_in, v_gain, w_s, b_s, w_out):
    gv = fmaj(g)
    wsT = np.ascontiguousarray(w_s.transpose(2, 0, 1))
    bsb = np.ascontiguousarray(np.broadcast_to(b_s[None], (128, 16, 128)))
    vg = fmaj(v_gain)
    in_maps = [{"hT": hs[c], "modv": modv_for(modT_l, 1, c), "gv": gv, "w_in": w_in, "vg": vg, "wsT": wsT, "bsb": bsb,
                "w_out": w_out} for c in range(NCORES)]
    res = _run("mixa", build_mixa, in_maps)
    return [r["oT"] for r in res]


def _nat_bias_index():
    if "natidx" in _cache:
        return _cache["natidx"]
    kpar = np.arange(128) // 64
    kcol = np.arange(128) % 64
    idx = np.full((4, 128, 8, NOFF, 128), 15 * 31, np.int32)
    qpar = kpar[None, :]
    qcol = kcol[None, :]
    cstart = np.clip(qcol - WIN_W // 2, 0, GRID_W - WIN_W)
    colv = (kcol[:, None] >= cstart) & (kcol[:, None] < cstart + WIN_W)
    dc = np.clip(kcol[:, None] - qcol, 1 - WIN_W, WIN_W - 1) + (WIN_W - 1)
    for kq in range(4):
        r0 = 16 * kq
        for i in range(8):
            for o in range(NOFF):
                kr = r0 - 6 + 2 * (i + o) + kpar[:, None]
                qr = r0 + 2 * i + qpar
                rstart = np.clip(qr - WIN_H // 2, 0, ROWS - WIN_H)
                valid = (kr >= 0) & (kr < ROWS) & (kr >= rstart) & (kr < rstart + WIN_H) & colv
                dr = kr - qr + (WIN_H - 1)
                lin = np.clip(dr, 0, 14) * 31 + dc
                idx[kq, :, i, o, :] = np.where(valid, lin, 15 * 31)
    idx = idx.reshape(4, 128, 8 * NOFF, 128)
    _cache["natidx"] = idx
    return idx


def run_nat(hs, modT_l, g, w_qkv, q_gain, k_gain, rpb, w_out):
    gv = fmaj(g)
    qkg = np.ascontiguousarray(np.stack([q_gain, k_gain], axis=1).astype(np.float32))
    in_maps = [{"hT": hs[c], "modv": modv_for(modT_l, 1, c), "gv": gv, "wqkv": w_qkv, "qkg": qkg} for c in range(NCORES)]
    r1 = _run("nat1", build_nat1, in_maps)
    idx = _nat_bias_index()
    rp = np.concatenate([rpb.reshape(NH, -1), np.full((NH, 1), NEG, np.float32)], axis=1)
    in_maps = []
    for c in range(NCORES):
        b, k = divmod(c, 4)
        Kb = np.concatenate([np.asarray(r1[4 * b + kk]["ko"])[:, :, :TL] for kk in range(4)], axis=2)
        Vb = np.concatenate([np.asarray(r1[4 * b + kk]["vo"])[:TL] for kk in range(4)], axis=0)
        Kc = np.concatenate([np.asarray(r1[4 * b + kk]["ko"])[:, :, TL:] for kk in range(2)], axis=2)
        Vc = np.concatenate([np.asarray(r1[4 * b + kk]["vo"])[TL:] for kk in range(2)], axis=0)
        kx = np.zeros((NH, 128, NKT * 128), NPBF)
        vx = np.zeros((NKT * 128, D), NPBF)
        lo = (16 * k - 6) * 64
        hi = lo + NSLAB * 128
        a, bnd = max(lo, 0), min(hi, 4096)
        kx[:, :, a - lo:bnd - lo] = Kb[:, :, a:bnd]
        vx[a - lo:bnd - lo] = Vb[a:bnd]
        kx[:, :, NSLAB * 128:] = Kc
        vx[NSLAB * 128:] = Vc
        bias = np.ascontiguousarray(rp[:, idx[k]])
        in_maps.append({"hT": hs[c], "modv": modv_for(modT_l, 1, c), "qT": np.asarray(r1[c]["qo"]), "kx": kx, "vx": vx,
                        "bias": bias, "w_out": w_out})
    r2 = _run("nat2", build_nat2, in_maps)
    return [r["oT"] for r in r2]


def build_s5a():
    from contextlib import ExitStack
    nc = bass.Bass("TRN2", target_bir_lowering=False)
    with ExitStack() as stack:
        cx = Ctx(nc, stack)
        p = cx.p
        hT = cx.din("hT", [D, T])
        modv = cx.din("modv", [128, KC, 3, 2])
        gv = cx.din("gv", [128, KC])
        w_in = cx.din("w_in", [D, D])
        uo = cx.dout("uo", [D, T])
        consts = make_consts(cx)
        m, A, bm, bA = load_mod(cx, modv, gv)
        xn = cx.sb([128, KC, T], BF16, "xn")
        bxn = Buf("xn")
        nr = norm_rings(cx)
        rms_adaln_stream(cx, hT, xn, bxn, m, A, bm, bA, consts, nr)
        w_r = Ring(cx, 2, [128, KC, 256], BF16, "w")
        ps_r = Ring(cx, 2, [128, 512], F32, "ps", psum=True)
        uov = uo.rearrange("(kc p) t -> p kc t", p=128)
        toks = []

        def evac(j, t0, n, seg, ps, bps):
            sg, bsg = nr[0].next()
            actf(p, sg[:, :n], ps[:, :n], AF.Identity, [bps], [bsg])
            toks.append(dmaq(p, "sp", uov[:, j, t0:t0 + n], sg[:, :n], reads=[bsg]))

        linear_fm(cx, wview(w_in), 0, KC, xn, bxn, evac, w_r, ps_r)
        p.wait_tokens("sp", toks)
        with nc.Block() as block:
            p.emit(block)
    return nc


NPOS = 256 + 4096
SB = 128
NBLK = NPOS // SB
GL = 16


def build_s5b():
    from contextlib import ExitStack
    nc = bass.Bass("TRN2", target_bir_lowering=False)
    with ExitStack() as stack:
        cx = Ctx(nc, stack)
        p = cx.p
        U = cx.din("U", [32, GL, 2, NPOS])
        areT = cx.din("areT", [128, GL])
        aimT = cx.din("aimT", [128, GL])
        ldtT = cx.din("ldtT", [128, GL])
        breT = cx.din("breT", [128, GL, 16])
        bimT = cx.din("bimT", [128, GL, 16])
        creT = cx.din("creT", [128, GL, 16])
        cimT = cx.din("cimT", [128, GL, 16])
        ident = cx.din("ident", [128, 128])
        YF = cx.dout("YF", [2, 128, 2, NPOS])
        YB = cx.dout("YB", [2, 128, 2, NPOS])

        def small(shape, name):
            return cx.sb(shape, F32, name), Buf(name)

        def load(src, shape, name):
            t, b = small(shape, name)
            dmaq(p, "sp", t[:], src, writes=[b])
            return t, b

        are, b_are = load(areT, [128, GL], "are")
        aim, b_aim = load(aimT, [128, GL], "aim")
        ldt, b_ldt = load(ldtT, [128, GL], "ldt")
        bre, b_bre = load(breT, [128, GL, 16], "bre")
        bim, b_bim = load(bimT, [128, GL, 16], "bim")
        cre, b_cre = load(creT, [128, GL, 16], "cre")
        cim, b_cim = load(cimT, [128, GL, 16], "cim")
        idt, b_idt = load(ident, [128, 128], "idt")

        dt_, b_dt = small([128, GL], "dt")
        actf(p, dt_[:], ldt[:], AF.Exp, [b_ldt], [b_dt])
        xr, b_xr = small([128, GL], "xr")
        xi, b_xi = small([128, GL], "xi")
        tt(p, "dve", xr[:], are[:], dt_[:], ALU.mult, [b_are, b_dt], [b_xr])
        tt(p, "dve", xi[:], aim[:], dt_[:], ALU.mult, [b_aim, b_dt], [b_xi])
        mag, b_mag = small([128, GL], "mag")
        actf(p, mag[:], xr[:], AF.Exp, [b_xr], [b_mag])
        sn, b_sn = small([128, GL], "sn")
        cs, b_cs = small([128, GL], "cs")
        t1, b_t1 = small([128, GL], "t1")
        t2, b_t2 = small([128, GL], "t2")
        actf(p, sn[:], xi[:], AF.Sin, [b_xi], [b_sn], scale=1.0 / 16)
        actf(p, t1[:], xi[:], AF.Sin, [b_xi], [b_t1], scale=1.0 / 32)
        tt(p, "dve", t1[:], t1[:], t1[:], ALU.mult, [b_t1], [b_t1])
        ts(p, "dve", cs[:], t1[:], -2.0, 1.0, ALU.mult, ALU.add, [b_t1], [b_cs])
        for _ in range(4):
            tt(p, "dve", t1[:], cs[:], cs[:], ALU.mult, [b_cs], [b_t1])
            tt(p, "dve", t2[:], sn[:], sn[:], ALU.mult, [b_sn], [b_t2])
            tt(p, "dve", sn[:], sn[:], cs[:], ALU.mult, [b_sn, b_cs], [b_sn])
            ts(p, "dve", sn[:], sn[:], 2.0, None, ALU.mult, None, [b_sn], [b_sn])
            tt(p, "dve", cs[:], t1[:], t2[:], ALU.subtract, [b_t1, b_t2], [b_cs])
        abr, b_abr = small([128, GL], "abr")
        abi, b_abi = small([128, GL], "abi")
        tt(p, "dve", abr[:], mag[:], cs[:], ALU.mult, [b_mag, b_cs], [b_abr])
        tt(p, "dve", abi[:], mag[:], sn[:], ALU.mult, [b_mag, b_sn], [b_abi])
        nr_, b_nr = small([128, GL], "nr")
        ts(p, "dve", nr_[:], abr[:], -1.0, None, ALU.add, None, [b_abr], [b_nr])
        den, b_den = small([128, GL], "den")
        tt(p, "dve", den[:], are[:], are[:], ALU.mult, [b_are], [b_den])
        tt(p, "dve", t1[:], aim[:], aim[:], ALU.mult, [b_aim], [b_t1])
        tt(p, "dve", den[:], den[:], t1[:], ALU.add, [b_den, b_t1], [b_den])
        p.op("dve", lambda e: e.reciprocal(out=den[:], in_=den[:]), reads=[b_den], writes=[b_den])
        kr, b_kr = small([128, GL], "kr")
        ki, b_ki = small([128, GL], "ki")
        tt(p, "dve", kr[:], nr_[:], are[:], ALU.mult, [b_nr, b_are], [b_kr])
        tt(p, "dve", t1[:], abi[:], aim[:], ALU.mult, [b_abi, b_aim], [b_t1])
        tt(p, "dve", kr[:], kr[:], t1[:], ALU.add, [b_kr, b_t1], [b_kr])
        tt(p, "dve", kr[:], kr[:], den[:], ALU.mult, [b_kr, b_den], [b_kr])
        tt(p, "dve", ki[:], abi[:], are[:], ALU.mult, [b_abi, b_are], [b_ki])
        tt(p, "dve", t1[:], nr_[:], aim[:], ALU.mult, [b_nr, b_aim], [b_t1])
        tt(p, "dve", ki[:], ki[:], t1[:], ALU.subtract, [b_ki, b_t1], [b_ki])
        tt(p, "dve", ki[:], ki[:], den[:], ALU.mult, [b_ki, b_den], [b_ki])
        nki, b_nki = small([128, GL], "nki")
        ts(p, "dve", nki[:], ki[:], -1.0, None, ALU.mult, None, [b_ki], [b_nki])
        Bw, b_Bw = small([128, GL, 2, 32], "Bw")
        p.op("pool", lambda e: e.memset(Bw[:], 0.0), writes=[b_Bw])
        tb, b_tb = small([128, 16], "tb")
        for g in range(GL):
            for d in range(2):
                ps_ = slice(64 * d, 64 * d + 64)
                cols = slice(16 * d, 16 * d + 16)
                ts(p, "dve", tb[ps_, :], bim[ps_, g, :], nki[ps_, g:g + 1], None, ALU.mult, None, [b_bim, b_nki], [b_tb])
                stt(p, Bw[ps_, g, 0, cols], bre[ps_, g, :], kr[ps_, g:g + 1], tb[ps_, :], ALU.mult, ALU.add, [b_bre, b_kr, b_tb], [b_Bw])
                ts(p, "dve", tb[ps_, :], bre[ps_, g, :], ki[ps_, g:g + 1], None, ALU.mult, None, [b_bre, b_ki], [b_tb])
                stt(p, Bw[ps_, g, 1, cols], bim[ps_, g, :], kr[ps_, g:g + 1], tb[ps_, :], ALU.mult, ALU.add, [b_bim, b_kr, b_tb], [b_Bw])
        Blk = cx.sb([32, GL, 2, 128], BF16, "Blk")
        b_Blk = Buf("Blk")
        tp_r = Ring(cx, 2, [32, 128], F32, "tps", psum=True)
        for g in range(GL):
            for ri in range(2):
                tp, btp = tp_r.next()
                p.op("pe", lambda e, tp=tp, g=g, ri=ri: e.transpose(tp[:, :], Bw[:, g, ri, :], idt[:]), reads=[b_Bw, b_idt], writes=[btp])
                actf(p, Blk[:, g, ri, :], tp[:, :], AF.Identity, [btp], [b_Blk])
        Cp = cx.sb([128, GL, 2, 128], BF16, "Cp")
        b_Cp = Buf("Cp")
        p.op("pool", lambda e: e.memset(Cp[:], 0.0), writes=[b_Cp])
        for g in range(GL):
            c0 = (g % 8) * 16
            p.op("pool", lambda e, g=g, c0=c0: e.tensor_copy(out=Cp[:, g, 0, c0:c0 + 16], in_=cre[:, g, :]), reads=[b_cre], writes=[b_Cp])
            ts(p, "pool", Cp[:, g, 1, c0:c0 + 16], cim[:, g, :], -1.0, None, ALU.mult, None, [b_cim], [b_Cp])
        Ar, b_Ar = small([128, 32, 2], "Ar")
        AiN, b_AiN = small([128, 32], "AiN")
        AiP, b_AiP = small([128, 32], "AiP")
        nabi, b_nabi = small([128, GL], "nabi")
        ts(p, "dve", nabi[:], abi[:], -1.0, None, ALU.mult, None, [b_abi], [b_nabi])
        Ar4 = Ar[:].rearrange("p (g b) r -> p g b r", b=2)
        AiN3 = AiN[:].rearrange("p (g b) -> p g b", b=2)
        AiP3 = AiP[:].rearrange("p (g b) -> p g b", b=2)
        for b in range(2):
            for ri in range(2):
                p.op("pool", lambda e, b=b, ri=ri: e.tensor_copy(out=Ar4[:, :, b, ri], in_=abr[:]), reads=[b_abr], writes=[b_Ar])
            p.op("pool", lambda e, b=b: e.tensor_copy(out=AiN3[:, :, b], in_=nabi[:]), reads=[b_nabi], writes=[b_AiN])
            p.op("pool", lambda e, b=b: e.tensor_copy(out=AiP3[:, :, b], in_=abi[:]), reads=[b_abi], writes=[b_AiP])

        ub_r = Ring(cx, 2, [32, GL, 2, SB], BF16, "ub")
        bu_r = Ring(cx, 2, [128, SB, 32, 2], F32, "bu")
        H_r = Ring(cx, 2, [128, SB, 32, 2], F32, "H")
        Hb_r = Ring(cx, 2, [128, SB, 32, 2], BF16, "Hb")
        dps_r = Ring(cx, 2, [128, 4, SB], F32, "dps", psum=True)
        yps_r = Ring(cx, 2, [128, SB], F32, "yps", psum=True)
        ys_r = Ring(cx, 3, [128, SB], F32, "ys")
        m1, b_m1 = small([128, 32, 2], "m1")
        m2, b_m2 = small([128, 32, 2], "m2")
        zero, b_zero = small([128, 32, 2], "zero")
        p.op("pool", lambda e: e.memset(zero[:], 0.0), writes=[b_zero])
        toks = []
        prevX, b_prevX = zero[:, :, :], b_zero
        for blk in range(NBLK):
            k0 = blk * SB
            ub, b_ub = ub_r.next()
            dmaq(p, "pool", ub[:], U[:, :, :, k0:k0 + SB], writes=[b_ub])
            bu, b_bu = bu_r.next()
            bu_v = bu[:].rearrange("p k c r -> p k (c r)")
            for q4 in range(GL):
                dps, b_dps = dps_r.next()
                for b in range(2):
                    for ri in range(2):
                        mm(p, dps[:, b * 2 + ri, :], Blk[:, q4, ri, :], ub[:, q4, b, :], True, True, [b_Blk, b_ub], [b_dps])
                actf(p, bu_v[:, :, q4 * 4:q4 * 4 + 4], dps[:].rearrange("p c k -> p k c"), AF.Identity, [b_dps], [b_bu])
            H, b_H = H_r.next()
            for k in range(SB):
                tt(p, "dve", m1[:], Ar[:], prevX, ALU.mult, [b_Ar, b_prevX], [b_m1])
                tt(p, "dve", m2[:, :, 0], AiN[:], prevX[:, :, 1], ALU.mult, [b_AiN, b_prevX], [b_m2])
                tt(p, "dve", m2[:, :, 1], AiP[:], prevX[:, :, 0], ALU.mult, [b_AiP, b_prevX], [b_m2])
                tt(p, "dve", m1[:], m1[:], m2[:], ALU.add, [b_m1, b_m2], [b_m1])
                tt(p, "dve", H[:, k, :, :], m1[:], bu[:, k, :, :], ALU.add, [b_m1, b_bu], [b_H])
                prevX, b_prevX = H[:, k, :, :], b_H
            Hb, b_Hb = Hb_r.next()
            p.op("pool", lambda e, Hb=Hb, H=H: e.tensor_copy(out=Hb[:], in_=H[:]), reads=[b_H], writes=[b_Hb])
            for cc in range(2):
                for b in range(2):
                    for d in range(2):
                        ps_ = slice(64 * d, 64 * d + 64)
                        yps, b_yps = yps_r.next()
                        n_ = 0
                        for g8 in range(8):
                            g = cc * 8 + g8
                            for ri in range(2):
                                mm(p, yps[:, :], Cp[ps_, g, ri, :], Hb[ps_, :, g * 2 + b, ri], n_ == 0, n_ == 15, [b_Cp, b_Hb], [b_yps])
                                n_ += 1
                        ys, b_ys = ys_r.next()
                        actf(p, ys[:, :], yps[:, :], AF.Identity, [b_yps], [b_ys])
                        dst = (YF if d == 0 else YB)
                        toks.append(dmaq(p, "sp", dst[cc, :, b, k0:k0 + SB], ys[:, :], reads=[b_ys]))
        p.wait_tokens("sp", toks)
        with nc.Block() as block:
            p.emit(block)
    return nc


def build_s5c():
    from contextlib import ExitStack
    nc = bass.Bass("TRN2", target_bir_lowering=False)
    with ExitStack() as stack:
        cx = Ctx(nc, stack)
        p = cx.p
        hT = cx.din("hT", [D, T])
        modv = cx.din("modv", [128, KC, 3, 2])
        uT = cx.din("uT", [D, T])
        yfT = cx.din("yfT", [D, T])
        ybT = cx.din("ybT", [D, T])
        dsk = cx.din("dsk", [128, KC])
        w_glu = cx.din("w_glu", [D, 2 * D])
        oT = cx.dout("oT", [D, T])
        m = cx.sb([128, KC, 3, 2], F32, "m")
        bm = Buf("m")
        dmaq(p, "sp", m[:], modv, writes=[bm])
        dk = cx.sb([128, KC], F32, "dk")
        bdk = Buf("dk")
        dmaq(p, "sp", dk[:], dsk, writes=[bdk])
        gl = cx.sb([128, KC, T], BF16, "gl")
        bgl = Buf("gl")
        a_r = Ring(cx, 2, [128, T], F32, "ya")
        b_r = Ring(cx, 2, [128, T], F32, "yb")
        c_r = Ring(cx, 2, [128, T], F32, "yc")
        gelu = Gelu(cx, width=T)
        uv = uT.rearrange("(kc p) t -> p kc t", p=128)
        fv = yfT.rearrange("(kc p) t -> p kc t", p=128)
        bv = ybT.rearrange("(kc p) t -> p kc t", p=128)
        for kc in range(KC):
            ya, bya = a_r.next()
            yb, byb = b_r.next()
            yc, byc = c_r.next()
            dmaq(p, "sp", ya[:], uv[:, kc, :], writes=[bya])
            dmaq(p, "sp", yb[:], fv[:, kc, :], writes=[byb])
            dmaq(p, "sp", yc[:], bv[:, kc, :], writes=[byc])
            tt(p, "dve", yb[:], yb[:], yc[:], ALU.add, [byb, byc], [byb])
            stt(p, ya[:], ya[:], dk[:, kc:kc + 1], yb[:], ALU.mult, ALU.add, [bya, bdk, byb], [bya])
            gelu(p, gl[:, kc, :], ya[:], T, [bya], [bgl])
        w_r = Ring(cx, 2, [128, KC, 128], BF16, "wa")
        w2_r = Ring(cx, 2, [128, KC, 128], BF16, "wg")
        pa_r = Ring(cx, 2, [128, 512], F32, "pa", psum=True)
        pg_r = Ring(cx, 2, [128, 512], F32, "pg", psum=True)
        sg_r = Ring(cx, 2, [128, 512], F32, "sg")
        hb_r = Ring(cx, 2, [128, 512], F32, "hb")
        wv = wview(w_glu)
        hv = hT.rearrange("(kc p) t -> p kc t", p=128)
        ov = oT.rearrange("(kc p) t -> p kc t", p=128)
        toks = []
        for j in range(KC):
            wa, bwa = w_r.next()
            wg, bwg = w2_r.next()
            dmaq(p, "pool", wa[:], wv[:, :, j * 128:(j + 1) * 128], writes=[bwa])
            dmaq(p, "pool", wg[:], wv[:, :, D + j * 128:D + (j + 1) * 128], writes=[bwg])
            for (t0, n, seg) in BLOCKS:
                pa, bpa = pa_r.next()
                pg, bpg = pg_r.next()
                for kc in range(KC):
                    mm(p, pa[:, :n], wa[:, kc, :], gl[:, kc, t0:t0 + n], kc == 0, kc == KC - 1, [bwa, bgl], [bpa])
                for kc in range(KC):
                    mm(p, pg[:, :n], wg[:, kc, :], gl[:, kc, t0:t0 + n], kc == 0, kc == KC - 1, [bwg, bgl], [bpg])
                sg, bsg = sg_r.next()
                actf(p, sg[:, :n], pg[:, :n], AF.Sigmoid, [bpg], [bsg])
                tt(p, "dve", sg[:, :n], pa[:, :n], sg[:, :n], ALU.mult, [bpa, bsg], [bsg])
                hb, bhb = hb_r.next()
                dmaq(p, "sp", hb[:, :n], hv[:, j, t0:t0 + n], writes=[bhb])
                stt(p, hb[:, :n], sg[:, :n], m[:, j, 2, seg:seg + 1], hb[:, :n], ALU.mult, ALU.add, [bsg, bm, bhb], [bhb])
                toks.append(dmaq(p, "sp", ov[:, j, t0:t0 + n], hb[:, :n], reads=[bhb]))
        p.wait_tokens("sp", toks)
        with nc.Block() as block:
            p.emit(block)
    return nc


def run_s5(hs, modT_l, g, w_in, a_re, a_im, log_dt, b_re, b_im, c_re, c_im, d_skip, w_glu):
    gv = fmaj(g)
    in_maps = [{"hT": hs[c], "modv": modv_for(modT_l, 1, c), "gv": gv, "w_in": w_in} for c in range(NCORES)]
    r1 = _run("s5a", build_s5a, in_maps)
    us = [r["uo"] for r in r1]
    seq = []
    for b in range(2):
        parts = [us[4 * b + k][:, TL:] for k in range(2)] + [us[4 * b + k][:, :TL] for k in range(4)]
        seq.append(np.concatenate(parts, axis=1))
    seq = np.stack(seq, axis=1)
    order_b = np.concatenate([np.arange(255, -1, -1), 256 + np.arange(4095, -1, -1)])
    ident = np.eye(128, dtype=np.float32)
    in_maps = []
    for c in range(NCORES):
        gs = slice(GL * c, GL * (c + 1))
        sc = seq[256 * c:256 * (c + 1)].reshape(GL, 16, 2, NPOS)
        Uc = np.empty((32, GL, 2, NPOS), np.float32)
        Uc[:16] = sc.transpose(1, 0, 2, 3)
        Uc[16:] = sc[:, :, :, order_b].transpose(1, 0, 2, 3)

        def dp(a):
            a = a[:, gs]
            if a.ndim == 3:
                return np.ascontiguousarray(a.transpose(0, 2, 1).reshape(128, GL))
            return np.ascontiguousarray(a.transpose(0, 2, 1, 3).reshape(128, GL, a.shape[3]))

        ldt = np.ascontiguousarray(np.broadcast_to(log_dt[:, gs][:, None, :], (2, 64, GL)).reshape(128, GL))
        in_maps.append({"U": Uc, "areT": dp(a_re), "aimT": dp(a_im), "ldtT": ldt,
                        "breT": dp(b_re), "bimT": dp(b_im),
                        "creT": dp(c_re.transpose(0, 1, 3, 2)), "cimT": dp(c_im.transpose(0, 1, 3, 2)), "ident": ident})
    r2 = _run("s5b", build_s5b, in_maps)
    YF = np.concatenate([r["YF"].reshape(256, 2, NPOS) for r in r2], axis=0)
    YBo = np.concatenate([r["YB"].reshape(256, 2, NPOS) for r in r2], axis=0)
    YB = np.empty_like(YBo)
    YB[:, :, order_b] = YBo

    def percore(Y, c):
        b, k = divmod(c, 4)
        a = np.zeros((D, T), np.float32)
        a[:, :TL] = Y[:, b, 256 + k * TL:256 + (k + 1) * TL]
        if k < 2:
            a[:, TL:] = Y[:, b, k * TCX:(k + 1) * TCX]
        return a

    dsk = fmaj(d_skip)
    in_maps = [{"hT": hs[c], "modv": modv_for(modT_l, 1, c), "uT": us[c], "yfT": percore(YF, c), "ybT": percore(YB, c),
                "dsk": dsk, "w_glu": w_glu} for c in range(NCORES)]
    r3 = _run("s5c", build_s5c, in_maps)
    return [r["oT"] for r in r3]


def kernel(x, c, ctx, c_ctx, w_ada, b_ada, norm_g, ffn_w_gu, ffn_w_down,
           a_w_in, a_v_gain, a_w_s, a_b_s, a_w_out,
           b_w_qkv, b_q_gain, b_k_gain, b_rpb, b_w_out,
           c_w_in, c_a_re, c_a_im, c_log_dt, c_b_re, c_b_im, c_c_re, c_c_im, c_d, c_w_glu):
    f = lambda a: np.asarray(a, dtype=np.float32)
    x, c, ctx, c_ctx = f(x), f(c), f(ctx), f(c_ctx)
    modT = run_ada(c, c_ctx, f(w_ada), f(b_ada))
    hs = to_cores(x, ctx)
    depth = 4
    for i in range(depth):
        kind, j = i % 3, i // 3
        hs = run_ffn(hs, modT[i], 0, f(norm_g[i, 0]), f(ffn_w_gu[i, 0]), f(ffn_w_down[i, 0]))
        if kind == 0:
            hs = run_mixa(hs, modT[i], f(norm_g[i, 1]), f(a_w_in[j]), f(a_v_gain[j]), f(a_w_s[j]), f(a_b_s[j]), f(a_w_out[j]))
        elif kind == 1:
            hs = run_nat(hs, modT[i], f(norm_g[i, 1]), f(b_w_qkv[j]), f(b_q_gain[j]), f(b_k_gain[j]), f(b_rpb[j]), f(b_w_out[j]))
        else:
            hs = run_s5(hs, modT[i], f(norm_g[i, 1]), f(c_w_in[j]), f(c_a_re[j]), f(c_a_im[j]), f(c_log_dt[j]),
                        f(c_b_re[j]), f(c_b_im[j]), f(c_c_re[j]), f(c_c_im[j]), f(c_d[j]), f(c_w_glu[j]))
        hs = run_ffn(hs, modT[i], 2, f(norm_g[i, 2]), f(ffn_w_gu[i, 1]), f(ffn_w_down[i, 1]))
    return from_cores(hs)
```

```python
import numpy as np
import concourse.bass as bass
import concourse.mybir as mybir
from concourse.bass_utils import run_bass_kernel_spmd
from concourse.alu_op_type import AluOpType as ALU

F32 = mybir.dt.float32
BF16 = mybir.dt.bfloat16
AF = mybir.ActivationFunctionType

D = 2048
KC = 16
DFF = 5632
FC = 44
NCORES = 8
TL = 1024
TCX = 128
T = TL + TCX
EPS = 1e-6
GLOBAL_NOSYNC = ()


class Buf:
    __slots__ = ("lw", "rd", "name")

    def __init__(self, name=""):
        self.lw = None
        self.rd = {}
        self.name = name


class Prog:
    CENG = ("pe", "act", "dve", "pool")

    def __init__(self, nc, stack, n_dsem=20):
        self.nc = nc
        self.eng = {"pe": nc.tensor, "act": nc.scalar, "dve": nc.vector,
                    "pool": nc.gpsimd, "sp": nc.sync}
        self.q = {e: [] for e in self.eng}
        self.cnt = {e: 0 for e in self.CENG}
        self.seen = {e: {} for e in self.eng}
        self.nosync = set(GLOBAL_NOSYNC)
        self.csem = {e: stack.enter_context(nc.semaphore("c_" + e)) for e in self.CENG}
        self.dsem = {}
        self.dcum = {}
        self.drr = {}
        for qn in ("sp", "pool", "act"):
            n = n_dsem if qn != "act" else 6
            self.dsem[qn] = [stack.enter_context(nc.semaphore("d_%s%d" % (qn, i))) for i in range(n)]
            self.dcum[qn] = [0] * n
            self.drr[qn] = 0

    def _need(self, eng, tok, waits):
        if tok is None:
            return
        key, val = tok
        if key == ("c", "pe") and eng == "pe":
            return
        if key[0] == "c" and key[1] == eng and eng in self.nosync:
            return
        if self.seen[eng].get(key, 0) >= val:
            return
        if waits.get(key, 0) < val:
            waits[key] = val

    def _deps(self, eng, reads, writes):
        waits = {}
        for b in reads:
            self._need(eng, b.lw, waits)
        for b in writes:
            self._need(eng, b.lw, waits)
            for k, v in b.rd.items():
                self._need(eng, (k, v), waits)
        for k, v in waits.items():
            self.seen[eng][k] = v
        return waits

    def _commit(self, tok, reads, writes):
        key, val = tok
        for b in reads:
            if b.rd.get(key, 0) < val:
                b.rd[key] = val
        for b in writes:
            b.lw = tok
            b.rd = {}

    def op(self, eng, fn, reads=(), writes=()):
        waits = self._deps(eng, reads, writes)
        self.cnt[eng] += 1
        tok = (("c", eng), self.cnt[eng])
        self.q[eng].append((list(waits.items()), fn, ("c", eng, 1)))
        self._commit(tok, reads, writes)
        return tok

    def dma(self, qn, fns, reads=(), writes=()):
        if not isinstance(fns, (list, tuple)):
            fns = [fns]
        waits = self._deps(qn, reads, writes)
        i = self.drr[qn]
        self.drr[qn] = (i + 1) % len(self.dsem[qn])
        key = ("d", qn, i)
        prev = self.dcum[qn][i]
        if prev > 0 and self.seen[qn].get(key, 0) < prev:
            waits[key] = prev
            self.seen[qn][key] = prev
        for j, fn in enumerate(fns):
            self.dcum[qn][i] += 16
            self.q[qn].append((list(waits.items()) if j == 0 else [], fn, ("d", qn, i)))
        tok = (key, self.dcum[qn][i])
        self._commit(tok, reads, writes)
        return tok

    def wait_tokens(self, eng, toks):
        waits = {}
        for t in toks:
            self._need(eng, t, waits)
        for k, v in waits.items():
            self.seen[eng][k] = v
        self.q[eng].append((list(waits.items()), None, None))

    def _sem(self, key):
        if key[0] == "c":
            return self.csem[key[1]]
        return self.dsem[key[1]][key[2]]

    def emit(self, block):
        decos = {"pe": block.tensor, "act": block.scalar, "dve": block.vector,
                 "pool": block.gpsimd, "sp": block.sync}
        for e in self.eng:
            items = self.q[e]
            if not items:
                continue

            def body(engine, items=items):
                for waits, fn, inc in items:
                    for key, val in waits:
                        engine.wait_ge(self._sem(key), val)
                    if fn is None:
                        continue
                    ins = fn(engine)
                    if inc[0] == "c":
                        ins.then_inc(self.csem[inc[1]], 1)
                    else:
                        ins.then_inc(self.dsem[inc[1]][inc[2]], 16)

            decos[e](body)


class Ctx:
    def __init__(self, nc, stack):
        self.nc = nc
        self.stack = stack
        self.p = Prog(nc, stack)
        self._n = 0

    def sb(self, shape, dt, name=None):
        self._n += 1
        return self.stack.enter_context(self.nc.sbuf_tensor(name or ("sb%d" % self._n), list(shape), dt))

    def ps(self, shape, dt=F32, name=None):
        self._n += 1
        return self.stack.enter_context(self.nc.psum_tensor(name or ("ps%d" % self._n), list(shape), dt))

    def din(self, name, shape, dt=F32):
        return self.nc.dram_tensor(name, list(shape), dt, kind="ExternalInput").ap()

    def dout(self, name, shape, dt=F32):
        return self.nc.dram_tensor(name, list(shape), dt, kind="ExternalOutput").ap()


class Ring:
    def __init__(self, cx, n, shape, dt, name, psum=False):
        self.t = [(cx.ps(shape, dt, "%s%d" % (name, i)) if psum else cx.sb(shape, dt, "%s%d" % (name, i)))
                  for i in range(n)]
        self.b = [Buf("%s%d" % (name, i)) for i in range(n)]
        self.i = 0

    def next(self):
        i = self.i
        self.i = (i + 1) % len(self.t)
        return self.t[i], self.b[i]


def make_consts(cx):
    p = cx.p
    ones = cx.sb([128, 128], F32, "ones")
    epsc = cx.sb([128, 1], F32, "epsc")
    b = Buf("consts")
    p.op("pool", lambda e: e.memset(ones[:], 1.0), writes=[b])
    p.op("pool", lambda e: e.memset(epsc[:], EPS), writes=[b])
    return ones, epsc, b


def load_mod(cx, modv, gv):
    p = cx.p
    m = cx.sb([128, KC, 3, 2], F32)
    g = cx.sb([128, KC], F32)
    A = cx.sb([128, KC, 2], F32)
    bm, bg, bA = Buf("m"), Buf("g"), Buf("A")
    p.dma("sp", lambda e: e.dma_start(out=m[:], in_=modv), writes=[bm])
    p.dma("sp", lambda e: e.dma_start(out=g[:], in_=gv), writes=[bg])
    for s in range(2):
        p.op("dve", lambda e, s=s: e.scalar_tensor_tensor(out=A[:, :, s], in0=m[:, :, 1, s], scalar=1.0,
                                                           in1=g[:, :], op0=ALU.add, op1=ALU.mult),
             reads=[bm, bg], writes=[bA])
    return m, A, bm, bA


def rms_adaln(cx, src, bsrc, dst, bdst, blocks, m, A, bm, bA, consts, rings):
    p = cx.p
    ones, epsc, bc = consts
    sq_r, ps_r, rs_r, tmp_r = rings
    for (t0, n, seg) in blocks:
        _rms_block(p, src, bsrc, dst, bdst, t0, n, seg, m, A, bm, bA, ones, epsc, bc, sq_r, ps_r, rs_r, tmp_r)


def _rms_block(p, src, bsrc, dst, bdst, t0, n, seg, m, A, bm, bA, ones, epsc, bc, sq_r, ps_r, rs_r, tmp_r, s0=None):
    s0 = t0 if s0 is None else s0
    ss, bss = ps_r.next()
    for kc in range(KC):
        sq, bsq = sq_r.next()
        p.op("act", lambda e, sq=sq, kc=kc: e.activation(out=sq[:, :n], in_=src[:, kc, s0:s0 + n], func=AF.Square),
             reads=[bsrc], writes=[bsq])
        p.op("pe", lambda e, sq=sq, kc=kc: e.matmul(ss[:, :n], lhsT=ones[:], rhs=sq[:, :n],
                                                    start=(kc == 0), stop=(kc == KC - 1)),
             reads=[bsq, bc], writes=[bss])
    rs, brs = rs_r.next()
    p.op("act", lambda e: e.activation(out=rs[:, :n], in_=ss[:, :n], func=AF.Sqrt, bias=epsc[:], scale=1.0 / D),
         reads=[bss, bc], writes=[brs])
    p.op("dve", lambda e: e.reciprocal(out=rs[:, :n], in_=rs[:, :n]), reads=[brs], writes=[brs])
    for kc in range(KC):
        tmp, btmp = tmp_r.next()
        p.op("dve", lambda e, tmp=tmp, kc=kc: e.scalar_tensor_tensor(
            out=tmp[:, :n], in0=src[:, kc, s0:s0 + n], scalar=A[:, kc, seg:seg + 1], in1=rs[:, :n],
            op0=ALU.mult, op1=ALU.mult), reads=[bsrc, brs, bA], writes=[btmp])
        p.op("act", lambda e, tmp=tmp, kc=kc: e.activation(out=dst[:, kc, t0:t0 + n], in_=tmp[:, :n],
                                                           func=AF.Identity, bias=m[:, kc, 0, seg:seg + 1], scale=1.0),
             reads=[btmp, bm], writes=[bdst])


BLOCKS = [(0, 512, 0), (512, 512, 0), (1024, 128, 1)]


def wview(w):
    return w.rearrange("(kc p) f -> p kc f", p=128)


def build_ffn(dbg=False):
    from contextlib import ExitStack
    nc = bass.Bass("TRN2", target_bir_lowering=False)
    with ExitStack() as stack:
        cx = Ctx(nc, stack)
        p = cx.p
        hT = cx.din("hT", [D, T])
        modv = cx.din("modv", [128, KC, 3, 2])
        gv = cx.din("gv", [128, KC])
        wgu = cx.din("wgu", [D, 2 * DFF])
        wdn = cx.din("wdn", [DFF, D])
        oT = cx.dout("oT", [D, T])

        consts = make_consts(cx)
        h = cx.sb([128, KC, T], F32, "h")
        bh = Buf("h")
        hv = hT.rearrange("(kc p) t -> p kc t", p=128)
        for q in range(4):
            p.dma("sp", lambda e, q=q: e.dma_start(out=h[:, 4 * q:4 * q + 4, :], in_=hv[:, 4 * q:4 * q + 4, :]),
                  writes=[bh])
        m, A, bm, bA = load_mod(cx, modv, gv)
        G = cx.sb([128, KC, 2], F32, "G")
        bG = Buf("G")
        p.op("dve", lambda e: e.tensor_scalar(out=G[:], in0=m[:, :, 2, :], scalar1=0.5, scalar2=None, op0=ALU.mult),
             reads=[bm], writes=[bG])

        xn = cx.sb([128, KC, T], BF16, "xn")
        bxn = Buf("xn")
        rings = (Ring(cx, 2, [128, 512], F32, "sq"), Ring(cx, 1, [128, 512], F32, "ssps", psum=True),
                 Ring(cx, 2, [128, 512], F32, "rs"), Ring(cx, 2, [128, 512], F32, "tmp"))
        rms_adaln(cx, h, bh, xn, bxn, BLOCKS, m, A, bm, bA, consts, rings)

        if dbg:
            xo = cx.dout("xo", [128, KC, T], BF16)
            p.wait_tokens("sp", [p.dma("sp", lambda e: e.dma_start(out=xo, in_=xn[:]), reads=[bxn])])
        ffn_core(cx, xn, bxn, h, bh, G, bG, wgu, wdn)

        ov = oT.rearrange("(kc p) t -> p kc t", p=128)
        toks = []
        for q in range(4):
            toks.append(p.dma("sp", lambda e, q=q: e.dma_start(out=ov[:, 4 * q:4 * q + 4, :], in_=h[:, 4 * q:4 * q + 4, :]),
                              reads=[bh]))
        p.wait_tokens("sp", toks)
        with nc.Block() as block:
            p.emit(block)
    return nc


def ffn_core(cx, xn, bxn, h, bh, G, bG, wgu, wdn, GRP=4):
    p = cx.p
    wguv = wview(wgu)
    wdnv = wdn.rearrange("(fc p) d -> p fc d", p=128)
    wg_r = Ring(cx, 2, [128, KC, 256], BF16, "wg")
    wu_r = Ring(cx, 2, [128, KC, 256], BF16, "wu")
    wd_r = Ring(cx, 2, [128, GRP, D], BF16, "wd")
    act_r = Ring(cx, 2, [128, GRP, T], BF16, "act")
    gps_r = Ring(cx, 2, [128, 512], F32, "gps", psum=True)
    ups_r = Ring(cx, 2, [128, 512], F32, "ups", psum=True)
    yps_r = Ring(cx, 2, [128, 512], F32, "yps", psum=True)
    sg_r = Ring(cx, 2, [128, 512], F32, "sg")
    wg = wu = None
    for grp in range(FC // GRP):
        act, bact = act_r.next()
        wd, bwd = wd_r.next()
        for fl in range(GRP):
            p.dma("pool", lambda e, wd=wd, fl=fl, grp=grp: e.dma_start(out=wd[:, fl, :], in_=wdnv[:, grp * GRP + fl, :]),
                  writes=[bwd])
        for fl in range(GRP):
            fc = grp * GRP + fl
            if fc % 2 == 0:
                wg, bwg = wg_r.next()
                wu, bwu = wu_r.next()
                p.dma("pool", lambda e, wg=wg, fc=fc: e.dma_start(out=wg[:], in_=wguv[:, :, fc * 128:fc * 128 + 256]),
                      writes=[bwg])
                p.dma("pool", lambda e, wu=wu, fc=fc: e.dma_start(out=wu[:], in_=wguv[:, :, DFF + fc * 128:DFF + fc * 128 + 256]),
                      writes=[bwu])
            c0 = (fc % 2) * 128
            for (t0, n, seg) in BLOCKS:
                gps, bgps = gps_r.next()
                ups, bups = ups_r.next()
                for kc in range(KC):
                    p.op("pe", lambda e, gps=gps, wg=wg, kc=kc, c0=c0, t0=t0, n=n: e.matmul(
                        gps[:, :n], lhsT=wg[:, kc, c0:c0 + 128], rhs=xn[:, kc, t0:t0 + n],
                        start=(kc == 0), stop=(kc == KC - 1)), reads=[bwg, bxn], writes=[bgps])
                for kc in range(KC):
                    p.op("pe", lambda e, ups=ups, wu=wu, kc=kc, c0=c0, t0=t0, n=n: e.matmul(
                        ups[:, :n], lhsT=wu[:, kc, c0:c0 + 128], rhs=xn[:, kc, t0:t0 + n],
                        start=(kc == 0), stop=(kc == KC - 1)), reads=[bwu, bxn], writes=[bups])
                sg, bsg = sg_r.next()
                p.op("act", lambda e, sg=sg, gps=gps, n=n: e.activation(out=sg[:, :n], in_=gps[:, :n], func=AF.Silu),
                     reads=[bgps], writes=[bsg])
                p.op("dve", lambda e, sg=sg, ups=ups, act=act, fl=fl, t0=t0, n=n: e.tensor_tensor(
                    out=act[:, fl, t0:t0 + n], in0=ups[:, :n], in1=sg[:, :n], op=ALU.mult),
                    reads=[bups, bsg], writes=[bact])
        for dc in range(KC):
            for (t0, n, seg) in BLOCKS:
                yps, byps = yps_r.next()
                for fl in range(GRP):
                    p.op("pe", lambda e, yps=yps, wd=wd, fl=fl, dc=dc, act=act, t0=t0, n=n: e.matmul(
                        yps[:, :n], lhsT=wd[:, fl, dc * 128:(dc + 1) * 128], rhs=act[:, fl, t0:t0 + n],
                        start=(fl == 0), stop=(fl == GRP - 1)), reads=[bwd, bact], writes=[byps])
                p.op("dve", lambda e, yps=yps, dc=dc, t0=t0, n=n, seg=seg: e.scalar_tensor_tensor(
                    out=h[:, dc, t0:t0 + n], in0=yps[:, :n], scalar=G[:, dc, seg:seg + 1], in1=h[:, dc, t0:t0 + n],
                    op0=ALU.mult, op1=ALU.add), reads=[byps, bG, bh], writes=[bh])


NF_ADA = 4 * 9 * KC // NCORES


def build_ada():
    from contextlib import ExitStack
    nc = bass.Bass("TRN2", target_bir_lowering=False)
    with ExitStack() as stack:
        cx = Ctx(nc, stack)
        p = cx.p
        condT = cx.din("condT", [128, KC, 4])
        wa = cx.din("wa", [D, NF_ADA * 128])
        ba = cx.din("ba", [128, NF_ADA])
        mo = cx.dout("mo", [128, NF_ADA, 4])
        ct = cx.sb([128, KC, 4], F32)
        cs = cx.sb([128, KC, 4], BF16)
        bt = cx.sb([128, NF_ADA], F32)
        res = cx.sb([128, NF_ADA, 4], F32)
        bct, bcs, bbt, bres = Buf(), Buf(), Buf(), Buf()
        p.dma("sp", lambda e: e.dma_start(out=ct[:], in_=condT), writes=[bct])
        p.dma("sp", lambda e: e.dma_start(out=bt[:], in_=ba), writes=[bbt])
        p.op("act", lambda e: e.activation(out=cs[:], in_=ct[:], func=AF.Silu), reads=[bct], writes=[bcs])
        w_r = Ring(cx, 3, [128, KC, 512], BF16, "w")
        ps_r = Ring(cx, 2, [128, 4, 4], F32, "ps", psum=True)
        wav = wview(wa)
        for g4 in range(NF_ADA // 4):
            w, bw = w_r.next()
            p.dma("pool", lambda e, w=w, g4=g4: e.dma_start(out=w[:], in_=wav[:, :, g4 * 512:(g4 + 1) * 512]), writes=[bw])
            ps, bps = ps_r.next()
            for j in range(4):
                for kc in range(KC):
                    p.op("pe", lambda e, ps=ps, w=w, j=j, kc=kc: e.matmul(
                        ps[:, j, :], lhsT=w[:, kc, j * 128:(j + 1) * 128], rhs=cs[:, kc, :],
                        start=(kc == 0), stop=(kc == KC - 1)), reads=[bw, bcs], writes=[bps])
            for j in range(4):
                f = g4 * 4 + j
                p.op("dve", lambda e, ps=ps, j=j, f=f: e.tensor_scalar(out=res[:, f, :], in0=ps[:, j, :], scalar1=bt[:, f:f + 1],
                                                                      scalar2=None, op0=ALU.add),
                     reads=[bps, bbt], writes=[bres])
        tok = p.dma("sp", lambda e: e.dma_start(out=mo, in_=res[:]), reads=[bres])
        p.wait_tokens("sp", [tok])
        with nc.Block() as block:
            p.emit(block)
    return nc


def mm(p, out, lhsT, rhs, start, stop, reads, writes):
    return p.op("pe", lambda e: e.matmul(out, lhsT=lhsT, rhs=rhs, start=start, stop=stop), reads=reads, writes=writes)


def actf(p, out, in_, func, reads, writes, bias=None, scale=None, accum_out=None):
    kw = {}
    if bias is not None:
        kw["bias"] = bias
    if scale is not None:
        kw["scale"] = scale
    if accum_out is not None:
        kw["accum_out"] = accum_out
    return p.op("act", lambda e: e.activation(out=out, in_=in_, func=func, **kw), reads=reads, writes=writes)


def tt(p, eng, out, in0, in1, op, reads, writes):
    return p.op(eng, lambda e: e.tensor_tensor(out=out, in0=in0, in1=in1, op=op), reads=reads, writes=writes)


def ts(p, eng, out, in0, s1, s2, op0, op1, reads, writes):
    if op1 is None:
        return p.op(eng, lambda e: e.tensor_scalar(out=out, in0=in0, scalar1=s1, scalar2=None, op0=op0), reads=reads, writes=writes)
    return p.op(eng, lambda e: e.tensor_scalar(out=out, in0=in0, scalar1=s1, scalar2=s2, op0=op0, op1=op1), reads=reads, writes=writes)


def stt(p, out, in0, scalar, in1, op0, op1, reads, writes):
    return p.op("dve", lambda e: e.scalar_tensor_tensor(out=out, in0=in0, scalar=scalar, in1=in1, op0=op0, op1=op1),
                reads=reads, writes=writes)


def dmaq(p, q, out, in_, reads=(), writes=()):
    return p.dma(q, lambda e: e.dma_start(out=out, in_=in_), reads=reads, writes=writes)


class Gelu:
    def __init__(self, cx, width=512):
        self.a = Ring(cx, 2, [128, width], F32, "gl_a")
        self.b = Ring(cx, 2, [128, width], F32, "gl_b")

    def __call__(self, p, out, ps, n, rd, wr, accum_sq=None):
        a, ba = self.a.next()
        b, bb = self.b.next()
        actf(p, a[:, :n], ps, AF.Square, rd, [ba])
        ts(p, "dve", a[:, :n], a[:, :n], 0.044715, 1.0, ALU.mult, ALU.add, [ba], [ba])
        tt(p, "dve", a[:, :n], a[:, :n], ps, ALU.mult, [ba] + rd, [ba])
        actf(p, b[:, :n], a[:, :n], AF.Sigmoid, [ba], [bb], scale=1.5957691216057308)
        tt(p, "dve", out, b[:, :n], ps, ALU.mult, [bb] + rd, wr)


def load_h(cx, hT, name="h"):
    p = cx.p
    h = cx.sb([128, KC, T], F32, name)
    bh = Buf(name)
    hv = hT.rearrange("(kc p) t -> p kc t", p=128)
    for q in range(4):
        dmaq(p, "sp", h[:, 4 * q:4 * q + 4, :], hv[:, 4 * q:4 * q + 4, :], writes=[bh])
    return h, bh


def store_h(cx, oT, h, bh):
    p = cx.p
    ov = oT.rearrange("(kc p) t -> p kc t", p=128)
    toks = [dmaq(p, "sp", ov[:, 4 * q:4 * q + 4, :], h[:, 4 * q:4 * q + 4, :], reads=[bh]) for q in range(4)]
    p.wait_tokens("sp", toks)


def rms_adaln_stream(cx, hT, dst, bdst, m, A, bm, bA, consts, rings, blocks=BLOCKS):
    p = cx.p
    ones, epsc, bc = consts
    hb = cx.sb([128, KC, 512], F32, "hblk")
    bhb = Buf("hblk")
    hv = hT.rearrange("(kc p) t -> p kc t", p=128)
    for (t0, n, seg) in blocks:
        for q in range(2):
            dmaq(p, "sp", hb[:, 8 * q:8 * q + 8, :n], hv[:, 8 * q:8 * q + 8, t0:t0 + n], writes=[bhb])
        _rms_block(p, hb, bhb, dst, bdst, t0, n, seg, m, A, bm, bA, ones, epsc, bc, *rings, s0=0)


def make_residual_evac(cx, hT, oT, m, bm, ring, toks, gate_idx=2):
    p = cx.p
    hv = hT.rearrange("(kc p) t -> p kc t", p=128)
    ov = oT.rearrange("(kc p) t -> p kc t", p=128)

    def evac(j, t0, n, seg, ps, bps):
        hb, bhb = ring.next()
        dmaq(p, "sp", hb[:, :n], hv[:, j, t0:t0 + n], writes=[bhb])
        stt(p, hb[:, :n], ps[:, :n], m[:, j, gate_idx, seg:seg + 1], hb[:, :n], ALU.mult, ALU.add, [bps, bm, bhb], [bhb])
        toks.append(dmaq(p, "sp", ov[:, j, t0:t0 + n], hb[:, :n], reads=[bhb]))

    return evac


def norm_rings(cx):
    return (Ring(cx, 2, [128, 512], F32, "sq"), Ring(cx, 1, [128, 512], F32, "ssps", psum=True),
            Ring(cx, 2, [128, 512], F32, "rs"), Ring(cx, 2, [128, 512], F32, "tmp"))


def linear_fm(cx, wv, col0, nchunks, x, bx, evac, w_r, ps_r, nk=KC, blocks=BLOCKS):
    p = cx.p
    w = bw = None
    for j in range(nchunks):
        if j % 2 == 0:
            w, bw = w_r.next()
            c = col0 + j * 128
            wd = 256 if j + 1 < nchunks else 128
            dmaq(p, "pool", w[:, :, :wd], wv[:, :, c:c + wd], writes=[bw])
        c0 = (j % 2) * 128
        for (t0, n, seg) in blocks:
            ps, bps = ps_r.next()
            for kc in range(nk):
                mm(p, ps[:, :n], w[:, kc, c0:c0 + 128], x[:, kc, t0:t0 + n], kc == 0, kc == nk - 1, [bw, bx], [bps])
            evac(j, t0, n, seg, ps, bps)


NT = T // 128


def build_mixa():
    from contextlib import ExitStack
    nc = bass.Bass("TRN2", target_bir_lowering=False)
    with ExitStack() as stack:
        cx = Ctx(nc, stack)
        p = cx.p
        hT = cx.din("hT", [D, T])
        modv = cx.din("modv", [128, KC, 3, 2])
        gv = cx.din("gv", [128, KC])
        w_in = cx.din("w_in", [D, 2 * D])
        vg = cx.din("vg", [128, KC])
        wsT = cx.din("wsT", [128, 16, 128])
        bsb = cx.din("bsb", [128, 16, 128])
        w_out = cx.din("w_out", [D, D])
        oT = cx.dout("oT", [D, T])

        consts = make_consts(cx)
        m, A, bm, bA = load_mod(cx, modv, gv)
        xn = cx.sb([128, KC, T], BF16, "xn")
        bxn = Buf("xn")
        nr = norm_rings(cx)
        rms_adaln_stream(cx, hT, xn, bxn, m, A, bm, bA, consts, nr)

        vgt = cx.sb([128, KC], F32, "vgt")
        wst = cx.sb([128, 16, 128], F32, "wst")
        bst = cx.sb([128, 16, 128], F32, "bst")
        bvg, bws, bbs = Buf(), Buf(), Buf()
        dmaq(p, "sp", vgt[:], vg, writes=[bvg])
        dmaq(p, "sp", wst[:], wsT, writes=[bws])
        dmaq(p, "sp", bst[:], bsb, writes=[bbs])

        gelu = Gelu(cx)
        w_r = Ring(cx, 2, [128, KC, 256], BF16, "w")
        ps_r = Ring(cx, 2, [128, 512], F32, "ps", psum=True)
        winv = wview(w_in)

        uT = cx.sb([128, KC, T], BF16, "uT")
        buT = Buf("uT")

        def evac_u(j, t0, n, seg, ps, bps):
            gelu(p, uT[:, j, t0:t0 + n], ps[:, :n], n, [bps], [buT])

        linear_fm(cx, winv, 0, KC, xn, bxn, evac_u, w_r, ps_r)

        v = cx.sb([128, NT, D], BF16, "v")
        bv = Buf("v")
        NCB = 8
        ssq = cx.sb([128, NT, NCB], F32, "ssq")
        bssq = Buf("ssq")
        for cb in range(NCB):
            wv_, bwv = w_r.next()
            dmaq(p, "pool", wv_[:, :, :], winv[:, :, D + cb * 256:D + (cb + 1) * 256], writes=[bwv])
            for ti in range(NT):
                ps, bps = ps_r.next()
                for kc in range(KC):
                    mm(p, ps[:, :256], xn[:, kc, ti * 128:(ti + 1) * 128], wv_[:, kc, :], kc == 0, kc == KC - 1, [bxn, bwv], [bps])
                vt, bvt = nr[3].next()
                gelu(p, vt[:, :256], ps[:, :256], 256, [bps], [bvt])
                junk, bjunk = nr[0].next()
                actf(p, junk[:, :256], vt[:, :256], AF.Square, [bvt], [bjunk, bssq], accum_out=ssq[:, ti, cb:cb + 1])
                p.op("pool", lambda e, vt=vt, ti=ti, cb=cb: e.tensor_copy(out=v[:, ti, cb * 256:(cb + 1) * 256], in_=vt[:, :256]),
                     reads=[bvt], writes=[bv])
        rv = cx.sb([128, NT], F32, "rv")
        brv = Buf("rv")
        p.op("dve", lambda e: e.tensor_reduce(out=rv[:, :], in_=ssq[:, :, :], axis=mybir.AxisListType.X, op=ALU.add),
             reads=[bssq], writes=[brv])
        actf(p, rv[:, :], rv[:, :], AF.Sqrt, [brv, consts[2]], [brv], bias=consts[1][:], scale=1.0 / D)
        p.op("dve", lambda e: e.reciprocal(out=rv[:, :], in_=rv[:, :]), reads=[brv], writes=[brv])

        z, bz = xn, bxn
        wp_r = Ring(cx, 3, [128, 128], BF16, "wp")
        st_r = Ring(cx, 2, [128, 128], F32, "st")
        sps_r = Ring(cx, 2, [128, 128], F32, "sps", psum=True)
        for ti in range(NT):
            for g in range(16):
                wp, bwp = wp_r.next()
                ts(p, "pool", wp[:, :], wst[:, g, :], rv[:, ti:ti + 1], None, ALU.mult, None, [bws, brv], [bwp])
                sp_, bsp = sps_r.next()
                mm(p, sp_[:, :], v[:, ti, g * 128:(g + 1) * 128], wp[:, :], True, True, [bv, bwp], [bsp])
                st, bst_ = st_r.next()
                stt(p, st[:, :], sp_[:, :], vgt[:, g:g + 1], bst[:, g, :], ALU.mult, ALU.add, [bsp, bvg, bbs], [bst_])
                tt(p, "dve", z[:, g, ti * 128:(ti + 1) * 128], st[:, :], uT[:, g, ti * 128:(ti + 1) * 128], ALU.mult,
                   [bst_, buT], [bz])

        woutv = wview(w_out)

        toks = []
        evac_o = make_residual_evac(cx, hT, oT, m, bm, nr[0], toks)
        linear_fm(cx, woutv, 0, KC, z, bz, evac_o, w_r, ps_r)
        p.wait_tokens("sp", toks)
        with nc.Block() as block:
            p.emit(block)
    return nc


HD = 128
NH = 16
ATT_SCALE = HD ** -0.5


def build_nat1():
    from contextlib import ExitStack
    nc = bass.Bass("TRN2", target_bir_lowering=False)
    with ExitStack() as stack:
        cx = Ctx(nc, stack)
        p = cx.p
        hT = cx.din("hT", [D, T])
        modv = cx.din("modv", [128, KC, 3, 2])
        gv = cx.din("gv", [128, KC])
        wqkv = cx.din("wqkv", [D, 3 * D])
        qkg = cx.din("qkg", [128, 2])
        qo = cx.dout("qo", [NH, 128, T], BF16)
        ko = cx.dout("ko", [NH, 128, T], BF16)
        vo = cx.dout("vo", [T, D], BF16)

        consts = make_consts(cx)
        ones, epsc, bc = consts
        m, A, bm, bA = load_mod(cx, modv, gv)
        xn = cx.sb([128, KC, T], BF16, "xn")
        bxn = Buf("xn")
        nr = norm_rings(cx)
        rms_adaln_stream(cx, hT, xn, bxn, m, A, bm, bA, consts, nr)
        gq = cx.sb([128, 2], F32, "gq")
        bgq = Buf()
        dmaq(p, "sp", gq[:], qkg, writes=[bgq])
        ts(p, "dve", gq[:, 0:1], gq[:, 0:1], ATT_SCALE, None, ALU.mult, None, [bgq], [bgq])

        w_r = Ring(cx, 2, [128, KC, 256], BF16, "w")
        ps_r = Ring(cx, 2, [128, 512], F32, "ps", psum=True)
        wv = wview(wqkv)
        qf_r = Ring(cx, 2, [128, 512], F32, "qf")
        st_r = Ring(cx, 3, [128, 512], BF16, "stg")
        toks = []
        for which, dst in ((0, qo), (1, ko)):
            def evac(j, t0, n, seg, ps, bps, which=which, dst=dst):
                qf, bqf = qf_r.next()
                actf(p, qf[:, :n], ps[:, :n], AF.Identity, [bps], [bqf])
                sq, bsq = nr[0].next()
                actf(p, sq[:, :n], qf[:, :n], AF.Square, [bqf], [bsq])
                ss, bss = nr[1].next()
                mm(p, ss[:, :n], ones[:], sq[:, :n], True, True, [bsq, bc], [bss])
                rs, brs = nr[2].next()
                actf(p, rs[:, :n], ss[:, :n], AF.Sqrt, [bss, bc], [brs], bias=epsc[:], scale=1.0 / HD)
                p.op("dve", lambda e: e.reciprocal(out=rs[:, :n], in_=rs[:, :n]), reads=[brs], writes=[brs])
                sg, bsg = st_r.next()
                stt(p, sg[:, :n], qf[:, :n], gq[:, which:which + 1], rs[:, :n], ALU.mult, ALU.mult, [bqf, bgq, brs], [bsg])
                toks.append(dmaq(p, "sp", dst[j, :, t0:t0 + n], sg[:, :n], reads=[bsg]))

            linear_fm(cx, wv, which * D, KC, xn, bxn, evac, w_r, ps_r)
        vs = cx.sb([128, NT, D], BF16, "vs")
        bvs = Buf("vs")
        for cb in range(8):
            w, bw = w_r.next()
            dmaq(p, "pool", w[:, :, :], wv[:, :, 2 * D + cb * 256:2 * D + (cb + 1) * 256], writes=[bw])
            for ti in range(NT):
                ps, bps = ps_r.next()
                for kc in range(KC):
                    mm(p, ps[:, :256], xn[:, kc, ti * 128:(ti + 1) * 128], w[:, kc, :], kc == 0, kc == KC - 1, [bxn, bw], [bps])
                actf(p, vs[:, ti, cb * 256:(cb + 1) * 256], ps[:, :256], AF.Identity, [bps], [bvs])
        vov = vo.rearrange("(ti p) f -> p ti f", p=128)
        toks.append(dmaq(p, "sp", vov, vs[:], reads=[bvs]))
        p.wait_tokens("sp", toks)
        with nc.Block() as block:
            p.emit(block)
    return nc


NSLAB = 14
NKT = NSLAB + 2
NOFF = 7


def build_nat2():
    from contextlib import ExitStack
    nc = bass.Bass("TRN2", target_bir_lowering=False)
    with ExitStack() as stack:
        cx = Ctx(nc, stack)
        p = cx.p
        hT = cx.din("hT", [D, T])
        modv = cx.din("modv", [128, KC, 3, 2])
        qT = cx.din("qT", [NH, 128, T], BF16)
        kx = cx.din("kx", [NH, 128, NKT * 128], BF16)
        vx = cx.din("vx", [NKT * 128, D], BF16)
        bias = cx.din("bias", [NH, 128, 8 * NOFF, 128])
        w_out = cx.din("w_out", [D, D])
        oT = cx.dout("oT", [D, T])

        m = cx.sb([128, KC, 3, 2], F32, "m")
        bm = Buf("m")
        dmaq(p, "sp", m[:], modv, writes=[bm])
        onesb = cx.sb([128, 128], BF16, "onesb")
        bob = Buf("onesb")
        p.op("pool", lambda e: e.memset(onesb[:], 1.0), writes=[bob])

        q = cx.sb([128, NH, T], BF16, "q")
        bq = Buf("q")
        for hh in range(4):
            dmaq(p, "sp", q[:, 4 * hh:4 * hh + 4, :], qT.rearrange("h d t -> d h t")[:, 4 * hh:4 * hh + 4, :], writes=[bq])
        o = cx.sb([128, NH, T], BF16, "o")
        bo = Buf("o")

        kh_r = Ring(cx, 2, [128, NKT * 128], BF16, "kh")
        vh_r = Ring(cx, 2, [128, NKT, 128], BF16, "vh")
        bi_r = Ring(cx, 2, [128, 8 * NOFF, 128], F32, "bi")
        s_r = Ring(cx, 4, [128, 128], F32, "sps", psum=True)
        o_r = Ring(cx, 1, [128, 128], F32, "ops", psum=True)
        d_r = Ring(cx, 1, [128, 128], F32, "dps", psum=True)
        t_r = Ring(cx, 4, [128, 128], F32, "tmpa")
        p_r = Ring(cx, 5, [128, 128], BF16, "pa")
        r_r = Ring(cx, 2, [128, 128], F32, "rden")
        vxv = vx.rearrange("(kt p) f -> p kt f", p=128)
        for h in range(NH):
            kh, bkh = kh_r.next()
            vh, bvh = vh_r.next()
            bi, bbi = bi_r.next()
            dmaq(p, "sp", kh[:], kx[h], writes=[bkh])
            dmaq(p, "sp", vh[:], vxv[:, :, h * 128:(h + 1) * 128], writes=[bvh])
            for hf in range(2):
                dmaq(p, "sp", bi[:, hf * 28:(hf + 1) * 28, :], bias[h, :, hf * 28:(hf + 1) * 28, :], writes=[bbi])
            for i in range(NT):
                qs = q[:, h, i * 128:(i + 1) * 128]
                tiles = [(i + oo, oo) for oo in range(NOFF)] if i < 8 else []
                tiles += [(NSLAB, None), (NSLAB + 1, None)]
                ops, bops = o_r.next()
                dps, bdps = d_r.next()
                pend = []

                def score(j, oo):
                    sps, bsps = s_r.next()
                    mm(p, sps[:, :], kh[:, j * 128:(j + 1) * 128], qs, True, True, [bkh, bq], [bsps])
                    pa, bpa = p_r.next()
                    if oo is not None:
                        tm, btm = t_r.next()
                        tt(p, "dve", tm[:, :], sps[:, :], bi[:, i * NOFF + oo, :], ALU.add, [bsps, bbi], [btm])
                        actf(p, pa[:, :], tm[:, :], AF.Exp, [btm], [bpa])
                    else:
                        actf(p, pa[:, :], sps[:, :], AF.Exp, [bsps], [bpa])
                    return pa, bpa

                def accum(n_, j, pa, bpa):
                    first, last = n_ == 0, n_ == len(tiles) - 1
                    mm(p, ops[:, :], vh[:, j, :], pa[:, :], first, last, [bvh, bpa], [bops])
                    mm(p, dps[:, :], onesb[:], pa[:, :], first, last, [bob, bpa], [bdps])

                LOOK = 2
                for n_, (j, oo) in enumerate(tiles):
                    pend.append((n_, j) + score(j, oo))
                    if len(pend) > LOOK:
                        accum(*pend.pop(0))
                while pend:
                    accum(*pend.pop(0))
                rd, brd = r_r.next()
                p.op("dve", lambda e, rd=rd, dps=dps: e.reciprocal(out=rd[:, :], in_=dps[:, :]), reads=[bdps], writes=[brd])
                tt(p, "dve", o[:, h, i * 128:(i + 1) * 128], ops[:, :], rd[:, :], ALU.mult, [bops, brd], [bo])

        w_r = Ring(cx, 2, [128, KC, 256], BF16, "w")
        ps_r = Ring(cx, 2, [128, 512], F32, "ps", psum=True)
        hb_r = Ring(cx, 2, [128, 512], F32, "hb")
        toks = []
        evac_o = make_residual_evac(cx, hT, oT, m, bm, hb_r, toks)
        linear_fm(cx, wview(w_out), 0, KC, o, bo, evac_o, w_r, ps_r)
        p.wait_tokens("sp", toks)
        with nc.Block() as block:
            p.emit(block)
    return nc


import ml_dtypes
NPBF = ml_dtypes.bfloat16
GRID_W = 64
ROWS = 64
WIN_H, WIN_W = 8, 16
NEG = -1e30
_cache = {}
_TRACE = False


def _prog(name, builder):
    if name not in _cache:
        _cache[name] = builder()
    return _cache[name]


def _run(name, builder, in_maps):
    nc = _prog(name, builder)
    if _TRACE:
        res = run_bass_kernel_spmd(nc, in_maps, core_ids=list(range(NCORES)), trace=True)
        print("KTRACE", name, res.exec_time_ns)
    else:
        res = run_bass_kernel_spmd(nc, in_maps, core_ids=list(range(NCORES)))
    return res.results


def fmaj(v):
    return np.ascontiguousarray(np.asarray(v).reshape(-1, 128).T)


def to_cores(x, ctx):
    outs = []
    for c in range(NCORES):
        b, k = divmod(c, 4)
        a = np.zeros((T, D), np.float32)
        a[:TL] = x[b, k * TL:(k + 1) * TL]
        if k < 2:
            a[TL:] = ctx[b, k * TCX:(k + 1) * TCX]
        outs.append(np.ascontiguousarray(a.T))
    return outs


def from_cores(hs):
    out = np.empty((2, 4096, D), np.float32)
    for c in range(NCORES):
        b, k = divmod(c, 4)
        out[b, k * TL:(k + 1) * TL] = hs[c][:, :TL].T
    return out


def modv_for(modT_layer, s, c):
    b = c // 4
    mv = np.empty((128, KC, 3, 2), np.float32)
    for j in range(3):
        blk = modT_layer[:, (3 * s + j) * KC:(3 * s + j + 1) * KC, :]
        mv[:, :, j, 0] = blk[:, :, b]
        mv[:, :, j, 1] = blk[:, :, 2]
    return mv


def run_ada(c, c_ctx, w_ada, b_ada):
    cond = np.zeros((4, D), np.float32)
    cond[0:2] = c
    cond[2] = c_ctx
    condT = np.ascontiguousarray(cond.T.reshape(KC, 128, 4).transpose(1, 0, 2))
    in_maps = []
    for core in range(NCORES):
        layer, half = divmod(core, 2)
        cols = slice(half * 9216, (half + 1) * 9216)
        in_maps.append({"condT": condT, "wa": np.ascontiguousarray(w_ada[layer][:, cols]),
                        "ba": np.ascontiguousarray(b_ada[layer][cols].reshape(72, 128).T)})
    res = _run("ada", build_ada, in_maps)
    modT = []
    for layer in range(4):
        modT.append(np.concatenate([res[2 * layer]["mo"], res[2 * layer + 1]["mo"]], axis=1))
    return modT


def run_ffn(hs, modT_l, s, g, wgu, wdn):
    gv = fmaj(g)
    in_maps = [{"hT": hs[c], "modv": modv_for(modT_l, s, c), "gv": gv, "wgu": wgu, "wdn": wdn} for c in range(NCORES)]
    res = _run("ffn", build_ffn, in_maps)
    return [r["oT"] for r in res]


def run_mixa(hs, modT_l, g, w_in, v_gain, w_s, b_s, w_out):
    gv = fmaj(g)
    wsT = np.ascontiguousarray(w_s.transpose(2, 0, 1))
    bsb = np.ascontiguousarray(np.broadcast_to(b_s[None], (128, 16, 128)))
    vg = fmaj(v_gain)
    in_maps = [{"hT": hs[c], "modv": modv_for(modT_l, 1, c), "gv": gv, "w_in": w_in, "vg": vg, "wsT": wsT, "bsb": bsb,
                "w_out": w_out} for c in range(NCORES)]
    res = _run("mixa", build_mixa, in_maps)
    return [r["oT"] for r in res]


def _nat_bias_index():
    if "natidx" in _cache:
        return _cache["natidx"]
    kpar = np.arange(128) // 64
    kcol = np.arange(128) % 64
    idx = np.full((4, 128, 8, NOFF, 128), 15 * 31, np.int32)
    qpar = kpar[None, :]
    qcol = kcol[None, :]
    cstart = np.clip(qcol - WIN_W // 2, 0, GRID_W - WIN_W)
    colv = (kcol[:, None] >= cstart) & (kcol[:, None] < cstart + WIN_W)
    dc = np.clip(kcol[:, None] - qcol, 1 - WIN_W, WIN_W - 1) + (WIN_W - 1)
    for kq in range(4):
        r0 = 16 * kq
        for i in range(8):
            for o in range(NOFF):
                kr = r0 - 6 + 2 * (i + o) + kpar[:, None]
                qr = r0 + 2 * i + qpar
                rstart = np.clip(qr - WIN_H // 2, 0, ROWS - WIN_H)
                valid = (kr >= 0) & (kr < ROWS) & (kr >= rstart) & (kr < rstart + WIN_H) & colv
                dr = kr - qr + (WIN_H - 1)
                lin = np.clip(dr, 0, 14) * 31 + dc
                idx[kq, :, i, o, :] = np.where(valid, lin, 15 * 31)
    idx = idx.reshape(4, 128, 8 * NOFF, 128)
    _cache["natidx"] = idx
    return idx


def run_nat(hs, modT_l, g, w_qkv, q_gain, k_gain, rpb, w_out):
    gv = fmaj(g)
    qkg = np.ascontiguousarray(np.stack([q_gain, k_gain], axis=1).astype(np.float32))
    in_maps = [{"hT": hs[c], "modv": modv_for(modT_l, 1, c), "gv": gv, "wqkv": w_qkv, "qkg": qkg} for c in range(NCORES)]
    r1 = _run("nat1", build_nat1, in_maps)
    idx = _nat_bias_index()
    rp = np.concatenate([rpb.reshape(NH, -1), np.full((NH, 1), NEG, np.float32)], axis=1)
    in_maps = []
    for c in range(NCORES):
        b, k = divmod(c, 4)
        Kb = np.concatenate([np.asarray(r1[4 * b + kk]["ko"])[:, :, :TL] for kk in range(4)], axis=2)
        Vb = np.concatenate([np.asarray(r1[4 * b + kk]["vo"])[:TL] for kk in range(4)], axis=0)
        Kc = np.concatenate([np.asarray(r1[4 * b + kk]["ko"])[:, :, TL:] for kk in range(2)], axis=2)
        Vc = np.concatenate([np.asarray(r1[4 * b + kk]["vo"])[TL:] for kk in range(2)], axis=0)
        kx = np.zeros((NH, 128, NKT * 128), NPBF)
        vx = np.zeros((NKT * 128, D), NPBF)
        lo = (16 * k - 6) * 64
        hi = lo + NSLAB * 128
        a, bnd = max(lo, 0), min(hi, 4096)
        kx[:, :, a - lo:bnd - lo] = Kb[:, :, a:bnd]
        vx[a - lo:bnd - lo] = Vb[a:bnd]
        kx[:, :, NSLAB * 128:] = Kc
        vx[NSLAB * 128:] = Vc
        bias = np.ascontiguousarray(rp[:, idx[k]])
        in_maps.append({"hT": hs[c], "modv": modv_for(modT_l, 1, c), "qT": np.asarray(r1[c]["qo"]), "kx": kx, "vx": vx,
                        "bias": bias, "w_out": w_out})
    r2 = _run("nat2", build_nat2, in_maps)
    return [r["oT"] for r in r2]


def build_s5a():
    from contextlib import ExitStack
    nc = bass.Bass("TRN2", target_bir_lowering=False)
    with ExitStack() as stack:
        cx = Ctx(nc, stack)
        p = cx.p
        hT = cx.din("hT", [D, T])
        modv = cx.din("modv", [128, KC, 3, 2])
        gv = cx.din("gv", [128, KC])
        w_in = cx.din("w_in", [D, D])
        uo = cx.dout("uo", [D, T])
        consts = make_consts(cx)
        m, A, bm, bA = load_mod(cx, modv, gv)
        xn = cx.sb([128, KC, T], BF16, "xn")
        bxn = Buf("xn")
        nr = norm_rings(cx)
        rms_adaln_stream(cx, hT, xn, bxn, m, A, bm, bA, consts, nr)
        w_r = Ring(cx, 2, [128, KC, 256], BF16, "w")
        ps_r = Ring(cx, 2, [128, 512], F32, "ps", psum=True)
        uov = uo.rearrange("(kc p) t -> p kc t", p=128)
        toks = []

        def evac(j, t0, n, seg, ps, bps):
            sg, bsg = nr[0].next()
            actf(p, sg[:, :n], ps[:, :n], AF.Identity, [bps], [bsg])
            toks.append(dmaq(p, "sp", uov[:, j, t0:t0 + n], sg[:, :n], reads=[bsg]))

        linear_fm(cx, wview(w_in), 0, KC, xn, bxn, evac, w_r, ps_r)
        p.wait_tokens("sp", toks)
        with nc.Block() as block:
            p.emit(block)
    return nc


NPOS = 256 + 4096
S5_NOSYNC = ("dve",)
SB = 128
NBLK = NPOS // SB
GL = 16


def build_s5b():
    from contextlib import ExitStack
    nc = bass.Bass("TRN2", target_bir_lowering=False)
    with ExitStack() as stack:
        cx = Ctx(nc, stack)
        p = cx.p
        U = cx.din("U", [32, GL, 2, NPOS])
        areT = cx.din("areT", [128, GL])
        aimT = cx.din("aimT", [128, GL])
        ldtT = cx.din("ldtT", [128, GL])
        breT = cx.din("breT", [128, GL, 16])
        bimT = cx.din("bimT", [128, GL, 16])
        creT = cx.din("creT", [128, GL, 16])
        cimT = cx.din("cimT", [128, GL, 16])
        ident = cx.din("ident", [128, 128])
        YF = cx.dout("YF", [2, 128, 2, NPOS])
        YB = cx.dout("YB", [2, 128, 2, NPOS])

        def small(shape, name):
            return cx.sb(shape, F32, name), Buf(name)

        def load(src, shape, name):
            t, b = small(shape, name)
            dmaq(p, "sp", t[:], src, writes=[b])
            return t, b

        are, b_are = load(areT, [128, GL], "are")
        aim, b_aim = load(aimT, [128, GL], "aim")
        ldt, b_ldt = load(ldtT, [128, GL], "ldt")
        bre, b_bre = load(breT, [128, GL, 16], "bre")
        bim, b_bim = load(bimT, [128, GL, 16], "bim")
        cre, b_cre = load(creT, [128, GL, 16], "cre")
        cim, b_cim = load(cimT, [128, GL, 16], "cim")
        idt, b_idt = load(ident, [128, 128], "idt")

        dt_, b_dt = small([128, GL], "dt")
        actf(p, dt_[:], ldt[:], AF.Exp, [b_ldt], [b_dt])
        xr, b_xr = small([128, GL], "xr")
        xi, b_xi = small([128, GL], "xi")
        tt(p, "dve", xr[:], are[:], dt_[:], ALU.mult, [b_are, b_dt], [b_xr])
        tt(p, "dve", xi[:], aim[:], dt_[:], ALU.mult, [b_aim, b_dt], [b_xi])
        mag, b_mag = small([128, GL], "mag")
        actf(p, mag[:], xr[:], AF.Exp, [b_xr], [b_mag])
        sn, b_sn = small([128, GL], "sn")
        cs, b_cs = small([128, GL], "cs")
        t1, b_t1 = small([128, GL], "t1")
        t2, b_t2 = small([128, GL], "t2")
        actf(p, sn[:], xi[:], AF.Sin, [b_xi], [b_sn], scale=1.0 / 16)
        actf(p, t1[:], xi[:], AF.Sin, [b_xi], [b_t1], scale=1.0 / 32)
        tt(p, "dve", t1[:], t1[:], t1[:], ALU.mult, [b_t1], [b_t1])
        ts(p, "dve", cs[:], t1[:], -2.0, 1.0, ALU.mult, ALU.add, [b_t1], [b_cs])
        for _ in range(4):
            tt(p, "dve", t1[:], cs[:], cs[:], ALU.mult, [b_cs], [b_t1])
            tt(p, "dve", t2[:], sn[:], sn[:], ALU.mult, [b_sn], [b_t2])
            tt(p, "dve", sn[:], sn[:], cs[:], ALU.mult, [b_sn, b_cs], [b_sn])
            ts(p, "dve", sn[:], sn[:], 2.0, None, ALU.mult, None, [b_sn], [b_sn])
            tt(p, "dve", cs[:], t1[:], t2[:], ALU.subtract, [b_t1, b_t2], [b_cs])
        abr, b_abr = small([128, GL], "abr")
        abi, b_abi = small([128, GL], "abi")
        tt(p, "dve", abr[:], mag[:], cs[:], ALU.mult, [b_mag, b_cs], [b_abr])
        tt(p, "dve", abi[:], mag[:], sn[:], ALU.mult, [b_mag, b_sn], [b_abi])
        nr_, b_nr = small([128, GL], "nr")
        ts(p, "dve", nr_[:], abr[:], -1.0, None, ALU.add, None, [b_abr], [b_nr])
        den, b_den = small([128, GL], "den")
        tt(p, "dve", den[:], are[:], are[:], ALU.mult, [b_are], [b_den])
        tt(p, "dve", t1[:], aim[:], aim[:], ALU.mult, [b_aim], [b_t1])
        tt(p, "dve", den[:], den[:], t1[:], ALU.add, [b_den, b_t1], [b_den])
        p.op("dve", lambda e: e.reciprocal(out=den[:], in_=den[:]), reads=[b_den], writes=[b_den])
        kr, b_kr = small([128, GL], "kr")
        ki, b_ki = small([128, GL], "ki")
        tt(p, "dve", kr[:], nr_[:], are[:], ALU.mult, [b_nr, b_are], [b_kr])
        tt(p, "dve", t1[:], abi[:], aim[:], ALU.mult, [b_abi, b_aim], [b_t1])
        tt(p, "dve", kr[:], kr[:], t1[:], ALU.add, [b_kr, b_t1], [b_kr])
        tt(p, "dve", kr[:], kr[:], den[:], ALU.mult, [b_kr, b_den], [b_kr])
        tt(p, "dve", ki[:], abi[:], are[:], ALU.mult, [b_abi, b_are], [b_ki])
        tt(p, "dve", t1[:], nr_[:], aim[:], ALU.mult, [b_nr, b_aim], [b_t1])
        tt(p, "dve", ki[:], ki[:], t1[:], ALU.subtract, [b_ki, b_t1], [b_ki])
        tt(p, "dve", ki[:], ki[:], den[:], ALU.mult, [b_ki, b_den], [b_ki])
        nki, b_nki = small([128, GL], "nki")
        ts(p, "dve", nki[:], ki[:], -1.0, None, ALU.mult, None, [b_ki], [b_nki])
        Bw, b_Bw = small([128, GL, 2, 32], "Bw")
        p.op("pool", lambda e: e.memset(Bw[:], 0.0), writes=[b_Bw])
        tb, b_tb = small([128, 16], "tb")
        for g in range(GL):
            for d in range(2):
                ps_ = slice(64 * d, 64 * d + 64)
                cols = slice(16 * d, 16 * d + 16)
                ts(p, "dve", tb[ps_, :], bim[ps_, g, :], nki[ps_, g:g + 1], None, ALU.mult, None, [b_bim, b_nki], [b_tb])
                stt(p, Bw[ps_, g, 0, cols], bre[ps_, g, :], kr[ps_, g:g + 1], tb[ps_, :], ALU.mult, ALU.add, [b_bre, b_kr, b_tb], [b_Bw])
                ts(p, "dve", tb[ps_, :], bre[ps_, g, :], ki[ps_, g:g + 1], None, ALU.mult, None, [b_bre, b_ki], [b_tb])
                stt(p, Bw[ps_, g, 1, cols], bim[ps_, g, :], kr[ps_, g:g + 1], tb[ps_, :], ALU.mult, ALU.add, [b_bim, b_kr, b_tb], [b_Bw])
        Blk = cx.sb([32, GL, 2, 128], BF16, "Blk")
        b_Blk = Buf("Blk")
        tp_r = Ring(cx, 2, [32, 128], F32, "tps", psum=True)
        for g in range(GL):
            for ri in range(2):
                tp, btp = tp_r.next()
                p.op("pe", lambda e, tp=tp, g=g, ri=ri: e.transpose(tp[:, :], Bw[:, g, ri, :], idt[:]), reads=[b_Bw, b_idt], writes=[btp])
                actf(p, Blk[:, g, ri, :], tp[:, :], AF.Identity, [btp], [b_Blk])
        Cp = cx.sb([128, GL, 2, 128], BF16, "Cp")
        b_Cp = Buf("Cp")
        p.op("pool", lambda e: e.memset(Cp[:], 0.0), writes=[b_Cp])
        for g in range(GL):
            c0 = (g % 8) * 16
            p.op("pool", lambda e, g=g, c0=c0: e.tensor_copy(out=Cp[:, g, 0, c0:c0 + 16], in_=cre[:, g, :]), reads=[b_cre], writes=[b_Cp])
            ts(p, "pool", Cp[:, g, 1, c0:c0 + 16], cim[:, g, :], -1.0, None, ALU.mult, None, [b_cim], [b_Cp])
        Ar, b_Ar = small([128, 32, 2], "Ar")
        AiN, b_AiN = small([128, 32], "AiN")
        AiP, b_AiP = small([128, 32], "AiP")
        nabi, b_nabi = small([128, GL], "nabi")
        ts(p, "dve", nabi[:], abi[:], -1.0, None, ALU.mult, None, [b_abi], [b_nabi])
        Ar4 = Ar[:].rearrange("p (g b) r -> p g b r", b=2)
        AiN3 = AiN[:].rearrange("p (g b) -> p g b", b=2)
        AiP3 = AiP[:].rearrange("p (g b) -> p g b", b=2)
        for b in range(2):
            for ri in range(2):
                p.op("pool", lambda e, b=b, ri=ri: e.tensor_copy(out=Ar4[:, :, b, ri], in_=abr[:]), reads=[b_abr], writes=[b_Ar])
            p.op("pool", lambda e, b=b: e.tensor_copy(out=AiN3[:, :, b], in_=nabi[:]), reads=[b_nabi], writes=[b_AiN])
            p.op("pool", lambda e, b=b: e.tensor_copy(out=AiP3[:, :, b], in_=abi[:]), reads=[b_abi], writes=[b_AiP])

        ub_r = Ring(cx, 2, [32, GL, 2, SB], BF16, "ub")
        bu_r = Ring(cx, 2, [128, SB, 32, 2], F32, "bu")
        H_r = Ring(cx, 2, [128, SB, 32, 2], F32, "H")
        Hb_r = Ring(cx, 2, [128, SB, 32, 2], BF16, "Hb")
        dps_r = Ring(cx, 2, [128, 4, SB], F32, "dps", psum=True)
        yps_r = Ring(cx, 2, [128, SB], F32, "yps", psum=True)
        ys_r = Ring(cx, 3, [128, SB], F32, "ys")
        m1, b_m1 = small([128, 32, 2], "m1")
        m2, b_m2 = small([128, 32, 2], "m2")
        zero, b_zero = small([128, 32, 2], "zero")
        p.op("pool", lambda e: e.memset(zero[:], 0.0), writes=[b_zero])
        toks = []
        prevX, b_prevX = zero[:, :, :], b_zero
        for blk in range(NBLK):
            k0 = blk * SB
            ub, b_ub = ub_r.next()
            dmaq(p, "pool", ub[:], U[:, :, :, k0:k0 + SB], writes=[b_ub])
            bu, b_bu = bu_r.next()
            bu_v = bu[:].rearrange("p k c r -> p k (c r)")
            for q4 in range(GL):
                dps, b_dps = dps_r.next()
                for b in range(2):
                    for ri in range(2):
                        mm(p, dps[:, b * 2 + ri, :], Blk[:, q4, ri, :], ub[:, q4, b, :], True, True, [b_Blk, b_ub], [b_dps])
                actf(p, bu_v[:, :, q4 * 4:q4 * 4 + 4], dps[:].rearrange("p c k -> p k c"), AF.Identity, [b_dps], [b_bu])
            H, b_H = H_r.next()
            p.nosync = set(S5_NOSYNC) | set(GLOBAL_NOSYNC)
            for k in range(SB):
                tt(p, "dve", m1[:], Ar[:], prevX, ALU.mult, [b_Ar, b_prevX], [b_m1])
                tt(p, "dve", m2[:, :, 0], AiN[:], prevX[:, :, 1], ALU.mult, [b_AiN, b_prevX], [b_m2])
                tt(p, "dve", m2[:, :, 1], AiP[:], prevX[:, :, 0], ALU.mult, [b_AiP, b_prevX], [b_m2])
                tt(p, "dve", m1[:], m1[:], m2[:], ALU.add, [b_m1, b_m2], [b_m1])
                tt(p, "dve", H[:, k, :, :], m1[:], bu[:, k, :, :], ALU.add, [b_m1, b_bu], [b_H])
                prevX, b_prevX = H[:, k, :, :], b_H
            p.nosync = set(GLOBAL_NOSYNC)
            Hb, b_Hb = Hb_r.next()
            p.op("pool", lambda e, Hb=Hb, H=H: e.tensor_copy(out=Hb[:], in_=H[:]), reads=[b_H], writes=[b_Hb])
            for cc in range(2):
                for b in range(2):
                    for d in range(2):
                        ps_ = slice(64 * d, 64 * d + 64)
                        yps, b_yps = yps_r.next()
                        n_ = 0
                        for g8 in range(8):
                            g = cc * 8 + g8
                            for ri in range(2):
                                mm(p, yps[:, :], Cp[ps_, g, ri, :], Hb[ps_, :, g * 2 + b, ri], n_ == 0, n_ == 15, [b_Cp, b_Hb], [b_yps])
                                n_ += 1
                        ys, b_ys = ys_r.next()
                        actf(p, ys[:, :], yps[:, :], AF.Identity, [b_yps], [b_ys])
                        dst = (YF if d == 0 else YB)
                        toks.append(dmaq(p, "sp", dst[cc, :, b, k0:k0 + SB], ys[:, :], reads=[b_ys]))
        p.wait_tokens("sp", toks)
        with nc.Block() as block:
            p.emit(block)
    return nc


def build_s5c():
    from contextlib import ExitStack
    nc = bass.Bass("TRN2", target_bir_lowering=False)
    with ExitStack() as stack:
        cx = Ctx(nc, stack)
        p = cx.p
        hT = cx.din("hT", [D, T])
        modv = cx.din("modv", [128, KC, 3, 2])
        uT = cx.din("uT", [D, T])
        yfT = cx.din("yfT", [D, T])
        ybT = cx.din("ybT", [D, T])
        dsk = cx.din("dsk", [128, KC])
        w_glu = cx.din("w_glu", [D, 2 * D])
        oT = cx.dout("oT", [D, T])
        m = cx.sb([128, KC, 3, 2], F32, "m")
        bm = Buf("m")
        dmaq(p, "sp", m[:], modv, writes=[bm])
        dk = cx.sb([128, KC], F32, "dk")
        bdk = Buf("dk")
        dmaq(p, "sp", dk[:], dsk, writes=[bdk])
        gl = cx.sb([128, KC, T], BF16, "gl")
        bgl = Buf("gl")
        a_r = Ring(cx, 2, [128, T], F32, "ya")
        b_r = Ring(cx, 2, [128, T], F32, "yb")
        c_r = Ring(cx, 2, [128, T], F32, "yc")
        gelu = Gelu(cx, width=T)
        uv = uT.rearrange("(kc p) t -> p kc t", p=128)
        fv = yfT.rearrange("(kc p) t -> p kc t", p=128)
        bv = ybT.rearrange("(kc p) t -> p kc t", p=128)
        for kc in range(KC):
            ya, bya = a_r.next()
            yb, byb = b_r.next()
            yc, byc = c_r.next()
            dmaq(p, "sp", ya[:], uv[:, kc, :], writes=[bya])
            dmaq(p, "sp", yb[:], fv[:, kc, :], writes=[byb])
            dmaq(p, "sp", yc[:], bv[:, kc, :], writes=[byc])
            tt(p, "dve", yb[:], yb[:], yc[:], ALU.add, [byb, byc], [byb])
            stt(p, ya[:], ya[:], dk[:, kc:kc + 1], yb[:], ALU.mult, ALU.add, [bya, bdk, byb], [bya])
            gelu(p, gl[:, kc, :], ya[:], T, [bya], [bgl])
        w_r = Ring(cx, 2, [128, KC, 128], BF16, "wa")
        w2_r = Ring(cx, 2, [128, KC, 128], BF16, "wg")
        pa_r = Ring(cx, 2, [128, 512], F32, "pa", psum=True)
        pg_r = Ring(cx, 2, [128, 512], F32, "pg", psum=True)
        sg_r = Ring(cx, 2, [128, 512], F32, "sg")
        hb_r = Ring(cx, 2, [128, 512], F32, "hb")
        wv = wview(w_glu)
        hv = hT.rearrange("(kc p) t -> p kc t", p=128)
        ov = oT.rearrange("(kc p) t -> p kc t", p=128)
        toks = []
        for j in range(KC):
            wa, bwa = w_r.next()
            wg, bwg = w2_r.next()
            dmaq(p, "pool", wa[:], wv[:, :, j * 128:(j + 1) * 128], writes=[bwa])
            dmaq(p, "pool", wg[:], wv[:, :, D + j * 128:D + (j + 1) * 128], writes=[bwg])
            for (t0, n, seg) in BLOCKS:
                pa, bpa = pa_r.next()
                pg, bpg = pg_r.next()
                for kc in range(KC):
                    mm(p, pa[:, :n], wa[:, kc, :], gl[:, kc, t0:t0 + n], kc == 0, kc == KC - 1, [bwa, bgl], [bpa])
                for kc in range(KC):
                    mm(p, pg[:, :n], wg[:, kc, :], gl[:, kc, t0:t0 + n], kc == 0, kc == KC - 1, [bwg, bgl], [bpg])
                sg, bsg = sg_r.next()
                actf(p, sg[:, :n], pg[:, :n], AF.Sigmoid, [bpg], [bsg])
                tt(p, "dve", sg[:, :n], pa[:, :n], sg[:, :n], ALU.mult, [bpa, bsg], [bsg])
                hb, bhb = hb_r.next()
                dmaq(p, "sp", hb[:, :n], hv[:, j, t0:t0 + n], writes=[bhb])
                stt(p, hb[:, :n], sg[:, :n], m[:, j, 2, seg:seg + 1], hb[:, :n], ALU.mult, ALU.add, [bsg, bm, bhb], [bhb])
                toks.append(dmaq(p, "sp", ov[:, j, t0:t0 + n], hb[:, :n], reads=[bhb]))
        p.wait_tokens("sp", toks)
        with nc.Block() as block:
            p.emit(block)
    return nc


def run_s5(hs, modT_l, g, w_in, a_re, a_im, log_dt, b_re, b_im, c_re, c_im, d_skip, w_glu):
    gv = fmaj(g)
    in_maps = [{"hT": hs[c], "modv": modv_for(modT_l, 1, c), "gv": gv, "w_in": w_in} for c in range(NCORES)]
    r1 = _run("s5a", build_s5a, in_maps)
    us = [r["uo"] for r in r1]
    seq = []
    for b in range(2):
        parts = [us[4 * b + k][:, TL:] for k in range(2)] + [us[4 * b + k][:, :TL] for k in range(4)]
        seq.append(np.concatenate(parts, axis=1))
    seq = np.stack(seq, axis=1)
    order_b = np.concatenate([np.arange(255, -1, -1), 256 + np.arange(4095, -1, -1)])
    ident = np.eye(128, dtype=np.float32)
    in_maps = []
    for c in range(NCORES):
        gs = slice(GL * c, GL * (c + 1))
        sc = seq[256 * c:256 * (c + 1)].reshape(GL, 16, 2, NPOS)
        Uc = np.empty((32, GL, 2, NPOS), np.float32)
        Uc[:16] = sc.transpose(1, 0, 2, 3)
        Uc[16:] = sc[:, :, :, order_b].transpose(1, 0, 2, 3)

        def dp(a):
            a = a[:, gs]
            if a.ndim == 3:
                return np.ascontiguousarray(a.transpose(0, 2, 1).reshape(128, GL))
            return np.ascontiguousarray(a.transpose(0, 2, 1, 3).reshape(128, GL, a.shape[3]))

        ldt = np.ascontiguousarray(np.broadcast_to(log_dt[:, gs][:, None, :], (2, 64, GL)).reshape(128, GL))
        in_maps.append({"U": Uc, "areT": dp(a_re), "aimT": dp(a_im), "ldtT": ldt,
                        "breT": dp(b_re), "bimT": dp(b_im),
                        "creT": dp(c_re.transpose(0, 1, 3, 2)), "cimT": dp(c_im.transpose(0, 1, 3, 2)), "ident": ident})
    r2 = _run("s5b", build_s5b, in_maps)
    YF = np.concatenate([r["YF"].reshape(256, 2, NPOS) for r in r2], axis=0)
    YBo = np.concatenate([r["YB"].reshape(256, 2, NPOS) for r in r2], axis=0)
    YB = np.empty_like(YBo)
    YB[:, :, order_b] = YBo

    def percore(Y, c):
        b, k = divmod(c, 4)
        a = np.zeros((D, T), np.float32)
        a[:, :TL] = Y[:, b, 256 + k * TL:256 + (k + 1) * TL]
        if k < 2:
            a[:, TL:] = Y[:, b, k * TCX:(k + 1) * TCX]
        return a

    dsk = fmaj(d_skip)
    in_maps = [{"hT": hs[c], "modv": modv_for(modT_l, 1, c), "uT": us[c], "yfT": percore(YF, c), "ybT": percore(YB, c),
                "dsk": dsk, "w_glu": w_glu} for c in range(NCORES)]
    r3 = _run("s5c", build_s5c, in_maps)
    return [r["oT"] for r in r3]


def kernel(x, c, ctx, c_ctx, w_ada, b_ada, norm_g, ffn_w_gu, ffn_w_down,
           a_w_in, a_v_gain, a_w_s, a_b_s, a_w_out,
           b_w_qkv, b_q_gain, b_k_gain, b_rpb, b_w_out,
           c_w_in, c_a_re, c_a_im, c_log_dt, c_b_re, c_b_im, c_c_re, c_c_im, c_d, c_w_glu):
    f = lambda a: np.asarray(a, dtype=np.float32)
    x, c, ctx, c_ctx = f(x), f(c), f(ctx), f(c_ctx)
    modT = run_ada(c, c_ctx, f(w_ada), f(b_ada))
    hs = to_cores(x, ctx)
    depth = 4
    for i in range(depth):
        kind, j = i % 3, i // 3
        hs = run_ffn(hs, modT[i], 0, f(norm_g[i, 0]), f(ffn_w_gu[i, 0]), f(ffn_w_down[i, 0]))
        if kind == 0:
            hs = run_mixa(hs, modT[i], f(norm_g[i, 1]), f(a_w_in[j]), f(a_v_gain[j]), f(a_w_s[j]), f(a_b_s[j]), f(a_w_out[j]))
        elif kind == 1:
            hs = run_nat(hs, modT[i], f(norm_g[i, 1]), f(b_w_qkv[j]), f(b_q_gain[j]), f(b_k_gain[j]), f(b_rpb[j]), f(b_w_out[j]))
        else:
            hs = run_s5(hs, modT[i], f(norm_g[i, 1]), f(c_w_in[j]), f(c_a_re[j]), f(c_a_im[j]), f(c_log_dt[j]),
                        f(c_b_re[j]), f(c_b_im[j]), f(c_c_re[j]), f(c_c_im[j]), f(c_d[j]), f(c_w_glu[j]))
        hs = run_ffn(hs, modT[i], 2, f(norm_g[i, 2]), f(ffn_w_gu[i, 1]), f(ffn_w_down[i, 1]))
    return from_cores(hs)
```

```python
import numpy as np
import concourse.bass as bass
import concourse.mybir as mybir
from concourse.bass_utils import run_bass_kernel_spmd
from concourse.alu_op_type import AluOpType as ALU

F32 = mybir.dt.float32
BF16 = mybir.dt.bfloat16
AF = mybir.ActivationFunctionType

D = 2048
KC = 16
DFF = 5632
FC = 44
NCORES = 8
TL = 1024
TCX = 128
T = TL + TCX
EPS = 1e-6
GLOBAL_NOSYNC = ()


class Buf:
    __slots__ = ("lw", "rd", "name")

    def __init__(self, name=""):
        self.lw = None
        self.rd = {}
        self.name = name


class Prog:
    CENG = ("pe", "act", "dve", "pool")

    def __init__(self, nc, stack, n_dsem=20):
        self.nc = nc
        self.eng = {"pe": nc.tensor, "act": nc.scalar, "dve": nc.vector,
                    "pool": nc.gpsimd, "sp": nc.sync}
        self.q = {e: [] for e in self.eng}
        self.cnt = {e: 0 for e in self.CENG}
        self.seen = {e: {} for e in self.eng}
        self.nosync = set(GLOBAL_NOSYNC)
        self.csem = {e: stack.enter_context(nc.semaphore("c_" + e)) for e in self.CENG}
        self.dsem = {}
        self.dcum = {}
        self.drr = {}
        for qn in ("sp", "pool", "act"):
            n = n_dsem if qn != "act" else 6
            self.dsem[qn] = [stack.enter_context(nc.semaphore("d_%s%d" % (qn, i))) for i in range(n)]
            self.dcum[qn] = [0] * n
            self.drr[qn] = 0

    def _need(self, eng, tok, waits):
        if tok is None:
            return
        key, val = tok
        if key == ("c", "pe") and eng == "pe":
            return
        if key[0] == "c" and key[1] == eng and eng in self.nosync:
            return
        if self.seen[eng].get(key, 0) >= val:
            return
        if waits.get(key, 0) < val:
            waits[key] = val

    def _deps(self, eng, reads, writes):
        waits = {}
        for b in reads:
            self._need(eng, b.lw, waits)
        for b in writes:
            self._need(eng, b.lw, waits)
            for k, v in b.rd.items():
                self._need(eng, (k, v), waits)
        for k, v in waits.items():
            self.seen[eng][k] = v
        return waits

    def _commit(self, tok, reads, writes):
        key, val = tok
        for b in reads:
            if b.rd.get(key, 0) < val:
                b.rd[key] = val
        for b in writes:
            b.lw = tok
            b.rd = {}

    def op(self, eng, fn, reads=(), writes=()):
        waits = self._deps(eng, reads, writes)
        self.cnt[eng] += 1
        tok = (("c", eng), self.cnt[eng])
        self.q[eng].append((list(waits.items()), fn, ("c", eng, 1)))
        self._commit(tok, reads, writes)
        return tok

    def dma(self, qn, fns, reads=(), writes=()):
        if not isinstance(fns, (list, tuple)):
            fns = [fns]
        waits = self._deps(qn, reads, writes)
        i = self.drr[qn]
        self.drr[qn] = (i + 1) % len(self.dsem[qn])
        key = ("d", qn, i)
        prev = self.dcum[qn][i]
        if prev > 0 and self.seen[qn].get(key, 0) < prev:
            waits[key] = prev
            self.seen[qn][key] = prev
        for j, fn in enumerate(fns):
            self.dcum[qn][i] += 16
            self.q[qn].append((list(waits.items()) if j == 0 else [], fn, ("d", qn, i)))
        tok = (key, self.dcum[qn][i])
        self._commit(tok, reads, writes)
        return tok

    def wait_tokens(self, eng, toks):
        waits = {}
        for t in toks:
            self._need(eng, t, waits)
        for k, v in waits.items():
            self.seen[eng][k] = v
        self.q[eng].append((list(waits.items()), None, None))

    def _sem(self, key):
        if key[0] == "c":
            return self.csem[key[1]]
        return self.dsem[key[1]][key[2]]

    def emit(self, block):
        decos = {"pe": block.tensor, "act": block.scalar, "dve": block.vector,
                 "pool": block.gpsimd, "sp": block.sync}
        for e in self.eng:
            items = self.q[e]
            if not items:
                continue

            def body(engine, items=items):
                for waits, fn, inc in items:
                    for key, val in waits:
                        engine.wait_ge(self._sem(key), val)
                    if fn is None:
                        continue
                    ins = fn(engine)
                    if inc[0] == "c":
                        ins.then_inc(self.csem[inc[1]], 1)
                    else:
                        ins.then_inc(self.dsem[inc[1]][inc[2]], 16)

            decos[e](body)


class Ctx:
    def __init__(self, nc, stack):
        self.nc = nc
        self.stack = stack
        self.p = Prog(nc, stack)
        self._n = 0

    def sb(self, shape, dt, name=None):
        self._n += 1
        return self.stack.enter_context(self.nc.sbuf_tensor(name or ("sb%d" % self._n), list(shape), dt))

    def ps(self, shape, dt=F32, name=None):
        self._n += 1
        return self.stack.enter_context(self.nc.psum_tensor(name or ("ps%d" % self._n), list(shape), dt))

    def din(self, name, shape, dt=F32):
        return self.nc.dram_tensor(name, list(shape), dt, kind="ExternalInput").ap()

    def dout(self, name, shape, dt=F32):
        return self.nc.dram_tensor(name, list(shape), dt, kind="ExternalOutput").ap()


class Ring:
    def __init__(self, cx, n, shape, dt, name, psum=False):
        self.t = [(cx.ps(shape, dt, "%s%d" % (name, i)) if psum else cx.sb(shape, dt, "%s%d" % (name, i)))
                  for i in range(n)]
        self.b = [Buf("%s%d" % (name, i)) for i in range(n)]
        self.i = 0

    def next(self):
        i = self.i
        self.i = (i + 1) % len(self.t)
        return self.t[i], self.b[i]


def make_consts(cx):
    p = cx.p
    ones = cx.sb([128, 128], F32, "ones")
    epsc = cx.sb([128, 1], F32, "epsc")
    b = Buf("consts")
    p.op("pool", lambda e: e.memset(ones[:], 1.0), writes=[b])
    p.op("pool", lambda e: e.memset(epsc[:], EPS), writes=[b])
    return ones, epsc, b


def load_mod(cx, modv, gv):
    p = cx.p
    m = cx.sb([128, KC, 3, 2], F32)
    g = cx.sb([128, KC], F32)
    A = cx.sb([128, KC, 2], F32)
    bm, bg, bA = Buf("m"), Buf("g"), Buf("A")
    p.dma("sp", lambda e: e.dma_start(out=m[:], in_=modv), writes=[bm])
    p.dma("sp", lambda e: e.dma_start(out=g[:], in_=gv), writes=[bg])
    for s in range(2):
        p.op("dve", lambda e, s=s: e.scalar_tensor_tensor(out=A[:, :, s], in0=m[:, :, 1, s], scalar=1.0,
                                                           in1=g[:, :], op0=ALU.add, op1=ALU.mult),
             reads=[bm, bg], writes=[bA])
    return m, A, bm, bA


def rms_adaln(cx, src, bsrc, dst, bdst, blocks, m, A, bm, bA, consts, rings):
    p = cx.p
    ones, epsc, bc = consts
    sq_r, ps_r, rs_r, tmp_r = rings
    for (t0, n, seg) in blocks:
        _rms_block(p, src, bsrc, dst, bdst, t0, n, seg, m, A, bm, bA, ones, epsc, bc, sq_r, ps_r, rs_r, tmp_r)


def _rms_block(p, src, bsrc, dst, bdst, t0, n, seg, m, A, bm, bA, ones, epsc, bc, sq_r, ps_r, rs_r, tmp_r, s0=None):
    s0 = t0 if s0 is None else s0
    ss, bss = ps_r.next()
    for kc in range(KC):
        sq, bsq = sq_r.next()
        p.op("act", lambda e, sq=sq, kc=kc: e.activation(out=sq[:, :n], in_=src[:, kc, s0:s0 + n], func=AF.Square),
             reads=[bsrc], writes=[bsq])
        p.op("pe", lambda e, sq=sq, kc=kc: e.matmul(ss[:, :n], lhsT=ones[:], rhs=sq[:, :n],
                                                    start=(kc == 0), stop=(kc == KC - 1)),
             reads=[bsq, bc], writes=[bss])
    rs, brs = rs_r.next()
    p.op("act", lambda e: e.activation(out=rs[:, :n], in_=ss[:, :n], func=AF.Sqrt, bias=epsc[:], scale=1.0 / D),
         reads=[bss, bc], writes=[brs])
    p.op("dve", lambda e: e.reciprocal(out=rs[:, :n], in_=rs[:, :n]), reads=[brs], writes=[brs])
    for kc in range(KC):
        tmp, btmp = tmp_r.next()
        p.op("dve", lambda e, tmp=tmp, kc=kc: e.scalar_tensor_tensor(
            out=tmp[:, :n], in0=src[:, kc, s0:s0 + n], scalar=A[:, kc, seg:seg + 1], in1=rs[:, :n],
            op0=ALU.mult, op1=ALU.mult), reads=[bsrc, brs, bA], writes=[btmp])
        p.op("act", lambda e, tmp=tmp, kc=kc: e.activation(out=dst[:, kc, t0:t0 + n], in_=tmp[:, :n],
                                                           func=AF.Identity, bias=m[:, kc, 0, seg:seg + 1], scale=1.0),
             reads=[btmp, bm], writes=[bdst])


BLOCKS = [(0, 512, 0), (512, 512, 0), (1024, 128, 1)]


def wview(w):
    return w.rearrange("(kc p) f -> p kc f", p=128)


def build_ffn(dbg=False):
    from contextlib import ExitStack
    nc = bass.Bass("TRN2", target_bir_lowering=False)
    with ExitStack() as stack:
        cx = Ctx(nc, stack)
        p = cx.p
        hT = cx.din("hT", [D, T])
        modv = cx.din("modv", [128, KC, 3, 2])
        gv = cx.din("gv", [128, KC])
        wgu = cx.din("wgu", [D, 2 * DFF])
        wdn = cx.din("wdn", [DFF, D])
        oT = cx.dout("oT", [D, T])

        consts = make_consts(cx)
        h = cx.sb([128, KC, T], F32, "h")
        bh = Buf("h")
        hv = hT.rearrange("(kc p) t -> p kc t", p=128)
        for q in range(4):
            p.dma("sp", lambda e, q=q: e.dma_start(out=h[:, 4 * q:4 * q + 4, :], in_=hv[:, 4 * q:4 * q + 4, :]),
                  writes=[bh])
        m, A, bm, bA = load_mod(cx, modv, gv)
        G = cx.sb([128, KC, 2], F32, "G")
        bG = Buf("G")
        p.op("dve", lambda e: e.tensor_scalar(out=G[:], in0=m[:, :, 2, :], scalar1=0.5, scalar2=None, op0=ALU.mult),
             reads=[bm], writes=[bG])

        xn = cx.sb([128, KC, T], BF16, "xn")
        bxn = Buf("xn")
        rings = (Ring(cx, 2, [128, 512], F32, "sq"), Ring(cx, 1, [128, 512], F32, "ssps", psum=True),
                 Ring(cx, 2, [128, 512], F32, "rs"), Ring(cx, 2, [128, 512], F32, "tmp"))
        rms_adaln(cx, h, bh, xn, bxn, BLOCKS, m, A, bm, bA, consts, rings)

        if dbg:
            xo = cx.dout("xo", [128, KC, T], BF16)
            p.wait_tokens("sp", [p.dma("sp", lambda e: e.dma_start(out=xo, in_=xn[:]), reads=[bxn])])
        ffn_core(cx, xn, bxn, h, bh, G, bG, wgu, wdn)

        ov = oT.rearrange("(kc p) t -> p kc t", p=128)
        toks = []
        for q in range(4):
            toks.append(p.dma("sp", lambda e, q=q: e.dma_start(out=ov[:, 4 * q:4 * q + 4, :], in_=h[:, 4 * q:4 * q + 4, :]),
                              reads=[bh]))
        p.wait_tokens("sp", toks)
        with nc.Block() as block:
            p.emit(block)
    return nc


def ffn_core(cx, xn, bxn, h, bh, G, bG, wgu, wdn, GRP=4):
    p = cx.p
    wguv = wview(wgu)
    wdnv = wdn.rearrange("(fc p) d -> p fc d", p=128)
    wg_r = Ring(cx, 2, [128, KC, 256], BF16, "wg")
    wu_r = Ring(cx, 2, [128, KC, 256], BF16, "wu")
    wd_r = Ring(cx, 2, [128, GRP, D], BF16, "wd")
    act_r = Ring(cx, 2, [128, GRP, T], BF16, "act")
    gps_r = Ring(cx, 2, [128, 512], F32, "gps", psum=True)
    ups_r = Ring(cx, 2, [128, 512], F32, "ups", psum=True)
    yps_r = Ring(cx, 2, [128, 512], F32, "yps", psum=True)
    sg_r = Ring(cx, 2, [128, 512], F32, "sg")
    wg = wu = None
    for grp in range(FC // GRP):
        act, bact = act_r.next()
        wd, bwd = wd_r.next()
        for fl in range(GRP):
            p.dma("pool", lambda e, wd=wd, fl=fl, grp=grp: e.dma_start(out=wd[:, fl, :], in_=wdnv[:, grp * GRP + fl, :]),
                  writes=[bwd])
        for fl in range(GRP):
            fc = grp * GRP + fl
            if fc % 2 == 0 and (not _DBG_SKIPW or fc < 4):
                wg, bwg = wg_r.next()
                wu, bwu = wu_r.next()
                p.dma("pool", lambda e, wg=wg, fc=fc: e.dma_start(out=wg[:], in_=wguv[:, :, fc * 128:fc * 128 + 256]),
                      writes=[bwg])
                p.dma("pool", lambda e, wu=wu, fc=fc: e.dma_start(out=wu[:], in_=wguv[:, :, DFF + fc * 128:DFF + fc * 128 + 256]),
                      writes=[bwu])
            c0 = (fc % 2) * 128
            for (t0, n, seg) in BLOCKS:
                gps, bgps = gps_r.next()
                ups, bups = ups_r.next()
                for kc in range(KC):
                    p.op("pe", lambda e, gps=gps, wg=wg, kc=kc, c0=c0, t0=t0, n=n: e.matmul(
                        gps[:, :n], lhsT=wg[:, kc, c0:c0 + 128], rhs=xn[:, kc, t0:t0 + n],
                        start=(kc == 0), stop=(kc == KC - 1)), reads=[bwg, bxn], writes=[bgps])
                for kc in range(KC):
                    p.op("pe", lambda e, ups=ups, wu=wu, kc=kc, c0=c0, t0=t0, n=n: e.matmul(
                        ups[:, :n], lhsT=wu[:, kc, c0:c0 + 128], rhs=xn[:, kc, t0:t0 + n],
                        start=(kc == 0), stop=(kc == KC - 1)), reads=[bwu, bxn], writes=[bups])
                sg, bsg = sg_r.next()
                p.op("act", lambda e, sg=sg, gps=gps, n=n: e.activation(out=sg[:, :n], in_=gps[:, :n], func=AF.Silu),
                     reads=[bgps], writes=[bsg])
                p.op("dve", lambda e, sg=sg, ups=ups, act=act, fl=fl, t0=t0, n=n: e.tensor_tensor(
                    out=act[:, fl, t0:t0 + n], in0=ups[:, :n], in1=sg[:, :n], op=ALU.mult),
                    reads=[bups, bsg], writes=[bact])
        for dc in range(KC):
            for (t0, n, seg) in BLOCKS:
                yps, byps = yps_r.next()
                for fl in range(GRP):
                    p.op("pe", lambda e, yps=yps, wd=wd, fl=fl, dc=dc, act=act, t0=t0, n=n: e.matmul(
                        yps[:, :n], lhsT=wd[:, fl, dc * 128:(dc + 1) * 128], rhs=act[:, fl, t0:t0 + n],
                        start=(fl == 0), stop=(fl == GRP - 1)), reads=[bwd, bact], writes=[byps])
                p.op("dve", lambda e, yps=yps, dc=dc, t0=t0, n=n, seg=seg: e.scalar_tensor_tensor(
                    out=h[:, dc, t0:t0 + n], in0=yps[:, :n], scalar=G[:, dc, seg:seg + 1], in1=h[:, dc, t0:t0 + n],
                    op0=ALU.mult, op1=ALU.add), reads=[byps, bG, bh], writes=[bh])


NF_ADA = 4 * 9 * KC // NCORES


def build_ada():
    from contextlib import ExitStack
    nc = bass.Bass("TRN2", target_bir_lowering=False)
    with ExitStack() as stack:
        cx = Ctx(nc, stack)
        p = cx.p
        condT = cx.din("condT", [128, KC, 4])
        wa = cx.din("wa", [D, NF_ADA * 128])
        ba = cx.din("ba", [128, NF_ADA])
        mo = cx.dout("mo", [128, NF_ADA, 4])
        ct = cx.sb([128, KC, 4], F32)
        cs = cx.sb([128, KC, 4], BF16)
        bt = cx.sb([128, NF_ADA], F32)
        res = cx.sb([128, NF_ADA, 4], F32)
        bct, bcs, bbt, bres = Buf(), Buf(), Buf(), Buf()
        p.dma("sp", lambda e: e.dma_start(out=ct[:], in_=condT), writes=[bct])
        p.dma("sp", lambda e: e.dma_start(out=bt[:], in_=ba), writes=[bbt])
        p.op("act", lambda e: e.activation(out=cs[:], in_=ct[:], func=AF.Silu), reads=[bct], writes=[bcs])
        w_r = Ring(cx, 3, [128, KC, 512], BF16, "w")
        ps_r = Ring(cx, 2, [128, 4, 4], F32, "ps", psum=True)
        wav = wview(wa)
        for g4 in range(NF_ADA // 4):
            w, bw = w_r.next()
            p.dma("pool", lambda e, w=w, g4=g4: e.dma_start(out=w[:], in_=wav[:, :, g4 * 512:(g4 + 1) * 512]), writes=[bw])
            ps, bps = ps_r.next()
            for j in range(4):
                for kc in range(KC):
                    p.op("pe", lambda e, ps=ps, w=w, j=j, kc=kc: e.matmul(
                        ps[:, j, :], lhsT=w[:, kc, j * 128:(j + 1) * 128], rhs=cs[:, kc, :],
                        start=(kc == 0), stop=(kc == KC - 1)), reads=[bw, bcs], writes=[bps])
            for j in range(4):
                f = g4 * 4 + j
                p.op("dve", lambda e, ps=ps, j=j, f=f: e.tensor_scalar(out=res[:, f, :], in0=ps[:, j, :], scalar1=bt[:, f:f + 1],
                                                                      scalar2=None, op0=ALU.add),
                     reads=[bps, bbt], writes=[bres])
        tok = p.dma("sp", lambda e: e.dma_start(out=mo, in_=res[:]), reads=[bres])
        p.wait_tokens("sp", [tok])
        with nc.Block() as block:
            p.emit(block)
    return nc


def mm(p, out, lhsT, rhs, start, stop, reads, writes):
    return p.op("pe", lambda e: e.matmul(out, lhsT=lhsT, rhs=rhs, start=start, stop=stop), reads=reads, writes=writes)


def actf(p, out, in_, func, reads, writes, bias=None, scale=None, accum_out=None):
    kw = {}
    if bias is not None:
        kw["bias"] = bias
    if scale is not None:
        kw["scale"] = scale
    if accum_out is not None:
        kw["accum_out"] = accum_out
    return p.op("act", lambda e: e.activation(out=out, in_=in_, func=func, **kw), reads=reads, writes=writes)


def tt(p, eng, out, in0, in1, op, reads, writes):
    return p.op(eng, lambda e: e.tensor_tensor(out=out, in0=in0, in1=in1, op=op), reads=reads, writes=writes)


def ts(p, eng, out, in0, s1, s2, op0, op1, reads, writes):
    if op1 is None:
        return p.op(eng, lambda e: e.tensor_scalar(out=out, in0=in0, scalar1=s1, scalar2=None, op0=op0), reads=reads, writes=writes)
    return p.op(eng, lambda e: e.tensor_scalar(out=out, in0=in0, scalar1=s1, scalar2=s2, op0=op0, op1=op1), reads=reads, writes=writes)


def stt(p, out, in0, scalar, in1, op0, op1, reads, writes):
    return p.op("dve", lambda e: e.scalar_tensor_tensor(out=out, in0=in0, scalar=scalar, in1=in1, op0=op0, op1=op1),
                reads=reads, writes=writes)


def dmaq(p, q, out, in_, reads=(), writes=()):
    return p.dma(q, lambda e: e.dma_start(out=out, in_=in_), reads=reads, writes=writes)


class Gelu:
    def __init__(self, cx, width=512):
        self.a = Ring(cx, 2, [128, width], F32, "gl_a")
        self.b = Ring(cx, 2, [128, width], F32, "gl_b")

    def __call__(self, p, out, ps, n, rd, wr, accum_sq=None):
        a, ba = self.a.next()
        b, bb = self.b.next()
        actf(p, a[:, :n], ps, AF.Square, rd, [ba])
        ts(p, "dve", a[:, :n], a[:, :n], 0.044715, 1.0, ALU.mult, ALU.add, [ba], [ba])
        tt(p, "dve", a[:, :n], a[:, :n], ps, ALU.mult, [ba] + rd, [ba])
        actf(p, b[:, :n], a[:, :n], AF.Sigmoid, [ba], [bb], scale=1.5957691216057308)
        tt(p, "dve", out, b[:, :n], ps, ALU.mult, [bb] + rd, wr)


def load_h(cx, hT, name="h"):
    p = cx.p
    h = cx.sb([128, KC, T], F32, name)
    bh = Buf(name)
    hv = hT.rearrange("(kc p) t -> p kc t", p=128)
    for q in range(4):
        dmaq(p, "sp", h[:, 4 * q:4 * q + 4, :], hv[:, 4 * q:4 * q + 4, :], writes=[bh])
    return h, bh


def store_h(cx, oT, h, bh):
    p = cx.p
    ov = oT.rearrange("(kc p) t -> p kc t", p=128)
    toks = [dmaq(p, "sp", ov[:, 4 * q:4 * q + 4, :], h[:, 4 * q:4 * q + 4, :], reads=[bh]) for q in range(4)]
    p.wait_tokens("sp", toks)


def rms_adaln_stream(cx, hT, dst, bdst, m, A, bm, bA, consts, rings, blocks=BLOCKS):
    p = cx.p
    ones, epsc, bc = consts
    hb = cx.sb([128, KC, 512], F32, "hblk")
    bhb = Buf("hblk")
    hv = hT.rearrange("(kc p) t -> p kc t", p=128)
    for (t0, n, seg) in blocks:
        for q in range(2):
            dmaq(p, "sp", hb[:, 8 * q:8 * q + 8, :n], hv[:, 8 * q:8 * q + 8, t0:t0 + n], writes=[bhb])
        _rms_block(p, hb, bhb, dst, bdst, t0, n, seg, m, A, bm, bA, ones, epsc, bc, *rings, s0=0)


def make_residual_evac(cx, hT, oT, m, bm, ring, toks, gate_idx=2):
    p = cx.p
    hv = hT.rearrange("(kc p) t -> p kc t", p=128)
    ov = oT.rearrange("(kc p) t -> p kc t", p=128)

    def evac(j, t0, n, seg, ps, bps):
        hb, bhb = ring.next()
        dmaq(p, "sp", hb[:, :n], hv[:, j, t0:t0 + n], writes=[bhb])
        stt(p, hb[:, :n], ps[:, :n], m[:, j, gate_idx, seg:seg + 1], hb[:, :n], ALU.mult, ALU.add, [bps, bm, bhb], [bhb])
        toks.append(dmaq(p, "sp", ov[:, j, t0:t0 + n], hb[:, :n], reads=[bhb]))

    return evac


def norm_rings(cx):
    return (Ring(cx, 2, [128, 512], F32, "sq"), Ring(cx, 1, [128, 512], F32, "ssps", psum=True),
            Ring(cx, 2, [128, 512], F32, "rs"), Ring(cx, 2, [128, 512], F32, "tmp"))


def linear_fm(cx, wv, col0, nchunks, x, bx, evac, w_r, ps_r, nk=KC, blocks=BLOCKS):
    p = cx.p
    w = bw = None
    for j in range(nchunks):
        if j % 2 == 0:
            w, bw = w_r.next()
            c = col0 + j * 128
            wd = 256 if j + 1 < nchunks else 128
            dmaq(p, "pool", w[:, :, :wd], wv[:, :, c:c + wd], writes=[bw])
        c0 = (j % 2) * 128
        for (t0, n, seg) in blocks:
            ps, bps = ps_r.next()
            for kc in range(nk):
                mm(p, ps[:, :n], w[:, kc, c0:c0 + 128], x[:, kc, t0:t0 + n], kc == 0, kc == nk - 1, [bw, bx], [bps])
            evac(j, t0, n, seg, ps, bps)


NT = T // 128


def build_mixa():
    from contextlib import ExitStack
    nc = bass.Bass("TRN2", target_bir_lowering=False)
    with ExitStack() as stack:
        cx = Ctx(nc, stack)
        p = cx.p
        hT = cx.din("hT", [D, T])
        modv = cx.din("modv", [128, KC, 3, 2])
        gv = cx.din("gv", [128, KC])
        w_in = cx.din("w_in", [D, 2 * D])
        vg = cx.din("vg", [128, KC])
        wsT = cx.din("wsT", [128, 16, 128])
        bsb = cx.din("bsb", [128, 16, 128])
        w_out = cx.din("w_out", [D, D])
        oT = cx.dout("oT", [D, T])

        consts = make_consts(cx)
        m, A, bm, bA = load_mod(cx, modv, gv)
        xn = cx.sb([128, KC, T], BF16, "xn")
        bxn = Buf("xn")
        nr = norm_rings(cx)
        rms_adaln_stream(cx, hT, xn, bxn, m, A, bm, bA, consts, nr)

        vgt = cx.sb([128, KC], F32, "vgt")
        wst = cx.sb([128, 16, 128], F32, "wst")
        bst = cx.sb([128, 16, 128], F32, "bst")
        bvg, bws, bbs = Buf(), Buf(), Buf()
        dmaq(p, "sp", vgt[:], vg, writes=[bvg])
        dmaq(p, "sp", wst[:], wsT, writes=[bws])
        dmaq(p, "sp", bst[:], bsb, writes=[bbs])

        gelu = Gelu(cx)
        w_r = Ring(cx, 2, [128, KC, 256], BF16, "w")
        ps_r = Ring(cx, 2, [128, 512], F32, "ps", psum=True)
        winv = wview(w_in)

        uT = cx.sb([128, KC, T], BF16, "uT")
        buT = Buf("uT")

        def evac_u(j, t0, n, seg, ps, bps):
            gelu(p, uT[:, j, t0:t0 + n], ps[:, :n], n, [bps], [buT])

        linear_fm(cx, winv, 0, KC, xn, bxn, evac_u, w_r, ps_r)

        v = cx.sb([128, NT, D], BF16, "v")
        bv = Buf("v")
        NCB = 8
        ssq = cx.sb([128, NT, NCB], F32, "ssq")
        bssq = Buf("ssq")
        for cb in range(NCB):
            wv_, bwv = w_r.next()
            dmaq(p, "pool", wv_[:, :, :], winv[:, :, D + cb * 256:D + (cb + 1) * 256], writes=[bwv])
            for ti in range(NT):
                ps, bps = ps_r.next()
                for kc in range(KC):
                    mm(p, ps[:, :256], xn[:, kc, ti * 128:(ti + 1) * 128], wv_[:, kc, :], kc == 0, kc == KC - 1, [bxn, bwv], [bps])
                vt, bvt = nr[3].next()
                gelu(p, vt[:, :256], ps[:, :256], 256, [bps], [bvt])
                junk, bjunk = nr[0].next()
                actf(p, junk[:, :256], vt[:, :256], AF.Square, [bvt], [bjunk, bssq], accum_out=ssq[:, ti, cb:cb + 1])
                p.op("pool", lambda e, vt=vt, ti=ti, cb=cb: e.tensor_copy(out=v[:, ti, cb * 256:(cb + 1) * 256], in_=vt[:, :256]),
                     reads=[bvt], writes=[bv])
        rv = cx.sb([128, NT], F32, "rv")
        brv = Buf("rv")
        p.op("dve", lambda e: e.tensor_reduce(out=rv[:, :], in_=ssq[:, :, :], axis=mybir.AxisListType.X, op=ALU.add),
             reads=[bssq], writes=[brv])
        actf(p, rv[:, :], rv[:, :], AF.Sqrt, [brv, consts[2]], [brv], bias=consts[1][:], scale=1.0 / D)
        p.op("dve", lambda e: e.reciprocal(out=rv[:, :], in_=rv[:, :]), reads=[brv], writes=[brv])

        z, bz = xn, bxn
        wp_r = Ring(cx, 3, [128, 128], BF16, "wp")
        st_r = Ring(cx, 2, [128, 128], F32, "st")
        sps_r = Ring(cx, 2, [128, 128], F32, "sps", psum=True)
        for ti in range(NT):
            for g in range(16):
                wp, bwp = wp_r.next()
                ts(p, "pool", wp[:, :], wst[:, g, :], rv[:, ti:ti + 1], None, ALU.mult, None, [bws, brv], [bwp])
                sp_, bsp = sps_r.next()
                mm(p, sp_[:, :], v[:, ti, g * 128:(g + 1) * 128], wp[:, :], True, True, [bv, bwp], [bsp])
                st, bst_ = st_r.next()
                stt(p, st[:, :], sp_[:, :], vgt[:, g:g + 1], bst[:, g, :], ALU.mult, ALU.add, [bsp, bvg, bbs], [bst_])
                tt(p, "dve", z[:, g, ti * 128:(ti + 1) * 128], st[:, :], uT[:, g, ti * 128:(ti + 1) * 128], ALU.mult,
                   [bst_, buT], [bz])

        woutv = wview(w_out)

        toks = []
        evac_o = make_residual_evac(cx, hT, oT, m, bm, nr[0], toks)
        linear_fm(cx, woutv, 0, KC, z, bz, evac_o, w_r, ps_r)
        p.wait_tokens("sp", toks)
        with nc.Block() as block:
            p.emit(block)
    return nc


HD = 128
NH = 16
ATT_SCALE = HD ** -0.5


def build_nat1():
    from contextlib import ExitStack
    nc = bass.Bass("TRN2", target_bir_lowering=False)
    with ExitStack() as stack:
        cx = Ctx(nc, stack)
        p = cx.p
        hT = cx.din("hT", [D, T])
        modv = cx.din("modv", [128, KC, 3, 2])
        gv = cx.din("gv", [128, KC])
        wqkv = cx.din("wqkv", [D, 3 * D])
        qkg = cx.din("qkg", [128, 2])
        qo = cx.dout("qo", [NH, 128, T], BF16)
        ko = cx.dout("ko", [NH, 128, T], BF16)
        vo = cx.dout("vo", [T, D], BF16)

        consts = make_consts(cx)
        ones, epsc, bc = consts
        m, A, bm, bA = load_mod(cx, modv, gv)
        xn = cx.sb([128, KC, T], BF16, "xn")
        bxn = Buf("xn")
        nr = norm_rings(cx)
        rms_adaln_stream(cx, hT, xn, bxn, m, A, bm, bA, consts, nr)
        gq = cx.sb([128, 2], F32, "gq")
        bgq = Buf()
        dmaq(p, "sp", gq[:], qkg, writes=[bgq])
        ts(p, "dve", gq[:, 0:1], gq[:, 0:1], ATT_SCALE, None, ALU.mult, None, [bgq], [bgq])

        w_r = Ring(cx, 2, [128, KC, 256], BF16, "w")
        ps_r = Ring(cx, 2, [128, 512], F32, "ps", psum=True)
        wv = wview(wqkv)
        qf_r = Ring(cx, 2, [128, 512], F32, "qf")
        st_r = Ring(cx, 3, [128, 512], BF16, "stg")
        toks = []
        for which, dst in ((0, qo), (1, ko)):
            def evac(j, t0, n, seg, ps, bps, which=which, dst=dst):
                qf, bqf = qf_r.next()
                actf(p, qf[:, :n], ps[:, :n], AF.Identity, [bps], [bqf])
                sq, bsq = nr[0].next()
                actf(p, sq[:, :n], qf[:, :n], AF.Square, [bqf], [bsq])
                ss, bss = nr[1].next()
                mm(p, ss[:, :n], ones[:], sq[:, :n], True, True, [bsq, bc], [bss])
                rs, brs = nr[2].next()
                actf(p, rs[:, :n], ss[:, :n], AF.Sqrt, [bss, bc], [brs], bias=epsc[:], scale=1.0 / HD)
                p.op("dve", lambda e: e.reciprocal(out=rs[:, :n], in_=rs[:, :n]), reads=[brs], writes=[brs])
                sg, bsg = st_r.next()
                stt(p, sg[:, :n], qf[:, :n], gq[:, which:which + 1], rs[:, :n], ALU.mult, ALU.mult, [bqf, bgq, brs], [bsg])
                toks.append(dmaq(p, "sp", dst[j, :, t0:t0 + n], sg[:, :n], reads=[bsg]))

            linear_fm(cx, wv, which * D, KC, xn, bxn, evac, w_r, ps_r)
        vs = cx.sb([128, NT, D], BF16, "vs")
        bvs = Buf("vs")
        for cb in range(8):
            w, bw = w_r.next()
            dmaq(p, "pool", w[:, :, :], wv[:, :, 2 * D + cb * 256:2 * D + (cb + 1) * 256], writes=[bw])
            for ti in range(NT):
                ps, bps = ps_r.next()
                for kc in range(KC):
                    mm(p, ps[:, :256], xn[:, kc, ti * 128:(ti + 1) * 128], w[:, kc, :], kc == 0, kc == KC - 1, [bxn, bw], [bps])
                actf(p, vs[:, ti, cb * 256:(cb + 1) * 256], ps[:, :256], AF.Identity, [bps], [bvs])
        vov = vo.rearrange("(ti p) f -> p ti f", p=128)
        toks.append(dmaq(p, "sp", vov, vs[:], reads=[bvs]))
        p.wait_tokens("sp", toks)
        with nc.Block() as block:
            p.emit(block)
    return nc


NSLAB = 14
NKT = NSLAB + 2
NOFF = 7


def build_nat2():
    from contextlib import ExitStack
    nc = bass.Bass("TRN2", target_bir_lowering=False)
    with ExitStack() as stack:
        cx = Ctx(nc, stack)
        p = cx.p
        hT = cx.din("hT", [D, T])
        modv = cx.din("modv", [128, KC, 3, 2])
        qT = cx.din("qT", [NH, 128, T], BF16)
        kx = cx.din("kx", [NH, 128, NKT * 128], BF16)
        vx = cx.din("vx", [NKT * 128, D], BF16)
        bias = cx.din("bias", [NH, 128, 8 * NOFF, 128])
        w_out = cx.din("w_out", [D, D])
        oT = cx.dout("oT", [D, T])

        m = cx.sb([128, KC, 3, 2], F32, "m")
        bm = Buf("m")
        dmaq(p, "sp", m[:], modv, writes=[bm])
        onesb = cx.sb([128, 128], BF16, "onesb")
        bob = Buf("onesb")
        p.op("pool", lambda e: e.memset(onesb[:], 1.0), writes=[bob])

        q = cx.sb([128, NH, T], BF16, "q")
        bq = Buf("q")
        for hh in range(4):
            dmaq(p, "sp", q[:, 4 * hh:4 * hh + 4, :], qT.rearrange("h d t -> d h t")[:, 4 * hh:4 * hh + 4, :], writes=[bq])
        o = cx.sb([128, NH, T], BF16, "o")
        bo = Buf("o")

        kh_r = Ring(cx, 2, [128, NKT * 128], BF16, "kh")
        vh_r = Ring(cx, 2, [128, NKT, 128], BF16, "vh")
        bi_r = Ring(cx, 2, [128, 8 * NOFF, 128], F32, "bi")
        s_r = Ring(cx, 4, [128, 128], F32, "sps", psum=True)
        o_r = Ring(cx, 1, [128, 128], F32, "ops", psum=True)
        d_r = Ring(cx, 1, [128, 128], F32, "dps", psum=True)
        t_r = Ring(cx, 4, [128, 128], F32, "tmpa")
        p_r = Ring(cx, 5, [128, 128], BF16, "pa")
        r_r = Ring(cx, 2, [128, 128], F32, "rden")
        vxv = vx.rearrange("(kt p) f -> p kt f", p=128)
        for h in range(NH):
            kh, bkh = kh_r.next()
            vh, bvh = vh_r.next()
            bi, bbi = bi_r.next()
            dmaq(p, "sp", kh[:], kx[h], writes=[bkh])
            dmaq(p, "sp", vh[:], vxv[:, :, h * 128:(h + 1) * 128], writes=[bvh])
            for hf in range(2):
                dmaq(p, "sp", bi[:, hf * 28:(hf + 1) * 28, :], bias[h, :, hf * 28:(hf + 1) * 28, :], writes=[bbi])
            for i in range(NT):
                qs = q[:, h, i * 128:(i + 1) * 128]
                tiles = [(i + oo, oo) for oo in range(NOFF)] if i < 8 else []
                tiles += [(NSLAB, None), (NSLAB + 1, None)]
                ops, bops = o_r.next()
                dps, bdps = d_r.next()
                pend = []

                def score(j, oo):
                    sps, bsps = s_r.next()
                    mm(p, sps[:, :], kh[:, j * 128:(j + 1) * 128], qs, True, True, [bkh, bq], [bsps])
                    pa, bpa = p_r.next()
                    if oo is not None:
                        tm, btm = t_r.next()
                        tt(p, "dve", tm[:, :], sps[:, :], bi[:, i * NOFF + oo, :], ALU.add, [bsps, bbi], [btm])
                        actf(p, pa[:, :], tm[:, :], AF.Exp, [btm], [bpa])
                    else:
                        actf(p, pa[:, :], sps[:, :], AF.Exp, [bsps], [bpa])
                    return pa, bpa

                def accum(n_, j, pa, bpa):
                    first, last = n_ == 0, n_ == len(tiles) - 1
                    mm(p, ops[:, :], vh[:, j, :], pa[:, :], first, last, [bvh, bpa], [bops])
                    mm(p, dps[:, :], onesb[:], pa[:, :], first, last, [bob, bpa], [bdps])

                LOOK = 2
                for n_, (j, oo) in enumerate(tiles):
                    pend.append((n_, j) + score(j, oo))
                    if len(pend) > LOOK:
                        accum(*pend.pop(0))
                while pend:
                    accum(*pend.pop(0))
                rd, brd = r_r.next()
                p.op("dve", lambda e, rd=rd, dps=dps: e.reciprocal(out=rd[:, :], in_=dps[:, :]), reads=[bdps], writes=[brd])
                tt(p, "dve", o[:, h, i * 128:(i + 1) * 128], ops[:, :], rd[:, :], ALU.mult, [bops, brd], [bo])

        w_r = Ring(cx, 2, [128, KC, 256], BF16, "w")
        ps_r = Ring(cx, 2, [128, 512], F32, "ps", psum=True)
        hb_r = Ring(cx, 2, [128, 512], F32, "hb")
        toks = []
        evac_o = make_residual_evac(cx, hT, oT, m, bm, hb_r, toks)
        linear_fm(cx, wview(w_out), 0, KC, o, bo, evac_o, w_r, ps_r)
        p.wait_tokens("sp", toks)
        with nc.Block() as block:
            p.emit(block)
    return nc


import ml_dtypes
NPBF = ml_dtypes.bfloat16
GRID_W = 64
ROWS = 64
WIN_H, WIN_W = 8, 16
NEG = -1e30
_cache = {}
_DBG_SKIPW = False
_TRACE = False


def _prog(name, builder):
    if name not in _cache:
        _cache[name] = builder()
    return _cache[name]


def _run(name, builder, in_maps):
    nc = _prog(name, builder)
    if _TRACE:
        res = run_bass_kernel_spmd(nc, in_maps, core_ids=list(range(NCORES)), trace=True)
        print("KTRACE", name, res.exec_time_ns)
    else:
        res = run_bass_kernel_spmd(nc, in_maps, core_ids=list(range(NCORES)))
    return res.results


def fmaj(v):
    return np.ascontiguousarray(np.asarray(v).reshape(-1, 128).T)


def to_cores(x, ctx):
    outs = []
    for c in range(NCORES):
        b, k = divmod(c, 4)
        a = np.zeros((T, D), np.float32)
        a[:TL] = x[b, k * TL:(k + 1) * TL]
        if k < 2:
            a[TL:] = ctx[b, k * TCX:(k + 1) * TCX]
        outs.append(np.ascontiguousarray(a.T))
    return outs


def from_cores(hs):
    out = np.empty((2, 4096, D), np.float32)
    for c in range(NCORES):
        b, k = divmod(c, 4)
        out[b, k * TL:(k + 1) * TL] = hs[c][:, :TL].T
    return out


def modv_for(modT_layer, s, c):
    b = c // 4
    mv = np.empty((128, KC, 3, 2), np.float32)
    for j in range(3):
        blk = modT_layer[:, (3 * s + j) * KC:(3 * s + j + 1) * KC, :]
        mv[:, :, j, 0] = blk[:, :, b]
        mv[:, :, j, 1] = blk[:, :, 2]
    return mv


def run_ada(c, c_ctx, w_ada, b_ada):
    cond = np.zeros((4, D), np.float32)
    cond[0:2] = c
    cond[2] = c_ctx
    condT = np.ascontiguousarray(cond.T.reshape(KC, 128, 4).transpose(1, 0, 2))
    in_maps = []
    for core in range(NCORES):
        layer, half = divmod(core, 2)
        cols = slice(half * 9216, (half + 1) * 9216)
        in_maps.append({"condT": condT, "wa": np.ascontiguousarray(w_ada[layer][:, cols]),
                        "ba": np.ascontiguousarray(b_ada[layer][cols].reshape(72, 128).T)})
    res = _run("ada", build_ada, in_maps)
    modT = []
    for layer in range(4):
        modT.append(np.concatenate([res[2 * layer]["mo"], res[2 * layer + 1]["mo"]], axis=1))
    return modT


def run_ffn(hs, modT_l, s, g, wgu, wdn):
    gv = fmaj(g)
    in_maps = [{"hT": hs[c], "modv": modv_for(modT_l, s, c), "gv": gv, "wgu": wgu, "wdn": wdn} for c in range(NCORES)]
    res = _run("ffn", build_ffn, in_maps)
    return [r["oT"] for r in res]


def run_mixa(hs, modT_l, g, w_in, v_gain, w_s, b_s, w_out):
    gv = fmaj(g)
    wsT = np.ascontiguousarray(w_s.transpose(2, 0, 1))
    bsb = np.ascontiguousarray(np.broadcast_to(b_s[None], (128, 16, 128)))
    vg = fmaj(v_gain)
    in_maps = [{"hT": hs[c], "modv": modv_for(modT_l, 1, c), "gv": gv, "w_in": w_in, "vg": vg, "wsT": wsT, "bsb": bsb,
                "w_out": w_out} for c in range(NCORES)]
    res = _run("mixa", build_mixa, in_maps)
    return [r["oT"] for r in res]


def _nat_bias_index():
    if "natidx" in _cache:
        return _cache["natidx"]
    kpar = np.arange(128) // 64
    kcol = np.arange(128) % 64
    idx = np.full((4, 128, 8, NOFF, 128), 15 * 31, np.int32)
    qpar = kpar[None, :]
    qcol = kcol[None, :]
    cstart = np.clip(qcol - WIN_W // 2, 0, GRID_W - WIN_W)
    colv = (kcol[:, None] >= cstart) & (kcol[:, None] < cstart + WIN_W)
    dc = np.clip(kcol[:, None] - qcol, 1 - WIN_W, WIN_W - 1) + (WIN_W - 1)
    for kq in range(4):
        r0 = 16 * kq
        for i in range(8):
            for o in range(NOFF):
                kr = r0 - 6 + 2 * (i + o) + kpar[:, None]
                qr = r0 + 2 * i + qpar
                rstart = np.clip(qr - WIN_H // 2, 0, ROWS - WIN_H)
                valid = (kr >= 0) & (kr < ROWS) & (kr >= rstart) & (kr < rstart + WIN_H) & colv
                dr = kr - qr + (WIN_H - 1)
                lin = np.clip(dr, 0, 14) * 31 + dc
                idx[kq, :, i, o, :] = np.where(valid, lin, 15 * 31)
    idx = idx.reshape(4, 128, 8 * NOFF, 128)
    _cache["natidx"] = idx
    return idx


def run_nat(hs, modT_l, g, w_qkv, q_gain, k_gain, rpb, w_out):
    gv = fmaj(g)
    qkg = np.ascontiguousarray(np.stack([q_gain, k_gain], axis=1).astype(np.float32))
    in_maps = [{"hT": hs[c], "modv": modv_for(modT_l, 1, c), "gv": gv, "wqkv": w_qkv, "qkg": qkg} for c in range(NCORES)]
    r1 = _run("nat1", build_nat1, in_maps)
    idx = _nat_bias_index()
    rp = np.concatenate([rpb.reshape(NH, -1), np.full((NH, 1), NEG, np.float32)], axis=1)
    in_maps = []
    for c in range(NCORES):
        b, k = divmod(c, 4)
        Kb = np.concatenate([np.asarray(r1[4 * b + kk]["ko"])[:, :, :TL] for kk in range(4)], axis=2)
        Vb = np.concatenate([np.asarray(r1[4 * b + kk]["vo"])[:TL] for kk in range(4)], axis=0)
        Kc = np.concatenate([np.asarray(r1[4 * b + kk]["ko"])[:, :, TL:] for kk in range(2)], axis=2)
        Vc = np.concatenate([np.asarray(r1[4 * b + kk]["vo"])[TL:] for kk in range(2)], axis=0)
        kx = np.zeros((NH, 128, NKT * 128), NPBF)
        vx = np.zeros((NKT * 128, D), NPBF)
        lo = (16 * k - 6) * 64
        hi = lo + NSLAB * 128
        a, bnd = max(lo, 0), min(hi, 4096)
        kx[:, :, a - lo:bnd - lo] = Kb[:, :, a:bnd]
        vx[a - lo:bnd - lo] = Vb[a:bnd]
        kx[:, :, NSLAB * 128:] = Kc
        vx[NSLAB * 128:] = Vc
        bias = np.ascontiguousarray(rp[:, idx[k]])
        in_maps.append({"hT": hs[c], "modv": modv_for(modT_l, 1, c), "qT": np.asarray(r1[c]["qo"]), "kx": kx, "vx": vx,
                        "bias": bias, "w_out": w_out})
    r2 = _run("nat2", build_nat2, in_maps)
    return [r["oT"] for r in r2]


def build_s5a():
    from contextlib import ExitStack
    nc = bass.Bass("TRN2", target_bir_lowering=False)
    with ExitStack() as stack:
        cx = Ctx(nc, stack)
        p = cx.p
        hT = cx.din("hT", [D, T])
        modv = cx.din("modv", [128, KC, 3, 2])
        gv = cx.din("gv", [128, KC])
        w_in = cx.din("w_in", [D, D])
        uo = cx.dout("uo", [D, T])
        consts = make_consts(cx)
        m, A, bm, bA = load_mod(cx, modv, gv)
        xn = cx.sb([128, KC, T], BF16, "xn")
        bxn = Buf("xn")
        nr = norm_rings(cx)
        rms_adaln_stream(cx, hT, xn, bxn, m, A, bm, bA, consts, nr)
        w_r = Ring(cx, 2, [128, KC, 256], BF16, "w")
        ps_r = Ring(cx, 2, [128, 512], F32, "ps", psum=True)
        uov = uo.rearrange("(kc p) t -> p kc t", p=128)
        toks = []

        def evac(j, t0, n, seg, ps, bps):
            sg, bsg = nr[0].next()
            actf(p, sg[:, :n], ps[:, :n], AF.Identity, [bps], [bsg])
            toks.append(dmaq(p, "sp", uov[:, j, t0:t0 + n], sg[:, :n], reads=[bsg]))

        linear_fm(cx, wview(w_in), 0, KC, xn, bxn, evac, w_r, ps_r)
        p.wait_tokens("sp", toks)
        with nc.Block() as block:
            p.emit(block)
    return nc


NPOS = 256 + 4096
S5_NOSYNC = ("dve", "pool")
S5_SPLIT = 32
SB = 128
NBLK = NPOS // SB
GL = 16


def build_s5b():
    from contextlib import ExitStack
    nc = bass.Bass("TRN2", target_bir_lowering=False)
    with ExitStack() as stack:
        cx = Ctx(nc, stack)
        p = cx.p
        U = cx.din("U", [32, GL, 2, NPOS])
        areT = cx.din("areT", [128, GL])
        aimT = cx.din("aimT", [128, GL])
        ldtT = cx.din("ldtT", [128, GL])
        breT = cx.din("breT", [128, GL, 16])
        bimT = cx.din("bimT", [128, GL, 16])
        creT = cx.din("creT", [128, GL, 16])
        cimT = cx.din("cimT", [128, GL, 16])
        ident = cx.din("ident", [128, 128])
        YF = cx.dout("YF", [2, 128, 2, NPOS])
        YB = cx.dout("YB", [2, 128, 2, NPOS])

        def small(shape, name):
            return cx.sb(shape, F32, name), Buf(name)

        def load(src, shape, name):
            t, b = small(shape, name)
            dmaq(p, "sp", t[:], src, writes=[b])
            return t, b

        are, b_are = load(areT, [128, GL], "are")
        aim, b_aim = load(aimT, [128, GL], "aim")
        ldt, b_ldt = load(ldtT, [128, GL], "ldt")
        bre, b_bre = load(breT, [128, GL, 16], "bre")
        bim, b_bim = load(bimT, [128, GL, 16], "bim")
        cre, b_cre = load(creT, [128, GL, 16], "cre")
        cim, b_cim = load(cimT, [128, GL, 16], "cim")
        idt, b_idt = load(ident, [128, 128], "idt")

        dt_, b_dt = small([128, GL], "dt")
        actf(p, dt_[:], ldt[:], AF.Exp, [b_ldt], [b_dt])
        xr, b_xr = small([128, GL], "xr")
        xi, b_xi = small([128, GL], "xi")
        tt(p, "dve", xr[:], are[:], dt_[:], ALU.mult, [b_are, b_dt], [b_xr])
        tt(p, "dve", xi[:], aim[:], dt_[:], ALU.mult, [b_aim, b_dt], [b_xi])
        mag, b_mag = small([128, GL], "mag")
        actf(p, mag[:], xr[:], AF.Exp, [b_xr], [b_mag])
        sn, b_sn = small([128, GL], "sn")
        cs, b_cs = small([128, GL], "cs")
        t1, b_t1 = small([128, GL], "t1")
        t2, b_t2 = small([128, GL], "t2")
        actf(p, sn[:], xi[:], AF.Sin, [b_xi], [b_sn], scale=1.0 / 16)
        actf(p, t1[:], xi[:], AF.Sin, [b_xi], [b_t1], scale=1.0 / 32)
        tt(p, "dve", t1[:], t1[:], t1[:], ALU.mult, [b_t1], [b_t1])
        ts(p, "dve", cs[:], t1[:], -2.0, 1.0, ALU.mult, ALU.add, [b_t1], [b_cs])
        for _ in range(4):
            tt(p, "dve", t1[:], cs[:], cs[:], ALU.mult, [b_cs], [b_t1])
            tt(p, "dve", t2[:], sn[:], sn[:], ALU.mult, [b_sn], [b_t2])
            tt(p, "dve", sn[:], sn[:], cs[:], ALU.mult, [b_sn, b_cs], [b_sn])
            ts(p, "dve", sn[:], sn[:], 2.0, None, ALU.mult, None, [b_sn], [b_sn])
            tt(p, "dve", cs[:], t1[:], t2[:], ALU.subtract, [b_t1, b_t2], [b_cs])
        abr, b_abr = small([128, GL], "abr")
        abi, b_abi = small([128, GL], "abi")
        tt(p, "dve", abr[:], mag[:], cs[:], ALU.mult, [b_mag, b_cs], [b_abr])
        tt(p, "dve", abi[:], mag[:], sn[:], ALU.mult, [b_mag, b_sn], [b_abi])
        nr_, b_nr = small([128, GL], "nr")
        ts(p, "dve", nr_[:], abr[:], -1.0, None, ALU.add, None, [b_abr], [b_nr])
        den, b_den = small([128, GL], "den")
        tt(p, "dve", den[:], are[:], are[:], ALU.mult, [b_are], [b_den])
        tt(p, "dve", t1[:], aim[:], aim[:], ALU.mult, [b_aim], [b_t1])
        tt(p, "dve", den[:], den[:], t1[:], ALU.add, [b_den, b_t1], [b_den])
        p.op("dve", lambda e: e.reciprocal(out=den[:], in_=den[:]), reads=[b_den], writes=[b_den])
        kr, b_kr = small([128, GL], "kr")
        ki, b_ki = small([128, GL], "ki")
        tt(p, "dve", kr[:], nr_[:], are[:], ALU.mult, [b_nr, b_are], [b_kr])
        tt(p, "dve", t1[:], abi[:], aim[:], ALU.mult, [b_abi, b_aim], [b_t1])
        tt(p, "dve", kr[:], kr[:], t1[:], ALU.add, [b_kr, b_t1], [b_kr])
        tt(p, "dve", kr[:], kr[:], den[:], ALU.mult, [b_kr, b_den], [b_kr])
        tt(p, "dve", ki[:], abi[:], are[:], ALU.mult, [b_abi, b_are], [b_ki])
        tt(p, "dve", t1[:], nr_[:], aim[:], ALU.mult, [b_nr, b_aim], [b_t1])
        tt(p, "dve", ki[:], ki[:], t1[:], ALU.subtract, [b_ki, b_t1], [b_ki])
        tt(p, "dve", ki[:], ki[:], den[:], ALU.mult, [b_ki, b_den], [b_ki])
        nki, b_nki = small([128, GL], "nki")
        ts(p, "dve", nki[:], ki[:], -1.0, None, ALU.mult, None, [b_ki], [b_nki])
        Bw, b_Bw = small([128, GL, 2, 32], "Bw")
        p.op("pool", lambda e: e.memset(Bw[:], 0.0), writes=[b_Bw])
        tb, b_tb = small([128, 16], "tb")
        for g in range(GL):
            for d in range(2):
                ps_ = slice(64 * d, 64 * d + 64)
                cols = slice(16 * d, 16 * d + 16)
                ts(p, "dve", tb[ps_, :], bim[ps_, g, :], nki[ps_, g:g + 1], None, ALU.mult, None, [b_bim, b_nki], [b_tb])
                stt(p, Bw[ps_, g, 0, cols], bre[ps_, g, :], kr[ps_, g:g + 1], tb[ps_, :], ALU.mult, ALU.add, [b_bre, b_kr, b_tb], [b_Bw])
                ts(p, "dve", tb[ps_, :], bre[ps_, g, :], ki[ps_, g:g + 1], None, ALU.mult, None, [b_bre, b_ki], [b_tb])
                stt(p, Bw[ps_, g, 1, cols], bim[ps_, g, :], kr[ps_, g:g + 1], tb[ps_, :], ALU.mult, ALU.add, [b_bim, b_kr, b_tb], [b_Bw])
        Blk = cx.sb([32, GL, 2, 128], BF16, "Blk")
        b_Blk = Buf("Blk")
        tp_r = Ring(cx, 2, [32, 128], F32, "tps", psum=True)
        for g in range(GL):
            for ri in range(2):
                tp, btp = tp_r.next()
                p.op("pe", lambda e, tp=tp, g=g, ri=ri: e.transpose(tp[:, :], Bw[:, g, ri, :], idt[:]), reads=[b_Bw, b_idt], writes=[btp])
                actf(p, Blk[:, g, ri, :], tp[:, :], AF.Identity, [btp], [b_Blk])
        Cp = cx.sb([128, GL, 2, 128], BF16, "Cp")
        b_Cp = Buf("Cp")
        p.op("pool", lambda e: e.memset(Cp[:], 0.0), writes=[b_Cp])
        for g in range(GL):
            c0 = (g % 8) * 16
            p.op("pool", lambda e, g=g, c0=c0: e.tensor_copy(out=Cp[:, g, 0, c0:c0 + 16], in_=cre[:, g, :]), reads=[b_cre], writes=[b_Cp])
            ts(p, "pool", Cp[:, g, 1, c0:c0 + 16], cim[:, g, :], -1.0, None, ALU.mult, None, [b_cim], [b_Cp])
        Ar, b_Ar = small([128, 32, 2], "Ar")
        AiS, b_AiS = small([128, 32, 2], "AiS")
        nabi, b_nabi = small([128, GL], "nabi")
        ts(p, "dve", nabi[:], abi[:], -1.0, None, ALU.mult, None, [b_abi], [b_nabi])
        Ar4 = Ar[:].rearrange("p (g b) r -> p g b r", b=2)
        Ai4 = AiS[:].rearrange("p (g b) r -> p g b r", b=2)
        for b in range(2):
            for ri in range(2):
                p.op("pool", lambda e, b=b, ri=ri: e.tensor_copy(out=Ar4[:, :, b, ri], in_=abr[:]), reads=[b_abr], writes=[b_Ar])
            p.op("pool", lambda e, b=b: e.tensor_copy(out=Ai4[:, :, b, 0], in_=nabi[:]), reads=[b_nabi], writes=[b_AiS])
            p.op("pool", lambda e, b=b: e.tensor_copy(out=Ai4[:, :, b, 1], in_=abi[:]), reads=[b_abi], writes=[b_AiS])

        ub_r = Ring(cx, 2, [32, GL, 2, SB], BF16, "ub")
        bu_r = Ring(cx, 2, [128, SB, 32, 2], F32, "bu")
        H_r = Ring(cx, 2, [128, SB, 32, 2], F32, "H")
        Hb_r = Ring(cx, 2, [128, SB, 32, 2], BF16, "Hb")
        dps_r = Ring(cx, 2, [128, 4, SB], F32, "dps", psum=True)
        yps_r = Ring(cx, 2, [128, SB], F32, "yps", psum=True)
        ys_r = Ring(cx, 3, [128, SB], F32, "ys")
        zero, b_zero = small([128, 32, 2], "zero")
        p.op("pool", lambda e: e.memset(zero[:], 0.0), writes=[b_zero])
        lanes = []
        for eng, sl in (("dve", slice(0, S5_SPLIT)), ("pool", slice(S5_SPLIT, 32))):
            n_ = sl.stop - sl.start
            if n_ <= 0:
                continue
            m1, b_m1 = small([128, n_, 2], "m1" + eng)
            m2, b_m2 = small([128, n_, 2], "m2" + eng)
            lanes.append(dict(eng=eng, sl=sl, m1=m1, b_m1=b_m1, m2=m2, b_m2=b_m2, prev=zero[:, sl, :], b_prev=b_zero))
        toks = []
        for blk in range(NBLK):
            k0 = blk * SB
            ub, b_ub = ub_r.next()
            dmaq(p, "pool", ub[:], U[:, :, :, k0:k0 + SB], writes=[b_ub])
            bu, b_bu = bu_r.next()
            bu_v = bu[:].rearrange("p k c r -> p k (c r)")
            for q4 in range(GL):
                dps, b_dps = dps_r.next()
                for b in range(2):
                    for ri in range(2):
                        mm(p, dps[:, b * 2 + ri, :], Blk[:, q4, ri, :], ub[:, q4, b, :], True, True, [b_Blk, b_ub], [b_dps])
                actf(p, bu_v[:, :, q4 * 4:q4 * 4 + 4], dps[:].rearrange("p c k -> p k c"), AF.Identity, [b_dps], [b_bu])
            H, b_H0 = H_r.next()
            b_Hl = [Buf("Hl%d" % li) for li in range(len(lanes))]
            p.nosync = set(S5_NOSYNC) | set(GLOBAL_NOSYNC)
            for k in range(SB):
                for li, L in enumerate(lanes):
                    eng, sl, m1, m2 = L["eng"], L["sl"], L["m1"], L["m2"]
                    pv, bpv = L["prev"], L["b_prev"]
                    wr = [b_Hl[li]] + ([b_H0] if k == 0 else [])
                    tt(p, eng, m1[:], Ar[:, sl, :], pv, ALU.mult, [b_Ar, bpv], [L["b_m1"]])
                    tt(p, eng, m2[:], AiS[:, sl, :], pv[:, :, ::-1], ALU.mult, [b_AiS, bpv], [L["b_m2"]])
                    tt(p, eng, m1[:], m1[:], m2[:], ALU.add, [L["b_m1"], L["b_m2"]], [L["b_m1"]])
                    tt(p, eng, H[:, k, sl, :], m1[:], bu[:, k, sl, :], ALU.add, [L["b_m1"], b_bu], wr)
                    L["prev"], L["b_prev"] = H[:, k, sl, :], b_Hl[li]
            p.nosync = set(GLOBAL_NOSYNC)
            Hb, b_Hb = Hb_r.next()
            actf(p, Hb[:].rearrange("p k c r -> p (k c r)"), H[:].rearrange("p k c r -> p (k c r)"), AF.Identity, b_Hl + [b_H0], [b_Hb, b_H0])
            for cc in range(2):
                for b in range(2):
                    for d in range(2):
                        ps_ = slice(64 * d, 64 * d + 64)
                        yps, b_yps = yps_r.next()
                        n_ = 0
                        for g8 in range(8):
                            g = cc * 8 + g8
                            for ri in range(2):
                                mm(p, yps[:, :], Cp[ps_, g, ri, :], Hb[ps_, :, g * 2 + b, ri], n_ == 0, n_ == 15, [b_Cp, b_Hb], [b_yps])
                                n_ += 1
                        ys, b_ys = ys_r.next()
                        actf(p, ys[:, :], yps[:, :], AF.Identity, [b_yps], [b_ys])
                        dst = (YF if d == 0 else YB)
                        toks.append(dmaq(p, "sp", dst[cc, :, b, k0:k0 + SB], ys[:, :], reads=[b_ys]))
        p.wait_tokens("sp", toks)
        with nc.Block() as block:
            p.emit(block)
    return nc


def build_s5c():
    from contextlib import ExitStack
    nc = bass.Bass("TRN2", target_bir_lowering=False)
    with ExitStack() as stack:
        cx = Ctx(nc, stack)
        p = cx.p
        hT = cx.din("hT", [D, T])
        modv = cx.din("modv", [128, KC, 3, 2])
        uT = cx.din("uT", [D, T])
        yfT = cx.din("yfT", [D, T])
        ybT = cx.din("ybT", [D, T])
        dsk = cx.din("dsk", [128, KC])
        w_glu = cx.din("w_glu", [D, 2 * D])
        oT = cx.dout("oT", [D, T])
        m = cx.sb([128, KC, 3, 2], F32, "m")
        bm = Buf("m")
        dmaq(p, "sp", m[:], modv, writes=[bm])
        dk = cx.sb([128, KC], F32, "dk")
        bdk = Buf("dk")
        dmaq(p, "sp", dk[:], dsk, writes=[bdk])
        gl = cx.sb([128, KC, T], BF16, "gl")
        bgl = Buf("gl")
        a_r = Ring(cx, 2, [128, T], F32, "ya")
        b_r = Ring(cx, 2, [128, T], F32, "yb")
        c_r = Ring(cx, 2, [128, T], F32, "yc")
        gelu = Gelu(cx, width=T)
        uv = uT.rearrange("(kc p) t -> p kc t", p=128)
        fv = yfT.rearrange("(kc p) t -> p kc t", p=128)
        bv = ybT.rearrange("(kc p) t -> p kc t", p=128)
        for kc in range(KC):
            ya, bya = a_r.next()
            yb, byb = b_r.next()
            yc, byc = c_r.next()
            dmaq(p, "sp", ya[:], uv[:, kc, :], writes=[bya])
            dmaq(p, "sp", yb[:], fv[:, kc, :], writes=[byb])
            dmaq(p, "sp", yc[:], bv[:, kc, :], writes=[byc])
            tt(p, "dve", yb[:], yb[:], yc[:], ALU.add, [byb, byc], [byb])
            stt(p, ya[:], ya[:], dk[:, kc:kc + 1], yb[:], ALU.mult, ALU.add, [bya, bdk, byb], [bya])
            gelu(p, gl[:, kc, :], ya[:], T, [bya], [bgl])
        w_r = Ring(cx, 2, [128, KC, 128], BF16, "wa")
        w2_r = Ring(cx, 2, [128, KC, 128], BF16, "wg")
        pa_r = Ring(cx, 2, [128, 512], F32, "pa", psum=True)
        pg_r = Ring(cx, 2, [128, 512], F32, "pg", psum=True)
        sg_r = Ring(cx, 2, [128, 512], F32, "sg")
        hb_r = Ring(cx, 2, [128, 512], F32, "hb")
        wv = wview(w_glu)
        hv = hT.rearrange("(kc p) t -> p kc t", p=128)
        ov = oT.rearrange("(kc p) t -> p kc t", p=128)
        toks = []
        for j in range(KC):
            wa, bwa = w_r.next()
            wg, bwg = w2_r.next()
            dmaq(p, "pool", wa[:], wv[:, :, j * 128:(j + 1) * 128], writes=[bwa])
            dmaq(p, "pool", wg[:], wv[:, :, D + j * 128:D + (j + 1) * 128], writes=[bwg])
            for (t0, n, seg) in BLOCKS:
                pa, bpa = pa_r.next()
                pg, bpg = pg_r.next()
                for kc in range(KC):
                    mm(p, pa[:, :n], wa[:, kc, :], gl[:, kc, t0:t0 + n], kc == 0, kc == KC - 1, [bwa, bgl], [bpa])
                for kc in range(KC):
                    mm(p, pg[:, :n], wg[:, kc, :], gl[:, kc, t0:t0 + n], kc == 0, kc == KC - 1, [bwg, bgl], [bpg])
                sg, bsg = sg_r.next()
                actf(p, sg[:, :n], pg[:, :n], AF.Sigmoid, [bpg], [bsg])
                tt(p, "dve", sg[:, :n], pa[:, :n], sg[:, :n], ALU.mult, [bpa, bsg], [bsg])
                hb, bhb = hb_r.next()
                dmaq(p, "sp", hb[:, :n], hv[:, j, t0:t0 + n], writes=[bhb])
                stt(p, hb[:, :n], sg[:, :n], m[:, j, 2, seg:seg + 1], hb[:, :n], ALU.mult, ALU.add, [bsg, bm, bhb], [bhb])
                toks.append(dmaq(p, "sp", ov[:, j, t0:t0 + n], hb[:, :n], reads=[bhb]))
        p.wait_tokens("sp", toks)
        with nc.Block() as block:
            p.emit(block)
    return nc


def run_s5(hs, modT_l, g, w_in, a_re, a_im, log_dt, b_re, b_im, c_re, c_im, d_skip, w_glu):
    gv = fmaj(g)
    in_maps = [{"hT": hs[c], "modv": modv_for(modT_l, 1, c), "gv": gv, "w_in": w_in} for c in range(NCORES)]
    r1 = _run("s5a", build_s5a, in_maps)
    us = [r["uo"] for r in r1]
    seq = []
    for b in range(2):
        parts = [us[4 * b + k][:, TL:] for k in range(2)] + [us[4 * b + k][:, :TL] for k in range(4)]
        seq.append(np.concatenate(parts, axis=1))
    seq = np.stack(seq, axis=1)
    order_b = np.concatenate([np.arange(255, -1, -1), 256 + np.arange(4095, -1, -1)])
    ident = np.eye(128, dtype=np.float32)
    in_maps = []
    for c in range(NCORES):
        gs = slice(GL * c, GL * (c + 1))
        sc = seq[256 * c:256 * (c + 1)].reshape(GL, 16, 2, NPOS)
        Uc = np.empty((32, GL, 2, NPOS), np.float32)
        Uc[:16] = sc.transpose(1, 0, 2, 3)
        Uc[16:] = sc[:, :, :, order_b].transpose(1, 0, 2, 3)

        def dp(a):
            a = a[:, gs]
            if a.ndim == 3:
                return np.ascontiguousarray(a.transpose(0, 2, 1).reshape(128, GL))
            return np.ascontiguousarray(a.transpose(0, 2, 1, 3).reshape(128, GL, a.shape[3]))

        ldt = np.ascontiguousarray(np.broadcast_to(log_dt[:, gs][:, None, :], (2, 64, GL)).reshape(128, GL))
        in_maps.append({"U": Uc, "areT": dp(a_re), "aimT": dp(a_im), "ldtT": ldt,
                        "breT": dp(b_re), "bimT": dp(b_im),
                        "creT": dp(c_re.transpose(0, 1, 3, 2)), "cimT": dp(c_im.transpose(0, 1, 3, 2)), "ident": ident})
    r2 = _run("s5b", build_s5b, in_maps)
    YF = np.concatenate([r["YF"].reshape(256, 2, NPOS) for r in r2], axis=0)
    YBo = np.concatenate([r["YB"].reshape(256, 2, NPOS) for r in r2], axis=0)
    YB = np.empty_like(YBo)
    YB[:, :, order_b] = YBo

    def percore(Y, c):
        b, k = divmod(c, 4)
        a = np.zeros((D, T), np.float32)
        a[:, :TL] = Y[:, b, 256 + k * TL:256 + (k + 1) * TL]
        if k < 2:
            a[:, TL:] = Y[:, b, k * TCX:(k + 1) * TCX]
        return a

    dsk = fmaj(d_skip)
    in_maps = [{"hT": hs[c], "modv": modv_for(modT_l, 1, c), "uT": us[c], "yfT": percore(YF, c), "ybT": percore(YB, c),
                "dsk": dsk, "w_glu": w_glu} for c in range(NCORES)]
    r3 = _run("s5c", build_s5c, in_maps)
    return [r["oT"] for r in r3]


def kernel(x, c, ctx, c_ctx, w_ada, b_ada, norm_g, ffn_w_gu, ffn_w_down,
           a_w_in, a_v_gain, a_w_s, a_b_s, a_w_out,
           b_w_qkv, b_q_gain, b_k_gain, b_rpb, b_w_out,
           c_w_in, c_a_re, c_a_im, c_log_dt, c_b_re, c_b_im, c_c_re, c_c_im, c_d, c_w_glu):
    f = lambda a: np.asarray(a, dtype=np.float32)
    x, c, ctx, c_ctx = f(x), f(c), f(ctx), f(c_ctx)
    modT = run_ada(c, c_ctx, f(w_ada), f(b_ada))
    hs = to_cores(x, ctx)
    depth = 4
    for i in range(depth):
        kind, j = i % 3, i // 3
        hs = run_ffn(hs, modT[i], 0, f(norm_g[i, 0]), f(ffn_w_gu[i, 0]), f(ffn_w_down[i, 0]))
        if kind == 0:
            hs = run_mixa(hs, modT[i], f(norm_g[i, 1]), f(a_w_in[j]), f(a_v_gain[j]), f(a_w_s[j]), f(a_b_s[j]), f(a_w_out[j]))
        elif kind == 1:
            hs = run_nat(hs, modT[i], f(norm_g[i, 1]), f(b_w_qkv[j]), f(b_q_gain[j]), f(b_k_gain[j]), f(b_rpb[j]), f(b_w_out[j]))
        else:
            hs = run_s5(hs, modT[i], f(norm_g[i, 1]), f(c_w_in[j]), f(c_a_re[j]), f(c_a_im[j]), f(c_log_dt[j]),
                        f(c_b_re[j]), f(c_b_im[j]), f(c_c_re[j]), f(c_c_im[j]), f(c_d[j]), f(c_w_glu[j]))
        hs = run_ffn(hs, modT[i], 2, f(norm_g[i, 2]), f(ffn_w_gu[i, 1]), f(ffn_w_down[i, 1]))
    return from_cores(hs)
```

```python
import numpy as np
import concourse.bass as bass
import concourse.mybir as mybir
from concourse.bass_utils import run_bass_kernel_spmd
from concourse.alu_op_type import AluOpType as ALU

F32 = mybir.dt.float32
BF16 = mybir.dt.bfloat16
AF = mybir.ActivationFunctionType

D = 2048
KC = 16
DFF = 5632
FC = 44
NCORES = 8
TL = 1024
TCX = 128
T = TL + TCX
EPS = 1e-6
GLOBAL_NOSYNC = ()
PE_LAZY_SIG = True


class Buf:
    __slots__ = ("lw", "rd", "name")

    def __init__(self, name=""):
        self.lw = None
        self.rd = {}
        self.name = name


class Prog:
    CENG = ("pe", "act", "dve", "pool")

    def __init__(self, nc, stack, n_dsem=20):
        self.nc = nc
        self.eng = {"pe": nc.tensor, "act": nc.scalar, "dve": nc.vector,
                    "pool": nc.gpsimd, "sp": nc.sync}
        self.q = {e: [] for e in self.eng}
        self.cnt = {e: 0 for e in self.CENG}
        self.seen = {e: {} for e in self.eng}
        self.nosync = set(GLOBAL_NOSYNC)
        self.csem = {e: stack.enter_context(nc.semaphore("c_" + e)) for e in self.CENG}
        self.dsem = {}
        self.dcum = {}
        self.drr = {}
        for qn in ("sp", "pool", "act"):
            n = n_dsem if qn != "act" else 6
            self.dsem[qn] = [stack.enter_context(nc.semaphore("d_%s%d" % (qn, i))) for i in range(n)]
            self.dcum[qn] = [0] * n
            self.drr[qn] = 0

    def _need(self, eng, tok, waits):
        if tok is None:
            return
        key, val = tok
        if key == ("c", "pe") and eng == "pe":
            return
        if key[0] == "c" and key[1] == eng and eng in self.nosync:
            return
        if self.seen[eng].get(key, 0) >= val:
            return
        if waits.get(key, 0) < val:
            waits[key] = val

    def _deps(self, eng, reads, writes):
        waits = {}
        for b in reads:
            self._need(eng, b.lw, waits)
        for b in writes:
            self._need(eng, b.lw, waits)
            for k, v in b.rd.items():
                self._need(eng, (k, v), waits)
        for k, v in waits.items():
            self.seen[eng][k] = v
        return waits

    def _commit(self, tok, reads, writes):
        key, val = tok
        for b in reads:
            if b.rd.get(key, 0) < val:
                b.rd[key] = val
        for b in writes:
            b.lw = tok
            b.rd = {}

    def op(self, eng, fn, reads=(), writes=(), sig=True):
        waits = self._deps(eng, reads, writes)
        if sig:
            self.cnt[eng] += 1
            tok = (("c", eng), self.cnt[eng])
            self.q[eng].append((list(waits.items()), fn, ("c", eng, 1)))
        else:
            tok = (("c", eng), self.cnt[eng] + 1)
            self.q[eng].append((list(waits.items()), fn, ("n",)))
        self._commit(tok, reads, writes)
        return tok

    def dma(self, qn, fns, reads=(), writes=()):
        if not isinstance(fns, (list, tuple)):
            fns = [fns]
        waits = self._deps(qn, reads, writes)
        i = self.drr[qn]
        self.drr[qn] = (i + 1) % len(self.dsem[qn])
        key = ("d", qn, i)
        prev = self.dcum[qn][i]
        if prev > 0 and self.seen[qn].get(key, 0) < prev:
            waits[key] = prev
            self.seen[qn][key] = prev
        for j, fn in enumerate(fns):
            self.dcum[qn][i] += 16
            self.q[qn].append((list(waits.items()) if j == 0 else [], fn, ("d", qn, i)))
        tok = (key, self.dcum[qn][i])
        self._commit(tok, reads, writes)
        return tok

    def wait_tokens(self, eng, toks):
        waits = {}
        for t in toks:
            self._need(eng, t, waits)
        for k, v in waits.items():
            self.seen[eng][k] = v
        self.q[eng].append((list(waits.items()), None, None))

    def _sem(self, key):
        if key[0] == "c":
            return self.csem[key[1]]
        return self.dsem[key[1]][key[2]]

    def emit(self, block):
        decos = {"pe": block.tensor, "act": block.scalar, "dve": block.vector,
                 "pool": block.gpsimd, "sp": block.sync}
        for e in self.eng:
            items = self.q[e]
            if not items:
                continue

            def body(engine, items=items):
                for waits, fn, inc in items:
                    for key, val in waits:
                        engine.wait_ge(self._sem(key), val)
                    if fn is None:
                        continue
                    ins = fn(engine)
                    if inc[0] == "n":
                        continue
                    if inc[0] == "c":
                        ins.then_inc(self.csem[inc[1]], 1)
                    else:
                        ins.then_inc(self.dsem[inc[1]][inc[2]], 16)

            decos[e](body)


class Ctx:
    def __init__(self, nc, stack):
        self.nc = nc
        self.stack = stack
        self.p = Prog(nc, stack)
        self._n = 0

    def sb(self, shape, dt, name=None):
        self._n += 1
        return self.stack.enter_context(self.nc.sbuf_tensor(name or ("sb%d" % self._n), list(shape), dt))

    def ps(self, shape, dt=F32, name=None):
        self._n += 1
        return self.stack.enter_context(self.nc.psum_tensor(name or ("ps%d" % self._n), list(shape), dt))

    def din(self, name, shape, dt=F32):
        return self.nc.dram_tensor(name, list(shape), dt, kind="ExternalInput").ap()

    def dout(self, name, shape, dt=F32):
        return self.nc.dram_tensor(name, list(shape), dt, kind="ExternalOutput").ap()


class Ring:
    def __init__(self, cx, n, shape, dt, name, psum=False):
        self.t = [(cx.ps(shape, dt, "%s%d" % (name, i)) if psum else cx.sb(shape, dt, "%s%d" % (name, i)))
                  for i in range(n)]
        self.b = [Buf("%s%d" % (name, i)) for i in range(n)]
        self.i = 0

    def next(self):
        i = self.i
        self.i = (i + 1) % len(self.t)
        return self.t[i], self.b[i]


def make_consts(cx):
    p = cx.p
    ones = cx.sb([128, 128], F32, "ones")
    epsc = cx.sb([128, 1], F32, "epsc")
    b = Buf("consts")
    p.op("pool", lambda e: e.memset(ones[:], 1.0), writes=[b])
    p.op("pool", lambda e: e.memset(epsc[:], EPS), writes=[b])
    return ones, epsc, b


def load_mod(cx, modv, gv):
    p = cx.p
    m = cx.sb([128, KC, 3, 2], F32)
    g = cx.sb([128, KC], F32)
    A = cx.sb([128, KC, 2], F32)
    bm, bg, bA = Buf("m"), Buf("g"), Buf("A")
    p.dma("sp", lambda e: e.dma_start(out=m[:], in_=modv), writes=[bm])
    p.dma("sp", lambda e: e.dma_start(out=g[:], in_=gv), writes=[bg])
    for s in range(2):
        p.op("dve", lambda e, s=s: e.scalar_tensor_tensor(out=A[:, :, s], in0=m[:, :, 1, s], scalar=1.0,
                                                           in1=g[:, :], op0=ALU.add, op1=ALU.mult),
             reads=[bm, bg], writes=[bA])
    return m, A, bm, bA


def rms_adaln(cx, src, bsrc, dst, bdst, blocks, m, A, bm, bA, consts, rings):
    p = cx.p
    ones, epsc, bc = consts
    sq_r, ps_r, rs_r, tmp_r = rings
    for (t0, n, seg) in blocks:
        _rms_block(p, src, bsrc, dst, bdst, t0, n, seg, m, A, bm, bA, ones, epsc, bc, sq_r, ps_r, rs_r, tmp_r)


def _rms_block(p, src, bsrc, dst, bdst, t0, n, seg, m, A, bm, bA, ones, epsc, bc, sq_r, ps_r, rs_r, tmp_r, s0=None):
    s0 = t0 if s0 is None else s0
    ss, bss = ps_r.next()
    for kc in range(KC):
        sq, bsq = sq_r.next()
        bs_ = bsrc[kc] if isinstance(bsrc, list) else bsrc
        p.op("act", lambda e, sq=sq, kc=kc: e.activation(out=sq[:, :n], in_=src[:, kc, s0:s0 + n], func=AF.Square),
             reads=[bs_], writes=[bsq])
        p.op("pe", lambda e, sq=sq, kc=kc: e.matmul(ss[:, :n], lhsT=ones[:], rhs=sq[:, :n],
                                                    start=(kc == 0), stop=(kc == KC - 1)),
             reads=[bsq, bc], writes=[bss])
    rs, brs = rs_r.next()
    p.op("act", lambda e: e.activation(out=rs[:, :n], in_=ss[:, :n], func=AF.Sqrt, bias=epsc[:], scale=1.0 / D),
         reads=[bss, bc], writes=[brs])
    p.op("dve", lambda e: e.reciprocal(out=rs[:, :n], in_=rs[:, :n]), reads=[brs], writes=[brs])
    for kc in range(KC):
        tmp, btmp = tmp_r.next()
        p.op("dve", lambda e, tmp=tmp, kc=kc: e.scalar_tensor_tensor(
            out=tmp[:, :n], in0=src[:, kc, s0:s0 + n], scalar=A[:, kc, seg:seg + 1], in1=rs[:, :n],
            op0=ALU.mult, op1=ALU.mult), reads=[bsrc[kc] if isinstance(bsrc, list) else bsrc, brs, bA], writes=[btmp])
        p.op("act", lambda e, tmp=tmp, kc=kc: e.activation(out=dst[:, kc, t0:t0 + n], in_=tmp[:, :n],
                                                           func=AF.Identity, bias=m[:, kc, 0, seg:seg + 1], scale=1.0),
             reads=[btmp, bm], writes=[bdst])


BLOCKS = [(0, 512, 0), (512, 512, 0), (1024, 128, 1)]


def wview(w):
    return w.rearrange("(kc p) f -> p kc f", p=128)


def build_ffn(dbg=False):
    from contextlib import ExitStack
    nc = bass.Bass("TRN2", target_bir_lowering=False)
    with ExitStack() as stack:
        cx = Ctx(nc, stack)
        p = cx.p
        hT = cx.din("hT", [D, T])
        modv = cx.din("modv", [128, KC, 3, 2])
        gv = cx.din("gv", [128, KC])
        wgu = cx.din("wgu", [D, 2 * DFF])
        wdn = cx.din("wdn", [DFF, D])
        oT = cx.dout("oT", [D, T])

        consts = make_consts(cx)
        h = cx.sb([128, KC, T], F32, "h")
        bh = [Buf("h%d" % k) for k in range(KC)]
        hv = hT.rearrange("(kc p) t -> p kc t", p=128)
        m, A, bm, bA = load_mod(cx, modv, gv)
        for k in range(KC):
            dmaq(p, "sp", h[:, k, :], hv[:, k, :], writes=[bh[k]])
        G = cx.sb([128, KC, 2], F32, "G")
        bG = Buf("G")
        p.op("dve", lambda e: e.tensor_scalar(out=G[:], in0=m[:, :, 2, :], scalar1=0.5, scalar2=None, op0=ALU.mult),
             reads=[bm], writes=[bG])

        xn = cx.sb([128, KC, T], BF16, "xn")
        bxn = Buf("xn")
        rings = (Ring(cx, 2, [128, 512], F32, "sq"), Ring(cx, 1, [128, 512], F32, "ssps", psum=True),
                 Ring(cx, 2, [128, 512], F32, "rs"), Ring(cx, 2, [128, 512], F32, "tmp"))
        rms_adaln(cx, h, bh, xn, bxn, BLOCKS, m, A, bm, bA, consts, rings)

        if dbg:
            xo = cx.dout("xo", [128, KC, T], BF16)
            p.wait_tokens("sp", [p.dma("sp", lambda e: e.dma_start(out=xo, in_=xn[:]), reads=[bxn])])
        ov = oT.rearrange("(kc p) t -> p kc t", p=128)
        toks = []

        def store_dc(dc):
            toks.append(dmaq(p, "sp", ov[:, dc, :], h[:, dc, :], reads=[bh[dc]]))

        ffn_core(cx, xn, bxn, h, bh, G, bG, wgu, wdn, on_final=store_dc)
        p.wait_tokens("sp", toks)
        with nc.Block() as block:
            p.emit(block)
    return nc


def ffn_core(cx, xn, bxn, h, bh, G, bG, wgu, wdn, GRP=4, on_final=None):
    p = cx.p
    wguv = wview(wgu)
    wdnv = wdn.rearrange("(fc p) d -> p fc d", p=128)
    wg_r = Ring(cx, 2, [128, KC, 256], BF16, "wg")
    wu_r = Ring(cx, 2, [128, KC, 256], BF16, "wu")
    wd_r = Ring(cx, 2, [128, GRP, D], BF16, "wd")
    act_r = Ring(cx, 2, [128, GRP, T], BF16, "act")
    gps_r = Ring(cx, 2, [128, 512], F32, "gps", psum=True)
    ups_r = Ring(cx, 2, [128, 512], F32, "ups", psum=True)
    yps_r = Ring(cx, 2, [128, 512], F32, "yps", psum=True)
    sg_r = Ring(cx, 2, [128, 512], F32, "sg")
    wg = wu = None
    for grp in range(FC // GRP):
        act, bact = act_r.next()
        wd, bwd = wd_r.next()
        for fl in range(GRP):
            p.dma("pool", lambda e, wd=wd, fl=fl, grp=grp: e.dma_start(out=wd[:, fl, :], in_=wdnv[:, grp * GRP + fl, :]),
                  writes=[bwd])
        for fl in range(GRP):
            fc = grp * GRP + fl
            if fc % 2 == 0 and (not _DBG_SKIPW or fc < 4):
                wg, bwg = wg_r.next()
                wu, bwu = wu_r.next()
                p.dma("pool", lambda e, wg=wg, fc=fc: e.dma_start(out=wg[:], in_=wguv[:, :, fc * 128:fc * 128 + 256]),
                      writes=[bwg])
                p.dma("pool", lambda e, wu=wu, fc=fc: e.dma_start(out=wu[:], in_=wguv[:, :, DFF + fc * 128:DFF + fc * 128 + 256]),
                      writes=[bwu])
            c0 = (fc % 2) * 128
            for (t0, n, seg) in BLOCKS:
                gps, bgps = gps_r.next()
                ups, bups = ups_r.next()
                for kc in range(KC):
                    p.op("pe", lambda e, gps=gps, wg=wg, kc=kc, c0=c0, t0=t0, n=n: e.matmul(
                        gps[:, :n], lhsT=wg[:, kc, c0:c0 + 128], rhs=xn[:, kc, t0:t0 + n],
                        start=(kc == 0), stop=(kc == KC - 1)), reads=[bwg, bxn], writes=[bgps], sig=(kc == KC - 1) or not PE_LAZY_SIG)
                for kc in range(KC):
                    p.op("pe", lambda e, ups=ups, wu=wu, kc=kc, c0=c0, t0=t0, n=n: e.matmul(
                        ups[:, :n], lhsT=wu[:, kc, c0:c0 + 128], rhs=xn[:, kc, t0:t0 + n],
                        start=(kc == 0), stop=(kc == KC - 1)), reads=[bwu, bxn], writes=[bups], sig=(kc == KC - 1) or not PE_LAZY_SIG)
                sg, bsg = sg_r.next()
                p.op("act", lambda e, sg=sg, gps=gps, n=n: e.activation(out=sg[:, :n], in_=gps[:, :n], func=AF.Silu),
                     reads=[bgps], writes=[bsg])
                p.op("dve", lambda e, sg=sg, ups=ups, act=act, fl=fl, t0=t0, n=n: e.tensor_tensor(
                    out=act[:, fl, t0:t0 + n], in0=ups[:, :n], in1=sg[:, :n], op=ALU.mult),
                    reads=[bups, bsg], writes=[bact])
        for dc in range(KC):
            for (t0, n, seg) in BLOCKS:
                yps, byps = yps_r.next()
                for fl in range(GRP):
                    p.op("pe", lambda e, yps=yps, wd=wd, fl=fl, dc=dc, act=act, t0=t0, n=n: e.matmul(
                        yps[:, :n], lhsT=wd[:, fl, dc * 128:(dc + 1) * 128], rhs=act[:, fl, t0:t0 + n],
                        start=(fl == 0), stop=(fl == GRP - 1)), reads=[bwd, bact], writes=[byps], sig=(fl == GRP - 1) or not PE_LAZY_SIG)
                p.op("dve", lambda e, yps=yps, dc=dc, t0=t0, n=n, seg=seg: e.scalar_tensor_tensor(
                    out=h[:, dc, t0:t0 + n], in0=yps[:, :n], scalar=G[:, dc, seg:seg + 1], in1=h[:, dc, t0:t0 + n],
                    op0=ALU.mult, op1=ALU.add), reads=[byps, bG, bh[dc]], writes=[bh[dc]])
            if on_final is not None and grp == FC // GRP - 1:
                on_final(dc)


NF_ADA = 4 * 9 * KC // NCORES


def build_ada():
    from contextlib import ExitStack
    nc = bass.Bass("TRN2", target_bir_lowering=False)
    with ExitStack() as stack:
        cx = Ctx(nc, stack)
        p = cx.p
        condT = cx.din("condT", [128, KC, 4])
        wa = cx.din("wa", [D, NF_ADA * 128])
        ba = cx.din("ba", [128, NF_ADA])
        mo = cx.dout("mo", [128, NF_ADA, 4])
        ct = cx.sb([128, KC, 4], F32)
        cs = cx.sb([128, KC, 4], BF16)
        bt = cx.sb([128, NF_ADA], F32)
        res = cx.sb([128, NF_ADA, 4], F32)
        bct, bcs, bbt, bres = Buf(), Buf(), Buf(), Buf()
        p.dma("sp", lambda e: e.dma_start(out=ct[:], in_=condT), writes=[bct])
        p.dma("sp", lambda e: e.dma_start(out=bt[:], in_=ba), writes=[bbt])
        p.op("act", lambda e: e.activation(out=cs[:], in_=ct[:], func=AF.Silu), reads=[bct], writes=[bcs])
        w_r = Ring(cx, 3, [128, KC, 512], BF16, "w")
        ps_r = Ring(cx, 2, [128, 4, 4], F32, "ps", psum=True)
        wav = wview(wa)
        for g4 in range(NF_ADA // 4):
            w, bw = w_r.next()
            p.dma("pool", lambda e, w=w, g4=g4: e.dma_start(out=w[:], in_=wav[:, :, g4 * 512:(g4 + 1) * 512]), writes=[bw])
            ps, bps = ps_r.next()
            for j in range(4):
                for kc in range(KC):
                    p.op("pe", lambda e, ps=ps, w=w, j=j, kc=kc: e.matmul(
                        ps[:, j, :], lhsT=w[:, kc, j * 128:(j + 1) * 128], rhs=cs[:, kc, :],
                        start=(kc == 0), stop=(kc == KC - 1)), reads=[bw, bcs], writes=[bps])
            for j in range(4):
                f = g4 * 4 + j
                p.op("dve", lambda e, ps=ps, j=j, f=f: e.tensor_scalar(out=res[:, f, :], in0=ps[:, j, :], scalar1=bt[:, f:f + 1],
                                                                      scalar2=None, op0=ALU.add),
                     reads=[bps, bbt], writes=[bres])
        tok = p.dma("sp", lambda e: e.dma_start(out=mo, in_=res[:]), reads=[bres])
        p.wait_tokens("sp", [tok])
        with nc.Block() as block:
            p.emit(block)
    return nc


def mm(p, out, lhsT, rhs, start, stop, reads, writes):
    return p.op("pe", lambda e: e.matmul(out, lhsT=lhsT, rhs=rhs, start=start, stop=stop), reads=reads, writes=writes,
                sig=bool(stop) or not PE_LAZY_SIG)


def actf(p, out, in_, func, reads, writes, bias=None, scale=None, accum_out=None):
    kw = {}
    if bias is not None:
        kw["bias"] = bias
    if scale is not None:
        kw["scale"] = scale
    if accum_out is not None:
        kw["accum_out"] = accum_out
    return p.op("act", lambda e: e.activation(out=out, in_=in_, func=func, **kw), reads=reads, writes=writes)


def tt(p, eng, out, in0, in1, op, reads, writes):
    return p.op(eng, lambda e: e.tensor_tensor(out=out, in0=in0, in1=in1, op=op), reads=reads, writes=writes)


def ts(p, eng, out, in0, s1, s2, op0, op1, reads, writes):
    if op1 is None:
        return p.op(eng, lambda e: e.tensor_scalar(out=out, in0=in0, scalar1=s1, scalar2=None, op0=op0), reads=reads, writes=writes)
    return p.op(eng, lambda e: e.tensor_scalar(out=out, in0=in0, scalar1=s1, scalar2=s2, op0=op0, op1=op1), reads=reads, writes=writes)


def stt(p, out, in0, scalar, in1, op0, op1, reads, writes):
    return p.op("dve", lambda e: e.scalar_tensor_tensor(out=out, in0=in0, scalar=scalar, in1=in1, op0=op0, op1=op1),
                reads=reads, writes=writes)


def dmaq(p, q, out, in_, reads=(), writes=()):
    return p.dma(q, lambda e: e.dma_start(out=out, in_=in_), reads=reads, writes=writes)


class Gelu:
    def __init__(self, cx, width=512):
        self.a = Ring(cx, 2, [128, width], F32, "gl_a")
        self.b = Ring(cx, 2, [128, width], F32, "gl_b")

    def __call__(self, p, out, ps, n, rd, wr, accum_sq=None):
        a, ba = self.a.next()
        b, bb = self.b.next()
        actf(p, a[:, :n], ps, AF.Square, rd, [ba])
        ts(p, "dve", a[:, :n], a[:, :n], 0.044715, 1.0, ALU.mult, ALU.add, [ba], [ba])
        tt(p, "dve", a[:, :n], a[:, :n], ps, ALU.mult, [ba] + rd, [ba])
        actf(p, b[:, :n], a[:, :n], AF.Sigmoid, [ba], [bb], scale=1.5957691216057308)
        tt(p, "dve", out, b[:, :n], ps, ALU.mult, [bb] + rd, wr)


def load_h(cx, hT, name="h"):
    p = cx.p
    h = cx.sb([128, KC, T], F32, name)
    bh = Buf(name)
    hv = hT.rearrange("(kc p) t -> p kc t", p=128)
    for q in range(4):
        dmaq(p, "sp", h[:, 4 * q:4 * q + 4, :], hv[:, 4 * q:4 * q + 4, :], writes=[bh])
    return h, bh


def store_h(cx, oT, h, bh):
    p = cx.p
    ov = oT.rearrange("(kc p) t -> p kc t", p=128)
    toks = [dmaq(p, "sp", ov[:, 4 * q:4 * q + 4, :], h[:, 4 * q:4 * q + 4, :], reads=[bh]) for q in range(4)]
    p.wait_tokens("sp", toks)


def rms_adaln_stream(cx, hT, dst, bdst, m, A, bm, bA, consts, rings, blocks=BLOCKS):
    p = cx.p
    ones, epsc, bc = consts
    hb = cx.sb([128, KC, 512], F32, "hblk")
    bhb = Buf("hblk")
    hv = hT.rearrange("(kc p) t -> p kc t", p=128)
    for (t0, n, seg) in blocks:
        for q in range(2):
            dmaq(p, "sp", hb[:, 8 * q:8 * q + 8, :n], hv[:, 8 * q:8 * q + 8, t0:t0 + n], writes=[bhb])
        _rms_block(p, hb, bhb, dst, bdst, t0, n, seg, m, A, bm, bA, ones, epsc, bc, *rings, s0=0)


def make_residual_evac(cx, hT, oT, m, bm, ring, toks, gate_idx=2):
    p = cx.p
    hv = hT.rearrange("(kc p) t -> p kc t", p=128)
    ov = oT.rearrange("(kc p) t -> p kc t", p=128)

    def evac(j, t0, n, seg, ps, bps):
        hb, bhb = ring.next()
        dmaq(p, "sp", hb[:, :n], hv[:, j, t0:t0 + n], writes=[bhb])
        stt(p, hb[:, :n], ps[:, :n], m[:, j, gate_idx, seg:seg + 1], hb[:, :n], ALU.mult, ALU.add, [bps, bm, bhb], [bhb])
        toks.append(dmaq(p, "sp", ov[:, j, t0:t0 + n], hb[:, :n], reads=[bhb]))

    return evac


def norm_rings(cx):
    return (Ring(cx, 2, [128, 512], F32, "sq"), Ring(cx, 1, [128, 512], F32, "ssps", psum=True),
            Ring(cx, 2, [128, 512], F32, "rs"), Ring(cx, 2, [128, 512], F32, "tmp"))


def linear_fm(cx, wv, col0, nchunks, x, bx, evac, w_r, ps_r, nk=KC, blocks=BLOCKS):
    p = cx.p
    w = bw = None
    for j in range(nchunks):
        if j % 2 == 0:
            w, bw = w_r.next()
            c = col0 + j * 128
            wd = 256 if j + 1 < nchunks else 128
            dmaq(p, "pool", w[:, :, :wd], wv[:, :, c:c + wd], writes=[bw])
        c0 = (j % 2) * 128
        for (t0, n, seg) in blocks:
            ps, bps = ps_r.next()
            for kc in range(nk):
                mm(p, ps[:, :n], w[:, kc, c0:c0 + 128], x[:, kc, t0:t0 + n], kc == 0, kc == nk - 1, [bw, bx], [bps])
            evac(j, t0, n, seg, ps, bps)


NT = T // 128


def build_mixa():
    from contextlib import ExitStack
    nc = bass.Bass("TRN2", target_bir_lowering=False)
    with ExitStack() as stack:
        cx = Ctx(nc, stack)
        p = cx.p
        hT = cx.din("hT", [D, T])
        modv = cx.din("modv", [128, KC, 3, 2])
        gv = cx.din("gv", [128, KC])
        w_in = cx.din("w_in", [D, 2 * D])
        vg = cx.din("vg", [128, KC])
        wsT = cx.din("wsT", [128, 16, 128])
        bsb = cx.din("bsb", [128, 16, 128])
        w_out = cx.din("w_out", [D, D])
        oT = cx.dout("oT", [D, T])

        consts = make_consts(cx)
        m, A, bm, bA = load_mod(cx, modv, gv)
        xn = cx.sb([128, KC, T], BF16, "xn")
        bxn = Buf("xn")
        nr = norm_rings(cx)
        rms_adaln_stream(cx, hT, xn, bxn, m, A, bm, bA, consts, nr)

        vgt = cx.sb([128, KC], F32, "vgt")
        wst = cx.sb([128, 16, 128], F32, "wst")
        bst = cx.sb([128, 16, 128], F32, "bst")
        bvg, bws, bbs = Buf(), Buf(), Buf()
        dmaq(p, "sp", vgt[:], vg, writes=[bvg])
        dmaq(p, "sp", wst[:], wsT, writes=[bws])
        dmaq(p, "sp", bst[:], bsb, writes=[bbs])

        gelu = Gelu(cx)
        w_r = Ring(cx, 2, [128, KC, 256], BF16, "w")
        ps_r = Ring(cx, 2, [128, 512], F32, "ps", psum=True)
        winv = wview(w_in)

        uT = cx.sb([128, KC, T], BF16, "uT")
        buT = Buf("uT")

        def evac_u(j, t0, n, seg, ps, bps):
            gelu(p, uT[:, j, t0:t0 + n], ps[:, :n], n, [bps], [buT])

        linear_fm(cx, winv, 0, KC, xn, bxn, evac_u, w_r, ps_r)

        v = cx.sb([128, NT, D], BF16, "v")
        bv = Buf("v")
        NCB = 8
        ssq = cx.sb([128, NT, NCB], F32, "ssq")
        bssq = Buf("ssq")
        for cb in range(NCB):
            wv_, bwv = w_r.next()
            dmaq(p, "pool", wv_[:, :, :], winv[:, :, D + cb * 256:D + (cb + 1) * 256], writes=[bwv])
            for ti in range(NT):
                ps, bps = ps_r.next()
                for kc in range(KC):
                    mm(p, ps[:, :256], xn[:, kc, ti * 128:(ti + 1) * 128], wv_[:, kc, :], kc == 0, kc == KC - 1, [bxn, bwv], [bps])
                vt, bvt = nr[3].next()
                gelu(p, vt[:, :256], ps[:, :256], 256, [bps], [bvt])
                junk, bjunk = nr[0].next()
                actf(p, junk[:, :256], vt[:, :256], AF.Square, [bvt], [bjunk, bssq], accum_out=ssq[:, ti, cb:cb + 1])
                p.op("pool", lambda e, vt=vt, ti=ti, cb=cb: e.tensor_copy(out=v[:, ti, cb * 256:(cb + 1) * 256], in_=vt[:, :256]),
                     reads=[bvt], writes=[bv])
        rv = cx.sb([128, NT], F32, "rv")
        brv = Buf("rv")
        p.op("dve", lambda e: e.tensor_reduce(out=rv[:, :], in_=ssq[:, :, :], axis=mybir.AxisListType.X, op=ALU.add),
             reads=[bssq], writes=[brv])
        actf(p, rv[:, :], rv[:, :], AF.Sqrt, [brv, consts[2]], [brv], bias=consts[1][:], scale=1.0 / D)
        p.op("dve", lambda e: e.reciprocal(out=rv[:, :], in_=rv[:, :]), reads=[brv], writes=[brv])

        z, bz = xn, bxn
        wp_r = Ring(cx, 3, [128, 128], BF16, "wp")
        st_r = Ring(cx, 2, [128, 128], F32, "st")
        sps_r = Ring(cx, 2, [128, 128], F32, "sps", psum=True)
        for ti in range(NT):
            for g in range(16):
                wp, bwp = wp_r.next()
                ts(p, "pool", wp[:, :], wst[:, g, :], rv[:, ti:ti + 1], None, ALU.mult, None, [bws, brv], [bwp])
                sp_, bsp = sps_r.next()
                mm(p, sp_[:, :], v[:, ti, g * 128:(g + 1) * 128], wp[:, :], True, True, [bv, bwp], [bsp])
                st, bst_ = st_r.next()
                stt(p, st[:, :], sp_[:, :], vgt[:, g:g + 1], bst[:, g, :], ALU.mult, ALU.add, [bsp, bvg, bbs], [bst_])
                tt(p, "dve", z[:, g, ti * 128:(ti + 1) * 128], st[:, :], uT[:, g, ti * 128:(ti + 1) * 128], ALU.mult,
                   [bst_, buT], [bz])

        woutv = wview(w_out)

        toks = []
        evac_o = make_residual_evac(cx, hT, oT, m, bm, nr[0], toks)
        linear_fm(cx, woutv, 0, KC, z, bz, evac_o, w_r, ps_r)
        p.wait_tokens("sp", toks)
        with nc.Block() as block:
            p.emit(block)
    return nc


HD = 128
NH = 16
ATT_SCALE = HD ** -0.5


def build_nat1():
    from contextlib import ExitStack
    nc = bass.Bass("TRN2", target_bir_lowering=False)
    with ExitStack() as stack:
        cx = Ctx(nc, stack)
        p = cx.p
        hT = cx.din("hT", [D, T])
        modv = cx.din("modv", [128, KC, 3, 2])
        gv = cx.din("gv", [128, KC])
        wqkv = cx.din("wqkv", [D, 3 * D])
        qkg = cx.din("qkg", [128, 2])
        qo = cx.dout("qo", [NH, 128, T], BF16)
        ko = cx.dout("ko", [NH, 128, T], BF16)
        vo = cx.dout("vo", [T, D], BF16)

        consts = make_consts(cx)
        ones, epsc, bc = consts
        m, A, bm, bA = load_mod(cx, modv, gv)
        xn = cx.sb([128, KC, T], BF16, "xn")
        bxn = Buf("xn")
        nr = norm_rings(cx)
        rms_adaln_stream(cx, hT, xn, bxn, m, A, bm, bA, consts, nr)
        gq = cx.sb([128, 2], F32, "gq")
        bgq = Buf()
        dmaq(p, "sp", gq[:], qkg, writes=[bgq])
        ts(p, "dve", gq[:, 0:1], gq[:, 0:1], ATT_SCALE, None, ALU.mult, None, [bgq], [bgq])

        w_r = Ring(cx, 2, [128, KC, 256], BF16, "w")
        ps_r = Ring(cx, 2, [128, 512], F32, "ps", psum=True)
        wv = wview(wqkv)
        qf_r = Ring(cx, 2, [128, 512], F32, "qf")
        st_r = Ring(cx, 3, [128, 512], BF16, "stg")
        toks = []
        for which, dst in ((0, qo), (1, ko)):
            def evac(j, t0, n, seg, ps, bps, which=which, dst=dst):
                qf, bqf = qf_r.next()
                actf(p, qf[:, :n], ps[:, :n], AF.Identity, [bps], [bqf])
                sq, bsq = nr[0].next()
                actf(p, sq[:, :n], qf[:, :n], AF.Square, [bqf], [bsq])
                ss, bss = nr[1].next()
                mm(p, ss[:, :n], ones[:], sq[:, :n], True, True, [bsq, bc], [bss])
                rs, brs = nr[2].next()
                actf(p, rs[:, :n], ss[:, :n], AF.Sqrt, [bss, bc], [brs], bias=epsc[:], scale=1.0 / HD)
                p.op("dve", lambda e: e.reciprocal(out=rs[:, :n], in_=rs[:, :n]), reads=[brs], writes=[brs])
                sg, bsg = st_r.next()
                stt(p, sg[:, :n], qf[:, :n], gq[:, which:which + 1], rs[:, :n], ALU.mult, ALU.mult, [bqf, bgq, brs], [bsg])
                toks.append(dmaq(p, "sp", dst[j, :, t0:t0 + n], sg[:, :n], reads=[bsg]))

            linear_fm(cx, wv, which * D, KC, xn, bxn, evac, w_r, ps_r)
        vs = cx.sb([128, NT, D], BF16, "vs")
        bvs = Buf("vs")
        for cb in range(8):
            w, bw = w_r.next()
            dmaq(p, "pool", w[:, :, :], wv[:, :, 2 * D + cb * 256:2 * D + (cb + 1) * 256], writes=[bw])
            for ti in range(NT):
                ps, bps = ps_r.next()
                for kc in range(KC):
                    mm(p, ps[:, :256], xn[:, kc, ti * 128:(ti + 1) * 128], w[:, kc, :], kc == 0, kc == KC - 1, [bxn, bw], [bps])
                actf(p, vs[:, ti, cb * 256:(cb + 1) * 256], ps[:, :256], AF.Identity, [bps], [bvs])
        vov = vo.rearrange("(ti p) f -> p ti f", p=128)
        toks.append(dmaq(p, "sp", vov, vs[:], reads=[bvs]))
        p.wait_tokens("sp", toks)
        with nc.Block() as block:
            p.emit(block)
    return nc


NSLAB = 14
NKT = NSLAB + 2
NOFF = 7


def build_nat2():
    from contextlib import ExitStack
    nc = bass.Bass("TRN2", target_bir_lowering=False)
    with ExitStack() as stack:
        cx = Ctx(nc, stack)
        p = cx.p
        hT = cx.din("hT", [D, T])
        modv = cx.din("modv", [128, KC, 3, 2])
        qT = cx.din("qT", [NH, 128, T], BF16)
        kx = cx.din("kx", [NH, 128, NKT * 128], BF16)
        vx = cx.din("vx", [NKT * 128, D], BF16)
        bias = cx.din("bias", [NH, 128, 8 * NOFF, 128])
        w_out = cx.din("w_out", [D, D])
        oT = cx.dout("oT", [D, T])

        m = cx.sb([128, KC, 3, 2], F32, "m")
        bm = Buf("m")
        dmaq(p, "sp", m[:], modv, writes=[bm])
        onesb = cx.sb([128, 128], BF16, "onesb")
        bob = Buf("onesb")
        p.op("pool", lambda e: e.memset(onesb[:], 1.0), writes=[bob])

        q = cx.sb([128, NH, T], BF16, "q")
        bq = Buf("q")
        for hh in range(4):
            dmaq(p, "sp", q[:, 4 * hh:4 * hh + 4, :], qT.rearrange("h d t -> d h t")[:, 4 * hh:4 * hh + 4, :], writes=[bq])
        o = cx.sb([128, NH, T], BF16, "o")
        bo = Buf("o")

        kh_r = Ring(cx, 2, [128, NKT * 128], BF16, "kh")
        vh_r = Ring(cx, 2, [128, NKT, 128], BF16, "vh")
        bi_r = Ring(cx, 2, [128, 8 * NOFF, 128], F32, "bi")
        s_r = Ring(cx, 4, [128, 128], F32, "sps", psum=True)
        o_r = Ring(cx, 1, [128, 128], F32, "ops", psum=True)
        d_r = Ring(cx, 1, [128, 128], F32, "dps", psum=True)
        t_r = Ring(cx, 4, [128, 128], F32, "tmpa")
        p_r = Ring(cx, 5, [128, 128], BF16, "pa")
        r_r = Ring(cx, 2, [128, 128], F32, "rden")
        vxv = vx.rearrange("(kt p) f -> p kt f", p=128)
        for h in range(NH):
            kh, bkh = kh_r.next()
            vh, bvh = vh_r.next()
            bi, bbi = bi_r.next()
            dmaq(p, "sp", kh[:], kx[h], writes=[bkh])
            dmaq(p, "sp", vh[:], vxv[:, :, h * 128:(h + 1) * 128], writes=[bvh])
            for hf in range(2):
                dmaq(p, "sp", bi[:, hf * 28:(hf + 1) * 28, :], bias[h, :, hf * 28:(hf + 1) * 28, :], writes=[bbi])
            for i in range(NT):
                qs = q[:, h, i * 128:(i + 1) * 128]
                tiles = [(i + oo, oo) for oo in range(NOFF)] if i < 8 else []
                tiles += [(NSLAB, None), (NSLAB + 1, None)]
                ops, bops = o_r.next()
                dps, bdps = d_r.next()
                pend = []

                def score(j, oo):
                    sps, bsps = s_r.next()
                    mm(p, sps[:, :], kh[:, j * 128:(j + 1) * 128], qs, True, True, [bkh, bq], [bsps])
                    pa, bpa = p_r.next()
                    if oo is not None:
                        tm, btm = t_r.next()
                        tt(p, "dve", tm[:, :], sps[:, :], bi[:, i * NOFF + oo, :], ALU.add, [bsps, bbi], [btm])
                        actf(p, pa[:, :], tm[:, :], AF.Exp, [btm], [bpa])
                    else:
                        actf(p, pa[:, :], sps[:, :], AF.Exp, [bsps], [bpa])
                    return pa, bpa

                def accum(n_, j, pa, bpa):
                    first, last = n_ == 0, n_ == len(tiles) - 1
                    mm(p, ops[:, :], vh[:, j, :], pa[:, :], first, last, [bvh, bpa], [bops])
                    mm(p, dps[:, :], onesb[:], pa[:, :], first, last, [bob, bpa], [bdps])

                LOOK = 2
                for n_, (j, oo) in enumerate(tiles):
                    pend.append((n_, j) + score(j, oo))
                    if len(pend) > LOOK:
                        accum(*pend.pop(0))
                while pend:
                    accum(*pend.pop(0))
                rd, brd = r_r.next()
                p.op("dve", lambda e, rd=rd, dps=dps: e.reciprocal(out=rd[:, :], in_=dps[:, :]), reads=[bdps], writes=[brd])
                tt(p, "dve", o[:, h, i * 128:(i + 1) * 128], ops[:, :], rd[:, :], ALU.mult, [bops, brd], [bo])

        w_r = Ring(cx, 2, [128, KC, 256], BF16, "w")
        ps_r = Ring(cx, 2, [128, 512], F32, "ps", psum=True)
        hb_r = Ring(cx, 2, [128, 512], F32, "hb")
        toks = []
        evac_o = make_residual_evac(cx, hT, oT, m, bm, hb_r, toks)
        linear_fm(cx, wview(w_out), 0, KC, o, bo, evac_o, w_r, ps_r)
        p.wait_tokens("sp", toks)
        with nc.Block() as block:
            p.emit(block)
    return nc


import ml_dtypes
NPBF = ml_dtypes.bfloat16
GRID_W = 64
ROWS = 64
WIN_H, WIN_W = 8, 16
NEG = -1e30
_cache = {}
_DBG_SKIPW = False
_TRACE = False


def _prog(name, builder):
    if name not in _cache:
        _cache[name] = builder()
    return _cache[name]


def _run(name, builder, in_maps):
    nc = _prog(name, builder)
    if _TRACE:
        res = run_bass_kernel_spmd(nc, in_maps, core_ids=list(range(NCORES)), trace=True)
        print("KTRACE", name, res.exec_time_ns)
    else:
        res = run_bass_kernel_spmd(nc, in_maps, core_ids=list(range(NCORES)))
    return res.results


def fmaj(v):
    return np.ascontiguousarray(np.asarray(v).reshape(-1, 128).T)


def to_cores(x, ctx):
    outs = []
    for c in range(NCORES):
        b, k = divmod(c, 4)
        a = np.zeros((T, D), np.float32)
        a[:TL] = x[b, k * TL:(k + 1) * TL]
        if k < 2:
            a[TL:] = ctx[b, k * TCX:(k + 1) * TCX]
        outs.append(np.ascontiguousarray(a.T))
    return outs


def from_cores(hs):
    out = np.empty((2, 4096, D), np.float32)
    for c in range(NCORES):
        b, k = divmod(c, 4)
        out[b, k * TL:(k + 1) * TL] = hs[c][:, :TL].T
    return out


def modv_for(modT_layer, s, c):
    b = c // 4
    mv = np.empty((128, KC, 3, 2), np.float32)
    for j in range(3):
        blk = modT_layer[:, (3 * s + j) * KC:(3 * s + j + 1) * KC, :]
        mv[:, :, j, 0] = blk[:, :, b]
        mv[:, :, j, 1] = blk[:, :, 2]
    return mv


def run_ada(c, c_ctx, w_ada, b_ada):
    cond = np.zeros((4, D), np.float32)
    cond[0:2] = c
    cond[2] = c_ctx
    condT = np.ascontiguousarray(cond.T.reshape(KC, 128, 4).transpose(1, 0, 2))
    in_maps = []
    for core in range(NCORES):
        layer, half = divmod(core, 2)
        cols = slice(half * 9216, (half + 1) * 9216)
        in_maps.append({"condT": condT, "wa": np.ascontiguousarray(w_ada[layer][:, cols]),
                        "ba": np.ascontiguousarray(b_ada[layer][cols].reshape(72, 128).T)})
    res = _run("ada", build_ada, in_maps)
    modT = []
    for layer in range(4):
        modT.append(np.concatenate([res[2 * layer]["mo"], res[2 * layer + 1]["mo"]], axis=1))
    return modT


def run_ffn(hs, modT_l, s, g, wgu, wdn):
    gv = fmaj(g)
    in_maps = [{"hT": hs[c], "modv": modv_for(modT_l, s, c), "gv": gv, "wgu": wgu, "wdn": wdn} for c in range(NCORES)]
    res = _run("ffn", build_ffn, in_maps)
    return [r["oT"] for r in res]


def run_mixa(hs, modT_l, g, w_in, v_gain, w_s, b_s, w_out):
    gv = fmaj(g)
    wsT = np.ascontiguousarray(w_s.transpose(2, 0, 1))
    bsb = np.ascontiguousarray(np.broadcast_to(b_s[None], (128, 16, 128)))
    vg = fmaj(v_gain)
    in_maps = [{"hT": hs[c], "modv": modv_for(modT_l, 1, c), "gv": gv, "w_in": w_in, "vg": vg, "wsT": wsT, "bsb": bsb,
                "w_out": w_out} for c in range(NCORES)]
    res = _run("mixa", build_mixa, in_maps)
    return [r["oT"] for r in res]


def _nat_bias_index():
    if "natidx" in _cache:
        return _cache["natidx"]
    kpar = np.arange(128) // 64
    kcol = np.arange(128) % 64
    idx = np.full((4, 128, 8, NOFF, 128), 15 * 31, np.int32)
    qpar = kpar[None, :]
    qcol = kcol[None, :]
    cstart = np.clip(qcol - WIN_W // 2, 0, GRID_W - WIN_W)
    colv = (kcol[:, None] >= cstart) & (kcol[:, None] < cstart + WIN_W)
    dc = np.clip(kcol[:, None] - qcol, 1 - WIN_W, WIN_W - 1) + (WIN_W - 1)
    for kq in range(4):
        r0 = 16 * kq
        for i in range(8):
            for o in range(NOFF):
                kr = r0 - 6 + 2 * (i + o) + kpar[:, None]
                qr = r0 + 2 * i + qpar
                rstart = np.clip(qr - WIN_H // 2, 0, ROWS - WIN_H)
                valid = (kr >= 0) & (kr < ROWS) & (kr >= rstart) & (kr < rstart + WIN_H) & colv
                dr = kr - qr + (WIN_H - 1)
                lin = np.clip(dr, 0, 14) * 31 + dc
                idx[kq, :, i, o, :] = np.where(valid, lin, 15 * 31)
    idx = idx.reshape(4, 128, 8 * NOFF, 128)
    _cache["natidx"] = idx
    return idx


def run_nat(hs, modT_l, g, w_qkv, q_gain, k_gain, rpb, w_out):
    gv = fmaj(g)
    qkg = np.ascontiguousarray(np.stack([q_gain, k_gain], axis=1).astype(np.float32))
    in_maps = [{"hT": hs[c], "modv": modv_for(modT_l, 1, c), "gv": gv, "wqkv": w_qkv, "qkg": qkg} for c in range(NCORES)]
    r1 = _run("nat1", build_nat1, in_maps)
    idx = _nat_bias_index()
    rp = np.concatenate([rpb.reshape(NH, -1), np.full((NH, 1), NEG, np.float32)], axis=1)
    in_maps = []
    for c in range(NCORES):
        b, k = divmod(c, 4)
        Kb = np.concatenate([np.asarray(r1[4 * b + kk]["ko"])[:, :, :TL] for kk in range(4)], axis=2)
        Vb = np.concatenate([np.asarray(r1[4 * b + kk]["vo"])[:TL] for kk in range(4)], axis=0)
        Kc = np.concatenate([np.asarray(r1[4 * b + kk]["ko"])[:, :, TL:] for kk in range(2)], axis=2)
        Vc = np.concatenate([np.asarray(r1[4 * b + kk]["vo"])[TL:] for kk in range(2)], axis=0)
        kx = np.zeros((NH, 128, NKT * 128), NPBF)
        vx = np.zeros((NKT * 128, D), NPBF)
        lo = (16 * k - 6) * 64
        hi = lo + NSLAB * 128
        a, bnd = max(lo, 0), min(hi, 4096)
        kx[:, :, a - lo:bnd - lo] = Kb[:, :, a:bnd]
        vx[a - lo:bnd - lo] = Vb[a:bnd]
        kx[:, :, NSLAB * 128:] = Kc
        vx[NSLAB * 128:] = Vc
        bias = np.ascontiguousarray(rp[:, idx[k]])
        in_maps.append({"hT": hs[c], "modv": modv_for(modT_l, 1, c), "qT": np.asarray(r1[c]["qo"]), "kx": kx, "vx": vx,
                        "bias": bias, "w_out": w_out})
    r2 = _run("nat2", build_nat2, in_maps)
    return [r["oT"] for r in r2]


def build_s5a():
    from contextlib import ExitStack
    nc = bass.Bass("TRN2", target_bir_lowering=False)
    with ExitStack() as stack:
        cx = Ctx(nc, stack)
        p = cx.p
        hT = cx.din("hT", [D, T])
        modv = cx.din("modv", [128, KC, 3, 2])
        gv = cx.din("gv", [128, KC])
        w_in = cx.din("w_in", [D, D])
        uo = cx.dout("uo", [D, T])
        consts = make_consts(cx)
        m, A, bm, bA = load_mod(cx, modv, gv)
        xn = cx.sb([128, KC, T], BF16, "xn")
        bxn = Buf("xn")
        nr = norm_rings(cx)
        rms_adaln_stream(cx, hT, xn, bxn, m, A, bm, bA, consts, nr)
        w_r = Ring(cx, 2, [128, KC, 256], BF16, "w")
        ps_r = Ring(cx, 2, [128, 512], F32, "ps", psum=True)
        uov = uo.rearrange("(kc p) t -> p kc t", p=128)
        toks = []

        def evac(j, t0, n, seg, ps, bps):
            sg, bsg = nr[0].next()
            actf(p, sg[:, :n], ps[:, :n], AF.Identity, [bps], [bsg])
            toks.append(dmaq(p, "sp", uov[:, j, t0:t0 + n], sg[:, :n], reads=[bsg]))

        linear_fm(cx, wview(w_in), 0, KC, xn, bxn, evac, w_r, ps_r)
        p.wait_tokens("sp", toks)
        with nc.Block() as block:
            p.emit(block)
    return nc


NPOS = 256 + 4096
S5_NOSYNC = ("dve", "pool")
S5_SPLIT = 32
SB = 128
NBLK = NPOS // SB
GL = 16


def build_s5b():
    from contextlib import ExitStack
    nc = bass.Bass("TRN2", target_bir_lowering=False)
    with ExitStack() as stack:
        cx = Ctx(nc, stack)
        p = cx.p
        U = cx.din("U", [32, GL, 2, NPOS])
        areT = cx.din("areT", [128, GL])
        aimT = cx.din("aimT", [128, GL])
        ldtT = cx.din("ldtT", [128, GL])
        breT = cx.din("breT", [128, GL, 16])
        bimT = cx.din("bimT", [128, GL, 16])
        creT = cx.din("creT", [128, GL, 16])
        cimT = cx.din("cimT", [128, GL, 16])
        ident = cx.din("ident", [128, 128])
        YF = cx.dout("YF", [2, 128, 2, NPOS])
        YB = cx.dout("YB", [2, 128, 2, NPOS])

        def small(shape, name):
            return cx.sb(shape, F32, name), Buf(name)

        def load(src, shape, name):
            t, b = small(shape, name)
            dmaq(p, "sp", t[:], src, writes=[b])
            return t, b

        are, b_are = load(areT, [128, GL], "are")
        aim, b_aim = load(aimT, [128, GL], "aim")
        ldt, b_ldt = load(ldtT, [128, GL], "ldt")
        bre, b_bre = load(breT, [128, GL, 16], "bre")
        bim, b_bim = load(bimT, [128, GL, 16], "bim")
        cre, b_cre = load(creT, [128, GL, 16], "cre")
        cim, b_cim = load(cimT, [128, GL, 16], "cim")
        idt, b_idt = load(ident, [128, 128], "idt")

        dt_, b_dt = small([128, GL], "dt")
        actf(p, dt_[:], ldt[:], AF.Exp, [b_ldt], [b_dt])
        xr, b_xr = small([128, GL], "xr")
        xi, b_xi = small([128, GL], "xi")
        tt(p, "dve", xr[:], are[:], dt_[:], ALU.mult, [b_are, b_dt], [b_xr])
        tt(p, "dve", xi[:], aim[:], dt_[:], ALU.mult, [b_aim, b_dt], [b_xi])
        mag, b_mag = small([128, GL], "mag")
        actf(p, mag[:], xr[:], AF.Exp, [b_xr], [b_mag])
        sn, b_sn = small([128, GL], "sn")
        cs, b_cs = small([128, GL], "cs")
        t1, b_t1 = small([128, GL], "t1")
        t2, b_t2 = small([128, GL], "t2")
        actf(p, sn[:], xi[:], AF.Sin, [b_xi], [b_sn], scale=1.0 / 16)
        actf(p, t1[:], xi[:], AF.Sin, [b_xi], [b_t1], scale=1.0 / 32)
        tt(p, "dve", t1[:], t1[:], t1[:], ALU.mult, [b_t1], [b_t1])
        ts(p, "dve", cs[:], t1[:], -2.0, 1.0, ALU.mult, ALU.add, [b_t1], [b_cs])
        for _ in range(4):
            tt(p, "dve", t1[:], cs[:], cs[:], ALU.mult, [b_cs], [b_t1])
            tt(p, "dve", t2[:], sn[:], sn[:], ALU.mult, [b_sn], [b_t2])
            tt(p, "dve", sn[:], sn[:], cs[:], ALU.mult, [b_sn, b_cs], [b_sn])
            ts(p, "dve", sn[:], sn[:], 2.0, None, ALU.mult, None, [b_sn], [b_sn])
            tt(p, "dve", cs[:], t1[:], t2[:], ALU.subtract, [b_t1, b_t2], [b_cs])
        abr, b_abr = small([128, GL], "abr")
        abi, b_abi = small([128, GL], "abi")
        tt(p, "dve", abr[:], mag[:], cs[:], ALU.mult, [b_mag, b_cs], [b_abr])
        tt(p, "dve", abi[:], mag[:], sn[:], ALU.mult, [b_mag, b_sn], [b_abi])
        nr_, b_nr = small([128, GL], "nr")
        ts(p, "dve", nr_[:], abr[:], -1.0, None, ALU.add, None, [b_abr], [b_nr])
        den, b_den = small([128, GL], "den")
        tt(p, "dve", den[:], are[:], are[:], ALU.mult, [b_are], [b_den])
        tt(p, "dve", t1[:], aim[:], aim[:], ALU.mult, [b_aim], [b_t1])
        tt(p, "dve", den[:], den[:], t1[:], ALU.add, [b_den, b_t1], [b_den])
        p.op("dve", lambda e: e.reciprocal(out=den[:], in_=den[:]), reads=[b_den], writes=[b_den])
        kr, b_kr = small([128, GL], "kr")
        ki, b_ki = small([128, GL], "ki")
        tt(p, "dve", kr[:], nr_[:], are[:], ALU.mult, [b_nr, b_are], [b_kr])
        tt(p, "dve", t1[:], abi[:], aim[:], ALU.mult, [b_abi, b_aim], [b_t1])
        tt(p, "dve", kr[:], kr[:], t1[:], ALU.add, [b_kr, b_t1], [b_kr])
        tt(p, "dve", kr[:], kr[:], den[:], ALU.mult, [b_kr, b_den], [b_kr])
        tt(p, "dve", ki[:], abi[:], are[:], ALU.mult, [b_abi, b_are], [b_ki])
        tt(p, "dve", t1[:], nr_[:], aim[:], ALU.mult, [b_nr, b_aim], [b_t1])
        tt(p, "dve", ki[:], ki[:], t1[:], ALU.subtract, [b_ki, b_t1], [b_ki])
        tt(p, "dve", ki[:], ki[:], den[:], ALU.mult, [b_ki, b_den], [b_ki])
        nki, b_nki = small([128, GL], "nki")
        ts(p, "dve", nki[:], ki[:], -1.0, None, ALU.mult, None, [b_ki], [b_nki])
        Bw, b_Bw = small([128, GL, 2, 32], "Bw")
        p.op("pool", lambda e: e.memset(Bw[:], 0.0), writes=[b_Bw])
        tb, b_tb = small([128, 16], "tb")
        for g in range(GL):
            for d in range(2):
                ps_ = slice(64 * d, 64 * d + 64)
                cols = slice(16 * d, 16 * d + 16)
                ts(p, "dve", tb[ps_, :], bim[ps_, g, :], nki[ps_, g:g + 1], None, ALU.mult, None, [b_bim, b_nki], [b_tb])
                stt(p, Bw[ps_, g, 0, cols], bre[ps_, g, :], kr[ps_, g:g + 1], tb[ps_, :], ALU.mult, ALU.add, [b_bre, b_kr, b_tb], [b_Bw])
                ts(p, "dve", tb[ps_, :], bre[ps_, g, :], ki[ps_, g:g + 1], None, ALU.mult, None, [b_bre, b_ki], [b_tb])
                stt(p, Bw[ps_, g, 1, cols], bim[ps_, g, :], kr[ps_, g:g + 1], tb[ps_, :], ALU.mult, ALU.add, [b_bim, b_kr, b_tb], [b_Bw])
        Blk = cx.sb([32, GL, 2, 128], BF16, "Blk")
        b_Blk = Buf("Blk")
        tp_r = Ring(cx, 2, [32, 128], F32, "tps", psum=True)
        for g in range(GL):
            for ri in range(2):
                tp, btp = tp_r.next()
                p.op("pe", lambda e, tp=tp, g=g, ri=ri: e.transpose(tp[:, :], Bw[:, g, ri, :], idt[:]), reads=[b_Bw, b_idt], writes=[btp])
                actf(p, Blk[:, g, ri, :], tp[:, :], AF.Identity, [btp], [b_Blk])
        Cp = cx.sb([128, GL, 2, 128], BF16, "Cp")
        b_Cp = Buf("Cp")
        p.op("pool", lambda e: e.memset(Cp[:], 0.0), writes=[b_Cp])
        for g in range(GL):
            c0 = (g % 8) * 16
            p.op("pool", lambda e, g=g, c0=c0: e.tensor_copy(out=Cp[:, g, 0, c0:c0 + 16], in_=cre[:, g, :]), reads=[b_cre], writes=[b_Cp])
            ts(p, "pool", Cp[:, g, 1, c0:c0 + 16], cim[:, g, :], -1.0, None, ALU.mult, None, [b_cim], [b_Cp])
        Ar, b_Ar = small([128, 32, 2], "Ar")
        AiS, b_AiS = small([128, 32, 2], "AiS")
        nabi, b_nabi = small([128, GL], "nabi")
        ts(p, "dve", nabi[:], abi[:], -1.0, None, ALU.mult, None, [b_abi], [b_nabi])
        Ar4 = Ar[:].rearrange("p (g b) r -> p g b r", b=2)
        Ai4 = AiS[:].rearrange("p (g b) r -> p g b r", b=2)
        for b in range(2):
            for ri in range(2):
                p.op("pool", lambda e, b=b, ri=ri: e.tensor_copy(out=Ar4[:, :, b, ri], in_=abr[:]), reads=[b_abr], writes=[b_Ar])
            p.op("pool", lambda e, b=b: e.tensor_copy(out=Ai4[:, :, b, 0], in_=nabi[:]), reads=[b_nabi], writes=[b_AiS])
            p.op("pool", lambda e, b=b: e.tensor_copy(out=Ai4[:, :, b, 1], in_=abi[:]), reads=[b_abi], writes=[b_AiS])

        ub_r = Ring(cx, 2, [32, GL, 2, SB], BF16, "ub")
        bu_r = Ring(cx, 2, [128, SB, 32, 2], F32, "bu")
        H_r = Ring(cx, 2, [128, SB, 32, 2], F32, "H")
        Hb_r = Ring(cx, 2, [128, SB, 32, 2], BF16, "Hb")
        dps_r = Ring(cx, 2, [128, 4, SB], F32, "dps", psum=True)
        yps_r = Ring(cx, 2, [128, SB], F32, "yps", psum=True)
        ys_r = Ring(cx, 3, [128, SB], F32, "ys")
        zero, b_zero = small([128, 32, 2], "zero")
        p.op("pool", lambda e: e.memset(zero[:], 0.0), writes=[b_zero])
        lanes = []
        for eng, sl in (("dve", slice(0, S5_SPLIT)), ("pool", slice(S5_SPLIT, 32))):
            n_ = sl.stop - sl.start
            if n_ <= 0:
                continue
            m1, b_m1 = small([128, n_, 2], "m1" + eng)
            m2, b_m2 = small([128, n_, 2], "m2" + eng)
            lanes.append(dict(eng=eng, sl=sl, m1=m1, b_m1=b_m1, m2=m2, b_m2=b_m2, prev=zero[:, sl, :], b_prev=b_zero))
        toks = []
        for blk in range(NBLK):
            k0 = blk * SB
            ub, b_ub = ub_r.next()
            dmaq(p, "pool", ub[:], U[:, :, :, k0:k0 + SB], writes=[b_ub])
            bu, b_bu = bu_r.next()
            bu_v = bu[:].rearrange("p k c r -> p k (c r)")
            for q4 in range(GL):
                dps, b_dps = dps_r.next()
                for b in range(2):
                    for ri in range(2):
                        mm(p, dps[:, b * 2 + ri, :], Blk[:, q4, ri, :], ub[:, q4, b, :], True, True, [b_Blk, b_ub], [b_dps])
                actf(p, bu_v[:, :, q4 * 4:q4 * 4 + 4], dps[:].rearrange("p c k -> p k c"), AF.Identity, [b_dps], [b_bu])
            H, b_H0 = H_r.next()
            b_Hl = [Buf("Hl%d" % li) for li in range(len(lanes))]
            p.nosync = set(S5_NOSYNC) | set(GLOBAL_NOSYNC)
            for k in range(SB):
                for li, L in enumerate(lanes):
                    eng, sl, m1, m2 = L["eng"], L["sl"], L["m1"], L["m2"]
                    pv, bpv = L["prev"], L["b_prev"]
                    wr = [b_Hl[li]] + ([b_H0] if k == 0 else [])
                    tt(p, eng, m1[:], Ar[:, sl, :], pv, ALU.mult, [b_Ar, bpv], [L["b_m1"]])
                    tt(p, eng, m2[:], AiS[:, sl, :], pv[:, :, ::-1], ALU.mult, [b_AiS, bpv], [L["b_m2"]])
                    tt(p, eng, m1[:], m1[:], m2[:], ALU.add, [L["b_m1"], L["b_m2"]], [L["b_m1"]])
                    tt(p, eng, H[:, k, sl, :], m1[:], bu[:, k, sl, :], ALU.add, [L["b_m1"], b_bu], wr)
                    L["prev"], L["b_prev"] = H[:, k, sl, :], b_Hl[li]
            p.nosync = set(GLOBAL_NOSYNC)
            Hb, b_Hb = Hb_r.next()
            actf(p, Hb[:].rearrange("p k c r -> p (k c r)"), H[:].rearrange("p k c r -> p (k c r)"), AF.Identity, b_Hl + [b_H0], [b_Hb, b_H0])
            for cc in range(2):
                for b in range(2):
                    for d in range(2):
                        ps_ = slice(64 * d, 64 * d + 64)
                        yps, b_yps = yps_r.next()
                        n_ = 0
                        for g8 in range(8):
                            g = cc * 8 + g8
                            for ri in range(2):
                                mm(p, yps[:, :], Cp[ps_, g, ri, :], Hb[ps_, :, g * 2 + b, ri], n_ == 0, n_ == 15, [b_Cp, b_Hb], [b_yps])
                                n_ += 1
                        ys, b_ys = ys_r.next()
                        actf(p, ys[:, :], yps[:, :], AF.Identity, [b_yps], [b_ys])
                        dst = (YF if d == 0 else YB)
                        toks.append(dmaq(p, "sp", dst[cc, :, b, k0:k0 + SB], ys[:, :], reads=[b_ys]))
        p.wait_tokens("sp", toks)
        with nc.Block() as block:
            p.emit(block)
    return nc


def build_s5c():
    from contextlib import ExitStack
    nc = bass.Bass("TRN2", target_bir_lowering=False)
    with ExitStack() as stack:
        cx = Ctx(nc, stack)
        p = cx.p
        hT = cx.din("hT", [D, T])
        modv = cx.din("modv", [128, KC, 3, 2])
        uT = cx.din("uT", [D, T])
        yfT = cx.din("yfT", [D, T])
        ybT = cx.din("ybT", [D, T])
        dsk = cx.din("dsk", [128, KC])
        w_glu = cx.din("w_glu", [D, 2 * D])
        oT = cx.dout("oT", [D, T])
        m = cx.sb([128, KC, 3, 2], F32, "m")
        bm = Buf("m")
        dmaq(p, "sp", m[:], modv, writes=[bm])
        dk = cx.sb([128, KC], F32, "dk")
        bdk = Buf("dk")
        dmaq(p, "sp", dk[:], dsk, writes=[bdk])
        gl = cx.sb([128, KC, T], BF16, "gl")
        bgl = Buf("gl")
        a_r = Ring(cx, 2, [128, T], F32, "ya")
        b_r = Ring(cx, 2, [128, T], F32, "yb")
        c_r = Ring(cx, 2, [128, T], F32, "yc")
        gelu = Gelu(cx, width=T)
        uv = uT.rearrange("(kc p) t -> p kc t", p=128)
        fv = yfT.rearrange("(kc p) t -> p kc t", p=128)
        bv = ybT.rearrange("(kc p) t -> p kc t", p=128)
        for kc in range(KC):
            ya, bya = a_r.next()
            yb, byb = b_r.next()
            yc, byc = c_r.next()
            dmaq(p, "sp", ya[:], uv[:, kc, :], writes=[bya])
            dmaq(p, "sp", yb[:], fv[:, kc, :], writes=[byb])
            dmaq(p, "sp", yc[:], bv[:, kc, :], writes=[byc])
            tt(p, "dve", yb[:], yb[:], yc[:], ALU.add, [byb, byc], [byb])
            stt(p, ya[:], ya[:], dk[:, kc:kc + 1], yb[:], ALU.mult, ALU.add, [bya, bdk, byb], [bya])
            gelu(p, gl[:, kc, :], ya[:], T, [bya], [bgl])
        w_r = Ring(cx, 2, [128, KC, 128], BF16, "wa")
        w2_r = Ring(cx, 2, [128, KC, 128], BF16, "wg")
        pa_r = Ring(cx, 2, [128, 512], F32, "pa", psum=True)
        pg_r = Ring(cx, 2, [128, 512], F32, "pg", psum=True)
        sg_r = Ring(cx, 2, [128, 512], F32, "sg")
        hb_r = Ring(cx, 2, [128, 512], F32, "hb")
        wv = wview(w_glu)
        hv = hT.rearrange("(kc p) t -> p kc t", p=128)
        ov = oT.rearrange("(kc p) t -> p kc t", p=128)
        toks = []
        for j in range(KC):
            wa, bwa = w_r.next()
            wg, bwg = w2_r.next()
            dmaq(p, "pool", wa[:], wv[:, :, j * 128:(j + 1) * 128], writes=[bwa])
            dmaq(p, "pool", wg[:], wv[:, :, D + j * 128:D + (j + 1) * 128], writes=[bwg])
            for (t0, n, seg) in BLOCKS:
                pa, bpa = pa_r.next()
                pg, bpg = pg_r.next()
                for kc in range(KC):
                    mm(p, pa[:, :n], wa[:, kc, :], gl[:, kc, t0:t0 + n], kc == 0, kc == KC - 1, [bwa, bgl], [bpa])
                for kc in range(KC):
                    mm(p, pg[:, :n], wg[:, kc, :], gl[:, kc, t0:t0 + n], kc == 0, kc == KC - 1, [bwg, bgl], [bpg])
                sg, bsg = sg_r.next()
                actf(p, sg[:, :n], pg[:, :n], AF.Sigmoid, [bpg], [bsg])
                tt(p, "dve", sg[:, :n], pa[:, :n], sg[:, :n], ALU.mult, [bpa, bsg], [bsg])
                hb, bhb = hb_r.next()
                dmaq(p, "sp", hb[:, :n], hv[:, j, t0:t0 + n], writes=[bhb])
                stt(p, hb[:, :n], sg[:, :n], m[:, j, 2, seg:seg + 1], hb[:, :n], ALU.mult, ALU.add, [bsg, bm, bhb], [bhb])
                toks.append(dmaq(p, "sp", ov[:, j, t0:t0 + n], hb[:, :n], reads=[bhb]))
        p.wait_tokens("sp", toks)
        with nc.Block() as block:
            p.emit(block)
    return nc


def run_s5(hs, modT_l, g, w_in, a_re, a_im, log_dt, b_re, b_im, c_re, c_im, d_skip, w_glu):
    gv = fmaj(g)
    in_maps = [{"hT": hs[c], "modv": modv_for(modT_l, 1, c), "gv": gv, "w_in": w_in} for c in range(NCORES)]
    r1 = _run("s5a", build_s5a, in_maps)
    us = [r["uo"] for r in r1]
    seq = []
    for b in range(2):
        parts = [us[4 * b + k][:, TL:] for k in range(2)] + [us[4 * b + k][:, :TL] for k in range(4)]
        seq.append(np.concatenate(parts, axis=1))
    seq = np.stack(seq, axis=1)
    order_b = np.concatenate([np.arange(255, -1, -1), 256 + np.arange(4095, -1, -1)])
    ident = np.eye(128, dtype=np.float32)
    in_maps = []
    for c in range(NCORES):
        gs = slice(GL * c, GL * (c + 1))
        sc = seq[256 * c:256 * (c + 1)].reshape(GL, 16, 2, NPOS)
        Uc = np.empty((32, GL, 2, NPOS), np.float32)
        Uc[:16] = sc.transpose(1, 0, 2, 3)
        Uc[16:] = sc[:, :, :, order_b].transpose(1, 0, 2, 3)

        def dp(a):
            a = a[:, gs]
            if a.ndim == 3:
                return np.ascontiguousarray(a.transpose(0, 2, 1).reshape(128, GL))
            return np.ascontiguousarray(a.transpose(0, 2, 1, 3).reshape(128, GL, a.shape[3]))

        ldt = np.ascontiguousarray(np.broadcast_to(log_dt[:, gs][:, None, :], (2, 64, GL)).reshape(128, GL))
        in_maps.append({"U": Uc, "areT": dp(a_re), "aimT": dp(a_im), "ldtT": ldt,
                        "breT": dp(b_re), "bimT": dp(b_im),
                        "creT": dp(c_re.transpose(0, 1, 3, 2)), "cimT": dp(c_im.transpose(0, 1, 3, 2)), "ident": ident})
    r2 = _run("s5b", build_s5b, in_maps)
    YF = np.concatenate([r["YF"].reshape(256, 2, NPOS) for r in r2], axis=0)
    YBo = np.concatenate([r["YB"].reshape(256, 2, NPOS) for r in r2], axis=0)
    YB = np.empty_like(YBo)
    YB[:, :, order_b] = YBo

    def percore(Y, c):
        b, k = divmod(c, 4)
        a = np.zeros((D, T), np.float32)
        a[:, :TL] = Y[:, b, 256 + k * TL:256 + (k + 1) * TL]
        if k < 2:
            a[:, TL:] = Y[:, b, k * TCX:(k + 1) * TCX]
        return a

    dsk = fmaj(d_skip)
    in_maps = [{"hT": hs[c], "modv": modv_for(modT_l, 1, c), "uT": us[c], "yfT": percore(YF, c), "ybT": percore(YB, c),
                "dsk": dsk, "w_glu": w_glu} for c in range(NCORES)]
    r3 = _run("s5c", build_s5c, in_maps)
    return [r["oT"] for r in r3]


def kernel(x, c, ctx, c_ctx, w_ada, b_ada, norm_g, ffn_w_gu, ffn_w_down,
           a_w_in, a_v_gain, a_w_s, a_b_s, a_w_out,
           b_w_qkv, b_q_gain, b_k_gain, b_rpb, b_w_out,
           c_w_in, c_a_re, c_a_im, c_log_dt, c_b_re, c_b_im, c_c_re, c_c_im, c_d, c_w_glu):
    f = lambda a: np.asarray(a, dtype=np.float32)
    x, c, ctx, c_ctx = f(x), f(c), f(ctx), f(c_ctx)
    modT = run_ada(c, c_ctx, f(w_ada), f(b_ada))
    hs = to_cores(x, ctx)
    depth = 4
    for i in range(depth):
        kind, j = i % 3, i // 3
        hs = run_ffn(hs, modT[i], 0, f(norm_g[i, 0]), f(ffn_w_gu[i, 0]), f(ffn_w_down[i, 0]))
        if kind == 0:
            hs = run_mixa(hs, modT[i], f(norm_g[i, 1]), f(a_w_in[j]), f(a_v_gain[j]), f(a_w_s[j]), f(a_b_s[j]), f(a_w_out[j]))
        elif kind == 1:
            hs = run_nat(hs, modT[i], f(norm_g[i, 1]), f(b_w_qkv[j]), f(b_q_gain[j]), f(b_k_gain[j]), f(b_rpb[j]), f(b_w_out[j]))
        else:
            hs = run_s5(hs, modT[i], f(norm_g[i, 1]), f(c_w_in[j]), f(c_a_re[j]), f(c_a_im[j]), f(c_log_dt[j]),
                        f(c_b_re[j]), f(c_b_im[j]), f(c_c_re[j]), f(c_c_im[j]), f(c_d[j]), f(c_w_glu[j]))
        hs = run_ffn(hs, modT[i], 2, f(norm_g[i, 2]), f(ffn_w_gu[i, 1]), f(ffn_w_down[i, 1]))
    return from_cores(hs)
```

```python
import numpy as np
import concourse.bass as bass
import concourse.mybir as mybir
from concourse.bass_utils import run_bass_kernel_spmd
from concourse.alu_op_type import AluOpType as ALU

F32 = mybir.dt.float32
BF16 = mybir.dt.bfloat16
AF = mybir.ActivationFunctionType

D = 2048
KC = 16
DFF = 5632
FC = 44
NCORES = 8
TL = 1024
TCX = 128
T = TL + TCX
EPS = 1e-6
GLOBAL_NOSYNC = ()
PE_LAZY_SIG = True


class Buf:
    __slots__ = ("lw", "rd", "name")

    def __init__(self, name=""):
        self.lw = None
        self.rd = {}
        self.name = name


class Prog:
    CENG = ("pe", "act", "dve", "pool")

    def __init__(self, nc, stack, n_dsem=20):
        self.nc = nc
        self.eng = {"pe": nc.tensor, "act": nc.scalar, "dve": nc.vector,
                    "pool": nc.gpsimd, "sp": nc.sync}
        self.q = {e: [] for e in self.eng}
        self.cnt = {e: 0 for e in self.CENG}
        self.seen = {e: {} for e in self.eng}
        self.nosync = set(GLOBAL_NOSYNC)
        self.csem = {e: stack.enter_context(nc.semaphore("c_" + e)) for e in self.CENG}
        self.dsem = {}
        self.dcum = {}
        self.drr = {}
        for qn in ("sp", "pool", "act"):
            n = n_dsem if qn != "act" else 6
            self.dsem[qn] = [stack.enter_context(nc.semaphore("d_%s%d" % (qn, i))) for i in range(n)]
            self.dcum[qn] = [0] * n
            self.drr[qn] = 0

    def _need(self, eng, tok, waits):
        if tok is None:
            return
        key, val = tok
        if key == ("c", "pe") and eng == "pe":
            return
        if key[0] == "c" and key[1] == eng and eng in self.nosync:
            return
        if self.seen[eng].get(key, 0) >= val:
            return
        if waits.get(key, 0) < val:
            waits[key] = val

    def _deps(self, eng, reads, writes):
        waits = {}
        for b in reads:
            self._need(eng, b.lw, waits)
        for b in writes:
            self._need(eng, b.lw, waits)
            for k, v in b.rd.items():
                self._need(eng, (k, v), waits)
        for k, v in waits.items():
            self.seen[eng][k] = v
        return waits

    def _commit(self, tok, reads, writes):
        key, val = tok
        for b in reads:
            if b.rd.get(key, 0) < val:
                b.rd[key] = val
        for b in writes:
            b.lw = tok
            b.rd = {}

    def op(self, eng, fn, reads=(), writes=(), sig=True):
        waits = self._deps(eng, reads, writes)
        if sig:
            self.cnt[eng] += 1
            tok = (("c", eng), self.cnt[eng])
            self.q[eng].append((list(waits.items()), fn, ("c", eng, 1)))
        else:
            tok = (("c", eng), self.cnt[eng] + 1)
            self.q[eng].append((list(waits.items()), fn, ("n",)))
        self._commit(tok, reads, writes)
        return tok

    def dma(self, qn, fns, reads=(), writes=()):
        if not isinstance(fns, (list, tuple)):
            fns = [fns]
        waits = self._deps(qn, reads, writes)
        i = self.drr[qn]
        self.drr[qn] = (i + 1) % len(self.dsem[qn])
        key = ("d", qn, i)
        prev = self.dcum[qn][i]
        if prev > 0 and self.seen[qn].get(key, 0) < prev:
            waits[key] = prev
            self.seen[qn][key] = prev
        for j, fn in enumerate(fns):
            self.dcum[qn][i] += 16
            self.q[qn].append((list(waits.items()) if j == 0 else [], fn, ("d", qn, i)))
        tok = (key, self.dcum[qn][i])
        self._commit(tok, reads, writes)
        return tok

    def wait_tokens(self, eng, toks):
        waits = {}
        for t in toks:
            self._need(eng, t, waits)
        for k, v in waits.items():
            self.seen[eng][k] = v
        self.q[eng].append((list(waits.items()), None, None))

    def _sem(self, key):
        if key[0] == "c":
            return self.csem[key[1]]
        return self.dsem[key[1]][key[2]]

    def emit(self, block):
        decos = {"pe": block.tensor, "act": block.scalar, "dve": block.vector,
                 "pool": block.gpsimd, "sp": block.sync}
        for e in self.eng:
            items = self.q[e]
            if not items:
                continue

            def body(engine, items=items):
                for waits, fn, inc in items:
                    for key, val in waits:
                        engine.wait_ge(self._sem(key), val)
                    if fn is None:
                        continue
                    ins = fn(engine)
                    if inc[0] == "n":
                        continue
                    if inc[0] == "c":
                        ins.then_inc(self.csem[inc[1]], 1)
                    else:
                        ins.then_inc(self.dsem[inc[1]][inc[2]], 16)

            decos[e](body)


class Ctx:
    def __init__(self, nc, stack):
        self.nc = nc
        self.stack = stack
        self.p = Prog(nc, stack)
        self._n = 0

    def sb(self, shape, dt, name=None):
        self._n += 1
        return self.stack.enter_context(self.nc.sbuf_tensor(name or ("sb%d" % self._n), list(shape), dt))

    def ps(self, shape, dt=F32, name=None):
        self._n += 1
        return self.stack.enter_context(self.nc.psum_tensor(name or ("ps%d" % self._n), list(shape), dt))

    def din(self, name, shape, dt=F32):
        return self.nc.dram_tensor(name, list(shape), dt, kind="ExternalInput").ap()

    def dout(self, name, shape, dt=F32):
        return self.nc.dram_tensor(name, list(shape), dt, kind="ExternalOutput").ap()


class Ring:
    def __init__(self, cx, n, shape, dt, name, psum=False):
        self.t = [(cx.ps(shape, dt, "%s%d" % (name, i)) if psum else cx.sb(shape, dt, "%s%d" % (name, i)))
                  for i in range(n)]
        self.b = [Buf("%s%d" % (name, i)) for i in range(n)]
        self.i = 0

    def next(self):
        i = self.i
        self.i = (i + 1) % len(self.t)
        return self.t[i], self.b[i]


def make_consts(cx):
    p = cx.p
    ones = cx.sb([128, 128], F32, "ones")
    epsc = cx.sb([128, 1], F32, "epsc")
    b = Buf("consts")
    p.op("pool", lambda e: e.memset(ones[:], 1.0), writes=[b])
    p.op("pool", lambda e: e.memset(epsc[:], EPS), writes=[b])
    return ones, epsc, b


def load_mod(cx, modv, gv):
    p = cx.p
    m = cx.sb([128, KC, 3, 2], F32)
    g = cx.sb([128, KC], F32)
    A = cx.sb([128, KC, 2], F32)
    bm, bg, bA = Buf("m"), Buf("g"), Buf("A")
    p.dma("sp", lambda e: e.dma_start(out=m[:], in_=modv), writes=[bm])
    p.dma("sp", lambda e: e.dma_start(out=g[:], in_=gv), writes=[bg])
    for s in range(2):
        p.op("dve", lambda e, s=s: e.scalar_tensor_tensor(out=A[:, :, s], in0=m[:, :, 1, s], scalar=1.0,
                                                           in1=g[:, :], op0=ALU.add, op1=ALU.mult),
             reads=[bm, bg], writes=[bA])
    return m, A, bm, bA


def rms_adaln(cx, src, bsrc, dst, bdst, blocks, m, A, bm, bA, consts, rings):
    p = cx.p
    ones, epsc, bc = consts
    sq_r, ps_r, rs_r, tmp_r = rings
    for (t0, n, seg) in blocks:
        _rms_block(p, src, bsrc, dst, bdst, t0, n, seg, m, A, bm, bA, ones, epsc, bc, sq_r, ps_r, rs_r, tmp_r)


def _rms_block(p, src, bsrc, dst, bdst, t0, n, seg, m, A, bm, bA, ones, epsc, bc, sq_r, ps_r, rs_r, tmp_r, s0=None):
    s0 = t0 if s0 is None else s0
    ss, bss = ps_r.next()
    for kc in range(KC):
        sq, bsq = sq_r.next()
        bs_ = bsrc[kc] if isinstance(bsrc, list) else bsrc
        p.op("act", lambda e, sq=sq, kc=kc: e.activation(out=sq[:, :n], in_=src[:, kc, s0:s0 + n], func=AF.Square),
             reads=[bs_], writes=[bsq])
        p.op("pe", lambda e, sq=sq, kc=kc: e.matmul(ss[:, :n], lhsT=ones[:], rhs=sq[:, :n],
                                                    start=(kc == 0), stop=(kc == KC - 1)),
             reads=[bsq, bc], writes=[bss])
    rs, brs = rs_r.next()
    p.op("act", lambda e: e.activation(out=rs[:, :n], in_=ss[:, :n], func=AF.Sqrt, bias=epsc[:], scale=1.0 / D),
         reads=[bss, bc], writes=[brs])
    p.op("dve", lambda e: e.reciprocal(out=rs[:, :n], in_=rs[:, :n]), reads=[brs], writes=[brs])
    for kc in range(KC):
        tmp, btmp = tmp_r.next()
        p.op("dve", lambda e, tmp=tmp, kc=kc: e.scalar_tensor_tensor(
            out=tmp[:, :n], in0=src[:, kc, s0:s0 + n], scalar=A[:, kc, seg:seg + 1], in1=rs[:, :n],
            op0=ALU.mult, op1=ALU.mult), reads=[bsrc[kc] if isinstance(bsrc, list) else bsrc, brs, bA], writes=[btmp])
        p.op("act", lambda e, tmp=tmp, kc=kc: e.activation(out=dst[:, kc, t0:t0 + n], in_=tmp[:, :n],
                                                           func=AF.Identity, bias=m[:, kc, 0, seg:seg + 1], scale=1.0),
             reads=[btmp, bm], writes=[bdst])


BLOCKS = [(0, 512, 0), (512, 512, 0), (1024, 128, 1)]


def wview(w):
    return w.rearrange("(kc p) f -> p kc f", p=128)


def build_ffn(dbg=False):
    from contextlib import ExitStack
    nc = bass.Bass("TRN2", target_bir_lowering=False)
    with ExitStack() as stack:
        cx = Ctx(nc, stack)
        p = cx.p
        hT = cx.din("hT", [D, T])
        modv = cx.din("modv", [128, KC, 3, 2])
        gv = cx.din("gv", [128, KC])
        wgu = cx.din("wgu", [D, 2 * DFF])
        wdn = cx.din("wdn", [DFF, D])
        oT = cx.dout("oT", [D, T])

        consts = make_consts(cx)
        h = cx.sb([128, KC, T], F32, "h")
        bh = [Buf("h%d" % k) for k in range(KC)]
        hv = hT.rearrange("(kc p) t -> p kc t", p=128)
        m, A, bm, bA = load_mod(cx, modv, gv)
        for k in range(KC):
            dmaq(p, "sp", h[:, k, :], hv[:, k, :], writes=[bh[k]])
        G = cx.sb([128, KC, 2], F32, "G")
        bG = Buf("G")
        p.op("dve", lambda e: e.tensor_scalar(out=G[:], in0=m[:, :, 2, :], scalar1=0.5, scalar2=None, op0=ALU.mult),
             reads=[bm], writes=[bG])

        xn = cx.sb([128, KC, T], BF16, "xn")
        bxn = Buf("xn")
        rings = (Ring(cx, 2, [128, 512], F32, "sq"), Ring(cx, 1, [128, 512], F32, "ssps", psum=True),
                 Ring(cx, 2, [128, 512], F32, "rs"), Ring(cx, 2, [128, 512], F32, "tmp"))
        rms_adaln(cx, h, bh, xn, bxn, BLOCKS, m, A, bm, bA, consts, rings)

        if dbg:
            xo = cx.dout("xo", [128, KC, T], BF16)
            p.wait_tokens("sp", [p.dma("sp", lambda e: e.dma_start(out=xo, in_=xn[:]), reads=[bxn])])
        ov = oT.rearrange("(kc p) t -> p kc t", p=128)
        toks = []

        def store_dc(dc):
            toks.append(dmaq(p, "sp", ov[:, dc, :], h[:, dc, :], reads=[bh[dc]]))

        ffn_core(cx, xn, bxn, h, bh, G, bG, wgu, wdn, on_final=store_dc)
        p.wait_tokens("sp", toks)
        with nc.Block() as block:
            p.emit(block)
    return nc


def ffn_core(cx, xn, bxn, h, bh, G, bG, wgu, wdn, GRP=4, on_final=None):
    p = cx.p
    wguv = wview(wgu)
    wdnv = wdn.rearrange("(fc p) d -> p fc d", p=128)
    wg_r = Ring(cx, 2, [128, KC, 256], BF16, "wg")
    wu_r = Ring(cx, 2, [128, KC, 256], BF16, "wu")
    wd_r = Ring(cx, 2, [128, GRP, D], BF16, "wd")
    act_r = Ring(cx, 2, [128, GRP, T], BF16, "act")
    gps_r = Ring(cx, 2, [128, 512], F32, "gps", psum=True)
    ups_r = Ring(cx, 2, [128, 512], F32, "ups", psum=True)
    yps_r = Ring(cx, 2, [128, 512], F32, "yps", psum=True)
    sg_r = Ring(cx, 2, [128, 512], F32, "sg")
    wg = wu = None
    for grp in range(FC // GRP):
        act, bact = act_r.next()
        wd, bwd = wd_r.next()
        for fl in range(GRP):
            p.dma("pool", lambda e, wd=wd, fl=fl, grp=grp: e.dma_start(out=wd[:, fl, :], in_=wdnv[:, grp * GRP + fl, :]),
                  writes=[bwd])
        for fl in range(GRP):
            fc = grp * GRP + fl
            if fc % 2 == 0 and (not _DBG_SKIPW or fc < 4):
                wg, bwg = wg_r.next()
                wu, bwu = wu_r.next()
                p.dma("pool", lambda e, wg=wg, fc=fc: e.dma_start(out=wg[:], in_=wguv[:, :, fc * 128:fc * 128 + 256]),
                      writes=[bwg])
                p.dma("pool", lambda e, wu=wu, fc=fc: e.dma_start(out=wu[:], in_=wguv[:, :, DFF + fc * 128:DFF + fc * 128 + 256]),
                      writes=[bwu])
            c0 = (fc % 2) * 128
            for (t0, n, seg) in BLOCKS:
                gps, bgps = gps_r.next()
                ups, bups = ups_r.next()
                for kc in range(KC):
                    p.op("pe", lambda e, gps=gps, wg=wg, kc=kc, c0=c0, t0=t0, n=n: e.matmul(
                        gps[:, :n], lhsT=wg[:, kc, c0:c0 + 128], rhs=xn[:, kc, t0:t0 + n],
                        start=(kc == 0), stop=(kc == KC - 1)), reads=[bwg, bxn], writes=[bgps], sig=(kc == KC - 1) or not PE_LAZY_SIG)
                for kc in range(KC):
                    p.op("pe", lambda e, ups=ups, wu=wu, kc=kc, c0=c0, t0=t0, n=n: e.matmul(
                        ups[:, :n], lhsT=wu[:, kc, c0:c0 + 128], rhs=xn[:, kc, t0:t0 + n],
                        start=(kc == 0), stop=(kc == KC - 1)), reads=[bwu, bxn], writes=[bups], sig=(kc == KC - 1) or not PE_LAZY_SIG)
                sg, bsg = sg_r.next()
                p.op("act", lambda e, sg=sg, gps=gps, n=n: e.activation(out=sg[:, :n], in_=gps[:, :n], func=AF.Silu),
                     reads=[bgps], writes=[bsg])
                p.op("dve", lambda e, sg=sg, ups=ups, act=act, fl=fl, t0=t0, n=n: e.tensor_tensor(
                    out=act[:, fl, t0:t0 + n], in0=ups[:, :n], in1=sg[:, :n], op=ALU.mult),
                    reads=[bups, bsg], writes=[bact])
        for dc in range(KC):
            for (t0, n, seg) in BLOCKS:
                yps, byps = yps_r.next()
                for fl in range(GRP):
                    p.op("pe", lambda e, yps=yps, wd=wd, fl=fl, dc=dc, act=act, t0=t0, n=n: e.matmul(
                        yps[:, :n], lhsT=wd[:, fl, dc * 128:(dc + 1) * 128], rhs=act[:, fl, t0:t0 + n],
                        start=(fl == 0), stop=(fl == GRP - 1)), reads=[bwd, bact], writes=[byps], sig=(fl == GRP - 1) or not PE_LAZY_SIG)
                p.op("dve", lambda e, yps=yps, dc=dc, t0=t0, n=n, seg=seg: e.scalar_tensor_tensor(
                    out=h[:, dc, t0:t0 + n], in0=yps[:, :n], scalar=G[:, dc, seg:seg + 1], in1=h[:, dc, t0:t0 + n],
                    op0=ALU.mult, op1=ALU.add), reads=[byps, bG, bh[dc]], writes=[bh[dc]])
            if on_final is not None and grp == FC // GRP - 1:
                on_final(dc)


NF_ADA = 4 * 9 * KC // NCORES


def build_ada():
    from contextlib import ExitStack
    nc = bass.Bass("TRN2", target_bir_lowering=False)
    with ExitStack() as stack:
        cx = Ctx(nc, stack)
        p = cx.p
        condT = cx.din("condT", [128, KC, 4])
        wa = cx.din("wa", [D, NF_ADA * 128])
        ba = cx.din("ba", [128, NF_ADA])
        mo = cx.dout("mo", [128, NF_ADA, 4])
        ct = cx.sb([128, KC, 4], F32)
        cs = cx.sb([128, KC, 4], BF16)
        bt = cx.sb([128, NF_ADA], F32)
        res = cx.sb([128, NF_ADA, 4], F32)
        bct, bcs, bbt, bres = Buf(), Buf(), Buf(), Buf()
        p.dma("sp", lambda e: e.dma_start(out=ct[:], in_=condT), writes=[bct])
        p.dma("sp", lambda e: e.dma_start(out=bt[:], in_=ba), writes=[bbt])
        p.op("act", lambda e: e.activation(out=cs[:], in_=ct[:], func=AF.Silu), reads=[bct], writes=[bcs])
        w_r = Ring(cx, 3, [128, KC, 512], BF16, "w")
        ps_r = Ring(cx, 2, [128, 4, 4], F32, "ps", psum=True)
        wav = wview(wa)
        for g4 in range(NF_ADA // 4):
            w, bw = w_r.next()
            p.dma("pool", lambda e, w=w, g4=g4: e.dma_start(out=w[:], in_=wav[:, :, g4 * 512:(g4 + 1) * 512]), writes=[bw])
            ps, bps = ps_r.next()
            for j in range(4):
                for kc in range(KC):
                    p.op("pe", lambda e, ps=ps, w=w, j=j, kc=kc: e.matmul(
                        ps[:, j, :], lhsT=w[:, kc, j * 128:(j + 1) * 128], rhs=cs[:, kc, :],
                        start=(kc == 0), stop=(kc == KC - 1)), reads=[bw, bcs], writes=[bps])
            for j in range(4):
                f = g4 * 4 + j
                p.op("dve", lambda e, ps=ps, j=j, f=f: e.tensor_scalar(out=res[:, f, :], in0=ps[:, j, :], scalar1=bt[:, f:f + 1],
                                                                      scalar2=None, op0=ALU.add),
                     reads=[bps, bbt], writes=[bres])
        tok = p.dma("sp", lambda e: e.dma_start(out=mo, in_=res[:]), reads=[bres])
        p.wait_tokens("sp", [tok])
        with nc.Block() as block:
            p.emit(block)
    return nc


def mm(p, out, lhsT, rhs, start, stop, reads, writes):
    return p.op("pe", lambda e: e.matmul(out, lhsT=lhsT, rhs=rhs, start=start, stop=stop), reads=reads, writes=writes,
                sig=bool(stop) or not PE_LAZY_SIG)


def actf(p, out, in_, func, reads, writes, bias=None, scale=None, accum_out=None):
    kw = {}
    if bias is not None:
        kw["bias"] = bias
    if scale is not None:
        kw["scale"] = scale
    if accum_out is not None:
        kw["accum_out"] = accum_out
    return p.op("act", lambda e: e.activation(out=out, in_=in_, func=func, **kw), reads=reads, writes=writes)


def tt(p, eng, out, in0, in1, op, reads, writes):
    return p.op(eng, lambda e: e.tensor_tensor(out=out, in0=in0, in1=in1, op=op), reads=reads, writes=writes)


def ts(p, eng, out, in0, s1, s2, op0, op1, reads, writes):
    if op1 is None:
        return p.op(eng, lambda e: e.tensor_scalar(out=out, in0=in0, scalar1=s1, scalar2=None, op0=op0), reads=reads, writes=writes)
    return p.op(eng, lambda e: e.tensor_scalar(out=out, in0=in0, scalar1=s1, scalar2=s2, op0=op0, op1=op1), reads=reads, writes=writes)


def stt(p, out, in0, scalar, in1, op0, op1, reads, writes):
    return p.op("dve", lambda e: e.scalar_tensor_tensor(out=out, in0=in0, scalar=scalar, in1=in1, op0=op0, op1=op1),
                reads=reads, writes=writes)


def dmaq(p, q, out, in_, reads=(), writes=()):
    return p.dma(q, lambda e: e.dma_start(out=out, in_=in_), reads=reads, writes=writes)


class Gelu:
    def __init__(self, cx, width=512):
        self.a = Ring(cx, 2, [128, width], F32, "gl_a")
        self.b = Ring(cx, 2, [128, width], F32, "gl_b")

    def __call__(self, p, out, ps, n, rd, wr, accum_sq=None):
        a, ba = self.a.next()
        b, bb = self.b.next()
        actf(p, a[:, :n], ps, AF.Square, rd, [ba])
        ts(p, "dve", a[:, :n], a[:, :n], 0.044715, 1.0, ALU.mult, ALU.add, [ba], [ba])
        tt(p, "dve", a[:, :n], a[:, :n], ps, ALU.mult, [ba] + rd, [ba])
        actf(p, b[:, :n], a[:, :n], AF.Sigmoid, [ba], [bb], scale=1.5957691216057308)
        tt(p, "dve", out, b[:, :n], ps, ALU.mult, [bb] + rd, wr)


def load_h(cx, hT, name="h"):
    p = cx.p
    h = cx.sb([128, KC, T], F32, name)
    bh = Buf(name)
    hv = hT.rearrange("(kc p) t -> p kc t", p=128)
    for q in range(4):
        dmaq(p, "sp", h[:, 4 * q:4 * q + 4, :], hv[:, 4 * q:4 * q + 4, :], writes=[bh])
    return h, bh


def store_h(cx, oT, h, bh):
    p = cx.p
    ov = oT.rearrange("(kc p) t -> p kc t", p=128)
    toks = [dmaq(p, "sp", ov[:, 4 * q:4 * q + 4, :], h[:, 4 * q:4 * q + 4, :], reads=[bh]) for q in range(4)]
    p.wait_tokens("sp", toks)


def rms_adaln_stream(cx, hT, dst, bdst, m, A, bm, bA, consts, rings, blocks=BLOCKS, nbuf=1):
    p = cx.p
    ones, epsc, bc = consts
    hbs = [cx.sb([128, KC, 512], F32, "hblk%d" % i) for i in range(nbuf)]
    bhs = [[Buf("hblk%d_%d" % (i, q)) for q in range(4)] for i in range(nbuf)]
    hv = hT.rearrange("(kc p) t -> p kc t", p=128)
    for bi, (t0, n, seg) in enumerate(blocks):
        hb, bq = hbs[bi % nbuf], bhs[bi % nbuf]
        for q in range(4):
            dmaq(p, "sp", hb[:, 4 * q:4 * q + 4, :n], hv[:, 4 * q:4 * q + 4, t0:t0 + n], writes=[bq[q]])
        bsrc = [bq[kc // 4] for kc in range(KC)]
        _rms_block(p, hb, bsrc, dst, bdst, t0, n, seg, m, A, bm, bA, ones, epsc, bc, *rings, s0=0)


def make_residual_evac(cx, hT, oT, m, bm, ring, toks, gate_idx=2):
    p = cx.p
    hv = hT.rearrange("(kc p) t -> p kc t", p=128)
    ov = oT.rearrange("(kc p) t -> p kc t", p=128)

    def evac(j, t0, n, seg, ps, bps):
        hb, bhb = ring.next()
        dmaq(p, "sp", hb[:, :n], hv[:, j, t0:t0 + n], writes=[bhb])
        stt(p, hb[:, :n], ps[:, :n], m[:, j, gate_idx, seg:seg + 1], hb[:, :n], ALU.mult, ALU.add, [bps, bm, bhb], [bhb])
        toks.append(dmaq(p, "sp", ov[:, j, t0:t0 + n], hb[:, :n], reads=[bhb]))

    return evac


def norm_rings(cx):
    return (Ring(cx, 2, [128, 512], F32, "sq"), Ring(cx, 1, [128, 512], F32, "ssps", psum=True),
            Ring(cx, 2, [128, 512], F32, "rs"), Ring(cx, 2, [128, 512], F32, "tmp"))


def linear_fm(cx, wv, col0, nchunks, x, bx, evac, w_r, ps_r, nk=KC, blocks=BLOCKS):
    p = cx.p
    w = bw = None
    for j in range(nchunks):
        if j % 2 == 0:
            w, bw = w_r.next()
            c = col0 + j * 128
            wd = 256 if j + 1 < nchunks else 128
            dmaq(p, "pool", w[:, :, :wd], wv[:, :, c:c + wd], writes=[bw])
        c0 = (j % 2) * 128
        for (t0, n, seg) in blocks:
            ps, bps = ps_r.next()
            for kc in range(nk):
                mm(p, ps[:, :n], w[:, kc, c0:c0 + 128], x[:, kc, t0:t0 + n], kc == 0, kc == nk - 1, [bw, bx], [bps])
            evac(j, t0, n, seg, ps, bps)


NT = T // 128


def build_mixa():
    from contextlib import ExitStack
    nc = bass.Bass("TRN2", target_bir_lowering=False)
    with ExitStack() as stack:
        cx = Ctx(nc, stack)
        p = cx.p
        hT = cx.din("hT", [D, T])
        modv = cx.din("modv", [128, KC, 3, 2])
        gv = cx.din("gv", [128, KC])
        w_in = cx.din("w_in", [D, 2 * D])
        vg = cx.din("vg", [128, KC])
        wsT = cx.din("wsT", [128, 16, 128])
        bsb = cx.din("bsb", [128, 16, 128])
        w_out = cx.din("w_out", [D, D])
        oT = cx.dout("oT", [D, T])

        consts = make_consts(cx)
        m, A, bm, bA = load_mod(cx, modv, gv)
        xn = cx.sb([128, KC, T], BF16, "xn")
        bxn = Buf("xn")
        nr = norm_rings(cx)
        rms_adaln_stream(cx, hT, xn, bxn, m, A, bm, bA, consts, nr)

        vgt = cx.sb([128, KC], F32, "vgt")
        wst = cx.sb([128, 16, 128], F32, "wst")
        bst = cx.sb([128, 16, 128], F32, "bst")
        bvg, bws, bbs = Buf(), Buf(), Buf()
        dmaq(p, "sp", vgt[:], vg, writes=[bvg])
        dmaq(p, "sp", wst[:], wsT, writes=[bws])
        dmaq(p, "sp", bst[:], bsb, writes=[bbs])

        gelu = Gelu(cx)
        w_r = Ring(cx, 2, [128, KC, 256], BF16, "w")
        ps_r = Ring(cx, 2, [128, 512], F32, "ps", psum=True)
        winv = wview(w_in)

        uT = cx.sb([128, KC, T], BF16, "uT")
        buT = Buf("uT")

        def evac_u(j, t0, n, seg, ps, bps):
            gelu(p, uT[:, j, t0:t0 + n], ps[:, :n], n, [bps], [buT])

        linear_fm(cx, winv, 0, KC, xn, bxn, evac_u, w_r, ps_r)

        v = cx.sb([128, NT, D], BF16, "v")
        bv = Buf("v")
        NCB = 8
        ssq = cx.sb([128, NT, NCB], F32, "ssq")
        bssq = Buf("ssq")
        for cb in range(NCB):
            wv_, bwv = w_r.next()
            dmaq(p, "pool", wv_[:, :, :], winv[:, :, D + cb * 256:D + (cb + 1) * 256], writes=[bwv])
            for ti in range(NT):
                ps, bps = ps_r.next()
                for kc in range(KC):
                    mm(p, ps[:, :256], xn[:, kc, ti * 128:(ti + 1) * 128], wv_[:, kc, :], kc == 0, kc == KC - 1, [bxn, bwv], [bps])
                vt, bvt = nr[3].next()
                gelu(p, vt[:, :256], ps[:, :256], 256, [bps], [bvt])
                junk, bjunk = nr[0].next()
                actf(p, junk[:, :256], vt[:, :256], AF.Square, [bvt], [bjunk, bssq], accum_out=ssq[:, ti, cb:cb + 1])
                p.op("pool", lambda e, vt=vt, ti=ti, cb=cb: e.tensor_copy(out=v[:, ti, cb * 256:(cb + 1) * 256], in_=vt[:, :256]),
                     reads=[bvt], writes=[bv])
        rv = cx.sb([128, NT], F32, "rv")
        brv = Buf("rv")
        p.op("dve", lambda e: e.tensor_reduce(out=rv[:, :], in_=ssq[:, :, :], axis=mybir.AxisListType.X, op=ALU.add),
             reads=[bssq], writes=[brv])
        actf(p, rv[:, :], rv[:, :], AF.Sqrt, [brv, consts[2]], [brv], bias=consts[1][:], scale=1.0 / D)
        p.op("dve", lambda e: e.reciprocal(out=rv[:, :], in_=rv[:, :]), reads=[brv], writes=[brv])

        z, bz = xn, bxn
        wp_r = Ring(cx, 3, [128, 128], BF16, "wp")
        st_r = Ring(cx, 2, [128, 128], F32, "st")
        sps_r = Ring(cx, 2, [128, 128], F32, "sps", psum=True)
        for ti in range(NT):
            for g in range(16):
                wp, bwp = wp_r.next()
                ts(p, "pool", wp[:, :], wst[:, g, :], rv[:, ti:ti + 1], None, ALU.mult, None, [bws, brv], [bwp])
                sp_, bsp = sps_r.next()
                mm(p, sp_[:, :], v[:, ti, g * 128:(g + 1) * 128], wp[:, :], True, True, [bv, bwp], [bsp])
                st, bst_ = st_r.next()
                stt(p, st[:, :], sp_[:, :], vgt[:, g:g + 1], bst[:, g, :], ALU.mult, ALU.add, [bsp, bvg, bbs], [bst_])
                tt(p, "dve", z[:, g, ti * 128:(ti + 1) * 128], st[:, :], uT[:, g, ti * 128:(ti + 1) * 128], ALU.mult,
                   [bst_, buT], [bz])

        woutv = wview(w_out)

        toks = []
        evac_o = make_residual_evac(cx, hT, oT, m, bm, nr[0], toks)
        linear_fm(cx, woutv, 0, KC, z, bz, evac_o, w_r, ps_r)
        p.wait_tokens("sp", toks)
        with nc.Block() as block:
            p.emit(block)
    return nc


HD = 128
NH = 16
ATT_SCALE = HD ** -0.5


def build_nat1():
    from contextlib import ExitStack
    nc = bass.Bass("TRN2", target_bir_lowering=False)
    with ExitStack() as stack:
        cx = Ctx(nc, stack)
        p = cx.p
        hT = cx.din("hT", [D, T])
        modv = cx.din("modv", [128, KC, 3, 2])
        gv = cx.din("gv", [128, KC])
        wqkv = cx.din("wqkv", [D, 3 * D])
        qkg = cx.din("qkg", [128, 2])
        qo = cx.dout("qo", [NH, 128, T], BF16)
        ko = cx.dout("ko", [NH, 128, T], BF16)
        vo = cx.dout("vo", [T, D], BF16)

        consts = make_consts(cx)
        ones, epsc, bc = consts
        m, A, bm, bA = load_mod(cx, modv, gv)
        xn = cx.sb([128, KC, T], BF16, "xn")
        bxn = Buf("xn")
        nr = norm_rings(cx)
        rms_adaln_stream(cx, hT, xn, bxn, m, A, bm, bA, consts, nr, nbuf=2)
        gq = cx.sb([128, 2], F32, "gq")
        bgq = Buf()
        dmaq(p, "sp", gq[:], qkg, writes=[bgq])
        ts(p, "dve", gq[:, 0:1], gq[:, 0:1], ATT_SCALE, None, ALU.mult, None, [bgq], [bgq])

        w_r = Ring(cx, 2, [128, KC, 256], BF16, "w")
        ps_r = Ring(cx, 2, [128, 512], F32, "ps", psum=True)
        wv = wview(wqkv)
        qf_r = Ring(cx, 2, [128, 512], F32, "qf")
        st_r = Ring(cx, 3, [128, 512], BF16, "stg")
        toks = []
        for which, dst in ((0, qo), (1, ko)):
            def evac(j, t0, n, seg, ps, bps, which=which, dst=dst):
                qf, bqf = qf_r.next()
                actf(p, qf[:, :n], ps[:, :n], AF.Identity, [bps], [bqf])
                sq, bsq = nr[0].next()
                actf(p, sq[:, :n], qf[:, :n], AF.Square, [bqf], [bsq])
                ss, bss = nr[1].next()
                mm(p, ss[:, :n], ones[:], sq[:, :n], True, True, [bsq, bc], [bss])
                rs, brs = nr[2].next()
                actf(p, rs[:, :n], ss[:, :n], AF.Sqrt, [bss, bc], [brs], bias=epsc[:], scale=1.0 / HD)
                p.op("dve", lambda e: e.reciprocal(out=rs[:, :n], in_=rs[:, :n]), reads=[brs], writes=[brs])
                sg, bsg = st_r.next()
                stt(p, sg[:, :n], qf[:, :n], gq[:, which:which + 1], rs[:, :n], ALU.mult, ALU.mult, [bqf, bgq, brs], [bsg])
                toks.append(dmaq(p, "sp", dst[j, :, t0:t0 + n], sg[:, :n], reads=[bsg]))

            linear_fm(cx, wv, which * D, KC, xn, bxn, evac, w_r, ps_r)
        vs = cx.sb([128, NT, D], BF16, "vs")
        bvs = Buf("vs")
        for cb in range(8):
            w, bw = w_r.next()
            dmaq(p, "pool", w[:, :, :], wv[:, :, 2 * D + cb * 256:2 * D + (cb + 1) * 256], writes=[bw])
            for ti in range(NT):
                ps, bps = ps_r.next()
                for kc in range(KC):
                    mm(p, ps[:, :256], xn[:, kc, ti * 128:(ti + 1) * 128], w[:, kc, :], kc == 0, kc == KC - 1, [bxn, bw], [bps])
                actf(p, vs[:, ti, cb * 256:(cb + 1) * 256], ps[:, :256], AF.Identity, [bps], [bvs])
        vov = vo.rearrange("(ti p) f -> p ti f", p=128)
        toks.append(dmaq(p, "sp", vov, vs[:], reads=[bvs]))
        p.wait_tokens("sp", toks)
        with nc.Block() as block:
            p.emit(block)
    return nc


NSLAB = 14
NKT = NSLAB + 2
NOFF = 7


def build_nat2():
    from contextlib import ExitStack
    nc = bass.Bass("TRN2", target_bir_lowering=False)
    with ExitStack() as stack:
        cx = Ctx(nc, stack)
        p = cx.p
        hT = cx.din("hT", [D, T])
        modv = cx.din("modv", [128, KC, 3, 2])
        qT = cx.din("qT", [NH, 128, T], BF16)
        kx = cx.din("kx", [NH, 128, NKT * 128], BF16)
        vx = cx.din("vx", [NKT * 128, D], BF16)
        bias = cx.din("bias", [NH, 128, 8 * NOFF, 128])
        w_out = cx.din("w_out", [D, D])
        oT = cx.dout("oT", [D, T])

        m = cx.sb([128, KC, 3, 2], F32, "m")
        bm = Buf("m")
        dmaq(p, "sp", m[:], modv, writes=[bm])
        onesb = cx.sb([128, 128], BF16, "onesb")
        bob = Buf("onesb")
        p.op("pool", lambda e: e.memset(onesb[:], 1.0), writes=[bob])

        q = cx.sb([128, NH, T], BF16, "q")
        bq = Buf("q")
        for hh in range(4):
            dmaq(p, "sp", q[:, 4 * hh:4 * hh + 4, :], qT.rearrange("h d t -> d h t")[:, 4 * hh:4 * hh + 4, :], writes=[bq])
        o = cx.sb([128, NH, T], BF16, "o")
        bo = Buf("o")

        kh_r = Ring(cx, 2, [128, NKT * 128], BF16, "kh")
        vh_r = Ring(cx, 2, [128, NKT, 128], BF16, "vh")
        bi_r = Ring(cx, 2, [128, 8 * NOFF, 128], F32, "bi")
        s_r = Ring(cx, 4, [128, 128], F32, "sps", psum=True)
        o_r = Ring(cx, 1, [128, 128], F32, "ops", psum=True)
        d_r = Ring(cx, 1, [128, 128], F32, "dps", psum=True)
        t_r = Ring(cx, 4, [128, 128], F32, "tmpa")
        p_r = Ring(cx, 5, [128, 128], BF16, "pa")
        r_r = Ring(cx, 2, [128, 128], F32, "rden")
        vxv = vx.rearrange("(kt p) f -> p kt f", p=128)
        for h in range(NH):
            kh, bkh = kh_r.next()
            vh, bvh = vh_r.next()
            bi, bbi = bi_r.next()
            dmaq(p, "sp", kh[:], kx[h], writes=[bkh])
            dmaq(p, "sp", vh[:], vxv[:, :, h * 128:(h + 1) * 128], writes=[bvh])
            for hf in range(2):
                dmaq(p, "sp", bi[:, hf * 28:(hf + 1) * 28, :], bias[h, :, hf * 28:(hf + 1) * 28, :], writes=[bbi])
            for i in range(NT):
                qs = q[:, h, i * 128:(i + 1) * 128]
                tiles = [(i + oo, oo) for oo in range(NOFF)] if i < 8 else []
                tiles += [(NSLAB, None), (NSLAB + 1, None)]
                ops, bops = o_r.next()
                dps, bdps = d_r.next()
                pend = []

                def score(j, oo):
                    sps, bsps = s_r.next()
                    mm(p, sps[:, :], kh[:, j * 128:(j + 1) * 128], qs, True, True, [bkh, bq], [bsps])
                    pa, bpa = p_r.next()
                    if oo is not None:
                        tm, btm = t_r.next()
                        tt(p, "dve", tm[:, :], sps[:, :], bi[:, i * NOFF + oo, :], ALU.add, [bsps, bbi], [btm])
                        actf(p, pa[:, :], tm[:, :], AF.Exp, [btm], [bpa])
                    else:
                        actf(p, pa[:, :], sps[:, :], AF.Exp, [bsps], [bpa])
                    return pa, bpa

                def accum(n_, j, pa, bpa):
                    first, last = n_ == 0, n_ == len(tiles) - 1
                    mm(p, ops[:, :], vh[:, j, :], pa[:, :], first, last, [bvh, bpa], [bops])
                    mm(p, dps[:, :], onesb[:], pa[:, :], first, last, [bob, bpa], [bdps])

                LOOK = 2
                for n_, (j, oo) in enumerate(tiles):
                    pend.append((n_, j) + score(j, oo))
                    if len(pend) > LOOK:
                        accum(*pend.pop(0))
                while pend:
                    accum(*pend.pop(0))
                rd, brd = r_r.next()
                p.op("dve", lambda e, rd=rd, dps=dps: e.reciprocal(out=rd[:, :], in_=dps[:, :]), reads=[bdps], writes=[brd])
                tt(p, "dve", o[:, h, i * 128:(i + 1) * 128], ops[:, :], rd[:, :], ALU.mult, [bops, brd], [bo])

        w_r = Ring(cx, 2, [128, KC, 256], BF16, "w")
        ps_r = Ring(cx, 2, [128, 512], F32, "ps", psum=True)
        hb_r = Ring(cx, 2, [128, 512], F32, "hb")
        toks = []
        evac_o = make_residual_evac(cx, hT, oT, m, bm, hb_r, toks)
        linear_fm(cx, wview(w_out), 0, KC, o, bo, evac_o, w_r, ps_r)
        p.wait_tokens("sp", toks)
        with nc.Block() as block:
            p.emit(block)
    return nc


import ml_dtypes
NPBF = ml_dtypes.bfloat16
GRID_W = 64
ROWS = 64
WIN_H, WIN_W = 8, 16
NEG = -1e30
_cache = {}
_DBG_SKIPW = False
_TRACE = False


def _prog(name, builder):
    if name not in _cache:
        _cache[name] = builder()
    return _cache[name]


def _run(name, builder, in_maps):
    nc = _prog(name, builder)
    if _TRACE:
        res = run_bass_kernel_spmd(nc, in_maps, core_ids=list(range(NCORES)), trace=True)
        print("KTRACE", name, res.exec_time_ns)
    else:
        res = run_bass_kernel_spmd(nc, in_maps, core_ids=list(range(NCORES)))
    return res.results


def fmaj(v):
    return np.ascontiguousarray(np.asarray(v).reshape(-1, 128).T)


def to_cores(x, ctx):
    outs = []
    for c in range(NCORES):
        b, k = divmod(c, 4)
        a = np.zeros((T, D), np.float32)
        a[:TL] = x[b, k * TL:(k + 1) * TL]
        if k < 2:
            a[TL:] = ctx[b, k * TCX:(k + 1) * TCX]
        outs.append(np.ascontiguousarray(a.T))
    return outs


def from_cores(hs):
    out = np.empty((2, 4096, D), np.float32)
    for c in range(NCORES):
        b, k = divmod(c, 4)
        out[b, k * TL:(k + 1) * TL] = hs[c][:, :TL].T
    return out


def modv_for(modT_layer, s, c):
    b = c // 4
    mv = np.empty((128, KC, 3, 2), np.float32)
    for j in range(3):
        blk = modT_layer[:, (3 * s + j) * KC:(3 * s + j + 1) * KC, :]
        mv[:, :, j, 0] = blk[:, :, b]
        mv[:, :, j, 1] = blk[:, :, 2]
    return mv


def run_ada(c, c_ctx, w_ada, b_ada):
    cond = np.zeros((4, D), np.float32)
    cond[0:2] = c
    cond[2] = c_ctx
    condT = np.ascontiguousarray(cond.T.reshape(KC, 128, 4).transpose(1, 0, 2))
    in_maps = []
    for core in range(NCORES):
        layer, half = divmod(core, 2)
        cols = slice(half * 9216, (half + 1) * 9216)
        in_maps.append({"condT": condT, "wa": np.ascontiguousarray(w_ada[layer][:, cols]),
                        "ba": np.ascontiguousarray(b_ada[layer][cols].reshape(72, 128).T)})
    res = _run("ada", build_ada, in_maps)
    modT = []
    for layer in range(4):
        modT.append(np.concatenate([res[2 * layer]["mo"], res[2 * layer + 1]["mo"]], axis=1))
    return modT


def run_ffn(hs, modT_l, s, g, wgu, wdn):
    gv = fmaj(g)
    in_maps = [{"hT": hs[c], "modv": modv_for(modT_l, s, c), "gv": gv, "wgu": wgu, "wdn": wdn} for c in range(NCORES)]
    res = _run("ffn", build_ffn, in_maps)
    return [r["oT"] for r in res]


def run_mixa(hs, modT_l, g, w_in, v_gain, w_s, b_s, w_out):
    gv = fmaj(g)
    wsT = np.ascontiguousarray(w_s.transpose(2, 0, 1))
    bsb = np.ascontiguousarray(np.broadcast_to(b_s[None], (128, 16, 128)))
    vg = fmaj(v_gain)
    in_maps = [{"hT": hs[c], "modv": modv_for(modT_l, 1, c), "gv": gv, "w_in": w_in, "vg": vg, "wsT": wsT, "bsb": bsb,
                "w_out": w_out} for c in range(NCORES)]
    res = _run("mixa", build_mixa, in_maps)
    return [r["oT"] for r in res]


def _nat_bias_index():
    if "natidx" in _cache:
        return _cache["natidx"]
    kpar = np.arange(128) // 64
    kcol = np.arange(128) % 64
    idx = np.full((4, 128, 8, NOFF, 128), 15 * 31, np.int32)
    qpar = kpar[None, :]
    qcol = kcol[None, :]
    cstart = np.clip(qcol - WIN_W // 2, 0, GRID_W - WIN_W)
    colv = (kcol[:, None] >= cstart) & (kcol[:, None] < cstart + WIN_W)
    dc = np.clip(kcol[:, None] - qcol, 1 - WIN_W, WIN_W - 1) + (WIN_W - 1)
    for kq in range(4):
        r0 = 16 * kq
        for i in range(8):
            for o in range(NOFF):
                kr = r0 - 6 + 2 * (i + o) + kpar[:, None]
                qr = r0 + 2 * i + qpar
                rstart = np.clip(qr - WIN_H // 2, 0, ROWS - WIN_H)
                valid = (kr >= 0) & (kr < ROWS) & (kr >= rstart) & (kr < rstart + WIN_H) & colv
                dr = kr - qr + (WIN_H - 1)
                lin = np.clip(dr, 0, 14) * 31 + dc
                idx[kq, :, i, o, :] = np.where(valid, lin, 15 * 31)
    idx = idx.reshape(4, 128, 8 * NOFF, 128)
    _cache["natidx"] = idx
    return idx


def run_nat(hs, modT_l, g, w_qkv, q_gain, k_gain, rpb, w_out):
    gv = fmaj(g)
    qkg = np.ascontiguousarray(np.stack([q_gain, k_gain], axis=1).astype(np.float32))
    in_maps = [{"hT": hs[c], "modv": modv_for(modT_l, 1, c), "gv": gv, "wqkv": w_qkv, "qkg": qkg} for c in range(NCORES)]
    r1 = _run("nat1", build_nat1, in_maps)
    idx = _nat_bias_index()
    rp = np.concatenate([rpb.reshape(NH, -1), np.full((NH, 1), NEG, np.float32)], axis=1)
    in_maps = []
    for c in range(NCORES):
        b, k = divmod(c, 4)
        Kb = np.concatenate([np.asarray(r1[4 * b + kk]["ko"])[:, :, :TL] for kk in range(4)], axis=2)
        Vb = np.concatenate([np.asarray(r1[4 * b + kk]["vo"])[:TL] for kk in range(4)], axis=0)
        Kc = np.concatenate([np.asarray(r1[4 * b + kk]["ko"])[:, :, TL:] for kk in range(2)], axis=2)
        Vc = np.concatenate([np.asarray(r1[4 * b + kk]["vo"])[TL:] for kk in range(2)], axis=0)
        kx = np.zeros((NH, 128, NKT * 128), NPBF)
        vx = np.zeros((NKT * 128, D), NPBF)
        lo = (16 * k - 6) * 64
        hi = lo + NSLAB * 128
        a, bnd = max(lo, 0), min(hi, 4096)
        kx[:, :, a - lo:bnd - lo] = Kb[:, :, a:bnd]
        vx[a - lo:bnd - lo] = Vb[a:bnd]
        kx[:, :, NSLAB * 128:] = Kc
        vx[NSLAB * 128:] = Vc
        bias = np.ascontiguousarray(rp[:, idx[k]])
        in_maps.append({"hT": hs[c], "modv": modv_for(modT_l, 1, c), "qT": np.asarray(r1[c]["qo"]), "kx": kx, "vx": vx,
                        "bias": bias, "w_out": w_out})
    r2 = _run("nat2", build_nat2, in_maps)
    return [r["oT"] for r in r2]


def build_s5a():
    from contextlib import ExitStack
    nc = bass.Bass("TRN2", target_bir_lowering=False)
    with ExitStack() as stack:
        cx = Ctx(nc, stack)
        p = cx.p
        hT = cx.din("hT", [D, T])
        modv = cx.din("modv", [128, KC, 3, 2])
        gv = cx.din("gv", [128, KC])
        w_in = cx.din("w_in", [D, D])
        uo = cx.dout("uo", [D, T])
        consts = make_consts(cx)
        m, A, bm, bA = load_mod(cx, modv, gv)
        xn = cx.sb([128, KC, T], BF16, "xn")
        bxn = Buf("xn")
        nr = norm_rings(cx)
        rms_adaln_stream(cx, hT, xn, bxn, m, A, bm, bA, consts, nr, nbuf=2)
        w_r = Ring(cx, 2, [128, KC, 256], BF16, "w")
        ps_r = Ring(cx, 2, [128, 512], F32, "ps", psum=True)
        uov = uo.rearrange("(kc p) t -> p kc t", p=128)
        toks = []

        def evac(j, t0, n, seg, ps, bps):
            sg, bsg = nr[0].next()
            actf(p, sg[:, :n], ps[:, :n], AF.Identity, [bps], [bsg])
            toks.append(dmaq(p, "sp", uov[:, j, t0:t0 + n], sg[:, :n], reads=[bsg]))

        linear_fm(cx, wview(w_in), 0, KC, xn, bxn, evac, w_r, ps_r)
        p.wait_tokens("sp", toks)
        with nc.Block() as block:
            p.emit(block)
    return nc


NPOS = 256 + 4096
S5_NOSYNC = ("dve", "pool")
S5_SPLIT = 32
SB = 128
NBLK = NPOS // SB
GL = 16


def build_s5b():
    from contextlib import ExitStack
    nc = bass.Bass("TRN2", target_bir_lowering=False)
    with ExitStack() as stack:
        cx = Ctx(nc, stack)
        p = cx.p
        U = cx.din("U", [32, GL, 2, NPOS])
        areT = cx.din("areT", [128, GL])
        aimT = cx.din("aimT", [128, GL])
        ldtT = cx.din("ldtT", [128, GL])
        breT = cx.din("breT", [128, GL, 16])
        bimT = cx.din("bimT", [128, GL, 16])
        creT = cx.din("creT", [128, GL, 16])
        cimT = cx.din("cimT", [128, GL, 16])
        ident = cx.din("ident", [128, 128])
        YF = cx.dout("YF", [2, 128, 2, NPOS])
        YB = cx.dout("YB", [2, 128, 2, NPOS])

        def small(shape, name):
            return cx.sb(shape, F32, name), Buf(name)

        def load(src, shape, name):
            t, b = small(shape, name)
            dmaq(p, "sp", t[:], src, writes=[b])
            return t, b

        are, b_are = load(areT, [128, GL], "are")
        aim, b_aim = load(aimT, [128, GL], "aim")
        ldt, b_ldt = load(ldtT, [128, GL], "ldt")
        bre, b_bre = load(breT, [128, GL, 16], "bre")
        bim, b_bim = load(bimT, [128, GL, 16], "bim")
        cre, b_cre = load(creT, [128, GL, 16], "cre")
        cim, b_cim = load(cimT, [128, GL, 16], "cim")
        idt, b_idt = load(ident, [128, 128], "idt")

        dt_, b_dt = small([128, GL], "dt")
        actf(p, dt_[:], ldt[:], AF.Exp, [b_ldt], [b_dt])
        xr, b_xr = small([128, GL], "xr")
        xi, b_xi = small([128, GL], "xi")
        tt(p, "dve", xr[:], are[:], dt_[:], ALU.mult, [b_are, b_dt], [b_xr])
        tt(p, "dve", xi[:], aim[:], dt_[:], ALU.mult, [b_aim, b_dt], [b_xi])
        mag, b_mag = small([128, GL], "mag")
        actf(p, mag[:], xr[:], AF.Exp, [b_xr], [b_mag])
        sn, b_sn = small([128, GL], "sn")
        cs, b_cs = small([128, GL], "cs")
        t1, b_t1 = small([128, GL], "t1")
        t2, b_t2 = small([128, GL], "t2")
        actf(p, sn[:], xi[:], AF.Sin, [b_xi], [b_sn], scale=1.0 / 16)
        actf(p, t1[:], xi[:], AF.Sin, [b_xi], [b_t1], scale=1.0 / 32)
        tt(p, "dve", t1[:], t1[:], t1[:], ALU.mult, [b_t1], [b_t1])
        ts(p, "dve", cs[:], t1[:], -2.0, 1.0, ALU.mult, ALU.add, [b_t1], [b_cs])
        for _ in range(4):
            tt(p, "dve", t1[:], cs[:], cs[:], ALU.mult, [b_cs], [b_t1])
            tt(p, "dve", t2[:], sn[:], sn[:], ALU.mult, [b_sn], [b_t2])
            tt(p, "dve", sn[:], sn[:], cs[:], ALU.mult, [b_sn, b_cs], [b_sn])
            ts(p, "dve", sn[:], sn[:], 2.0, None, ALU.mult, None, [b_sn], [b_sn])
            tt(p, "dve", cs[:], t1[:], t2[:], ALU.subtract, [b_t1, b_t2], [b_cs])
        abr, b_abr = small([128, GL], "abr")
        abi, b_abi = small([128, GL], "abi")
        tt(p, "dve", abr[:], mag[:], cs[:], ALU.mult, [b_mag, b_cs], [b_abr])
        tt(p, "dve", abi[:], mag[:], sn[:], ALU.mult, [b_mag, b_sn], [b_abi])
        nr_, b_nr = small([128, GL], "nr")
        ts(p, "dve", nr_[:], abr[:], -1.0, None, ALU.add, None, [b_abr], [b_nr])
        den, b_den = small([128, GL], "den")
        tt(p, "dve", den[:], are[:], are[:], ALU.mult, [b_are], [b_den])
        tt(p, "dve", t1[:], aim[:], aim[:], ALU.mult, [b_aim], [b_t1])
        tt(p, "dve", den[:], den[:], t1[:], ALU.add, [b_den, b_t1], [b_den])
        p.op("dve", lambda e: e.reciprocal(out=den[:], in_=den[:]), reads=[b_den], writes=[b_den])
        kr, b_kr = small([128, GL], "kr")
        ki, b_ki = small([128, GL], "ki")
        tt(p, "dve", kr[:], nr_[:], are[:], ALU.mult, [b_nr, b_are], [b_kr])
        tt(p, "dve", t1[:], abi[:], aim[:], ALU.mult, [b_abi, b_aim], [b_t1])
        tt(p, "dve", kr[:], kr[:], t1[:], ALU.add, [b_kr, b_t1], [b_kr])
        tt(p, "dve", kr[:], kr[:], den[:], ALU.mult, [b_kr, b_den], [b_kr])
        tt(p, "dve", ki[:], abi[:], are[:], ALU.mult, [b_abi, b_are], [b_ki])
        tt(p, "dve", t1[:], nr_[:], aim[:], ALU.mult, [b_nr, b_aim], [b_t1])
        tt(p, "dve", ki[:], ki[:], t1[:], ALU.subtract, [b_ki, b_t1], [b_ki])
        tt(p, "dve", ki[:], ki[:], den[:], ALU.mult, [b_ki, b_den], [b_ki])
        nki, b_nki = small([128, GL], "nki")
        ts(p, "dve", nki[:], ki[:], -1.0, None, ALU.mult, None, [b_ki], [b_nki])
        Bw, b_Bw = small([128, GL, 2, 32], "Bw")
        p.op("pool", lambda e: e.memset(Bw[:], 0.0), writes=[b_Bw])
        tb, b_tb = small([128, 16], "tb")
        for g in range(GL):
            for d in range(2):
                ps_ = slice(64 * d, 64 * d + 64)
                cols = slice(16 * d, 16 * d + 16)
                ts(p, "dve", tb[ps_, :], bim[ps_, g, :], nki[ps_, g:g + 1], None, ALU.mult, None, [b_bim, b_nki], [b_tb])
                stt(p, Bw[ps_, g, 0, cols], bre[ps_, g, :], kr[ps_, g:g + 1], tb[ps_, :], ALU.mult, ALU.add, [b_bre, b_kr, b_tb], [b_Bw])
                ts(p, "dve", tb[ps_, :], bre[ps_, g, :], ki[ps_, g:g + 1], None, ALU.mult, None, [b_bre, b_ki], [b_tb])
                stt(p, Bw[ps_, g, 1, cols], bim[ps_, g, :], kr[ps_, g:g + 1], tb[ps_, :], ALU.mult, ALU.add, [b_bim, b_kr, b_tb], [b_Bw])
        Blk = cx.sb([32, GL, 2, 128], BF16, "Blk")
        b_Blk = Buf("Blk")
        tp_r = Ring(cx, 2, [32, 128], F32, "tps", psum=True)
        for g in range(GL):
            for ri in range(2):
                tp, btp = tp_r.next()
                p.op("pe", lambda e, tp=tp, g=g, ri=ri: e.transpose(tp[:, :], Bw[:, g, ri, :], idt[:]), reads=[b_Bw, b_idt], writes=[btp])
                actf(p, Blk[:, g, ri, :], tp[:, :], AF.Identity, [btp], [b_Blk])
        Cp = cx.sb([128, GL, 2, 128], BF16, "Cp")
        b_Cp = Buf("Cp")
        p.op("pool", lambda e: e.memset(Cp[:], 0.0), writes=[b_Cp])
        for g in range(GL):
            c0 = (g % 8) * 16
            p.op("pool", lambda e, g=g, c0=c0: e.tensor_copy(out=Cp[:, g, 0, c0:c0 + 16], in_=cre[:, g, :]), reads=[b_cre], writes=[b_Cp])
            ts(p, "pool", Cp[:, g, 1, c0:c0 + 16], cim[:, g, :], -1.0, None, ALU.mult, None, [b_cim], [b_Cp])
        Ar, b_Ar = small([128, 32, 2], "Ar")
        AiS, b_AiS = small([128, 32, 2], "AiS")
        nabi, b_nabi = small([128, GL], "nabi")
        ts(p, "dve", nabi[:], abi[:], -1.0, None, ALU.mult, None, [b_abi], [b_nabi])
        Ar4 = Ar[:].rearrange("p (g b) r -> p g b r", b=2)
        Ai4 = AiS[:].rearrange("p (g b) r -> p g b r", b=2)
        for b in range(2):
            for ri in range(2):
                p.op("pool", lambda e, b=b, ri=ri: e.tensor_copy(out=Ar4[:, :, b, ri], in_=abr[:]), reads=[b_abr], writes=[b_Ar])
            p.op("pool", lambda e, b=b: e.tensor_copy(out=Ai4[:, :, b, 0], in_=nabi[:]), reads=[b_nabi], writes=[b_AiS])
            p.op("pool", lambda e, b=b: e.tensor_copy(out=Ai4[:, :, b, 1], in_=abi[:]), reads=[b_abi], writes=[b_AiS])

        ub_r = Ring(cx, 2, [32, GL, 2, SB], BF16, "ub")
        bu_r = Ring(cx, 2, [128, SB, 32, 2], F32, "bu")
        H_r = Ring(cx, 2, [128, SB, 32, 2], F32, "H")
        Hb_r = Ring(cx, 2, [128, SB, 32, 2], BF16, "Hb")
        dps_r = Ring(cx, 2, [128, 4, SB], F32, "dps", psum=True)
        yps_r = Ring(cx, 2, [128, SB], F32, "yps", psum=True)
        ys_r = Ring(cx, 3, [128, SB], F32, "ys")
        zero, b_zero = small([128, 32, 2], "zero")
        p.op("pool", lambda e: e.memset(zero[:], 0.0), writes=[b_zero])
        lanes = []
        for eng, sl in (("dve", slice(0, S5_SPLIT)), ("pool", slice(S5_SPLIT, 32))):
            n_ = sl.stop - sl.start
            if n_ <= 0:
                continue
            m1, b_m1 = small([128, n_, 2], "m1" + eng)
            m2, b_m2 = small([128, n_, 2], "m2" + eng)
            lanes.append(dict(eng=eng, sl=sl, m1=m1, b_m1=b_m1, m2=m2, b_m2=b_m2, prev=zero[:, sl, :], b_prev=b_zero))
        toks = []
        for blk in range(NBLK):
            k0 = blk * SB
            ub, b_ub = ub_r.next()
            dmaq(p, "pool", ub[:], U[:, :, :, k0:k0 + SB], writes=[b_ub])
            bu, b_bu = bu_r.next()
            bu_v = bu[:].rearrange("p k c r -> p k (c r)")
            for q4 in range(GL):
                dps, b_dps = dps_r.next()
                for b in range(2):
                    for ri in range(2):
                        mm(p, dps[:, b * 2 + ri, :], Blk[:, q4, ri, :], ub[:, q4, b, :], True, True, [b_Blk, b_ub], [b_dps])
                actf(p, bu_v[:, :, q4 * 4:q4 * 4 + 4], dps[:].rearrange("p c k -> p k c"), AF.Identity, [b_dps], [b_bu])
            H, b_H0 = H_r.next()
            b_Hl = [Buf("Hl%d" % li) for li in range(len(lanes))]
            p.nosync = set(S5_NOSYNC) | set(GLOBAL_NOSYNC)
            for k in range(SB):
                for li, L in enumerate(lanes):
                    eng, sl, m1, m2 = L["eng"], L["sl"], L["m1"], L["m2"]
                    pv, bpv = L["prev"], L["b_prev"]
                    wr = [b_Hl[li]] + ([b_H0] if k == 0 else [])
                    tt(p, eng, m1[:], Ar[:, sl, :], pv, ALU.mult, [b_Ar, bpv], [L["b_m1"]])
                    tt(p, eng, m2[:], AiS[:, sl, :], pv[:, :, ::-1], ALU.mult, [b_AiS, bpv], [L["b_m2"]])
                    tt(p, eng, m1[:], m1[:], m2[:], ALU.add, [L["b_m1"], L["b_m2"]], [L["b_m1"]])
                    tt(p, eng, H[:, k, sl, :], m1[:], bu[:, k, sl, :], ALU.add, [L["b_m1"], b_bu], wr)
                    L["prev"], L["b_prev"] = H[:, k, sl, :], b_Hl[li]
            p.nosync = set(GLOBAL_NOSYNC)
            Hb, b_Hb = Hb_r.next()
            actf(p, Hb[:].rearrange("p k c r -> p (k c r)"), H[:].rearrange("p k c r -> p (k c r)"), AF.Identity, b_Hl + [b_H0], [b_Hb, b_H0])
            for cc in range(2):
                for b in range(2):
                    for d in range(2):
                        ps_ = slice(64 * d, 64 * d + 64)
                        yps, b_yps = yps_r.next()
                        n_ = 0
                        for g8 in range(8):
                            g = cc * 8 + g8
                            for ri in range(2):
                                mm(p, yps[:, :], Cp[ps_, g, ri, :], Hb[ps_, :, g * 2 + b, ri], n_ == 0, n_ == 15, [b_Cp, b_Hb], [b_yps])
                                n_ += 1
                        ys, b_ys = ys_r.next()
                        actf(p, ys[:, :], yps[:, :], AF.Identity, [b_yps], [b_ys])
                        dst = (YF if d == 0 else YB)
                        toks.append(dmaq(p, "sp", dst[cc, :, b, k0:k0 + SB], ys[:, :], reads=[b_ys]))
        p.wait_tokens("sp", toks)
        with nc.Block() as block:
            p.emit(block)
    return nc


def build_s5c():
    from contextlib import ExitStack
    nc = bass.Bass("TRN2", target_bir_lowering=False)
    with ExitStack() as stack:
        cx = Ctx(nc, stack)
        p = cx.p
        hT = cx.din("hT", [D, T])
        modv = cx.din("modv", [128, KC, 3, 2])
        uT = cx.din("uT", [D, T])
        yfT = cx.din("yfT", [D, T])
        ybT = cx.din("ybT", [D, T])
        dsk = cx.din("dsk", [128, KC])
        w_glu = cx.din("w_glu", [D, 2 * D])
        oT = cx.dout("oT", [D, T])
        m = cx.sb([128, KC, 3, 2], F32, "m")
        bm = Buf("m")
        dmaq(p, "sp", m[:], modv, writes=[bm])
        dk = cx.sb([128, KC], F32, "dk")
        bdk = Buf("dk")
        dmaq(p, "sp", dk[:], dsk, writes=[bdk])
        gl = cx.sb([128, KC, T], BF16, "gl")
        bgl = Buf("gl")
        a_r = Ring(cx, 2, [128, T], F32, "ya")
        b_r = Ring(cx, 2, [128, T], F32, "yb")
        c_r = Ring(cx, 2, [128, T], F32, "yc")
        gelu = Gelu(cx, width=T)
        uv = uT.rearrange("(kc p) t -> p kc t", p=128)
        fv = yfT.rearrange("(kc p) t -> p kc t", p=128)
        bv = ybT.rearrange("(kc p) t -> p kc t", p=128)
        for kc in range(KC):
            ya, bya = a_r.next()
            yb, byb = b_r.next()
            yc, byc = c_r.next()
            dmaq(p, "sp", ya[:], uv[:, kc, :], writes=[bya])
            dmaq(p, "sp", yb[:], fv[:, kc, :], writes=[byb])
            dmaq(p, "sp", yc[:], bv[:, kc, :], writes=[byc])
            tt(p, "dve", yb[:], yb[:], yc[:], ALU.add, [byb, byc], [byb])
            stt(p, ya[:], ya[:], dk[:, kc:kc + 1], yb[:], ALU.mult, ALU.add, [bya, bdk, byb], [bya])
            gelu(p, gl[:, kc, :], ya[:], T, [bya], [bgl])
        w_r = Ring(cx, 2, [128, KC, 128], BF16, "wa")
        w2_r = Ring(cx, 2, [128, KC, 128], BF16, "wg")
        pa_r = Ring(cx, 2, [128, 512], F32, "pa", psum=True)
        pg_r = Ring(cx, 2, [128, 512], F32, "pg", psum=True)
        sg_r = Ring(cx, 2, [128, 512], F32, "sg")
        hb_r = Ring(cx, 2, [128, 512], F32, "hb")
        wv = wview(w_glu)
        hv = hT.rearrange("(kc p) t -> p kc t", p=128)
        ov = oT.rearrange("(kc p) t -> p kc t", p=128)
        toks = []
        for j in range(KC):
            wa, bwa = w_r.next()
            wg, bwg = w2_r.next()
            dmaq(p, "pool", wa[:], wv[:, :, j * 128:(j + 1) * 128], writes=[bwa])
            dmaq(p, "pool", wg[:], wv[:, :, D + j * 128:D + (j + 1) * 128], writes=[bwg])
            for (t0, n, seg) in BLOCKS:
                pa, bpa = pa_r.next()
                pg, bpg = pg_r.next()
                for kc in range(KC):
                    mm(p, pa[:, :n], wa[:, kc, :], gl[:, kc, t0:t0 + n], kc == 0, kc == KC - 1, [bwa, bgl], [bpa])
                for kc in range(KC):
                    mm(p, pg[:, :n], wg[:, kc, :], gl[:, kc, t0:t0 + n], kc == 0, kc == KC - 1, [bwg, bgl], [bpg])
                sg, bsg = sg_r.next()
                actf(p, sg[:, :n], pg[:, :n], AF.Sigmoid, [bpg], [bsg])
                tt(p, "dve", sg[:, :n], pa[:, :n], sg[:, :n], ALU.mult, [bpa, bsg], [bsg])
                hb, bhb = hb_r.next()
                dmaq(p, "sp", hb[:, :n], hv[:, j, t0:t0 + n], writes=[bhb])
                stt(p, hb[:, :n], sg[:, :n], m[:, j, 2, seg:seg + 1], hb[:, :n], ALU.mult, ALU.add, [bsg, bm, bhb], [bhb])
                toks.append(dmaq(p, "sp", ov[:, j, t0:t0 + n], hb[:, :n], reads=[bhb]))
        p.wait_tokens("sp", toks)
        with nc.Block() as block:
            p.emit(block)
    return nc


def run_s5(hs, modT_l, g, w_in, a_re, a_im, log_dt, b_re, b_im, c_re, c_im, d_skip, w_glu):
    gv = fmaj(g)
    in_maps = [{"hT": hs[c], "modv": modv_for(modT_l, 1, c), "gv": gv, "w_in": w_in} for c in range(NCORES)]
    r1 = _run("s5a", build_s5a, in_maps)
    us = [r["uo"] for r in r1]
    seq = []
    for b in range(2):
        parts = [us[4 * b + k][:, TL:] for k in range(2)] + [us[4 * b + k][:, :TL] for k in range(4)]
        seq.append(np.concatenate(parts, axis=1))
    seq = np.stack(seq, axis=1)
    order_b = np.concatenate([np.arange(255, -1, -1), 256 + np.arange(4095, -1, -1)])
    ident = np.eye(128, dtype=np.float32)
    in_maps = []
    for c in range(NCORES):
        gs = slice(GL * c, GL * (c + 1))
        sc = seq[256 * c:256 * (c + 1)].reshape(GL, 16, 2, NPOS)
        Uc = np.empty((32, GL, 2, NPOS), np.float32)
        Uc[:16] = sc.transpose(1, 0, 2, 3)
        Uc[16:] = sc[:, :, :, order_b].transpose(1, 0, 2, 3)

        def dp(a):
            a = a[:, gs]
            if a.ndim == 3:
                return np.ascontiguousarray(a.transpose(0, 2, 1).reshape(128, GL))
            return np.ascontiguousarray(a.transpose(0, 2, 1, 3).reshape(128, GL, a.shape[3]))

        ldt = np.ascontiguousarray(np.broadcast_to(log_dt[:, gs][:, None, :], (2, 64, GL)).reshape(128, GL))
        in_maps.append({"U": Uc, "areT": dp(a_re), "aimT": dp(a_im), "ldtT": ldt,
                        "breT": dp(b_re), "bimT": dp(b_im),
                        "creT": dp(c_re.transpose(0, 1, 3, 2)), "cimT": dp(c_im.transpose(0, 1, 3, 2)), "ident": ident})
    r2 = _run("s5b", build_s5b, in_maps)
    YF = np.concatenate([r["YF"].reshape(256, 2, NPOS) for r in r2], axis=0)
    YBo = np.concatenate([r["YB"].reshape(256, 2, NPOS) for r in r2], axis=0)
    YB = np.empty_like(YBo)
    YB[:, :, order_b] = YBo

    def percore(Y, c):
        b, k = divmod(c, 4)
        a = np.zeros((D, T), np.float32)
        a[:, :TL] = Y[:, b, 256 + k * TL:256 + (k + 1) * TL]
        if k < 2:
            a[:, TL:] = Y[:, b, k * TCX:(k + 1) * TCX]
        return a

    dsk = fmaj(d_skip)
    in_maps = [{"hT": hs[c], "modv": modv_for(modT_l, 1, c), "uT": us[c], "yfT": percore(YF, c), "ybT": percore(YB, c),
                "dsk": dsk, "w_glu": w_glu} for c in range(NCORES)]
    r3 = _run("s5c", build_s5c, in_maps)
    return [r["oT"] for r in r3]


def kernel(x, c, ctx, c_ctx, w_ada, b_ada, norm_g, ffn_w_gu, ffn_w_down,
           a_w_in, a_v_gain, a_w_s, a_b_s, a_w_out,
           b_w_qkv, b_q_gain, b_k_gain, b_rpb, b_w_out,
           c_w_in, c_a_re, c_a_im, c_log_dt, c_b_re, c_b_im, c_c_re, c_c_im, c_d, c_w_glu):
    f = lambda a: np.asarray(a, dtype=np.float32)
    x, c, ctx, c_ctx = f(x), f(c), f(ctx), f(c_ctx)
    modT = run_ada(c, c_ctx, f(w_ada), f(b_ada))
    hs = to_cores(x, ctx)
    depth = 4
    for i in range(depth):
        kind, j = i % 3, i // 3
        hs = run_ffn(hs, modT[i], 0, f(norm_g[i, 0]), f(ffn_w_gu[i, 0]), f(ffn_w_down[i, 0]))
        if kind == 0:
            hs = run_mixa(hs, modT[i], f(norm_g[i, 1]), f(a_w_in[j]), f(a_v_gain[j]), f(a_w_s[j]), f(a_b_s[j]), f(a_w_out[j]))
        elif kind == 1:
            hs = run_nat(hs, modT[i], f(norm_g[i, 1]), f(b_w_qkv[j]), f(b_q_gain[j]), f(b_k_gain[j]), f(b_rpb[j]), f(b_w_out[j]))
        else:
            hs = run_s5(hs, modT[i], f(norm_g[i, 1]), f(c_w_in[j]), f(c_a_re[j]), f(c_a_im[j]), f(c_log_dt[j]),
                        f(c_b_re[j]), f(c_b_im[j]), f(c_c_re[j]), f(c_c_im[j]), f(c_d[j]), f(c_w_glu[j]))
        hs = run_ffn(hs, modT[i], 2, f(norm_g[i, 2]), f(ffn_w_gu[i, 1]), f(ffn_w_down[i, 1]))
    return from_cores(hs)
```

```python
import numpy as np
import concourse.bass as bass
import concourse.mybir as mybir
from concourse.bass_utils import run_bass_kernel_spmd
from concourse.alu_op_type import AluOpType as ALU

F32 = mybir.dt.float32
BF16 = mybir.dt.bfloat16
AF = mybir.ActivationFunctionType

D = 2048
KC = 16
DFF = 5632
FC = 44
NCORES = 8
TL = 1024
TCX = 128
T = TL + TCX
EPS = 1e-6
GLOBAL_NOSYNC = ()
PE_LAZY_SIG = True


class Buf:
    __slots__ = ("lw", "rd", "name")

    def __init__(self, name=""):
        self.lw = None
        self.rd = {}
        self.name = name


class Prog:
    CENG = ("pe", "act", "dve", "pool")

    def __init__(self, nc, stack, n_dsem=20):
        self.nc = nc
        self.eng = {"pe": nc.tensor, "act": nc.scalar, "dve": nc.vector,
                    "pool": nc.gpsimd, "sp": nc.sync}
        self.q = {e: [] for e in self.eng}
        self.cnt = {e: 0 for e in self.CENG}
        self.seen = {e: {} for e in self.eng}
        self.nosync = set(GLOBAL_NOSYNC)
        self.csem = {e: stack.enter_context(nc.semaphore("c_" + e)) for e in self.CENG}
        self.dsem = {}
        self.dcum = {}
        self.drr = {}
        for qn in ("sp", "pool", "act"):
            n = n_dsem if qn != "act" else 6
            self.dsem[qn] = [stack.enter_context(nc.semaphore("d_%s%d" % (qn, i))) for i in range(n)]
            self.dcum[qn] = [0] * n
            self.drr[qn] = 0

    def _need(self, eng, tok, waits):
        if tok is None:
            return
        key, val = tok
        if key == ("c", "pe") and eng == "pe":
            return
        if key[0] == "c" and key[1] == eng and eng in self.nosync:
            return
        if self.seen[eng].get(key, 0) >= val:
            return
        if waits.get(key, 0) < val:
            waits[key] = val

    def _deps(self, eng, reads, writes):
        waits = {}
        for b in reads:
            self._need(eng, b.lw, waits)
        for b in writes:
            self._need(eng, b.lw, waits)
            for k, v in b.rd.items():
                self._need(eng, (k, v), waits)
        for k, v in waits.items():
            self.seen[eng][k] = v
        return waits

    def _commit(self, tok, reads, writes):
        key, val = tok
        for b in reads:
            if b.rd.get(key, 0) < val:
                b.rd[key] = val
        for b in writes:
            b.lw = tok
            b.rd = {}

    def op(self, eng, fn, reads=(), writes=(), sig=True):
        waits = self._deps(eng, reads, writes)
        if sig:
            self.cnt[eng] += 1
            tok = (("c", eng), self.cnt[eng])
            self.q[eng].append((list(waits.items()), fn, ("c", eng, 1)))
        else:
            tok = (("c", eng), self.cnt[eng] + 1)
            self.q[eng].append((list(waits.items()), fn, ("n",)))
        self._commit(tok, reads, writes)
        return tok

    def dma(self, qn, fns, reads=(), writes=()):
        if not isinstance(fns, (list, tuple)):
            fns = [fns]
        waits = self._deps(qn, reads, writes)
        i = self.drr[qn]
        self.drr[qn] = (i + 1) % len(self.dsem[qn])
        key = ("d", qn, i)
        prev = self.dcum[qn][i]
        if prev > 0 and self.seen[qn].get(key, 0) < prev:
            waits[key] = prev
            self.seen[qn][key] = prev
        for j, fn in enumerate(fns):
            self.dcum[qn][i] += 16
            self.q[qn].append((list(waits.items()) if j == 0 else [], fn, ("d", qn, i)))
        tok = (key, self.dcum[qn][i])
        self._commit(tok, reads, writes)
        return tok

    def wait_tokens(self, eng, toks):
        waits = {}
        for t in toks:
            self._need(eng, t, waits)
        for k, v in waits.items():
            self.seen[eng][k] = v
        self.q[eng].append((list(waits.items()), None, None))

    def _sem(self, key):
        if key[0] == "c":
            return self.csem[key[1]]
        return self.dsem[key[1]][key[2]]

    def emit(self, block):
        decos = {"pe": block.tensor, "act": block.scalar, "dve": block.vector,
                 "pool": block.gpsimd, "sp": block.sync}
        for e in self.eng:
            items = self.q[e]
            if not items:
                continue

            def body(engine, items=items):
                for waits, fn, inc in items:
                    for key, val in waits:
                        engine.wait_ge(self._sem(key), val)
                    if fn is None:
                        continue
                    ins = fn(engine)
                    if inc[0] == "n":
                        continue
                    if inc[0] == "c":
                        ins.then_inc(self.csem[inc[1]], 1)
                    else:
                        ins.then_inc(self.dsem[inc[1]][inc[2]], 16)

            decos[e](body)


class Ctx:
    def __init__(self, nc, stack):
        self.nc = nc
        self.stack = stack
        self.p = Prog(nc, stack)
        self._n = 0

    def sb(self, shape, dt, name=None):
        self._n += 1
        return self.stack.enter_context(self.nc.sbuf_tensor(name or ("sb%d" % self._n), list(shape), dt))

    def ps(self, shape, dt=F32, name=None):
        self._n += 1
        return self.stack.enter_context(self.nc.psum_tensor(name or ("ps%d" % self._n), list(shape), dt))

    def din(self, name, shape, dt=F32):
        return self.nc.dram_tensor(name, list(shape), dt, kind="ExternalInput").ap()

    def dout(self, name, shape, dt=F32):
        return self.nc.dram_tensor(name, list(shape), dt, kind="ExternalOutput").ap()


class Ring:
    def __init__(self, cx, n, shape, dt, name, psum=False):
        self.t = [(cx.ps(shape, dt, "%s%d" % (name, i)) if psum else cx.sb(shape, dt, "%s%d" % (name, i)))
                  for i in range(n)]
        self.b = [Buf("%s%d" % (name, i)) for i in range(n)]
        self.i = 0

    def next(self):
        i = self.i
        self.i = (i + 1) % len(self.t)
        return self.t[i], self.b[i]


def make_consts(cx):
    p = cx.p
    ones = cx.sb([128, 128], F32, "ones")
    epsc = cx.sb([128, 1], F32, "epsc")
    b = Buf("consts")
    p.op("pool", lambda e: e.memset(ones[:], 1.0), writes=[b])
    p.op("pool", lambda e: e.memset(epsc[:], EPS), writes=[b])
    return ones, epsc, b


def load_mod(cx, modv, gv, pre=None):
    p = cx.p
    if pre is None:
        m = cx.sb([128, KC, 3, 2], F32)
        g = cx.sb([128, KC], F32)
        A = cx.sb([128, KC, 2], F32)
        bm, bg, bA = Buf("m"), Buf("g"), Buf("A")
        LOAD_MOD_LAST[0] = (m, g, A, bm, bg, bA)
    else:
        m, g, A, bm, bg, bA = pre
    p.dma("sp", lambda e: e.dma_start(out=m[:], in_=modv), writes=[bm])
    p.dma("sp", lambda e: e.dma_start(out=g[:], in_=gv), writes=[bg])
    for s in range(2):
        p.op("dve", lambda e, s=s: e.scalar_tensor_tensor(out=A[:, :, s], in0=m[:, :, 1, s], scalar=1.0,
                                                           in1=g[:, :], op0=ALU.add, op1=ALU.mult),
             reads=[bm, bg], writes=[bA])
    return m, A, bm, bA


def rms_adaln(cx, src, bsrc, dst, bdst, blocks, m, A, bm, bA, consts, rings):
    p = cx.p
    ones, epsc, bc = consts
    sq_r, ps_r, rs_r, tmp_r = rings
    for (t0, n, seg) in blocks:
        _rms_block(p, src, bsrc, dst, bdst, t0, n, seg, m, A, bm, bA, ones, epsc, bc, sq_r, ps_r, rs_r, tmp_r)


def _rms_block(p, src, bsrc, dst, bdst, t0, n, seg, m, A, bm, bA, ones, epsc, bc, sq_r, ps_r, rs_r, tmp_r, s0=None):
    s0 = t0 if s0 is None else s0
    ss, bss = ps_r.next()
    for kc in range(KC):
        sq, bsq = sq_r.next()
        bs_ = bsrc[kc] if isinstance(bsrc, list) else bsrc
        p.op("act", lambda e, sq=sq, kc=kc: e.activation(out=sq[:, :n], in_=src[:, kc, s0:s0 + n], func=AF.Square),
             reads=[bs_], writes=[bsq])
        p.op("pe", lambda e, sq=sq, kc=kc: e.matmul(ss[:, :n], lhsT=ones[:], rhs=sq[:, :n],
                                                    start=(kc == 0), stop=(kc == KC - 1)),
             reads=[bsq, bc], writes=[bss])
    rs, brs = rs_r.next()
    p.op("act", lambda e: e.activation(out=rs[:, :n], in_=ss[:, :n], func=AF.Sqrt, bias=epsc[:], scale=1.0 / D),
         reads=[bss, bc], writes=[brs])
    p.op("dve", lambda e: e.reciprocal(out=rs[:, :n], in_=rs[:, :n]), reads=[brs], writes=[brs])
    for kc in range(KC):
        tmp, btmp = tmp_r.next()
        p.op("dve", lambda e, tmp=tmp, kc=kc: e.scalar_tensor_tensor(
            out=tmp[:, :n], in0=src[:, kc, s0:s0 + n], scalar=A[:, kc, seg:seg + 1], in1=rs[:, :n],
            op0=ALU.mult, op1=ALU.mult), reads=[bsrc[kc] if isinstance(bsrc, list) else bsrc, brs, bA], writes=[btmp])
        p.op("act", lambda e, tmp=tmp, kc=kc: e.activation(out=dst[:, kc, t0:t0 + n], in_=tmp[:, :n],
                                                           func=AF.Identity, bias=m[:, kc, 0, seg:seg + 1], scale=1.0),
             reads=[btmp, bm], writes=[bdst])


BLOCKS = [(0, 512, 0), (512, 512, 0), (1024, 128, 1)]
FFN_LAST_RINGS = [None]
LOAD_MOD_LAST = [None]


def wview(w):
    return w.rearrange("(kc p) f -> p kc f", p=128)


def build_ffn(dbg=False):
    from contextlib import ExitStack
    nc = bass.Bass("TRN2", target_bir_lowering=False)
    with ExitStack() as stack:
        cx = Ctx(nc, stack)
        p = cx.p
        hT = cx.din("hT", [D, T])
        modv = cx.din("modv", [128, KC, 3, 2])
        gv = cx.din("gv", [128, KC])
        wgu = cx.din("wgu", [D, 2 * DFF])
        wdn = cx.din("wdn", [DFF, D])
        oT = cx.dout("oT", [D, T])

        consts = make_consts(cx)
        h = cx.sb([128, KC, T], F32, "h")
        bh = [Buf("h%d" % k) for k in range(KC)]
        hv = hT.rearrange("(kc p) t -> p kc t", p=128)
        m, A, bm, bA = load_mod(cx, modv, gv)
        for k in range(KC):
            dmaq(p, "sp", h[:, k, :], hv[:, k, :], writes=[bh[k]])
        G = cx.sb([128, KC, 2], F32, "G")
        bG = Buf("G")
        p.op("dve", lambda e: e.tensor_scalar(out=G[:], in0=m[:, :, 2, :], scalar1=0.5, scalar2=None, op0=ALU.mult),
             reads=[bm], writes=[bG])

        xn = cx.sb([128, KC, T], BF16, "xn")
        bxn = Buf("xn")
        rings = (Ring(cx, 2, [128, 512], F32, "sq"), Ring(cx, 1, [128, 512], F32, "ssps", psum=True),
                 Ring(cx, 2, [128, 512], F32, "rs"), Ring(cx, 2, [128, 512], F32, "tmp"))
        rms_adaln(cx, h, bh, xn, bxn, BLOCKS, m, A, bm, bA, consts, rings)

        if dbg:
            xo = cx.dout("xo", [128, KC, T], BF16)
            p.wait_tokens("sp", [p.dma("sp", lambda e: e.dma_start(out=xo, in_=xn[:]), reads=[bxn])])
        ov = oT.rearrange("(kc p) t -> p kc t", p=128)
        toks = []

        def store_dc(dc):
            toks.append(dmaq(p, "sp", ov[:, dc, :], h[:, dc, :], reads=[bh[dc]]))

        ffn_core(cx, xn, bxn, h, bh, G, bG, wgu, wdn, on_final=store_dc)
        p.wait_tokens("sp", toks)
        with nc.Block() as block:
            p.emit(block)
    return nc


def ffn_core(cx, xn, bxn, h, bh, G, bG, wgu, wdn, GRP=4, on_final=None, rings=None):
    p = cx.p
    wguv = wview(wgu)
    wdnv = wdn.rearrange("(fc p) d -> p fc d", p=128)
    if rings is None:
        rings = dict(wg=Ring(cx, 2, [128, KC, 256], BF16, "wg"), wu=Ring(cx, 2, [128, KC, 256], BF16, "wu"),
                     wd=Ring(cx, 2, [128, GRP, D], BF16, "wd"), act=Ring(cx, 2, [128, GRP, T], BF16, "act"),
                     gps=Ring(cx, 2, [128, 512], F32, "gps", psum=True), ups=Ring(cx, 2, [128, 512], F32, "ups", psum=True),
                     yps=Ring(cx, 2, [128, 512], F32, "yps", psum=True), sg=Ring(cx, 2, [128, 512], F32, "sg"))
    wg_r, wu_r, wd_r, act_r = rings["wg"], rings["wu"], rings["wd"], rings["act"]
    gps_r, ups_r, yps_r, sg_r = rings["gps"], rings["ups"], rings["yps"], rings["sg"]
    FFN_LAST_RINGS[0] = rings
    wg = wu = None
    for grp in range(FC // GRP):
        act, bact = act_r.next()
        wd, bwd = wd_r.next()
        for fl in range(GRP):
            p.dma("pool", lambda e, wd=wd, fl=fl, grp=grp: e.dma_start(out=wd[:, fl, :], in_=wdnv[:, grp * GRP + fl, :]),
                  writes=[bwd])
        for fl in range(GRP):
            fc = grp * GRP + fl
            if fc % 2 == 0 and (not _DBG_SKIPW or fc < 4):
                wg, bwg = wg_r.next()
                wu, bwu = wu_r.next()
                p.dma("pool", lambda e, wg=wg, fc=fc: e.dma_start(out=wg[:], in_=wguv[:, :, fc * 128:fc * 128 + 256]),
                      writes=[bwg])
                p.dma("pool", lambda e, wu=wu, fc=fc: e.dma_start(out=wu[:], in_=wguv[:, :, DFF + fc * 128:DFF + fc * 128 + 256]),
                      writes=[bwu])
            c0 = (fc % 2) * 128
            for (t0, n, seg) in BLOCKS:
                gps, bgps = gps_r.next()
                ups, bups = ups_r.next()
                for kc in range(KC):
                    p.op("pe", lambda e, gps=gps, wg=wg, kc=kc, c0=c0, t0=t0, n=n: e.matmul(
                        gps[:, :n], lhsT=wg[:, kc, c0:c0 + 128], rhs=xn[:, kc, t0:t0 + n],
                        start=(kc == 0), stop=(kc == KC - 1)), reads=[bwg, bxn], writes=[bgps], sig=(kc == KC - 1) or not PE_LAZY_SIG)
                for kc in range(KC):
                    p.op("pe", lambda e, ups=ups, wu=wu, kc=kc, c0=c0, t0=t0, n=n: e.matmul(
                        ups[:, :n], lhsT=wu[:, kc, c0:c0 + 128], rhs=xn[:, kc, t0:t0 + n],
                        start=(kc == 0), stop=(kc == KC - 1)), reads=[bwu, bxn], writes=[bups], sig=(kc == KC - 1) or not PE_LAZY_SIG)
                sg, bsg = sg_r.next()
                p.op("act", lambda e, sg=sg, gps=gps, n=n: e.activation(out=sg[:, :n], in_=gps[:, :n], func=AF.Silu),
                     reads=[bgps], writes=[bsg])
                p.op("dve", lambda e, sg=sg, ups=ups, act=act, fl=fl, t0=t0, n=n: e.tensor_tensor(
                    out=act[:, fl, t0:t0 + n], in0=ups[:, :n], in1=sg[:, :n], op=ALU.mult),
                    reads=[bups, bsg], writes=[bact])
        for dc in range(KC):
            for (t0, n, seg) in BLOCKS:
                yps, byps = yps_r.next()
                for fl in range(GRP):
                    p.op("pe", lambda e, yps=yps, wd=wd, fl=fl, dc=dc, act=act, t0=t0, n=n: e.matmul(
                        yps[:, :n], lhsT=wd[:, fl, dc * 128:(dc + 1) * 128], rhs=act[:, fl, t0:t0 + n],
                        start=(fl == 0), stop=(fl == GRP - 1)), reads=[bwd, bact], writes=[byps], sig=(fl == GRP - 1) or not PE_LAZY_SIG)
                p.op("dve", lambda e, yps=yps, dc=dc, t0=t0, n=n, seg=seg: e.scalar_tensor_tensor(
                    out=h[:, dc, t0:t0 + n], in0=yps[:, :n], scalar=G[:, dc, seg:seg + 1], in1=h[:, dc, t0:t0 + n],
                    op0=ALU.mult, op1=ALU.add), reads=[byps, bG, bh[dc]], writes=[bh[dc]])
            if on_final is not None and grp == FC // GRP - 1:
                on_final(dc)


NF_ADA = 4 * 9 * KC // NCORES


def build_ada():
    from contextlib import ExitStack
    nc = bass.Bass("TRN2", target_bir_lowering=False)
    with ExitStack() as stack:
        cx = Ctx(nc, stack)
        p = cx.p
        condT = cx.din("condT", [128, KC, 4])
        wa = cx.din("wa", [D, NF_ADA * 128])
        ba = cx.din("ba", [128, NF_ADA])
        mo = cx.dout("mo", [128, NF_ADA, 4])
        ct = cx.sb([128, KC, 4], F32)
        cs = cx.sb([128, KC, 4], BF16)
        bt = cx.sb([128, NF_ADA], F32)
        res = cx.sb([128, NF_ADA, 4], F32)
        bct, bcs, bbt, bres = Buf(), Buf(), Buf(), Buf()
        p.dma("sp", lambda e: e.dma_start(out=ct[:], in_=condT), writes=[bct])
        p.dma("sp", lambda e: e.dma_start(out=bt[:], in_=ba), writes=[bbt])
        p.op("act", lambda e: e.activation(out=cs[:], in_=ct[:], func=AF.Silu), reads=[bct], writes=[bcs])
        w_r = Ring(cx, 3, [128, KC, 512], BF16, "w")
        ps_r = Ring(cx, 2, [128, 4, 4], F32, "ps", psum=True)
        wav = wview(wa)
        for g4 in range(NF_ADA // 4):
            w, bw = w_r.next()
            p.dma("pool", lambda e, w=w, g4=g4: e.dma_start(out=w[:], in_=wav[:, :, g4 * 512:(g4 + 1) * 512]), writes=[bw])
            ps, bps = ps_r.next()
            for j in range(4):
                for kc in range(KC):
                    p.op("pe", lambda e, ps=ps, w=w, j=j, kc=kc: e.matmul(
                        ps[:, j, :], lhsT=w[:, kc, j * 128:(j + 1) * 128], rhs=cs[:, kc, :],
                        start=(kc == 0), stop=(kc == KC - 1)), reads=[bw, bcs], writes=[bps])
            for j in range(4):
                f = g4 * 4 + j
                p.op("dve", lambda e, ps=ps, j=j, f=f: e.tensor_scalar(out=res[:, f, :], in0=ps[:, j, :], scalar1=bt[:, f:f + 1],
                                                                      scalar2=None, op0=ALU.add),
                     reads=[bps, bbt], writes=[bres])
        tok = p.dma("sp", lambda e: e.dma_start(out=mo, in_=res[:]), reads=[bres])
        p.wait_tokens("sp", [tok])
        with nc.Block() as block:
            p.emit(block)
    return nc


def mm(p, out, lhsT, rhs, start, stop, reads, writes):
    return p.op("pe", lambda e: e.matmul(out, lhsT=lhsT, rhs=rhs, start=start, stop=stop), reads=reads, writes=writes,
                sig=bool(stop) or not PE_LAZY_SIG)


def actf(p, out, in_, func, reads, writes, bias=None, scale=None, accum_out=None):
    kw = {}
    if bias is not None:
        kw["bias"] = bias
    if scale is not None:
        kw["scale"] = scale
    if accum_out is not None:
        kw["accum_out"] = accum_out
    return p.op("act", lambda e: e.activation(out=out, in_=in_, func=func, **kw), reads=reads, writes=writes)


def tt(p, eng, out, in0, in1, op, reads, writes):
    return p.op(eng, lambda e: e.tensor_tensor(out=out, in0=in0, in1=in1, op=op), reads=reads, writes=writes)


def ts(p, eng, out, in0, s1, s2, op0, op1, reads, writes):
    if op1 is None:
        return p.op(eng, lambda e: e.tensor_scalar(out=out, in0=in0, scalar1=s1, scalar2=None, op0=op0), reads=reads, writes=writes)
    return p.op(eng, lambda e: e.tensor_scalar(out=out, in0=in0, scalar1=s1, scalar2=s2, op0=op0, op1=op1), reads=reads, writes=writes)


def stt(p, out, in0, scalar, in1, op0, op1, reads, writes):
    return p.op("dve", lambda e: e.scalar_tensor_tensor(out=out, in0=in0, scalar=scalar, in1=in1, op0=op0, op1=op1),
                reads=reads, writes=writes)


def dmaq(p, q, out, in_, reads=(), writes=()):
    return p.dma(q, lambda e: e.dma_start(out=out, in_=in_), reads=reads, writes=writes)


class Gelu:
    def __init__(self, cx, width=512):
        self.a = Ring(cx, 2, [128, width], F32, "gl_a")
        self.b = Ring(cx, 2, [128, width], F32, "gl_b")

    def __call__(self, p, out, ps, n, rd, wr, accum_sq=None):
        a, ba = self.a.next()
        b, bb = self.b.next()
        actf(p, a[:, :n], ps, AF.Square, rd, [ba])
        ts(p, "dve", a[:, :n], a[:, :n], 0.044715, 1.0, ALU.mult, ALU.add, [ba], [ba])
        tt(p, "dve", a[:, :n], a[:, :n], ps, ALU.mult, [ba] + rd, [ba])
        actf(p, b[:, :n], a[:, :n], AF.Sigmoid, [ba], [bb], scale=1.5957691216057308)
        tt(p, "dve", out, b[:, :n], ps, ALU.mult, [bb] + rd, wr)


def load_h(cx, hT, name="h"):
    p = cx.p
    h = cx.sb([128, KC, T], F32, name)
    bh = Buf(name)
    hv = hT.rearrange("(kc p) t -> p kc t", p=128)
    for q in range(4):
        dmaq(p, "sp", h[:, 4 * q:4 * q + 4, :], hv[:, 4 * q:4 * q + 4, :], writes=[bh])
    return h, bh


def store_h(cx, oT, h, bh):
    p = cx.p
    ov = oT.rearrange("(kc p) t -> p kc t", p=128)
    toks = [dmaq(p, "sp", ov[:, 4 * q:4 * q + 4, :], h[:, 4 * q:4 * q + 4, :], reads=[bh]) for q in range(4)]
    p.wait_tokens("sp", toks)


def rms_adaln_stream(cx, hT, dst, bdst, m, A, bm, bA, consts, rings, blocks=BLOCKS, nbuf=1):
    p = cx.p
    ones, epsc, bc = consts
    hbs = [cx.sb([128, KC, 512], F32, "hblk%d" % i) for i in range(nbuf)]
    bhs = [[Buf("hblk%d_%d" % (i, q)) for q in range(4)] for i in range(nbuf)]
    hv = hT.rearrange("(kc p) t -> p kc t", p=128)
    for bi, (t0, n, seg) in enumerate(blocks):
        hb, bq = hbs[bi % nbuf], bhs[bi % nbuf]
        for q in range(4):
            dmaq(p, "sp", hb[:, 4 * q:4 * q + 4, :n], hv[:, 4 * q:4 * q + 4, t0:t0 + n], writes=[bq[q]])
        bsrc = [bq[kc // 4] for kc in range(KC)]
        _rms_block(p, hb, bsrc, dst, bdst, t0, n, seg, m, A, bm, bA, ones, epsc, bc, *rings, s0=0)


def make_residual_evac(cx, hT, oT, m, bm, ring, toks, gate_idx=2):
    p = cx.p
    hv = hT.rearrange("(kc p) t -> p kc t", p=128)
    ov = oT.rearrange("(kc p) t -> p kc t", p=128)

    def evac(j, t0, n, seg, ps, bps):
        hb, bhb = ring.next()
        dmaq(p, "sp", hb[:, :n], hv[:, j, t0:t0 + n], writes=[bhb])
        stt(p, hb[:, :n], ps[:, :n], m[:, j, gate_idx, seg:seg + 1], hb[:, :n], ALU.mult, ALU.add, [bps, bm, bhb], [bhb])
        toks.append(dmaq(p, "sp", ov[:, j, t0:t0 + n], hb[:, :n], reads=[bhb]))

    return evac


def norm_rings(cx):
    return (Ring(cx, 2, [128, 512], F32, "sq"), Ring(cx, 1, [128, 512], F32, "ssps", psum=True),
            Ring(cx, 2, [128, 512], F32, "rs"), Ring(cx, 2, [128, 512], F32, "tmp"))


def linear_fm(cx, wv, col0, nchunks, x, bx, evac, w_r, ps_r, nk=KC, blocks=BLOCKS):
    p = cx.p
    w = bw = None
    for j in range(nchunks):
        if j % 2 == 0:
            w, bw = w_r.next()
            c = col0 + j * 128
            wd = 256 if j + 1 < nchunks else 128
            dmaq(p, "pool", w[:, :, :wd], wv[:, :, c:c + wd], writes=[bw])
        c0 = (j % 2) * 128
        for (t0, n, seg) in blocks:
            ps, bps = ps_r.next()
            for kc in range(nk):
                mm(p, ps[:, :n], w[:, kc, c0:c0 + 128], x[:, kc, t0:t0 + n], kc == 0, kc == nk - 1, [bw, bx], [bps])
            evac(j, t0, n, seg, ps, bps)


NT = T // 128


def build_mixa():
    from contextlib import ExitStack
    nc = bass.Bass("TRN2", target_bir_lowering=False)
    with ExitStack() as stack:
        cx = Ctx(nc, stack)
        p = cx.p
        hT = cx.din("hT", [D, T])
        modv = cx.din("modv", [128, KC, 3, 2])
        gv = cx.din("gv", [128, KC])
        w_in = cx.din("w_in", [D, 2 * D])
        vg = cx.din("vg", [128, KC])
        wsT = cx.din("wsT", [128, 16, 128])
        bsb = cx.din("bsb", [128, 16, 128])
        w_out = cx.din("w_out", [D, D])
        oT = cx.dout("oT", [D, T])

        consts = make_consts(cx)
        m, A, bm, bA = load_mod(cx, modv, gv)
        xn = cx.sb([128, KC, T], BF16, "xn")
        bxn = Buf("xn")
        nr = norm_rings(cx)
        rms_adaln_stream(cx, hT, xn, bxn, m, A, bm, bA, consts, nr)

        vgt = cx.sb([128, KC], F32, "vgt")
        wst = cx.sb([128, 16, 128], F32, "wst")
        bst = cx.sb([128, 16, 128], F32, "bst")
        bvg, bws, bbs = Buf(), Buf(), Buf()
        dmaq(p, "sp", vgt[:], vg, writes=[bvg])
        dmaq(p, "sp", wst[:], wsT, writes=[bws])
        dmaq(p, "sp", bst[:], bsb, writes=[bbs])

        gelu = Gelu(cx)
        w_r = Ring(cx, 2, [128, KC, 256], BF16, "w")
        ps_r = Ring(cx, 2, [128, 512], F32, "ps", psum=True)
        winv = wview(w_in)

        uT = cx.sb([128, KC, T], BF16, "uT")
        buT = Buf("uT")

        def evac_u(j, t0, n, seg, ps, bps):
            gelu(p, uT[:, j, t0:t0 + n], ps[:, :n], n, [bps], [buT])

        linear_fm(cx, winv, 0, KC, xn, bxn, evac_u, w_r, ps_r)

        v = cx.sb([128, NT, D], BF16, "v")
        bv = Buf("v")
        NCB = 8
        ssq = cx.sb([128, NT, NCB], F32, "ssq")
        bssq = Buf("ssq")
        for cb in range(NCB):
            wv_, bwv = w_r.next()
            dmaq(p, "pool", wv_[:, :, :], winv[:, :, D + cb * 256:D + (cb + 1) * 256], writes=[bwv])
            for ti in range(NT):
                ps, bps = ps_r.next()
                for kc in range(KC):
                    mm(p, ps[:, :256], xn[:, kc, ti * 128:(ti + 1) * 128], wv_[:, kc, :], kc == 0, kc == KC - 1, [bxn, bwv], [bps])
                vt, bvt = nr[3].next()
                gelu(p, vt[:, :256], ps[:, :256], 256, [bps], [bvt])
                junk, bjunk = nr[0].next()
                actf(p, junk[:, :256], vt[:, :256], AF.Square, [bvt], [bjunk, bssq], accum_out=ssq[:, ti, cb:cb + 1])
                p.op("pool", lambda e, vt=vt, ti=ti, cb=cb: e.tensor_copy(out=v[:, ti, cb * 256:(cb + 1) * 256], in_=vt[:, :256]),
                     reads=[bvt], writes=[bv])
        rv = cx.sb([128, NT], F32, "rv")
        brv = Buf("rv")
        p.op("dve", lambda e: e.tensor_reduce(out=rv[:, :], in_=ssq[:, :, :], axis=mybir.AxisListType.X, op=ALU.add),
             reads=[bssq], writes=[brv])
        actf(p, rv[:, :], rv[:, :], AF.Sqrt, [brv, consts[2]], [brv], bias=consts[1][:], scale=1.0 / D)
        p.op("dve", lambda e: e.reciprocal(out=rv[:, :], in_=rv[:, :]), reads=[brv], writes=[brv])

        z, bz = xn, bxn
        wp_r = Ring(cx, 3, [128, 128], BF16, "wp")
        st_r = Ring(cx, 2, [128, 128], F32, "st")
        sps_r = Ring(cx, 2, [128, 128], F32, "sps", psum=True)
        for ti in range(NT):
            for g in range(16):
                wp, bwp = wp_r.next()
                ts(p, "pool", wp[:, :], wst[:, g, :], rv[:, ti:ti + 1], None, ALU.mult, None, [bws, brv], [bwp])
                sp_, bsp = sps_r.next()
                mm(p, sp_[:, :], v[:, ti, g * 128:(g + 1) * 128], wp[:, :], True, True, [bv, bwp], [bsp])
                st, bst_ = st_r.next()
                stt(p, st[:, :], sp_[:, :], vgt[:, g:g + 1], bst[:, g, :], ALU.mult, ALU.add, [bsp, bvg, bbs], [bst_])
                tt(p, "dve", z[:, g, ti * 128:(ti + 1) * 128], st[:, :], uT[:, g, ti * 128:(ti + 1) * 128], ALU.mult,
                   [bst_, buT], [bz])

        woutv = wview(w_out)

        toks = []
        evac_o = make_residual_evac(cx, hT, oT, m, bm, nr[0], toks)
        linear_fm(cx, woutv, 0, KC, z, bz, evac_o, w_r, ps_r)
        p.wait_tokens("sp", toks)
        with nc.Block() as block:
            p.emit(block)
    return nc


HD = 128
NH = 16
ATT_SCALE = HD ** -0.5


def build_nat1():
    from contextlib import ExitStack
    nc = bass.Bass("TRN2", target_bir_lowering=False)
    with ExitStack() as stack:
        cx = Ctx(nc, stack)
        p = cx.p
        hT = cx.din("hT", [D, T])
        modv = cx.din("modv", [128, KC, 3, 2])
        gv = cx.din("gv", [128, KC])
        wqkv = cx.din("wqkv", [D, 3 * D])
        qkg = cx.din("qkg", [128, 2])
        qo = cx.dout("qo", [NH, 128, T], BF16)
        ko = cx.dout("ko", [NH, 128, T], BF16)
        vo = cx.dout("vo", [T, D], BF16)

        consts = make_consts(cx)
        ones, epsc, bc = consts
        m, A, bm, bA = load_mod(cx, modv, gv)
        xn = cx.sb([128, KC, T], BF16, "xn")
        bxn = Buf("xn")
        nr = norm_rings(cx)
        rms_adaln_stream(cx, hT, xn, bxn, m, A, bm, bA, consts, nr, nbuf=2)
        gq = cx.sb([128, 2], F32, "gq")
        bgq = Buf()
        dmaq(p, "sp", gq[:], qkg, writes=[bgq])
        ts(p, "dve", gq[:, 0:1], gq[:, 0:1], ATT_SCALE, None, ALU.mult, None, [bgq], [bgq])

        w_r = Ring(cx, 2, [128, KC, 256], BF16, "w")
        ps_r = Ring(cx, 2, [128, 512], F32, "ps", psum=True)
        wv = wview(wqkv)
        qf_r = Ring(cx, 2, [128, 512], F32, "qf")
        st_r = Ring(cx, 3, [128, 512], BF16, "stg")
        toks = []
        for which, dst in ((0, qo), (1, ko)):
            def evac(j, t0, n, seg, ps, bps, which=which, dst=dst):
                qf, bqf = qf_r.next()
                actf(p, qf[:, :n], ps[:, :n], AF.Identity, [bps], [bqf])
                sq, bsq = nr[0].next()
                actf(p, sq[:, :n], qf[:, :n], AF.Square, [bqf], [bsq])
                ss, bss = nr[1].next()
                mm(p, ss[:, :n], ones[:], sq[:, :n], True, True, [bsq, bc], [bss])
                rs, brs = nr[2].next()
                actf(p, rs[:, :n], ss[:, :n], AF.Sqrt, [bss, bc], [brs], bias=epsc[:], scale=1.0 / HD)
                p.op("dve", lambda e: e.reciprocal(out=rs[:, :n], in_=rs[:, :n]), reads=[brs], writes=[brs])
                sg, bsg = st_r.next()
                stt(p, sg[:, :n], qf[:, :n], gq[:, which:which + 1], rs[:, :n], ALU.mult, ALU.mult, [bqf, bgq, brs], [bsg])
                toks.append(dmaq(p, "sp", dst[j, :, t0:t0 + n], sg[:, :n], reads=[bsg]))

            linear_fm(cx, wv, which * D, KC, xn, bxn, evac, w_r, ps_r)
        vs = cx.sb([128, NT, D], BF16, "vs")
        bvs = Buf("vs")
        for cb in range(8):
            w, bw = w_r.next()
            dmaq(p, "pool", w[:, :, :], wv[:, :, 2 * D + cb * 256:2 * D + (cb + 1) * 256], writes=[bw])
            for ti in range(NT):
                ps, bps = ps_r.next()
                for kc in range(KC):
                    mm(p, ps[:, :256], xn[:, kc, ti * 128:(ti + 1) * 128], w[:, kc, :], kc == 0, kc == KC - 1, [bxn, bw], [bps])
                actf(p, vs[:, ti, cb * 256:(cb + 1) * 256], ps[:, :256], AF.Identity, [bps], [bvs])
        vov = vo.rearrange("(ti p) f -> p ti f", p=128)
        toks.append(dmaq(p, "sp", vov, vs[:], reads=[bvs]))
        p.wait_tokens("sp", toks)
        with nc.Block() as block:
            p.emit(block)
    return nc


NSLAB = 14
NKT = NSLAB + 2
NOFF = 7


def build_nat2():
    from contextlib import ExitStack
    nc = bass.Bass("TRN2", target_bir_lowering=False)
    with ExitStack() as stack:
        cx = Ctx(nc, stack)
        p = cx.p
        hT = cx.din("hT", [D, T])
        modv = cx.din("modv", [128, KC, 3, 2])
        qT = cx.din("qT", [NH, 128, T], BF16)
        kx = cx.din("kx", [NH, 128, NKT * 128], BF16)
        vx = cx.din("vx", [NKT * 128, D], BF16)
        bias = cx.din("bias", [NH, 128, 8 * NOFF, 128])
        w_out = cx.din("w_out", [D, D])
        oT = cx.dout("oT", [D, T])

        m = cx.sb([128, KC, 3, 2], F32, "m")
        bm = Buf("m")
        dmaq(p, "sp", m[:], modv, writes=[bm])
        onesb = cx.sb([128, 128], BF16, "onesb")
        bob = Buf("onesb")
        p.op("pool", lambda e: e.memset(onesb[:], 1.0), writes=[bob])

        q = cx.sb([128, NH, T], BF16, "q")
        bq = Buf("q")
        for hh in range(4):
            dmaq(p, "sp", q[:, 4 * hh:4 * hh + 4, :], qT.rearrange("h d t -> d h t")[:, 4 * hh:4 * hh + 4, :], writes=[bq])
        o = cx.sb([128, NH, T], BF16, "o")
        bo = Buf("o")

        kh_r = Ring(cx, 2, [128, NKT * 128], BF16, "kh")
        vh_r = Ring(cx, 2, [128, NKT, 128], BF16, "vh")
        bi_r = Ring(cx, 2, [128, 8 * NOFF, 128], F32, "bi")
        s_r = Ring(cx, 4, [128, 128], F32, "sps", psum=True)
        o_r = Ring(cx, 1, [128, 128], F32, "ops", psum=True)
        d_r = Ring(cx, 1, [128, 128], F32, "dps", psum=True)
        t_r = Ring(cx, 4, [128, 128], F32, "tmpa")
        p_r = Ring(cx, 5, [128, 128], BF16, "pa")
        r_r = Ring(cx, 2, [128, 128], F32, "rden")
        vxv = vx.rearrange("(kt p) f -> p kt f", p=128)
        for h in range(NH):
            kh, bkh = kh_r.next()
            vh, bvh = vh_r.next()
            bi, bbi = bi_r.next()
            dmaq(p, "sp", kh[:], kx[h], writes=[bkh])
            dmaq(p, "sp", vh[:], vxv[:, :, h * 128:(h + 1) * 128], writes=[bvh])
            for hf in range(2):
                dmaq(p, "sp", bi[:, hf * 28:(hf + 1) * 28, :], bias[h, :, hf * 28:(hf + 1) * 28, :], writes=[bbi])
            for i in range(NT):
                qs = q[:, h, i * 128:(i + 1) * 128]
                tiles = [(i + oo, oo) for oo in range(NOFF)] if i < 8 else []
                tiles += [(NSLAB, None), (NSLAB + 1, None)]
                ops, bops = o_r.next()
                dps, bdps = d_r.next()
                pend = []

                def score(j, oo):
                    sps, bsps = s_r.next()
                    mm(p, sps[:, :], kh[:, j * 128:(j + 1) * 128], qs, True, True, [bkh, bq], [bsps])
                    pa, bpa = p_r.next()
                    if oo is not None:
                        tm, btm = t_r.next()
                        tt(p, "dve", tm[:, :], sps[:, :], bi[:, i * NOFF + oo, :], ALU.add, [bsps, bbi], [btm])
                        actf(p, pa[:, :], tm[:, :], AF.Exp, [btm], [bpa])
                    else:
                        actf(p, pa[:, :], sps[:, :], AF.Exp, [bsps], [bpa])
                    return pa, bpa

                def accum(n_, j, pa, bpa):
                    first, last = n_ == 0, n_ == len(tiles) - 1
                    mm(p, ops[:, :], vh[:, j, :], pa[:, :], first, last, [bvh, bpa], [bops])
                    mm(p, dps[:, :], onesb[:], pa[:, :], first, last, [bob, bpa], [bdps])

                LOOK = 2
                for n_, (j, oo) in enumerate(tiles):
                    pend.append((n_, j) + score(j, oo))
                    if len(pend) > LOOK:
                        accum(*pend.pop(0))
                while pend:
                    accum(*pend.pop(0))
                rd, brd = r_r.next()
                p.op("dve", lambda e, rd=rd, dps=dps: e.reciprocal(out=rd[:, :], in_=dps[:, :]), reads=[bdps], writes=[brd])
                tt(p, "dve", o[:, h, i * 128:(i + 1) * 128], ops[:, :], rd[:, :], ALU.mult, [bops, brd], [bo])

        w_r = Ring(cx, 2, [128, KC, 256], BF16, "w")
        ps_r = Ring(cx, 2, [128, 512], F32, "ps", psum=True)
        hb_r = Ring(cx, 2, [128, 512], F32, "hb")
        toks = []
        evac_o = make_residual_evac(cx, hT, oT, m, bm, hb_r, toks)
        linear_fm(cx, wview(w_out), 0, KC, o, bo, evac_o, w_r, ps_r)
        p.wait_tokens("sp", toks)
        with nc.Block() as block:
            p.emit(block)
    return nc


import ml_dtypes
NPBF = ml_dtypes.bfloat16
GRID_W = 64
ROWS = 64
WIN_H, WIN_W = 8, 16
NEG = -1e30
_cache = {}
_DBG_SKIPW = False
_TRACE = False


def _prog(name, builder):
    if name not in _cache:
        _cache[name] = builder()
    return _cache[name]


def _run(name, builder, in_maps):
    nc = _prog(name, builder)
    if _TRACE:
        res = run_bass_kernel_spmd(nc, in_maps, core_ids=list(range(NCORES)), trace=True)
        print("KTRACE", name, res.exec_time_ns)
    else:
        res = run_bass_kernel_spmd(nc, in_maps, core_ids=list(range(NCORES)))
    return res.results


def fmaj(v):
    return np.ascontiguousarray(np.asarray(v).reshape(-1, 128).T)


def to_cores(x, ctx):
    outs = []
    for c in range(NCORES):
        b, k = divmod(c, 4)
        a = np.zeros((T, D), np.float32)
        a[:TL] = x[b, k * TL:(k + 1) * TL]
        if k < 2:
            a[TL:] = ctx[b, k * TCX:(k + 1) * TCX]
        outs.append(np.ascontiguousarray(a.T))
    return outs


def from_cores(hs):
    out = np.empty((2, 4096, D), np.float32)
    for c in range(NCORES):
        b, k = divmod(c, 4)
        out[b, k * TL:(k + 1) * TL] = hs[c][:, :TL].T
    return out


def modv_for(modT_layer, s, c):
    b = c // 4
    mv = np.empty((128, KC, 3, 2), np.float32)
    for j in range(3):
        blk = modT_layer[:, (3 * s + j) * KC:(3 * s + j + 1) * KC, :]
        mv[:, :, j, 0] = blk[:, :, b]
        mv[:, :, j, 1] = blk[:, :, 2]
    return mv


def run_ada(c, c_ctx, w_ada, b_ada):
    cond = np.zeros((4, D), np.float32)
    cond[0:2] = c
    cond[2] = c_ctx
    condT = np.ascontiguousarray(cond.T.reshape(KC, 128, 4).transpose(1, 0, 2))
    in_maps = []
    for core in range(NCORES):
        layer, half = divmod(core, 2)
        cols = slice(half * 9216, (half + 1) * 9216)
        in_maps.append({"condT": condT, "wa": np.ascontiguousarray(w_ada[layer][:, cols]),
                        "ba": np.ascontiguousarray(b_ada[layer][cols].reshape(72, 128).T)})
    res = _run("ada", build_ada, in_maps)
    modT = []
    for layer in range(4):
        modT.append(np.concatenate([res[2 * layer]["mo"], res[2 * layer + 1]["mo"]], axis=1))
    return modT


def run_ffn(hs, modT_l, s, g, wgu, wdn):
    gv = fmaj(g)
    in_maps = [{"hT": hs[c], "modv": modv_for(modT_l, s, c), "gv": gv, "wgu": wgu, "wdn": wdn} for c in range(NCORES)]
    res = _run("ffn", build_ffn, in_maps)
    return [r["oT"] for r in res]


def run_mixa(hs, modT_l, g, w_in, v_gain, w_s, b_s, w_out):
    gv = fmaj(g)
    wsT = np.ascontiguousarray(w_s.transpose(2, 0, 1))
    bsb = np.ascontiguousarray(np.broadcast_to(b_s[None], (128, 16, 128)))
    vg = fmaj(v_gain)
    in_maps = [{"hT": hs[c], "modv": modv_for(modT_l, 1, c), "gv": gv, "w_in": w_in, "vg": vg, "wsT": wsT, "bsb": bsb,
                "w_out": w_out} for c in range(NCORES)]
    res = _run("mixa", build_mixa, in_maps)
    return [r["oT"] for r in res]


def _nat_bias_index():
    if "natidx" in _cache:
        return _cache["natidx"]
    kpar = np.arange(128) // 64
    kcol = np.arange(128) % 64
    idx = np.full((4, 128, 8, NOFF, 128), 15 * 31, np.int32)
    qpar = kpar[None, :]
    qcol = kcol[None, :]
    cstart = np.clip(qcol - WIN_W // 2, 0, GRID_W - WIN_W)
    colv = (kcol[:, None] >= cstart) & (kcol[:, None] < cstart + WIN_W)
    dc = np.clip(kcol[:, None] - qcol, 1 - WIN_W, WIN_W - 1) + (WIN_W - 1)
    for kq in range(4):
        r0 = 16 * kq
        for i in range(8):
            for o in range(NOFF):
                kr = r0 - 6 + 2 * (i + o) + kpar[:, None]
                qr = r0 + 2 * i + qpar
                rstart = np.clip(qr - WIN_H // 2, 0, ROWS - WIN_H)
                valid = (kr >= 0) & (kr < ROWS) & (kr >= rstart) & (kr < rstart + WIN_H) & colv
                dr = kr - qr + (WIN_H - 1)
                lin = np.clip(dr, 0, 14) * 31 + dc
                idx[kq, :, i, o, :] = np.where(valid, lin, 15 * 31)
    idx = idx.reshape(4, 128, 8 * NOFF, 128)
    _cache["natidx"] = idx
    return idx


def run_nat(hs, modT_l, g, w_qkv, q_gain, k_gain, rpb, w_out):
    gv = fmaj(g)
    qkg = np.ascontiguousarray(np.stack([q_gain, k_gain], axis=1).astype(np.float32))
    in_maps = [{"hT": hs[c], "modv": modv_for(modT_l, 1, c), "gv": gv, "wqkv": w_qkv, "qkg": qkg} for c in range(NCORES)]
    r1 = _run("nat1", build_nat1, in_maps)
    idx = _nat_bias_index()
    rp = np.concatenate([rpb.reshape(NH, -1), np.full((NH, 1), NEG, np.float32)], axis=1)
    in_maps = []
    for c in range(NCORES):
        b, k = divmod(c, 4)
        Kb = np.concatenate([np.asarray(r1[4 * b + kk]["ko"])[:, :, :TL] for kk in range(4)], axis=2)
        Vb = np.concatenate([np.asarray(r1[4 * b + kk]["vo"])[:TL] for kk in range(4)], axis=0)
        Kc = np.concatenate([np.asarray(r1[4 * b + kk]["ko"])[:, :, TL:] for kk in range(2)], axis=2)
        Vc = np.concatenate([np.asarray(r1[4 * b + kk]["vo"])[TL:] for kk in range(2)], axis=0)
        kx = np.zeros((NH, 128, NKT * 128), NPBF)
        vx = np.zeros((NKT * 128, D), NPBF)
        lo = (16 * k - 6) * 64
        hi = lo + NSLAB * 128
        a, bnd = max(lo, 0), min(hi, 4096)
        kx[:, :, a - lo:bnd - lo] = Kb[:, :, a:bnd]
        vx[a - lo:bnd - lo] = Vb[a:bnd]
        kx[:, :, NSLAB * 128:] = Kc
        vx[NSLAB * 128:] = Vc
        bias = np.ascontiguousarray(rp[:, idx[k]])
        in_maps.append({"hT": hs[c], "modv": modv_for(modT_l, 1, c), "qT": np.asarray(r1[c]["qo"]), "kx": kx, "vx": vx,
                        "bias": bias, "w_out": w_out})
    r2 = _run("nat2", build_nat2, in_maps)
    return [r["oT"] for r in r2]


def build_s5a():
    from contextlib import ExitStack
    nc = bass.Bass("TRN2", target_bir_lowering=False)
    with ExitStack() as stack:
        cx = Ctx(nc, stack)
        p = cx.p
        hT = cx.din("hT", [D, T])
        modv = cx.din("modv", [128, KC, 3, 2])
        gv = cx.din("gv", [128, KC])
        w_in = cx.din("w_in", [D, D])
        uo = cx.dout("uo", [D, T])
        consts = make_consts(cx)
        m, A, bm, bA = load_mod(cx, modv, gv)
        xn = cx.sb([128, KC, T], BF16, "xn")
        bxn = Buf("xn")
        nr = norm_rings(cx)
        rms_adaln_stream(cx, hT, xn, bxn, m, A, bm, bA, consts, nr, nbuf=2)
        w_r = Ring(cx, 2, [128, KC, 256], BF16, "w")
        ps_r = Ring(cx, 2, [128, 512], F32, "ps", psum=True)
        uov = uo.rearrange("(kc p) t -> p kc t", p=128)
        toks = []

        def evac(j, t0, n, seg, ps, bps):
            sg, bsg = nr[0].next()
            actf(p, sg[:, :n], ps[:, :n], AF.Identity, [bps], [bsg])
            toks.append(dmaq(p, "sp", uov[:, j, t0:t0 + n], sg[:, :n], reads=[bsg]))

        linear_fm(cx, wview(w_in), 0, KC, xn, bxn, evac, w_r, ps_r)
        p.wait_tokens("sp", toks)
        with nc.Block() as block:
            p.emit(block)
    return nc


NPOS = 256 + 4096
S5_NOSYNC = ("dve", "pool")
S5_SPLIT = 32
SB = 128
NBLK = NPOS // SB
GL = 16


def build_s5b():
    from contextlib import ExitStack
    nc = bass.Bass("TRN2", target_bir_lowering=False)
    with ExitStack() as stack:
        cx = Ctx(nc, stack)
        p = cx.p
        U = cx.din("U", [32, GL, 2, NPOS])
        areT = cx.din("areT", [128, GL])
        aimT = cx.din("aimT", [128, GL])
        ldtT = cx.din("ldtT", [128, GL])
        breT = cx.din("breT", [128, GL, 16])
        bimT = cx.din("bimT", [128, GL, 16])
        creT = cx.din("creT", [128, GL, 16])
        cimT = cx.din("cimT", [128, GL, 16])
        ident = cx.din("ident", [128, 128])
        YF = cx.dout("YF", [2, 128, 2, NPOS])
        YB = cx.dout("YB", [2, 128, 2, NPOS])

        def small(shape, name):
            return cx.sb(shape, F32, name), Buf(name)

        def load(src, shape, name):
            t, b = small(shape, name)
            dmaq(p, "sp", t[:], src, writes=[b])
            return t, b

        are, b_are = load(areT, [128, GL], "are")
        aim, b_aim = load(aimT, [128, GL], "aim")
        ldt, b_ldt = load(ldtT, [128, GL], "ldt")
        bre, b_bre = load(breT, [128, GL, 16], "bre")
        bim, b_bim = load(bimT, [128, GL, 16], "bim")
        cre, b_cre = load(creT, [128, GL, 16], "cre")
        cim, b_cim = load(cimT, [128, GL, 16], "cim")
        idt, b_idt = load(ident, [128, 128], "idt")

        dt_, b_dt = small([128, GL], "dt")
        actf(p, dt_[:], ldt[:], AF.Exp, [b_ldt], [b_dt])
        xr, b_xr = small([128, GL], "xr")
        xi, b_xi = small([128, GL], "xi")
        tt(p, "dve", xr[:], are[:], dt_[:], ALU.mult, [b_are, b_dt], [b_xr])
        tt(p, "dve", xi[:], aim[:], dt_[:], ALU.mult, [b_aim, b_dt], [b_xi])
        mag, b_mag = small([128, GL], "mag")
        actf(p, mag[:], xr[:], AF.Exp, [b_xr], [b_mag])
        sn, b_sn = small([128, GL], "sn")
        cs, b_cs = small([128, GL], "cs")
        t1, b_t1 = small([128, GL], "t1")
        t2, b_t2 = small([128, GL], "t2")
        actf(p, sn[:], xi[:], AF.Sin, [b_xi], [b_sn], scale=1.0 / 16)
        actf(p, t1[:], xi[:], AF.Sin, [b_xi], [b_t1], scale=1.0 / 32)
        tt(p, "dve", t1[:], t1[:], t1[:], ALU.mult, [b_t1], [b_t1])
        ts(p, "dve", cs[:], t1[:], -2.0, 1.0, ALU.mult, ALU.add, [b_t1], [b_cs])
        for _ in range(4):
            tt(p, "dve", t1[:], cs[:], cs[:], ALU.mult, [b_cs], [b_t1])
            tt(p, "dve", t2[:], sn[:], sn[:], ALU.mult, [b_sn], [b_t2])
            tt(p, "dve", sn[:], sn[:], cs[:], ALU.mult, [b_sn, b_cs], [b_sn])
            ts(p, "dve", sn[:], sn[:], 2.0, None, ALU.mult, None, [b_sn], [b_sn])
            tt(p, "dve", cs[:], t1[:], t2[:], ALU.subtract, [b_t1, b_t2], [b_cs])
        abr, b_abr = small([128, GL], "abr")
        abi, b_abi = small([128, GL], "abi")
        tt(p, "dve", abr[:], mag[:], cs[:], ALU.mult, [b_mag, b_cs], [b_abr])
        tt(p, "dve", abi[:], mag[:], sn[:], ALU.mult, [b_mag, b_sn], [b_abi])
        nr_, b_nr = small([128, GL], "nr")
        ts(p, "dve", nr_[:], abr[:], -1.0, None, ALU.add, None, [b_abr], [b_nr])
        den, b_den = small([128, GL], "den")
        tt(p, "dve", den[:], are[:], are[:], ALU.mult, [b_are], [b_den])
        tt(p, "dve", t1[:], aim[:], aim[:], ALU.mult, [b_aim], [b_t1])
        tt(p, "dve", den[:], den[:], t1[:], ALU.add, [b_den, b_t1], [b_den])
        p.op("dve", lambda e: e.reciprocal(out=den[:], in_=den[:]), reads=[b_den], writes=[b_den])
        kr, b_kr = small([128, GL], "kr")
        ki, b_ki = small([128, GL], "ki")
        tt(p, "dve", kr[:], nr_[:], are[:], ALU.mult, [b_nr, b_are], [b_kr])
        tt(p, "dve", t1[:], abi[:], aim[:], ALU.mult, [b_abi, b_aim], [b_t1])
        tt(p, "dve", kr[:], kr[:], t1[:], ALU.add, [b_kr, b_t1], [b_kr])
        tt(p, "dve", kr[:], kr[:], den[:], ALU.mult, [b_kr, b_den], [b_kr])
        tt(p, "dve", ki[:], abi[:], are[:], ALU.mult, [b_abi, b_are], [b_ki])
        tt(p, "dve", t1[:], nr_[:], aim[:], ALU.mult, [b_nr, b_aim], [b_t1])
        tt(p, "dve", ki[:], ki[:], t1[:], ALU.subtract, [b_ki, b_t1], [b_ki])
        tt(p, "dve", ki[:], ki[:], den[:], ALU.mult, [b_ki, b_den], [b_ki])
        nki, b_nki = small([128, GL], "nki")
        ts(p, "dve", nki[:], ki[:], -1.0, None, ALU.mult, None, [b_ki], [b_nki])
        Bw, b_Bw = small([128, GL, 2, 32], "Bw")
        p.op("pool", lambda e: e.memset(Bw[:], 0.0), writes=[b_Bw])
        tb, b_tb = small([128, 16], "tb")
        for g in range(GL):
            for d in range(2):
                ps_ = slice(64 * d, 64 * d + 64)
                cols = slice(16 * d, 16 * d + 16)
                ts(p, "dve", tb[ps_, :], bim[ps_, g, :], nki[ps_, g:g + 1], None, ALU.mult, None, [b_bim, b_nki], [b_tb])
                stt(p, Bw[ps_, g, 0, cols], bre[ps_, g, :], kr[ps_, g:g + 1], tb[ps_, :], ALU.mult, ALU.add, [b_bre, b_kr, b_tb], [b_Bw])
                ts(p, "dve", tb[ps_, :], bre[ps_, g, :], ki[ps_, g:g + 1], None, ALU.mult, None, [b_bre, b_ki], [b_tb])
                stt(p, Bw[ps_, g, 1, cols], bim[ps_, g, :], kr[ps_, g:g + 1], tb[ps_, :], ALU.mult, ALU.add, [b_bim, b_kr, b_tb], [b_Bw])
        Blk = cx.sb([32, GL, 2, 128], BF16, "Blk")
        b_Blk = Buf("Blk")
        tp_r = Ring(cx, 2, [32, 128], F32, "tps", psum=True)
        for g in range(GL):
            for ri in range(2):
                tp, btp = tp_r.next()
                p.op("pe", lambda e, tp=tp, g=g, ri=ri: e.transpose(tp[:, :], Bw[:, g, ri, :], idt[:]), reads=[b_Bw, b_idt], writes=[btp])
                actf(p, Blk[:, g, ri, :], tp[:, :], AF.Identity, [btp], [b_Blk])
        Cp = cx.sb([128, GL, 2, 128], BF16, "Cp")
        b_Cp = Buf("Cp")
        p.op("pool", lambda e: e.memset(Cp[:], 0.0), writes=[b_Cp])
        for g in range(GL):
            c0 = (g % 8) * 16
            p.op("pool", lambda e, g=g, c0=c0: e.tensor_copy(out=Cp[:, g, 0, c0:c0 + 16], in_=cre[:, g, :]), reads=[b_cre], writes=[b_Cp])
            ts(p, "pool", Cp[:, g, 1, c0:c0 + 16], cim[:, g, :], -1.0, None, ALU.mult, None, [b_cim], [b_Cp])
        Ar, b_Ar = small([128, 32, 2], "Ar")
        AiS, b_AiS = small([128, 32, 2], "AiS")
        nabi, b_nabi = small([128, GL], "nabi")
        ts(p, "dve", nabi[:], abi[:], -1.0, None, ALU.mult, None, [b_abi], [b_nabi])
        Ar4 = Ar[:].rearrange("p (g b) r -> p g b r", b=2)
        Ai4 = AiS[:].rearrange("p (g b) r -> p g b r", b=2)
        for b in range(2):
            for ri in range(2):
                p.op("pool", lambda e, b=b, ri=ri: e.tensor_copy(out=Ar4[:, :, b, ri], in_=abr[:]), reads=[b_abr], writes=[b_Ar])
            p.op("pool", lambda e, b=b: e.tensor_copy(out=Ai4[:, :, b, 0], in_=nabi[:]), reads=[b_nabi], writes=[b_AiS])
            p.op("pool", lambda e, b=b: e.tensor_copy(out=Ai4[:, :, b, 1], in_=abi[:]), reads=[b_abi], writes=[b_AiS])

        ub_r = Ring(cx, 2, [32, GL, 2, SB], BF16, "ub")
        bu_r = Ring(cx, 2, [128, SB, 32, 2], F32, "bu")
        H_r = Ring(cx, 2, [128, SB, 32, 2], F32, "H")
        Hb_r = Ring(cx, 2, [128, SB, 32, 2], BF16, "Hb")
        dps_r = Ring(cx, 2, [128, 4, SB], F32, "dps", psum=True)
        yps_r = Ring(cx, 2, [128, SB], F32, "yps", psum=True)
        ys_r = Ring(cx, 3, [128, SB], F32, "ys")
        zero, b_zero = small([128, 32, 2], "zero")
        p.op("pool", lambda e: e.memset(zero[:], 0.0), writes=[b_zero])
        lanes = []
        for eng, sl in (("dve", slice(0, S5_SPLIT)), ("pool", slice(S5_SPLIT, 32))):
            n_ = sl.stop - sl.start
            if n_ <= 0:
                continue
            m1, b_m1 = small([128, n_, 2], "m1" + eng)
            m2, b_m2 = small([128, n_, 2], "m2" + eng)
            lanes.append(dict(eng=eng, sl=sl, m1=m1, b_m1=b_m1, m2=m2, b_m2=b_m2, prev=zero[:, sl, :], b_prev=b_zero))
        toks = []
        for blk in range(NBLK):
            k0 = blk * SB
            ub, b_ub = ub_r.next()
            dmaq(p, "pool", ub[:], U[:, :, :, k0:k0 + SB], writes=[b_ub])
            bu, b_bu = bu_r.next()
            bu_v = bu[:].rearrange("p k c r -> p k (c r)")
            for q4 in range(GL):
                dps, b_dps = dps_r.next()
                for b in range(2):
                    for ri in range(2):
                        mm(p, dps[:, b * 2 + ri, :], Blk[:, q4, ri, :], ub[:, q4, b, :], True, True, [b_Blk, b_ub], [b_dps])
                actf(p, bu_v[:, :, q4 * 4:q4 * 4 + 4], dps[:].rearrange("p c k -> p k c"), AF.Identity, [b_dps], [b_bu])
            H, b_H0 = H_r.next()
            b_Hl = [Buf("Hl%d" % li) for li in range(len(lanes))]
            p.nosync = set(S5_NOSYNC) | set(GLOBAL_NOSYNC)
            for k in range(SB):
                for li, L in enumerate(lanes):
                    eng, sl, m1, m2 = L["eng"], L["sl"], L["m1"], L["m2"]
                    pv, bpv = L["prev"], L["b_prev"]
                    wr = [b_Hl[li]] + ([b_H0] if k == 0 else [])
                    tt(p, eng, m1[:], Ar[:, sl, :], pv, ALU.mult, [b_Ar, bpv], [L["b_m1"]])
                    tt(p, eng, m2[:], AiS[:, sl, :], pv[:, :, ::-1], ALU.mult, [b_AiS, bpv], [L["b_m2"]])
                    tt(p, eng, m1[:], m1[:], m2[:], ALU.add, [L["b_m1"], L["b_m2"]], [L["b_m1"]])
                    tt(p, eng, H[:, k, sl, :], m1[:], bu[:, k, sl, :], ALU.add, [L["b_m1"], b_bu], wr)
                    L["prev"], L["b_prev"] = H[:, k, sl, :], b_Hl[li]
            p.nosync = set(GLOBAL_NOSYNC)
            Hb, b_Hb = Hb_r.next()
            actf(p, Hb[:].rearrange("p k c r -> p (k c r)"), H[:].rearrange("p k c r -> p (k c r)"), AF.Identity, b_Hl + [b_H0], [b_Hb, b_H0])
            for cc in range(2):
                for b in range(2):
                    for d in range(2):
                        ps_ = slice(64 * d, 64 * d + 64)
                        yps, b_yps = yps_r.next()
                        n_ = 0
                        for g8 in range(8):
                            g = cc * 8 + g8
                            for ri in range(2):
                                mm(p, yps[:, :], Cp[ps_, g, ri, :], Hb[ps_, :, g * 2 + b, ri], n_ == 0, n_ == 15, [b_Cp, b_Hb], [b_yps])
                                n_ += 1
                        ys, b_ys = ys_r.next()
                        actf(p, ys[:, :], yps[:, :], AF.Identity, [b_yps], [b_ys])
                        dst = (YF if d == 0 else YB)
                        toks.append(dmaq(p, "sp", dst[cc, :, b, k0:k0 + SB], ys[:, :], reads=[b_ys]))
        p.wait_tokens("sp", toks)
        with nc.Block() as block:
            p.emit(block)
    return nc


def build_s5c():
    from contextlib import ExitStack
    nc = bass.Bass("TRN2", target_bir_lowering=False)
    with ExitStack() as stack:
        cx = Ctx(nc, stack)
        p = cx.p
        hT = cx.din("hT", [D, T])
        modv = cx.din("modv", [128, KC, 3, 2])
        uT = cx.din("uT", [D, T])
        yfT = cx.din("yfT", [D, T])
        ybT = cx.din("ybT", [D, T])
        dsk = cx.din("dsk", [128, KC])
        w_glu = cx.din("w_glu", [D, 2 * D])
        oT = cx.dout("oT", [D, T])
        m = cx.sb([128, KC, 3, 2], F32, "m")
        bm = Buf("m")
        dmaq(p, "sp", m[:], modv, writes=[bm])
        dk = cx.sb([128, KC], F32, "dk")
        bdk = Buf("dk")
        dmaq(p, "sp", dk[:], dsk, writes=[bdk])
        gl = cx.sb([128, KC, T], BF16, "gl")
        bgl = Buf("gl")
        a_r = Ring(cx, 2, [128, T], F32, "ya")
        b_r = Ring(cx, 2, [128, T], F32, "yb")
        c_r = Ring(cx, 2, [128, T], F32, "yc")
        gelu = Gelu(cx, width=T)
        uv = uT.rearrange("(kc p) t -> p kc t", p=128)
        fv = yfT.rearrange("(kc p) t -> p kc t", p=128)
        bv = ybT.rearrange("(kc p) t -> p kc t", p=128)
        for kc in range(KC):
            ya, bya = a_r.next()
            yb, byb = b_r.next()
            yc, byc = c_r.next()
            dmaq(p, "sp", ya[:], uv[:, kc, :], writes=[bya])
            dmaq(p, "sp", yb[:], fv[:, kc, :], writes=[byb])
            dmaq(p, "sp", yc[:], bv[:, kc, :], writes=[byc])
            tt(p, "dve", yb[:], yb[:], yc[:], ALU.add, [byb, byc], [byb])
            stt(p, ya[:], ya[:], dk[:, kc:kc + 1], yb[:], ALU.mult, ALU.add, [bya, bdk, byb], [bya])
            gelu(p, gl[:, kc, :], ya[:], T, [bya], [bgl])
        w_r = Ring(cx, 2, [128, KC, 128], BF16, "wa")
        w2_r = Ring(cx, 2, [128, KC, 128], BF16, "wg")
        pa_r = Ring(cx, 2, [128, 512], F32, "pa", psum=True)
        pg_r = Ring(cx, 2, [128, 512], F32, "pg", psum=True)
        sg_r = Ring(cx, 2, [128, 512], F32, "sg")
        hb_r = Ring(cx, 2, [128, 512], F32, "hb")
        wv = wview(w_glu)
        hv = hT.rearrange("(kc p) t -> p kc t", p=128)
        ov = oT.rearrange("(kc p) t -> p kc t", p=128)
        toks = []
        for j in range(KC):
            wa, bwa = w_r.next()
            wg, bwg = w2_r.next()
            dmaq(p, "pool", wa[:], wv[:, :, j * 128:(j + 1) * 128], writes=[bwa])
            dmaq(p, "pool", wg[:], wv[:, :, D + j * 128:D + (j + 1) * 128], writes=[bwg])
            for (t0, n, seg) in BLOCKS:
                pa, bpa = pa_r.next()
                pg, bpg = pg_r.next()
                for kc in range(KC):
                    mm(p, pa[:, :n], wa[:, kc, :], gl[:, kc, t0:t0 + n], kc == 0, kc == KC - 1, [bwa, bgl], [bpa])
                for kc in range(KC):
                    mm(p, pg[:, :n], wg[:, kc, :], gl[:, kc, t0:t0 + n], kc == 0, kc == KC - 1, [bwg, bgl], [bpg])
                sg, bsg = sg_r.next()
                actf(p, sg[:, :n], pg[:, :n], AF.Sigmoid, [bpg], [bsg])
                tt(p, "dve", sg[:, :n], pa[:, :n], sg[:, :n], ALU.mult, [bpa, bsg], [bsg])
                hb, bhb = hb_r.next()
                dmaq(p, "sp", hb[:, :n], hv[:, j, t0:t0 + n], writes=[bhb])
                stt(p, hb[:, :n], sg[:, :n], m[:, j, 2, seg:seg + 1], hb[:, :n], ALU.mult, ALU.add, [bsg, bm, bhb], [bhb])
                toks.append(dmaq(p, "sp", ov[:, j, t0:t0 + n], hb[:, :n], reads=[bhb]))
        p.wait_tokens("sp", toks)
        with nc.Block() as block:
            p.emit(block)
    return nc


def run_s5(hs, modT_l, g, w_in, a_re, a_im, log_dt, b_re, b_im, c_re, c_im, d_skip, w_glu):
    gv = fmaj(g)
    in_maps = [{"hT": hs[c], "modv": modv_for(modT_l, 1, c), "gv": gv, "w_in": w_in} for c in range(NCORES)]
    r1 = _run("s5a", build_s5a, in_maps)
    us = [r["uo"] for r in r1]
    seq = []
    for b in range(2):
        parts = [us[4 * b + k][:, TL:] for k in range(2)] + [us[4 * b + k][:, :TL] for k in range(4)]
        seq.append(np.concatenate(parts, axis=1))
    seq = np.stack(seq, axis=1)
    order_b = np.concatenate([np.arange(255, -1, -1), 256 + np.arange(4095, -1, -1)])
    ident = np.eye(128, dtype=np.float32)
    in_maps = []
    for c in range(NCORES):
        gs = slice(GL * c, GL * (c + 1))
        sc = seq[256 * c:256 * (c + 1)].reshape(GL, 16, 2, NPOS)
        Uc = np.empty((32, GL, 2, NPOS), np.float32)
        Uc[:16] = sc.transpose(1, 0, 2, 3)
        Uc[16:] = sc[:, :, :, order_b].transpose(1, 0, 2, 3)

        def dp(a):
            a = a[:, gs]
            if a.ndim == 3:
                return np.ascontiguousarray(a.transpose(0, 2, 1).reshape(128, GL))
            return np.ascontiguousarray(a.transpose(0, 2, 1, 3).reshape(128, GL, a.shape[3]))

        ldt = np.ascontiguousarray(np.broadcast_to(log_dt[:, gs][:, None, :], (2, 64, GL)).reshape(128, GL))
        in_maps.append({"U": Uc, "areT": dp(a_re), "aimT": dp(a_im), "ldtT": ldt,
                        "breT": dp(b_re), "bimT": dp(b_im),
                        "creT": dp(c_re.transpose(0, 1, 3, 2)), "cimT": dp(c_im.transpose(0, 1, 3, 2)), "ident": ident})
    r2 = _run("s5b", build_s5b, in_maps)
    YF = np.concatenate([r["YF"].reshape(256, 2, NPOS) for r in r2], axis=0)
    YBo = np.concatenate([r["YB"].reshape(256, 2, NPOS) for r in r2], axis=0)
    YB = np.empty_like(YBo)
    YB[:, :, order_b] = YBo

    def percore(Y, c):
        b, k = divmod(c, 4)
        a = np.zeros((D, T), np.float32)
        a[:, :TL] = Y[:, b, 256 + k * TL:256 + (k + 1) * TL]
        if k < 2:
            a[:, TL:] = Y[:, b, k * TCX:(k + 1) * TCX]
        return a

    dsk = fmaj(d_skip)
    in_maps = [{"hT": hs[c], "modv": modv_for(modT_l, 1, c), "uT": us[c], "yfT": percore(YF, c), "ybT": percore(YB, c),
                "dsk": dsk, "w_glu": w_glu} for c in range(NCORES)]
    r3 = _run("s5c", build_s5c, in_maps)
    return [r["oT"] for r in r3]


def kernel(x, c, ctx, c_ctx, w_ada, b_ada, norm_g, ffn_w_gu, ffn_w_down,
           a_w_in, a_v_gain, a_w_s, a_b_s, a_w_out,
           b_w_qkv, b_q_gain, b_k_gain, b_rpb, b_w_out,
           c_w_in, c_a_re, c_a_im, c_log_dt, c_b_re, c_b_im, c_c_re, c_c_im, c_d, c_w_glu):
    f = lambda a: np.asarray(a, dtype=np.float32)
    x, c, ctx, c_ctx = f(x), f(c), f(ctx), f(c_ctx)
    modT = run_ada(c, c_ctx, f(w_ada), f(b_ada))
    hs = to_cores(x, ctx)
    depth = 4
    hs = run_ffn(hs, modT[0], 0, f(norm_g[0, 0]), f(ffn_w_gu[0, 0]), f(ffn_w_down[0, 0]))
    for i in range(depth):
        kind, j = i % 3, i // 3
        if kind == 0:
            hs = run_mixa(hs, modT[i], f(norm_g[i, 1]), f(a_w_in[j]), f(a_v_gain[j]), f(a_w_s[j]), f(a_b_s[j]), f(a_w_out[j]))
        elif kind == 1:
            hs = run_nat(hs, modT[i], f(norm_g[i, 1]), f(b_w_qkv[j]), f(b_q_gain[j]), f(b_k_gain[j]), f(b_rpb[j]), f(b_w_out[j]))
        else:
            hs = run_s5(hs, modT[i], f(norm_g[i, 1]), f(c_w_in[j]), f(c_a_re[j]), f(c_a_im[j]), f(c_log_dt[j]),
                        f(c_b_re[j]), f(c_b_im[j]), f(c_c_re[j]), f(c_c_im[j]), f(c_d[j]), f(c_w_glu[j]))
        if i + 1 < depth:
            hs = run_ffn2(hs, modT[i], 2, f(norm_g[i, 2]), f(ffn_w_gu[i, 1]), f(ffn_w_down[i, 1]),
                          modT[i + 1], 0, f(norm_g[i + 1, 0]), f(ffn_w_gu[i + 1, 0]), f(ffn_w_down[i + 1, 0]))
        else:
            hs = run_ffn(hs, modT[i], 2, f(norm_g[i, 2]), f(ffn_w_gu[i, 1]), f(ffn_w_down[i, 1]))
    return from_cores(hs)


def build_ffn2():
    from contextlib import ExitStack
    nc = bass.Bass("TRN2", target_bir_lowering=False)
    with ExitStack() as stack:
        cx = Ctx(nc, stack)
        p = cx.p
        hT = cx.din("hT", [D, T])
        io = []
        for tag in ("a", "b"):
            io.append((cx.din("modv_" + tag, [128, KC, 3, 2]), cx.din("gv_" + tag, [128, KC]),
                       cx.din("wgu_" + tag, [D, 2 * DFF]), cx.din("wdn_" + tag, [DFF, D])))
        oT = cx.dout("oT", [D, T])
        consts = make_consts(cx)
        h = cx.sb([128, KC, T], F32, "h")
        bh = [Buf("h%d" % k) for k in range(KC)]
        hv = hT.rearrange("(kc p) t -> p kc t", p=128)
        for k in range(KC):
            dmaq(p, "sp", h[:, k, :], hv[:, k, :], writes=[bh[k]])
        xn = cx.sb([128, KC, T], BF16, "xn")
        bxn = Buf("xn")
        nrings = norm_rings(cx)
        ov = oT.rearrange("(kc p) t -> p kc t", p=128)
        toks = []

        def store_dc(dc):
            toks.append(dmaq(p, "sp", ov[:, dc, :], h[:, dc, :], reads=[bh[dc]]))

        rings = None
        G = cx.sb([128, KC, 2], F32, "G")
        bG = Buf("G")
        pre = None
        for idx, (modv, gv, wgu, wdn) in enumerate(io):
            m, A, bm, bA = load_mod(cx, modv, gv, pre=pre)
            pre = LOAD_MOD_LAST[0]
            ts(p, "dve", G[:], m[:, :, 2, :], 0.5, None, ALU.mult, None, [bm], [bG])
            rms_adaln(cx, h, bh, xn, bxn, BLOCKS, m, A, bm, bA, consts, nrings)
            ffn_core(cx, xn, bxn, h, bh, G, bG, wgu, wdn, on_final=(store_dc if idx == 1 else None), rings=rings)
            rings = FFN_LAST_RINGS[0]
        p.wait_tokens("sp", toks)
        with nc.Block() as block:
            p.emit(block)
    return nc


def run_ffn2(hs, modT_a, s_a, g_a, wgu_a, wdn_a, modT_b, s_b, g_b, wgu_b, wdn_b):
    ga, gb = fmaj(g_a), fmaj(g_b)
    in_maps = [{"hT": hs[c], "modv_a": modv_for(modT_a, s_a, c), "gv_a": ga, "wgu_a": wgu_a, "wdn_a": wdn_a,
                "modv_b": modv_for(modT_b, s_b, c), "gv_b": gb, "wgu_b": wgu_b, "wdn_b": wdn_b} for c in range(NCORES)]
    res = _run("ffn2", build_ffn2, in_maps)
    return [r["oT"] for r in res]
```
